# Optimizing a Trainium2 kernel written in Bass

```python
import jax, jax.numpy as jnp
from jax import lax
import numpy as np

D_MODEL = 1024
BATCH = 8
SEQ = 4096
DEPTH = 4

N_MIXERS = 2
N_ATTN_LAYERS = (DEPTH + 1) // 2
N_HGRN_LAYERS = DEPTH // 2
HEAD_DIM = 64
N_Q_HEADS = D_MODEL // HEAD_DIM
N_KV_HEADS = N_Q_HEADS // 4
Q_PER_KV = N_Q_HEADS // N_KV_HEADS
WINDOW = 128
ATTN_BLOCK = WINDOW
ATTN_IN = (N_Q_HEADS + 2 * N_KV_HEADS) * HEAD_DIM
HG_EXPAND = 128
HG_HEADS = D_MODEL // HG_EXPAND
HG_K = HG_EXPAND
HG_V = D_MODEL // HG_HEADS
HG_FDIM = HG_HEADS * HG_K
HG_IDIM = HG_HEADS * HG_V
HG_IN = 2 * HG_FDIM + 2 * HG_IDIM
HG_CHUNK = 64
D_FF = 2816
CONV_W = 3
EPS = 1e-6

kernel_name = 'hybrid_swa_sink_alibi_hgrn2_convffn'


def rmsnorm(x, g):
    xf = x.astype(jnp.float32)
    y = xf * lax.rsqrt(jnp.mean(xf * xf, axis=-1, keepdims=True) + EPS)
    return (y * g.astype(jnp.float32)).astype(x.dtype)


def alibi_slopes():
    h = jnp.arange(1, N_Q_HEADS + 1, dtype=jnp.float32)
    return jnp.exp2(-8.0 * h / N_Q_HEADS)


def sliding_window_attention(h, w_in, w_out, sinks):
    B, S, _ = h.shape
    nblk = S // ATTN_BLOCK
    proj = h @ w_in
    q, k, v = jnp.split(proj, [N_Q_HEADS * HEAD_DIM, (N_Q_HEADS + N_KV_HEADS) * HEAD_DIM], axis=-1)
    q = q.reshape(B, nblk, ATTN_BLOCK, N_KV_HEADS, Q_PER_KV, HEAD_DIM)
    k = k.reshape(B, S, N_KV_HEADS, HEAD_DIM)
    v = v.reshape(B, S, N_KV_HEADS, HEAD_DIM)
    pad = jnp.zeros((B, ATTN_BLOCK, N_KV_HEADS, HEAD_DIM), k.dtype)
    kb = jnp.concatenate([pad, k], axis=1).reshape(B, nblk + 1, ATTN_BLOCK, N_KV_HEADS, HEAD_DIM)
    vb = jnp.concatenate([pad, v], axis=1).reshape(B, nblk + 1, ATTN_BLOCK, N_KV_HEADS, HEAD_DIM)
    kw = jnp.concatenate([kb[:, :-1], kb[:, 1:]], axis=2)
    vw = jnp.concatenate([vb[:, :-1], vb[:, 1:]], axis=2)
    scale = HEAD_DIM ** -0.5
    scores = jnp.einsum('bnqhgd,bnkhd->bnhgqk', q, kw).astype(jnp.float32) * scale
    qi = jnp.arange(ATTN_BLOCK)[:, None]
    ki = jnp.arange(2 * ATTN_BLOCK)[None, :]
    dist = qi + ATTN_BLOCK - ki
    in_window = (dist >= 0) & (dist < WINDOW)
    key_pos = (jnp.arange(nblk) * ATTN_BLOCK - ATTN_BLOCK)[:, None, None] + ki[None]
    valid = in_window[None] & (key_pos >= 0)
    slopes = alibi_slopes().reshape(N_KV_HEADS, Q_PER_KV)
    bias = -slopes[:, :, None, None] * dist.astype(jnp.float32)
    scores = jnp.where(valid[None, :, None, None], scores + bias, -jnp.inf)
    sink = jnp.broadcast_to(sinks.astype(jnp.float32).reshape(1, 1, N_KV_HEADS, Q_PER_KV, 1, 1),
                            scores.shape[:-1] + (1,))
    p = jax.nn.softmax(jnp.concatenate([scores, sink], axis=-1), axis=-1)[..., :-1]
    o = jnp.einsum('bnhgqk,bnkhd->bnqhgd', p.astype(vw.dtype), vw)
    o = o.reshape(B, S, N_Q_HEADS * HEAD_DIM)
    return o @ w_out


def hgrn2(h, w_in, w_out, norm_g, lb):
    B, S, _ = h.shape
    nc = S // HG_CHUNK
    proj = h @ w_in
    q, f, i, g = jnp.split(proj, [HG_FDIM, 2 * HG_FDIM, 2 * HG_FDIM + HG_IDIM], axis=-1)
    q = jax.nn.silu(q.astype(jnp.float32))
    f = f.astype(jnp.float32)
    lb = lb.astype(jnp.float32)
    log_f = jnp.logaddexp(jnp.log(lb), jnp.log1p(-lb) + jax.nn.log_sigmoid(f))
    k = (1.0 - lb) * jax.nn.sigmoid(-f)

    def to_chunks(t, d):
        return t.reshape(B, nc, HG_CHUNK, HG_HEADS, d).transpose(1, 0, 3, 2, 4)

    qc = to_chunks(q, HG_K)
    kc = to_chunks(k, HG_K)
    gc = to_chunks(log_f, HG_K)
    vc = to_chunks(i.astype(jnp.float32), HG_V)
    causal = jnp.tril(jnp.ones((HG_CHUNK, HG_CHUNK), bool))

    def step(state, inp):
        qt, kt, gt, vt = inp
        b = jnp.cumsum(gt, axis=2)
        rel = b[:, :, :, None, :] - b[:, :, None, :, :]
        decay = jnp.exp(jnp.where(causal[:, :, None], rel, -jnp.inf))
        a = jnp.einsum('bhtk,bhsk,bhtsk->bhts', qt, kt, decay)
        o = jnp.einsum('bhts,bhsv->bhtv', a, vt) + jnp.einsum('bhtk,bhkv->bhtv', qt * jnp.exp(b), state)
        b_last = b[:, :, -1:, :]
        new_state = jnp.exp(b_last[:, :, 0, :])[..., None] * state + \
            jnp.einsum('bhsk,bhsv->bhkv', kt * jnp.exp(b_last - b), vt)
        return new_state, o

    state0 = jnp.zeros((B, HG_HEADS, HG_K, HG_V), jnp.float32)
    _, o = lax.scan(step, state0, (qc, kc, gc, vc))
    o = o.transpose(1, 0, 3, 2, 4).reshape(B, S, HG_HEADS, HG_V)
    gate = g.reshape(B, S, HG_HEADS, HG_V).astype(jnp.float32)
    o = rmsnorm(o, norm_g) * jax.nn.silu(gate)
    return o.reshape(B, S, HG_IDIM).astype(h.dtype) @ w_out


def conv_ffn(h, w_up, conv_w, conv_b, w_down):
    S = h.shape[1]
    u = h @ w_up
    up = jnp.pad(u, ((0, 0), (CONV_W - 1, 0), (0, 0)))
    c = conv_b + conv_w[0] * up[:, 0:S]
    for j in range(1, CONV_W):
        c = c + conv_w[j] * up[:, j:j + S]
    gate, val = jnp.split(c, 2, axis=-1)
    return (jax.nn.silu(gate) * val) @ w_down


def setup_inputs(seed: int = 0) -> dict:
    key = jax.random.key(seed)
    ks = jax.random.split(key, 16)
    f32 = jnp.float32
    nrm = lambda k, shape, s: jax.random.normal(k, shape, f32) * s
    return {
        'x': nrm(ks[0], (BATCH, SEQ, D_MODEL), 1.0),
        'norm_mix': 1.0 + nrm(ks[1], (DEPTH, D_MODEL), 0.02),
        'norm_ffn': 1.0 + nrm(ks[2], (DEPTH, D_MODEL), 0.02),
        'norm_final': 1.0 + nrm(ks[3], (D_MODEL,), 0.02),
        'attn_w_in': nrm(ks[4], (N_ATTN_LAYERS, D_MODEL, ATTN_IN), D_MODEL ** -0.5),
        'attn_w_out': nrm(ks[5], (N_ATTN_LAYERS, N_Q_HEADS * HEAD_DIM, D_MODEL), (N_Q_HEADS * HEAD_DIM) ** -0.5),
        'attn_sinks': nrm(ks[6], (N_ATTN_LAYERS, N_Q_HEADS), 1.0),
        'hgrn_w_in': nrm(ks[7], (N_HGRN_LAYERS, D_MODEL, HG_IN), D_MODEL ** -0.5),
        'hgrn_w_out': nrm(ks[8], (N_HGRN_LAYERS, HG_IDIM, D_MODEL), HG_IDIM ** -0.5),
        'hgrn_norm': 1.0 + nrm(ks[9], (N_HGRN_LAYERS, HG_V), 0.02),
        'hgrn_lb_logits': 1.0 + nrm(ks[10], (DEPTH, HG_FDIM), 0.1),
        'ffn_w_up': nrm(ks[11], (DEPTH, D_MODEL, 2 * D_FF), D_MODEL ** -0.5),
        'ffn_conv_w': nrm(ks[12], (DEPTH, CONV_W, 2 * D_FF), CONV_W ** -0.5),
        'ffn_conv_b': nrm(ks[13], (DEPTH, 2 * D_FF), 0.01),
        'ffn_w_down': nrm(ks[14], (DEPTH, D_FF, D_MODEL), D_FF ** -0.5),
    }


def reference(x, norm_mix, norm_ffn, norm_final, attn_w_in, attn_w_out, attn_sinks,
              hgrn_w_in, hgrn_w_out, hgrn_norm, hgrn_lb_logits,
              ffn_w_up, ffn_conv_w, ffn_conv_b, ffn_w_down):
    s = jax.nn.softmax(hgrn_lb_logits.astype(jnp.float32), axis=0)
    lower_bounds = jnp.cumsum(s, axis=0) - s[0]
    h = x
    for layer in range(DEPTH):
        idx = layer // N_MIXERS
        hn = rmsnorm(h, norm_mix[layer])
        if layer % N_MIXERS == 0:
            h = h + sliding_window_attention(hn, attn_w_in[idx], attn_w_out[idx], attn_sinks[idx])
        else:
            h = h + hgrn2(hn, hgrn_w_in[idx], hgrn_w_out[idx], hgrn_norm[idx], lower_bounds[layer])
        h = h + conv_ffn(rmsnorm(h, norm_ffn[layer]), ffn_w_up[layer], ffn_conv_w[layer],
                         ffn_conv_b[layer], ffn_w_down[layer])
    return rmsnorm(h, norm_final)
```

```python
import numpy as np
from contextlib import ExitStack
import concourse.bass as bass
import concourse.mybir as mybir
from concourse.bass_utils import run_bass_kernel_spmd

F32 = mybir.dt.float32
BF16 = mybir.dt.bfloat16
AF = mybir.ActivationFunctionType
ALU = mybir.AluOpType
AX = mybir.AxisListType

S = 4096
D = 1024
DEPTH = 4
T = 2048
NHALF = S // T
ST = 512
NSUB = T // ST
KC = D // 128
D_FF = 2816
NFC = D_FF // 128
G_SIZES = [5, 5, 4, 4, 4]
G_OFF = [0, 5, 10, 14, 18]
NG = len(G_SIZES)
GMAX = 5
EPS = 1e-6
NQH = 16
HD = 64

GAIN0 = 0
CONV0 = GAIN0 + 9 * 8
LG0 = CONV0 + DEPTH * 44 * 4
HN0 = LG0 + 32
SK0 = HN0 + 2
NPC = SK0 + 32


class Stream:
    def __init__(self, sem, step):
        self.sem = sem
        self.count = 0
        self.step = step


class Eng:
    def __init__(self, name, h, st):
        self.name = name
        self.h = h
        self.st = st
        self.seen = {}


class Buf:
    __slots__ = ("name", "w", "r")

    def __init__(self, name, deps=None):
        self.name = name
        self.w = None
        self.r = dict(deps) if deps else {}


def merge_deps(bufs):
    d = {}
    for b in bufs:
        if b.w is not None:
            s, v = b.w
            if d.get(s, 0) < v:
                d[s] = v
        for s, v in b.r.items():
            if d.get(s, 0) < v:
                d[s] = v
    return d


class K:
    def __init__(self):
        self.nc = bass.Bass("TRN2", target_bir_lowering=False)
        self.es = ExitStack()
        nc = self.nc
        self.pe = Eng("pe", nc.tensor, self.stream("s_pe", 1))
        self.act = Eng("act", nc.scalar, self.stream("s_act", 1))
        self.dve = Eng("dve", nc.vector, self.stream("s_dve", 1))
        self.pool = Eng("pool", nc.gpsimd, self.stream("s_pool", 1))
        self.sp = Eng("sp", nc.sync, self.stream("s_sp", 1))

    def stream(self, name, step):
        sem = self.es.enter_context(self.nc.semaphore(name))
        return Stream(sem, step)

    def sb(self, name, shape, dt):
        return self.es.enter_context(self.nc.sbuf_tensor("sb_" + name, shape, dt))

    def ps(self, name, shape, dt):
        return self.es.enter_context(self.nc.psum_tensor("ps_" + name, shape, dt))

    def _sync(self, eng, reads, writes, is_dma=False):
        need = {}
        for b in reads:
            if b.w is not None:
                s, v = b.w
                if need.get(s, 0) < v:
                    need[s] = v
        for b in writes:
            if b.w is not None:
                s, v = b.w
                if is_dma or s is not eng.st:
                    if need.get(s, 0) < v:
                        need[s] = v
            for s, v in b.r.items():
                if is_dma or s is not eng.st:
                    if need.get(s, 0) < v:
                        need[s] = v
        for s, v in need.items():
            if eng.seen.get(s, 0) < v:
                eng.h.wait_ge(s.sem, v)
                eng.seen[s] = v

    @staticmethod
    def _record(stream, tick, reads, writes):
        for b in reads:
            if b.r.get(stream, 0) < tick:
                b.r[stream] = tick
        for b in writes:
            b.w = (stream, tick)
            b.r = {}

    def op(self, eng, fn, reads=(), writes=(), inc=True):
        self._sync(eng, reads, writes)
        ins = fn(eng.h)
        if inc:
            eng.st.count += 1
            ins.then_inc(eng.st.sem, 1)
            tick = eng.st.count
        else:
            tick = eng.st.count + 1
        self._record(eng.st, tick, reads, writes)
        return ins

    def dma(self, eng, stream, out, in_, reads=(), writes=()):
        self._sync(eng, reads, writes, is_dma=True)
        ins = eng.h.dma_start(out=out, in_=in_)
        stream.count += 16
        ins.then_inc(stream.sem, 16)
        self._record(stream, stream.count, reads, writes)
        return ins


def build(plan, final_norm=True):
    k = K()
    nc = k.nc
    pe, act, dve, pool, sp = k.pe, k.act, k.dve, k.pool, k.sp

    x_d = nc.dram_tensor("x", [S, D], F32, kind="ExternalInput").ap()
    pc_d = nc.dram_tensor("pc", [128, NPC], F32, kind="ExternalInput").ap()
    awin_d = nc.dram_tensor("awin", [2, 128, KC * 1792], F32, kind="ExternalInput").ap()
    awout_d = nc.dram_tensor("awout", [2, 128, KC * D], F32, kind="ExternalInput").ap()
    hwin_d = nc.dram_tensor("hwin", [2, 8, 128, KC * 512], F32, kind="ExternalInput").ap()
    hwout_d = nc.dram_tensor("hwout", [2, 8, 128, D], F32, kind="ExternalInput").ap()
    fup_d = nc.dram_tensor("fup", [DEPTH, NG, 128, KC * 2 * GMAX * 128], F32, kind="ExternalInput").ap()
    fdn_d = nc.dram_tensor("fdn", [DEPTH, NG, 128, GMAX * D], F32, kind="ExternalInput").ap()
    out_d = nc.dram_tensor("out", [S, D], F32, kind="ExternalOutput").ap()

    h_t = k.sb("h", [128, KC, T], F32)
    h_b = [[Buf("h%d_%d" % (c, s)) for s in range(NSUB)] for c in range(KC)]
    hn_t = k.sb("hn", [128, KC, T], BF16)
    hn_b = [Buf("hn%d" % s) for s in range(NSUB)]
    pc_t = k.sb("pc", [128, NPC], F32)
    pc_b = Buf("pc")
    WSLOT = 15360
    w_t = [k.sb("w%d" % i, [128, WSLOT], BF16) for i in range(2)]
    w_b = [Buf("w%d" % i) for i in range(2)]
    w_st = [k.stream("s_w%d" % i, 16) for i in range(2)]
    wcount = [0]
    idb_t = k.sb("idb", [128, 128], BF16)
    idf_t = k.sb("idf", [128, 128], F32)
    ones_t = k.sb("ones", [128, 128], BF16)
    cm4_t = k.sb("cm4", [128, 512], BF16)
    mall_t = k.sb("mall", [128, 8, 512], BF16)
    const_b = Buf("const")
    lb_t = k.sb("lb", [128, 2, 8], F32)
    c1_t = k.sb("c1", [128, 2, 8], F32)
    esk_t = k.sb("esk", [128, 32], F32)
    S_t = k.sb("S", [128, 16, 128], F32)
    S_b = [Buf("S%d" % i) for i in range(16)]
    ktc_t = k.sb("ktc", [128, 2, 4, 128], BF16)
    vc_t = k.sb("vc", [128, 2, 256], BF16)
    kvc_b = [Buf("kvc%d" % i) for i in range(2)]
    halo_t = k.sb("halo", [128, DEPTH, 44, 2], F32)
    halo_b = [Buf("halo%d" % l) for l in range(DEPTH)]
    sq_t = [k.sb("sq%d" % i, [128, ST], BF16) for i in range(2)]
    sq_b = [Buf("sq%d" % i) for i in range(2)]
    rstd_t = k.sb("rstd", [128, ST], F32)
    rstd_b = Buf("rstd")
    TMPW = 5200
    tmp_t = k.sb("tmp", [128, TMPW], F32)
    tmp_live = []

    pb_t = [k.ps("pb%d" % i, [128, 512], F32) for i in range(8)]
    pb_b = [Buf("pb%d" % i) for i in range(8)]

    class TmpAlloc:
        def __init__(self):
            self.deps = merge_deps(tmp_live)
            del tmp_live[:]
            self.off = 0

        def get(self, name, shape_free, dt):
            n = int(np.prod(shape_free))
            words = n if dt == F32 else (n + 1) // 2
            assert self.off + words <= TMPW, (name, self.off, words)
            ap = tmp_t[:, self.off:self.off + words]
            if dt != F32:
                ap = ap.bitcast(dt)[:, 0:n]
            self.off += words
            b = Buf(name, self.deps)
            tmp_live.append(b)
            return ap, b

    def pcol(j):
        return pc_t[:, j:j + 1]

    st_pc = k.stream("s_pc", 16)
    k.dma(sp, st_pc, pc_t[:], pc_d, writes=[pc_b])
    k.op(pool, lambda e: e.memset(ones_t[:], 1.0), writes=[const_b])
    k.op(pool, lambda e: e.memset(idb_t[:], 1.0), writes=[const_b])
    k.op(pool, lambda e: e.affine_select(out=idb_t[:], in_=idb_t[:], pattern=[[-1, 128]], compare_op=ALU.is_equal,
                                         fill=0.0, base=0, channel_multiplier=1), reads=[const_b], writes=[const_b])
    k.op(pool, lambda e: e.memset(idf_t[:], 1.0), writes=[const_b])
    k.op(pool, lambda e: e.affine_select(out=idf_t[:], in_=idf_t[:], pattern=[[-1, 128]], compare_op=ALU.is_equal,
                                         fill=0.0, base=0, channel_multiplier=1), reads=[const_b], writes=[const_b])
    k.op(pool, lambda e: e.memset(cm4_t[:], 1.0), writes=[const_b])
    for blk in range(4):
        k.op(pool, lambda e, blk=blk: e.affine_select(out=cm4_t[:, blk * 128:(blk + 1) * 128],
                                                      in_=cm4_t[:, blk * 128:(blk + 1) * 128],
                                                      pattern=[[1, 128]], compare_op=ALU.is_ge, fill=0.0, base=0,
                                                      channel_multiplier=-1), reads=[const_b], writes=[const_b])
    k.op(pool, lambda e: e.memset(halo_t[:].rearrange("p a b c -> p (a b c)"), 0.0), writes=halo_b)
    k.op(pool, lambda e: e.memset(S_t[:].rearrange("p a b -> p (a b)"), 0.0), writes=S_b)
    k.op(pool, lambda e: e.memset(ktc_t[:].rearrange("p a b c -> p (a b c)"), 0.0), writes=kvc_b)
    k.op(pool, lambda e: e.memset(vc_t[:].rearrange("p a b -> p (a b)"), 0.0), writes=kvc_b)

    has_attn = any(p[0] == "attn" for p in plan)
    has_hgrn = any(p[0] == "hgrn" for p in plan)

    if has_attn:
        ta = TmpAlloc()
        d1, d1_b = ta.get("d1", [128], F32)
        dpos, dpos_b = ta.get("dpos", [128], F32)
        dneg, dneg_b = ta.get("dneg", [128], F32)
        mtmp, mtmp_b = ta.get("mtmp", [256], F32)
        k.op(pool, lambda e: e.iota(d1, pattern=[[1, 128]], base=0, channel_multiplier=-1,
                                    allow_small_or_imprecise_dtypes=True), writes=[d1_b])
        k.op(dve, lambda e: e.tensor_scalar_max(out=dpos, in0=d1, scalar1=0.0), reads=[d1_b], writes=[dpos_b])
        k.op(dve, lambda e: e.tensor_scalar_min(out=dneg, in0=d1, scalar1=0.0), reads=[d1_b], writes=[dneg_b])
        for gp in range(8):
            g, p = gp // 2, gp % 2
            for hh in range(2):
                hq = 4 * g + p + 2 * hh
                slope = float(2.0 ** (-8.0 * (hq + 1) / NQH))
                k.op(act, lambda e, slope=slope: e.activation(out=mtmp[:, 0:128], in_=dneg, func=AF.Exp,
                                                              bias=-slope * 128.0, scale=-slope),
                     reads=[dneg_b], writes=[mtmp_b])
                k.op(act, lambda e, slope=slope: e.activation(out=mtmp[:, 128:256], in_=dpos, func=AF.Exp,
                                                              scale=-slope),
                     reads=[dpos_b], writes=[mtmp_b])
                k.op(pool, lambda e, gp=gp, hh=hh: e.affine_select(
                    out=mall_t[:, gp, hh * 128:(hh + 1) * 128], in_=mtmp[:, 0:128], pattern=[[-1, 128]],
                    compare_op=ALU.is_gt, fill=0.0, base=0, channel_multiplier=1),
                    reads=[mtmp_b], writes=[const_b])
                k.op(pool, lambda e, gp=gp, hh=hh: e.affine_select(
                    out=mall_t[:, gp, 256 + hh * 128:256 + (hh + 1) * 128], in_=mtmp[:, 128:256],
                    pattern=[[1, 128]], compare_op=ALU.is_ge, fill=0.0, base=0, channel_multiplier=-1),
                    reads=[mtmp_b], writes=[const_b])
        k.op(act, lambda e: e.activation(out=esk_t[:], in_=pc_t[:, SK0:SK0 + 32], func=AF.Exp),
             reads=[pc_b], writes=[const_b])

    if has_hgrn:
        ta = TmpAlloc()
        lg = pc_t[:, LG0:LG0 + 32].rearrange("p (h l) -> p h l", l=4)
        mx, mx_b = ta.get("mx", [8], F32)
        ex, ex_b = ta.get("ex", [32], F32)
        sm, sm_b = ta.get("sm", [8], F32)
        ex3 = ex.rearrange("p (h l) -> p h l", l=4)
        k.op(dve, lambda e: e.tensor_reduce(out=mx, in_=lg, axis=AX.X, op=ALU.max), reads=[pc_b], writes=[mx_b])
        for li in range(4):
            k.op(dve, lambda e, li=li: e.tensor_tensor(out=ex3[:, :, li], in0=lg[:, :, li], in1=mx, op=ALU.subtract),
                 reads=[pc_b, mx_b, ex_b], writes=[ex_b])
        k.op(act, lambda e: e.activation(out=ex, in_=ex, func=AF.Exp), reads=[ex_b], writes=[ex_b])
        k.op(dve, lambda e: e.tensor_reduce(out=sm, in_=ex3, axis=AX.X, op=ALU.add), reads=[ex_b], writes=[sm_b])
        k.op(dve, lambda e: e.reciprocal(out=sm, in_=sm), reads=[sm_b], writes=[sm_b])
        for li in range(4):
            k.op(dve, lambda e, li=li: e.tensor_tensor(out=ex3[:, :, li], in0=ex3[:, :, li], in1=sm, op=ALU.mult),
                 reads=[ex_b, sm_b], writes=[ex_b])
        k.op(dve, lambda e: e.tensor_copy(out=lb_t[:, 0, :], in_=ex3[:, :, 1]), reads=[ex_b], writes=[const_b])
        k.op(dve, lambda e: e.tensor_tensor(out=lb_t[:, 1, :], in0=ex3[:, :, 1], in1=ex3[:, :, 2], op=ALU.add),
             reads=[ex_b], writes=[const_b])
        k.op(dve, lambda e: e.tensor_tensor(out=lb_t[:, 1, :], in0=lb_t[:, 1, :], in1=ex3[:, :, 3], op=ALU.add),
             reads=[ex_b, const_b], writes=[const_b])
        k.op(act, lambda e: e.activation(out=c1_t[:].rearrange("p a b -> p (a b)"),
                                         in_=lb_t[:].rearrange("p a b -> p (a b)"), func=AF.Ln, bias=1.0, scale=-1.0),
             reads=[const_b], writes=[const_b])

    def next_slot():
        i = wcount[0] % 2
        wcount[0] += 1
        return i

    evac_rr = [0]

    def evac_copy(out, in_, reads, writes):
        evac_rr[0] += 1
        if evac_rr[0] % 2 == 0:
            k.op(act, lambda e: e.copy(out=out, in_=in_), reads=reads, writes=writes)
        else:
            k.op(dve, lambda e: e.tensor_copy(out=out, in_=in_), reads=reads, writes=writes)

    def rmsnorm_sub(s, gain_idx, nbank, in_place=False):
        sl = slice(s * ST, (s + 1) * ST)
        for c in range(KC):
            i = c % 2
            k.op(act, lambda e, c=c, i=i: e.activation(out=sq_t[i][:], in_=h_t[:, c, sl], func=AF.Square),
                 reads=[h_b[c][s]], writes=[sq_b[i]])
            k.op(pe, lambda e, c=c, i=i: e.matmul(pb_t[nbank][:], lhsT=ones_t[:], rhs=sq_t[i][:],
                                                   start=(c == 0), stop=(c == KC - 1)),
                 reads=[sq_b[i], const_b], writes=[pb_b[nbank]], inc=True)
        k.op(act, lambda e: e.activation(out=rstd_t[:], in_=pb_t[nbank][:], func=AF.Ln, bias=EPS, scale=1.0 / D),
             reads=[pb_b[nbank]], writes=[rstd_b])
        k.op(act, lambda e: e.activation(out=rstd_t[:], in_=rstd_t[:], func=AF.Exp, scale=-0.5),
             reads=[rstd_b], writes=[rstd_b])
        for c in range(KC):
            if in_place:
                k.op(dve, lambda e, c=c: e.scalar_tensor_tensor(out=h_t[:, c, sl], in0=h_t[:, c, sl],
                                                                scalar=pcol(GAIN0 + gain_idx * 8 + c),
                                                                in1=rstd_t[:], op0=ALU.mult, op1=ALU.mult),
                     reads=[h_b[c][s], rstd_b, pc_b], writes=[h_b[c][s]])
            else:
                k.op(dve, lambda e, c=c: e.scalar_tensor_tensor(out=hn_t[:, c, sl], in0=h_t[:, c, sl],
                                                                scalar=pcol(GAIN0 + gain_idx * 8 + c),
                                                                in1=rstd_t[:], op0=ALU.mult, op1=ALU.mult),
                     reads=[h_b[c][s], rstd_b, pc_b], writes=[hn_b[s]])

    def load_half(hf):
        ta = TmpAlloc()
        xin = [ta.get("xin%d" % i, [D], F32) for i in range(2)]
        xst = [k.stream("s_xin%d_%d" % (hf, i), 16) for i in range(2)]
        for blk in range(T // 128):
            i = blk % 2
            t0 = hf * T + blk * 128
            s = blk // 4
            col = blk * 128
            k.dma(sp, xst[i], xin[i][0], x_d[t0:t0 + 128, :], writes=[xin[i][1]])
            for half in range(2):
                bank = half
                for cc in range(4):
                    c = half * 4 + cc
                    k.op(pe, lambda e, c=c, cc=cc, bank=bank, i=i: e.transpose(
                        pb_t[bank][:, cc * 128:(cc + 1) * 128], xin[i][0][:, c * 128:(c + 1) * 128], idf_t[:]),
                        reads=[xin[i][1], const_b], writes=[pb_b[bank]], inc=(cc == 3))
                evac_copy(h_t[:, half * 4:half * 4 + 4, col:col + 128],
                          pb_t[bank][:].rearrange("p (c t) -> p c t", c=4),
                          reads=[pb_b[bank]], writes=[h_b[half * 4 + cc][s] for cc in range(4)])

    def store_half(hf):
        ta = TmpAlloc()
        yo = [ta.get("yo%d" % i, [D], F32) for i in range(2)]
        yst = [k.stream("s_yo%d_%d" % (hf, i), 16) for i in range(2)]
        for s in range(NSUB):
            if final_norm:
                rmsnorm_sub(s, 8, 6, in_place=True)
            for bb in range(4):
                blk = s * 4 + bb
                i = blk % 2
                t0 = hf * T + blk * 128
                col = blk * 128
                for half in range(2):
                    bank = half
                    for cc in range(4):
                        c = half * 4 + cc
                        k.op(pe, lambda e, c=c, cc=cc, bank=bank: e.transpose(
                            pb_t[bank][:, cc * 128:(cc + 1) * 128], h_t[:, c, col:col + 128], idf_t[:]),
                            reads=[h_b[c][s], const_b], writes=[pb_b[bank]], inc=(cc == 3))
                    evac_copy(yo[i][0][:, half * 512:(half + 1) * 512], pb_t[bank][:],
                              reads=[pb_b[bank]], writes=[yo[i][1]])
                k.dma(sp, yst[i], out_d[t0:t0 + 128, :], yo[i][0], reads=[yo[i][1]])
        return yst

    def ffn_phase(l, hf):
        for s in range(NSUB):
            rmsnorm_sub(s, 4 + l, 6)
        ta = TmpAlloc()
        cg = [ta.get("cg%d" % i, [ST], F32) for i in range(2)]
        cv = [ta.get("cv%d" % i, [ST], F32) for i in range(2)]
        sg = [ta.get("sg%d" % i, [ST], BF16) for i in range(2)]
        actb = [ta.get("act%d" % i, [GMAX, ST], BF16) for i in range(2)]
        items = []
        gs_list = []
        for g in range(NG):
            n = G_SIZES[g]
            slot = next_slot()
            for s in range(NSUB):
                gs_list.append((g, s, slot))
                for jj in range(n):
                    items.append((g, s, jj, slot, len(gs_list) - 1))

        def wup(slot, kc, col0):
            base = kc * 2 * GMAX * 128 + col0
            return w_t[slot][:, base:base + 128]

        def wdn(slot, jj, dc):
            base = KC * 2 * GMAX * 128 + jj * D + dc * 128
            return w_t[slot][:, base:base + 128]

        def stageA(it, idx):
            g, s, jj, slot, gsi = it
            par = idx % 2
            sl = slice(s * ST, (s + 1) * ST)
            for which in range(2):
                bank = par * 2 + which
                col0 = which * GMAX * 128 + jj * 128
                for kc in range(KC):
                    k.op(pe, lambda e, kc=kc, bank=bank, col0=col0: e.matmul(
                        pb_t[bank][:], lhsT=wup(slot, kc, col0), rhs=hn_t[:, kc, sl],
                        start=(kc == 0), stop=(kc == KC - 1)),
                        reads=[w_b[slot], hn_b[s]], writes=[pb_b[bank]], inc=(kc == KC - 1))

        def stageB(it, idx):
            g, s, jj, slot, gsi = it
            par = idx % 2
            first_tok = (hf == 0 and s == 0)
            for which in range(2):
                bank = par * 2 + which
                P = pb_t[bank]
                C, Cb = (cg, cv)[which][par]
                jc = which * NFC + G_OFF[g] + jj
                cb = CONV0 + (l * 44 + jc) * 4
                hl = halo_t[:, l, jc, :]
                k.op(act, lambda e, P=P, C=C, cb=cb: e.activation(out=C, in_=P[:], func=AF.Identity,
                                                                  bias=pcol(cb + 3), scale=pcol(cb + 2)),
                     reads=[pb_b[bank], pc_b], writes=[Cb])
                k.op(dve, lambda e, P=P, C=C, cb=cb: e.scalar_tensor_tensor(
                    out=C[:, 1:ST], in0=P[:, 0:ST - 1], scalar=pcol(cb + 1), in1=C[:, 1:ST],
                    op0=ALU.mult, op1=ALU.add), reads=[pb_b[bank], Cb, pc_b], writes=[Cb])
                k.op(dve, lambda e, P=P, C=C, cb=cb: e.scalar_tensor_tensor(
                    out=C[:, 2:ST], in0=P[:, 0:ST - 2], scalar=pcol(cb + 0), in1=C[:, 2:ST],
                    op0=ALU.mult, op1=ALU.add), reads=[pb_b[bank], Cb, pc_b], writes=[Cb])
                if not first_tok:
                    k.op(dve, lambda e, C=C, cb=cb, hl=hl: e.scalar_tensor_tensor(
                        out=C[:, 0:1], in0=hl[:, 1:2], scalar=pcol(cb + 1), in1=C[:, 0:1],
                        op0=ALU.mult, op1=ALU.add), reads=[halo_b[l], Cb, pc_b], writes=[Cb])
                    k.op(dve, lambda e, C=C, cb=cb, hl=hl: e.scalar_tensor_tensor(
                        out=C[:, 0:2], in0=hl[:, 0:2], scalar=pcol(cb + 0), in1=C[:, 0:2],
                        op0=ALU.mult, op1=ALU.add), reads=[halo_b[l], Cb, pc_b], writes=[Cb])
                k.op(dve, lambda e, P=P, hl=hl: e.tensor_copy(out=hl, in_=P[:, ST - 2:ST]),
                     reads=[pb_b[bank]], writes=[halo_b[l]])

        def stageC(it, idx):
            g, s, jj, slot, gsi = it
            par = idx % 2
            Cg, Cgb = cg[par]
            Cv, Cvb = cv[par]
            Sg, Sgb = sg[par]
            A, Ab = actb[gsi % 2]
            k.op(act, lambda e: e.activation(out=Sg, in_=Cg, func=AF.Silu), reads=[Cgb], writes=[Sgb])
            k.op(dve, lambda e: e.tensor_tensor(out=A[:, jj * ST:(jj + 1) * ST], in0=Sg, in1=Cv, op=ALU.mult),
                 reads=[Sgb, Cvb], writes=[Ab])

        def stageD(gsi):
            g, s, slot = gs_list[gsi]
            n = G_SIZES[g]
            A, Ab = actb[gsi % 2]
            sl = slice(s * ST, (s + 1) * ST)
            for dc in range(KC):
                bank = 4 + dc % 2
                for jj in range(n):
                    k.op(pe, lambda e, jj=jj, dc=dc, bank=bank: e.matmul(
                        pb_t[bank][:], lhsT=wdn(slot, jj, dc), rhs=A[:, jj * ST:(jj + 1) * ST],
                        start=(jj == 0), stop=(jj == n - 1)),
                        reads=[w_b[slot], Ab], writes=[pb_b[bank]], inc=(jj == n - 1))
                k.op(dve, lambda e, dc=dc, bank=bank: e.tensor_tensor(out=h_t[:, dc, sl], in0=pb_t[bank][:],
                                                                       in1=h_t[:, dc, sl], op=ALU.add),
                     reads=[pb_b[bank], h_b[dc][s]], writes=[h_b[dc][s]])

        def load_w(g, slot):
            wt, wb, wst = w_t[slot], w_b[slot], w_st[slot]
            k.dma(pool, wst, wt[:, 0:KC * 2 * GMAX * 128], fup_d[l, g], writes=[wb])
            k.dma(pool, wst, wt[:, KC * 2 * GMAX * 128:KC * 2 * GMAX * 128 + GMAX * D], fdn_d[l, g], writes=[wb])

        nI = len(items)
        for step in range(nI + 2):
            if step < nI:
                g_, s_, jj_, slot_, _ = items[step]
                if s_ == 0 and jj_ == 0:
                    load_w(g_, slot_)
                stageA(items[step], step)
            if 0 <= step - 1 < nI:
                stageB(items[step - 1], step - 1)
            if 0 <= step - 2 < nI:
                it = items[step - 2]
                stageC(it, step - 2)
                g, s, jj, slot, gsi = it
                if jj == G_SIZES[g] - 1:
                    stageD(gsi)

    def attn_phase(l, hf):
        idx = l // 2
        sA = next_slot()
        k.dma(pool, w_st[sA], w_t[sA][:, 0:KC * 1792], awin_d[idx], writes=[w_b[sA]])
        sB = next_slot()
        k.dma(pool, w_st[sB], w_t[sB][:, 0:KC * D], awout_d[idx], writes=[w_b[sB]])
        ta = TmpAlloc()
        kt, kt_b = ta.get("kt", [4, 640], BF16)
        vb, vb_b = ta.get("vb", [5, 256], BF16)
        et = [ta.get("e%d" % i, [512], BF16) for i in range(2)]
        pt = [ta.get("p%d" % i, [512], BF16) for i in range(4)]
        rd = [ta.get("rd%d" % i, [256], F32) for i in range(2)]
        kt3 = kt.rearrange("p (g t) -> p g t", g=4)
        vb3 = vb.rearrange("p (b d) -> p b d", b=5)
        hnS_b, qt_b, ot_b = hn_b[0], hn_b[1], hn_b[2]

        def hnS(kc, a, b):
            return hn_t[:, kc, a:b]

        QT0, OT0 = ST, 2 * ST
        k.op(dve, lambda e: e.tensor_copy(out=kt3[:, :, 0:128], in_=ktc_t[:, idx]), reads=[kvc_b[idx]], writes=[kt_b])
        k.op(dve, lambda e: e.tensor_copy(out=vb3[:, 0, :], in_=vc_t[:, idx]), reads=[kvc_b[idx]], writes=[vb_b])

        def win(kc, c0, n):
            base = kc * 1792 + c0
            return w_t[sA][:, base:base + n]

        def wout(c, dc):
            base = c * D + dc * 128
            return w_t[sB][:, base:base + 128]

        pcnt = [0]
        for s in range(NSUB):
            sl = slice(s * ST, (s + 1) * ST)
            for c in range(KC):
                i = c % 2
                k.op(act, lambda e, c=c, i=i: e.activation(out=sq_t[i][:], in_=h_t[:, c, sl], func=AF.Square),
                     reads=[h_b[c][s]], writes=[sq_b[i]])
                k.op(pe, lambda e, c=c, i=i: e.matmul(pb_t[6][:], lhsT=ones_t[:], rhs=sq_t[i][:],
                                                       start=(c == 0), stop=(c == KC - 1)),
                     reads=[sq_b[i], const_b], writes=[pb_b[6]], inc=True)
            k.op(act, lambda e: e.activation(out=rstd_t[:], in_=pb_t[6][:], func=AF.Ln, bias=EPS, scale=1.0 / D),
                 reads=[pb_b[6]], writes=[rstd_b])
            k.op(act, lambda e: e.activation(out=rstd_t[:], in_=rstd_t[:], func=AF.Exp, scale=-0.5),
                 reads=[rstd_b], writes=[rstd_b])
            for c in range(KC):
                k.op(dve, lambda e, c=c: e.scalar_tensor_tensor(out=hn_t[:, c, 0:ST], in0=h_t[:, c, sl],
                                                                scalar=pcol(GAIN0 + l * 8 + c),
                                                                in1=rstd_t[:], op0=ALU.mult, op1=ALU.mult),
                     reads=[h_b[c][s], rstd_b, pc_b], writes=[hnS_b])
            for c in range(KC + 4):
                bank = pcnt[0] % 2
                pcnt[0] += 1
                for kc in range(KC):
                    k.op(pe, lambda e, c=c, kc=kc, bank=bank: e.matmul(
                        pb_t[bank][:], lhsT=win(kc, c * 128, 128), rhs=hnS(kc, 0, ST),
                        start=(kc == 0), stop=(kc == KC - 1)),
                        reads=[w_b[sA], hnS_b], writes=[pb_b[bank]], inc=(kc == KC - 1))
                if c < KC:
                    evac_copy(hn_t[:, c, QT0:QT0 + ST], pb_t[bank][:], reads=[pb_b[bank]], writes=[qt_b])
                else:
                    evac_copy(kt3[:, c - KC, 128:640], pb_t[bank][:], reads=[pb_b[bank]], writes=[kt_b])
            for pair in range(2):
                bank = pcnt[0] % 2
                pcnt[0] += 1
                for bb in range(2):
                    blk = pair * 2 + bb
                    for kc in range(KC):
                        k.op(pe, lambda e, kc=kc, blk=blk, bb=bb, bank=bank: e.matmul(
                            pb_t[bank][:, bb * 256:(bb + 1) * 256], lhsT=hnS(kc, blk * 128, (blk + 1) * 128),
                            rhs=win(kc, 1536, 256), start=(kc == 0), stop=(kc == KC - 1)),
                            reads=[w_b[sA], hnS_b], writes=[pb_b[bank]], inc=(kc == KC - 1))
                evac_copy(vb[:, (1 + pair * 2) * 256:(3 + pair * 2) * 256], pb_t[bank][:],
                          reads=[pb_b[bank]], writes=[vb_b])
            for bq in range(4):
                nglob = hf * (T // 128) + s * 4 + bq
                has_prev = nglob > 0
                for g in range(4):
                    pp = []
                    for p in range(2):
                        gp = g * 2 + p
                        bank = 2 + gp % 2
                        X = pb_t[bank]
                        rows = slice(p * 64, (p + 1) * 64)
                        qrhs = hn_t[rows, 2 * g:2 * g + 2, QT0 + bq * 128:QT0 + (bq + 1) * 128]
                        if has_prev:
                            k.op(pe, lambda e, X=X, rows=rows, qrhs=qrhs, g=g, bq=bq: e.matmul(
                                X[:, 0:256].rearrange("p (h q) -> p h q", h=2),
                                lhsT=kt3[rows, g, bq * 128:(bq + 1) * 128], rhs=qrhs, start=True, stop=True),
                                reads=[kt_b, qt_b], writes=[pb_b[bank]], inc=False)
                        k.op(pe, lambda e, X=X, rows=rows, qrhs=qrhs, g=g, bq=bq: e.matmul(
                            X[:, 256:512].rearrange("p (h q) -> p h q", h=2),
                            lhsT=kt3[rows, g, (bq + 1) * 128:(bq + 2) * 128], rhs=qrhs, start=True, stop=True),
                            reads=[kt_b, qt_b], writes=[pb_b[bank]], inc=True)
                        E, Eb = et[gp % 2]
                        Pm, Pb = pt[gp % 4]
                        lo = 0 if has_prev else 256
                        k.op(act, lambda e, X=X, E=E, lo=lo: e.activation(out=E[:, lo:512], in_=X[:, lo:512],
                                                                           func=AF.Exp, scale=HD ** -0.5),
                             reads=[pb_b[bank]], writes=[Eb])
                        k.op(dve, lambda e, E=E, Pm=Pm, gp=gp, lo=lo: e.tensor_tensor(
                            out=Pm[:, lo:512], in0=E[:, lo:512], in1=mall_t[:, gp, lo:512], op=ALU.mult),
                            reads=[Eb, const_b], writes=[Pb])
                        pp.append((Pm, Pb))
                    for p in range(2):
                        gp = g * 2 + p
                        Pm, Pb = pp[p]
                        bank = (4, 5, 7)[gp % 3]
                        Y = pb_t[bank]
                        rows = slice(p * 64, (p + 1) * 64)
                        vcols = slice(g * 64, (g + 1) * 64)
                        if has_prev:
                            k.op(pe, lambda e, Y=Y, rows=rows, Pm=Pm, bq=bq, vcols=vcols: e.matmul(
                                Y[rows, 0:256], lhsT=vb3[:, bq, vcols], rhs=Pm[:, 0:256], start=True, stop=False),
                                reads=[vb_b, Pb], writes=[pb_b[bank]], inc=False)
                        k.op(pe, lambda e, Y=Y, rows=rows, Pm=Pm, bq=bq, vcols=vcols: e.matmul(
                            Y[rows, 0:256], lhsT=vb3[:, bq + 1, vcols], rhs=Pm[:, 256:512],
                            start=(not has_prev), stop=True),
                            reads=[vb_b, Pb], writes=[pb_b[bank]], inc=False)
                        if has_prev:
                            k.op(pe, lambda e, Y=Y, rows=rows, Pm=Pm: e.matmul(
                                Y[rows, 256:512], lhsT=ones_t[:, 0:64], rhs=Pm[:, 0:256], start=True, stop=False),
                                reads=[const_b, Pb], writes=[pb_b[bank]], inc=False)
                        k.op(pe, lambda e, Y=Y, rows=rows, Pm=Pm: e.matmul(
                            Y[rows, 256:512], lhsT=ones_t[:, 0:64], rhs=Pm[:, 256:512],
                            start=(not has_prev), stop=True),
                            reads=[const_b, Pb], writes=[pb_b[bank]], inc=True)
                        R, Rb = rd[p]
                        for hh in range(2):
                            hq = 4 * g + p + 2 * hh
                            k.op(dve, lambda e, Y=Y, R=R, rows=rows, hh=hh, hq=hq: e.tensor_scalar(
                                out=R[rows, hh * 128:(hh + 1) * 128], in0=Y[rows, 256 + hh * 128:256 + (hh + 1) * 128],
                                scalar1=esk_t[rows, idx * 16 + hq:idx * 16 + hq + 1], scalar2=None, op0=ALU.add),
                                reads=[pb_b[bank], const_b], writes=[Rb])
                        k.op(dve, lambda e, R=R, rows=rows: e.reciprocal(out=R[rows, :], in_=R[rows, :]),
                             reads=[Rb], writes=[Rb])
                        k.op(dve, lambda e, Y=Y, R=R, rows=rows, g=g, bq=bq: e.tensor_tensor(
                            out=hn_t[rows, 2 * g:2 * g + 2, OT0 + bq * 128:OT0 + (bq + 1) * 128],
                            in0=Y[rows, 0:256].rearrange("p (h q) -> p h q", h=2),
                            in1=R[rows, :].rearrange("p (h q) -> p h q", h=2), op=ALU.mult),
                            reads=[pb_b[bank], Rb], writes=[ot_b])
            for dc in range(KC):
                bank = pcnt[0] % 2
                pcnt[0] += 1
                for c in range(KC):
                    k.op(pe, lambda e, c=c, dc=dc, bank=bank: e.matmul(
                        pb_t[bank][:], lhsT=wout(c, dc), rhs=hn_t[:, c, OT0:OT0 + ST],
                        start=(c == 0), stop=(c == KC - 1)),
                        reads=[w_b[sB], ot_b], writes=[pb_b[bank]], inc=(c == KC - 1))
                k.op(dve, lambda e, dc=dc, bank=bank: e.tensor_tensor(out=h_t[:, dc, sl], in0=pb_t[bank][:],
                                                                       in1=h_t[:, dc, sl], op=ALU.add),
                     reads=[pb_b[bank], h_b[dc][s]], writes=[h_b[dc][s]])
            k.op(dve, lambda e: e.tensor_copy(out=kt3[:, :, 0:128], in_=kt3[:, :, 512:640]),
                 reads=[kt_b], writes=[kt_b])
            k.op(dve, lambda e: e.tensor_copy(out=vb3[:, 0, :], in_=vb3[:, 4, :]), reads=[vb_b], writes=[vb_b])
        k.op(dve, lambda e: e.tensor_copy(out=ktc_t[:, idx], in_=kt3[:, :, 0:128]), reads=[kt_b], writes=[kvc_b[idx]])
        k.op(dve, lambda e: e.tensor_copy(out=vc_t[:, idx], in_=vb3[:, 0, :]), reads=[vb_b], writes=[kvc_b[idx]])

    def hgrn_phase(l, hf):
        idx = (l - 1) // 2
        for s in range(NSUB):
            rmsnorm_sub(s, l, 4)
        ta = TmpAlloc()
        Te, Te_b = ta.get("Te", [ST], F32)
        TL1, TL1_b = ta.get("TL1", [ST], F32)
        TL2, TL2_b = ta.get("TL2", [ST], F32)
        Tb, Tb_b = ta.get("Tb", [ST], F32)
        Tt, Tt_b = ta.get("Tt", [ST], F32)
        Tgs, Tgs_b = ta.get("Tgs", [ST], F32)
        To, To_b = Tt, Tt_b
        Tr, Tr_b = Te, Te_b
        ebl, ebl_b = ta.get("ebl", [4], F32)
        ebm, ebm_b = ta.get("ebm", [4], F32)
        eblm, eblm_b = ta.get("eblm", [4], F32)
        cbm, cbm_b = ta.get("cbm", [4], F32)
        nbm, nbm_b = ta.get("nbm", [4], F32)
        Sp, Sp_b = ta.get("Sp", [128], F32)
        KD, KD_b = ta.get("KD", [ST], BF16)
        QD, QD_b = ta.get("QD", [ST], BF16)
        KDT, KDT_b = ta.get("KDT", [ST], BF16)
        VT, VT_b = ta.get("VT", [ST], BF16)
        AT, AT_b = ta.get("AT", [ST], BF16)
        OSQ, OSQ_b = ta.get("OSQ", [ST], BF16)
        OG, OG_b = ta.get("OG", [ST], BF16)
        Sbf, Sbf_b = ta.get("Sbf", [128], BF16)
        Bf, Bq, Bv, Bg, Ba, Bt, Bo, Bp = range(8)
        BtB = pb_t[Bt][:].bitcast(BF16)

        for head in range(8):
            slot = next_slot()
            wt = w_t[slot]
            k.dma(pool, w_st[slot], wt[:, 0:KC * 512], hwin_d[idx, head], writes=[w_b[slot]])
            k.dma(pool, w_st[slot], wt[:, KC * 512:KC * 512 + D], hwout_d[idx, head], writes=[w_b[slot]])
            Sst = S_t[:, idx * 8 + head, :]
            Sb_ = S_b[idx * 8 + head]
            lbc = lb_t[:, idx, head:head + 1]
            c1c = c1_t[:, idx, head:head + 1]

            def wi(kc, which):
                base = kc * 512 + which * 128
                return wt[:, base:base + 128]

            for s in range(NSUB):
                sl = slice(s * ST, (s + 1) * ST)
                for which, bank in ((1, Bf), (0, Bq)):
                    for kc in range(KC):
                        k.op(pe, lambda e, kc=kc, which=which, bank=bank: e.matmul(
                            pb_t[bank][:], lhsT=wi(kc, which), rhs=hn_t[:, kc, sl],
                            start=(kc == 0), stop=(kc == KC - 1)),
                            reads=[w_b[slot], hn_b[s]], writes=[pb_b[bank]], inc=(kc == KC - 1))
                for blk in range(4):
                    for kc in range(KC):
                        k.op(pe, lambda e, kc=kc, blk=blk: e.matmul(
                            pb_t[Bv][:, blk * 128:(blk + 1) * 128],
                            lhsT=hn_t[:, kc, s * ST + blk * 128:s * ST + (blk + 1) * 128], rhs=wi(kc, 2),
                            start=(kc == 0), stop=(kc == KC - 1)),
                            reads=[w_b[slot], hn_b[s]], writes=[pb_b[Bv]], inc=(kc == KC - 1))
                for kc in range(KC):
                    k.op(pe, lambda e, kc=kc: e.matmul(
                        pb_t[Bg][:], lhsT=wi(kc, 3), rhs=hn_t[:, kc, sl],
                        start=(kc == 0), stop=(kc == KC - 1)),
                        reads=[w_b[slot], hn_b[s]], writes=[pb_b[Bg]], inc=(kc == KC - 1))
                k.op(act, lambda e: e.activation(out=Te, in_=pb_t[Bf][:], func=AF.Exp, scale=-1.0),
                     reads=[pb_b[Bf]], writes=[Te_b])
                k.op(act, lambda e: e.activation(out=TL1, in_=Te, func=AF.Ln, bias=1.0, scale=1.0),
                     reads=[Te_b], writes=[TL1_b])
                k.op(act, lambda e: e.activation(out=TL2, in_=Te, func=AF.Ln, bias=1.0, scale=lbc),
                     reads=[Te_b, const_b], writes=[TL2_b])
                for blk in range(4):
                    cs = slice(blk * 128, (blk + 1) * 128)
                    k.op(dve, lambda e, cs=cs: e.tensor_tensor_scan(out=Tb[:, cs], data0=TL2[:, cs], data1=TL1[:, cs],
                                                                    initial=0.0, op0=ALU.add, op1=ALU.subtract),
                         reads=[TL1_b, TL2_b], writes=[Tb_b])
                k.op(dve, lambda e: e.tensor_tensor(out=Tt, in0=pb_t[Bf][:], in1=TL1, op=ALU.add),
                     reads=[pb_b[Bf], TL1_b], writes=[Tt_b])
                k.op(dve, lambda e: e.tensor_tensor(out=Tt, in0=Tt, in1=Tb, op=ALU.add),
                     reads=[Tt_b, Tb_b], writes=[Tt_b])
                Tb4 = Tb.rearrange("p (b t) -> p b t", b=4)
                bm_v, bl_v = Tb4[:, :, 63], Tb4[:, :, 127]
                k.op(act, lambda e: e.activation(out=ebl, in_=bl_v, func=AF.Exp), reads=[Tb_b], writes=[ebl_b])
                k.op(act, lambda e: e.activation(out=ebm, in_=bm_v, func=AF.Exp), reads=[Tb_b], writes=[ebm_b])
                k.op(dve, lambda e: e.tensor_tensor(out=eblm, in0=bl_v, in1=bm_v, op=ALU.subtract),
                     reads=[Tb_b], writes=[eblm_b])
                k.op(act, lambda e: e.activation(out=eblm, in_=eblm, func=AF.Exp), reads=[eblm_b], writes=[eblm_b])
                k.op(dve, lambda e: e.tensor_scalar(out=cbm, in0=bm_v, scalar1=c1c, scalar2=None, op0=ALU.add),
                     reads=[Tb_b, const_b], writes=[cbm_b])
                k.op(dve, lambda e: e.tensor_scalar(out=nbm, in0=bm_v, scalar1=-1.0, scalar2=None, op0=ALU.mult),
                     reads=[Tb_b], writes=[nbm_b])
                for blk in range(4):
                    cs = slice(blk * 128, (blk + 1) * 128)
                    k.op(act, lambda e, cs=cs, blk=blk: e.activation(out=KD[:, cs], in_=Tt[:, cs], func=AF.Exp,
                                                                     bias=cbm[:, blk:blk + 1], scale=-1.0),
                         reads=[Tt_b, cbm_b], writes=[KD_b])
                k.op(act, lambda e: e.activation(out=Te, in_=pb_t[Bq][:], func=AF.Exp, scale=-1.0),
                     reads=[pb_b[Bq]], writes=[Te_b])
                k.op(act, lambda e: e.activation(out=TL1, in_=Te, func=AF.Ln, bias=1.0, scale=1.0),
                     reads=[Te_b], writes=[TL1_b])
                k.op(dve, lambda e: e.tensor_tensor(out=TL2, in0=Tb, in1=TL1, op=ALU.subtract),
                     reads=[Tb_b, TL1_b], writes=[TL2_b])
                for blk in range(4):
                    cs = slice(blk * 128, (blk + 1) * 128)
                    k.op(act, lambda e, cs=cs, blk=blk: e.activation(out=TL2[:, cs], in_=TL2[:, cs], func=AF.Exp,
                                                                     bias=nbm[:, blk:blk + 1], scale=1.0),
                         reads=[TL2_b, nbm_b], writes=[TL2_b])
                k.op(dve, lambda e: e.tensor_tensor(out=QD, in0=pb_t[Bq][:], in1=TL2, op=ALU.mult),
                     reads=[pb_b[Bq], TL2_b], writes=[QD_b])
                for blk in range(4):
                    cs = slice(blk * 128, (blk + 1) * 128)
                    k.op(pe, lambda e, cs=cs: e.transpose(BtB[:, cs], KD[:, cs], idb_t[:]),
                         reads=[KD_b, const_b], writes=[pb_b[Bt]], inc=(blk == 3))
                k.op(dve, lambda e: e.tensor_copy(out=KDT, in_=BtB[:, 0:ST]), reads=[pb_b[Bt]], writes=[KDT_b])
                k.op(act, lambda e: e.copy(out=VT, in_=pb_t[Bv][:]), reads=[pb_b[Bv]], writes=[VT_b])
                k.op(act, lambda e: e.activation(out=Te, in_=pb_t[Bg][:], func=AF.Exp, scale=-1.0),
                     reads=[pb_b[Bg]], writes=[Te_b])
                k.op(act, lambda e: e.activation(out=TL1, in_=Te, func=AF.Ln, bias=1.0, scale=1.0),
                     reads=[Te_b], writes=[TL1_b])
                k.op(act, lambda e: e.activation(out=TL1, in_=TL1, func=AF.Exp, scale=-1.0),
                     reads=[TL1_b], writes=[TL1_b])
                k.op(dve, lambda e: e.tensor_tensor(out=Tgs, in0=pb_t[Bg][:], in1=TL1, op=ALU.mult),
                     reads=[pb_b[Bg], TL1_b], writes=[Tgs_b])
                for blk in range(4):
                    cs = slice(blk * 128, (blk + 1) * 128)
                    k.op(pe, lambda e, cs=cs: e.matmul(pb_t[Ba][:, cs], lhsT=KD[:, cs], rhs=QD[:, cs],
                                                       start=True, stop=True),
                         reads=[KD_b, QD_b], writes=[pb_b[Ba]], inc=(blk == 3))
                k.op(dve, lambda e: e.tensor_tensor(out=AT, in0=pb_t[Ba][:], in1=cm4_t[:], op=ALU.mult),
                     reads=[pb_b[Ba], const_b], writes=[AT_b])
                for blk in range(4):
                    cs = slice(blk * 128, (blk + 1) * 128)
                    k.op(act, lambda e, blk=blk: e.activation(out=Sbf, in_=Sst, func=AF.Identity,
                                                              scale=ebm[:, blk:blk + 1]),
                         reads=[Sb_, ebm_b], writes=[Sbf_b])
                    k.op(pe, lambda e, cs=cs: e.matmul(pb_t[Bo][:, cs], lhsT=VT[:, cs], rhs=AT[:, cs],
                                                       start=True, stop=False),
                         reads=[VT_b, AT_b], writes=[pb_b[Bo]], inc=False)
                    k.op(pe, lambda e, cs=cs: e.matmul(pb_t[Bo][:, cs], lhsT=Sbf, rhs=QD[:, cs],
                                                       start=False, stop=True),
                         reads=[Sbf_b, QD_b], writes=[pb_b[Bo]], inc=True)
                    k.op(pe, lambda e, cs=cs: e.matmul(pb_t[Bt][:, 256:384], lhsT=KDT[:, cs], rhs=VT[:, cs],
                                                       start=True, stop=True),
                         reads=[KDT_b, VT_b], writes=[pb_b[Bt]], inc=True)
                    k.op(act, lambda e, blk=blk: e.activation(out=Sp, in_=Sst, func=AF.Identity,
                                                              scale=ebl[:, blk:blk + 1]),
                         reads=[Sb_, ebl_b], writes=[Sp_b])
                    k.op(dve, lambda e, blk=blk: e.scalar_tensor_tensor(
                        out=Sst, in0=pb_t[Bt][:, 256:384], scalar=eblm[:, blk:blk + 1], in1=Sp,
                        op0=ALU.mult, op1=ALU.add), reads=[pb_b[Bt], eblm_b, Sp_b], writes=[Sb_])
                k.op(act, lambda e: e.activation(out=OSQ, in_=pb_t[Bo][:], func=AF.Square),
                     reads=[pb_b[Bo]], writes=[OSQ_b])
                k.op(pe, lambda e: e.matmul(pb_t[Ba][:], lhsT=ones_t[:], rhs=OSQ, start=True, stop=True),
                     reads=[OSQ_b, const_b], writes=[pb_b[Ba]], inc=True)
                k.op(act, lambda e: e.activation(out=Tr, in_=pb_t[Ba][:], func=AF.Ln, bias=EPS, scale=1.0 / 128),
                     reads=[pb_b[Ba]], writes=[Tr_b])
                k.op(act, lambda e: e.activation(out=Tr, in_=Tr, func=AF.Exp, scale=-0.5), reads=[Tr_b], writes=[Tr_b])
                k.op(dve, lambda e: e.scalar_tensor_tensor(out=To, in0=pb_t[Bo][:], scalar=pcol(HN0 + idx), in1=Tr,
                                                           op0=ALU.mult, op1=ALU.mult),
                     reads=[pb_b[Bo], Tr_b, pc_b], writes=[To_b])
                k.op(dve, lambda e: e.tensor_tensor(out=OG, in0=To, in1=Tgs, op=ALU.mult),
                     reads=[To_b, Tgs_b], writes=[OG_b])
                for dc in range(KC):
                    k.op(pe, lambda e, dc=dc: e.matmul(pb_t[Bp][:], lhsT=wt[:, KC * 512 + dc * 128:KC * 512 + (dc + 1) * 128],
                                                       rhs=OG, start=True, stop=True),
                         reads=[w_b[slot], OG_b], writes=[pb_b[Bp]], inc=True)
                    k.op(dve, lambda e, dc=dc: e.tensor_tensor(out=h_t[:, dc, sl], in0=pb_t[Bp][:],
                                                               in1=h_t[:, dc, sl], op=ALU.add),
                         reads=[pb_b[Bp], h_b[dc][s]], writes=[h_b[dc][s]])

    out_streams = []
    for hf in range(NHALF):
        load_half(hf)
        for kind, l in plan:
            if kind == "attn":
                attn_phase(l, hf)
            elif kind == "hgrn":
                hgrn_phase(l, hf)
            else:
                ffn_phase(l, hf)
        out_streams += store_half(hf)
    for st in out_streams:
        sp.h.wait_ge(st.sem, st.count)
    return k


def pack_inputs(inp):
    f = np.float32
    pc = np.zeros((128, NPC), f)
    norms = [inp["norm_mix"][i] for i in range(4)] + [inp["norm_ffn"][i] for i in range(4)] + [inp["norm_final"]]
    for n, gvec in enumerate(norms):
        pc[:, GAIN0 + n * 8:GAIN0 + (n + 1) * 8] = np.asarray(gvec, f).reshape(8, 128).T
    cw = np.asarray(inp["ffn_conv_w"], f)
    cb = np.asarray(inp["ffn_conv_b"], f)
    for l in range(DEPTH):
        blk = np.concatenate([cw[l], cb[l][None]], axis=0)
        blk = blk.reshape(4, 44, 128).transpose(2, 1, 0)
        pc[:, CONV0 + l * 176:CONV0 + (l + 1) * 176] = blk.reshape(128, 176)
    lg = np.asarray(inp["hgrn_lb_logits"], f)
    pc[:, LG0:LG0 + 32] = lg.reshape(4, 8, 128).transpose(2, 1, 0).reshape(128, 32)
    pc[:, HN0:HN0 + 2] = np.asarray(inp["hgrn_norm"], f).T
    pc[:, SK0:SK0 + 32] = np.broadcast_to(np.asarray(inp["attn_sinks"], f).reshape(1, 32), (128, 32))

    awi = np.asarray(inp["attn_w_in"], f)
    q = awi[:, :, 0:1024]
    kk = awi[:, :, 1024:1280].reshape(2, D, 4, 64)
    kdup = np.concatenate([kk, kk], axis=3).reshape(2, D, 512)
    v = awi[:, :, 1280:1536]
    win = np.concatenate([q, kdup, v], axis=2)
    awin = np.ascontiguousarray(win.reshape(2, KC, 128, 1792).transpose(0, 2, 1, 3)).reshape(2, 128, KC * 1792)
    awo = np.asarray(inp["attn_w_out"], f)
    awout = np.ascontiguousarray(awo.reshape(2, KC, 128, D).transpose(0, 2, 1, 3)).reshape(2, 128, KC * D)

    hwi = np.asarray(inp["hgrn_w_in"], f)
    hwi = hwi.reshape(2, KC, 128, 4, 8, 128)
    hwin = np.ascontiguousarray(hwi.transpose(0, 4, 2, 1, 3, 5)).reshape(2, 8, 128, KC * 512)
    hwo = np.asarray(inp["hgrn_w_out"], f)
    hwout = np.ascontiguousarray(hwo.reshape(2, 8, 128, D))

    fu = np.asarray(inp["ffn_w_up"], f)
    fd = np.asarray(inp["ffn_w_down"], f)
    fup = np.zeros((DEPTH, NG, 128, KC, 2, GMAX * 128), f)
    fdn = np.zeros((DEPTH, NG, 128, GMAX, D), f)
    fu_r = fu.reshape(DEPTH, KC, 128, 2, D_FF)
    for g in range(NG):
        n = G_SIZES[g]
        c0 = G_OFF[g] * 128
        fup[:, g, :, :, :, 0:n * 128] = fu_r[:, :, :, :, c0:c0 + n * 128].transpose(0, 2, 1, 3, 4)
        fdn[:, g, :, 0:n, :] = fd[:, c0:c0 + n * 128, :].reshape(DEPTH, n, 128, D).transpose(0, 2, 1, 3)
    fup = fup.reshape(DEPTH, NG, 128, KC * 2 * GMAX * 128)
    fdn = fdn.reshape(DEPTH, NG, 128, GMAX * D)
    return dict(pc=pc, awin=awin, awout=awout, hwin=hwin, hwout=hwout, fup=fup, fdn=fdn)


FULL_PLAN = [("attn", 0), ("ffn", 0), ("hgrn", 1), ("ffn", 1), ("attn", 2), ("ffn", 2), ("hgrn", 3), ("ffn", 3)]
_CACHE = {}


def run_plan(inputs, plan, final_norm=True, x_override=None, trace=False, n_cores=8):
    key = (tuple(plan), final_norm)
    if key not in _CACHE:
        _CACHE[key] = build(plan, final_norm)
    k = _CACHE[key]
    shared = pack_inputs(inputs)
    x = np.asarray(inputs["x"] if x_override is None else x_override, np.float32)
    in_maps = []
    for b in range(n_cores):
        m = dict(shared)
        m["x"] = np.ascontiguousarray(x[b])
        in_maps.append(m)
    res = run_bass_kernel_spmd(k.nc, in_maps, core_ids=list(range(n_cores)), trace=trace)
    out = np.stack([np.asarray(r["out"], np.float32) for r in res.results], axis=0)
    return out, res


def kernel(**inputs):
    out, _ = run_plan(inputs, FULL_PLAN, True)
    return out
```

```python
import numpy as np
from contextlib import ExitStack
import concourse.bass as bass
import concourse.mybir as mybir
from concourse.bass_utils import run_bass_kernel_spmd

F32 = mybir.dt.float32
BF16 = mybir.dt.bfloat16
AF = mybir.ActivationFunctionType
ALU = mybir.AluOpType
AX = mybir.AxisListType

S = 4096
D = 1024
DEPTH = 4
T = 2048
NHALF = S // T
ST = 512
NSUB = T // ST
KC = D // 128
D_FF = 2816
NFC = D_FF // 128
G_SIZES = [5, 5, 4, 4, 4]
G_OFF = [0, 5, 10, 14, 18]
NG = len(G_SIZES)
GMAX = 5
EPS = 1e-6
NQH = 16
HD = 64

GAIN0 = 0
CONV0 = GAIN0 + 9 * 8
LG0 = CONV0 + DEPTH * 44 * 4
HN0 = LG0 + 32
SK0 = HN0 + 2
NPC = SK0 + 32


class Stream:
    def __init__(self, sem, step):
        self.sem = sem
        self.count = 0
        self.step = step


class Eng:
    def __init__(self, name, h, st):
        self.name = name
        self.h = h
        self.st = st
        self.seen = {}


class Buf:
    __slots__ = ("name", "w", "r")

    def __init__(self, name, deps=None):
        self.name = name
        self.w = None
        self.r = dict(deps) if deps else {}


def merge_deps(bufs):
    d = {}
    for b in bufs:
        if b.w is not None:
            s, v = b.w
            if d.get(s, 0) < v:
                d[s] = v
        for s, v in b.r.items():
            if d.get(s, 0) < v:
                d[s] = v
    return d


class K:
    def __init__(self):
        self.nc = bass.Bass("TRN2", target_bir_lowering=False)
        self.es = ExitStack()
        nc = self.nc
        self.pe = Eng("pe", nc.tensor, self.stream("s_pe", 1))
        self.act = Eng("act", nc.scalar, self.stream("s_act", 1))
        self.dve = Eng("dve", nc.vector, self.stream("s_dve", 1))
        self.pool = Eng("pool", nc.gpsimd, self.stream("s_pool", 1))
        self.sp = Eng("sp", nc.sync, self.stream("s_sp", 1))

    def stream(self, name, step):
        sem = self.es.enter_context(self.nc.semaphore(name))
        return Stream(sem, step)

    def sb(self, name, shape, dt):
        return self.es.enter_context(self.nc.sbuf_tensor("sb_" + name, shape, dt))

    def ps(self, name, shape, dt):
        return self.es.enter_context(self.nc.psum_tensor("ps_" + name, shape, dt))

    def _sync(self, eng, reads, writes, is_dma=False):
        need = {}
        for b in reads:
            if b.w is not None:
                s, v = b.w
                if need.get(s, 0) < v:
                    need[s] = v
        for b in writes:
            if b.w is not None:
                s, v = b.w
                if is_dma or s is not eng.st:
                    if need.get(s, 0) < v:
                        need[s] = v
            for s, v in b.r.items():
                if is_dma or s is not eng.st:
                    if need.get(s, 0) < v:
                        need[s] = v
        for s, v in need.items():
            if eng.seen.get(s, 0) < v:
                eng.h.wait_ge(s.sem, v)
                eng.seen[s] = v

    @staticmethod
    def _record(stream, tick, reads, writes):
        for b in reads:
            if b.r.get(stream, 0) < tick:
                b.r[stream] = tick
        for b in writes:
            b.w = (stream, tick)
            b.r = {}

    def op(self, eng, fn, reads=(), writes=(), inc=True):
        self._sync(eng, reads, writes)
        ins = fn(eng.h)
        if inc:
            eng.st.count += 1
            ins.then_inc(eng.st.sem, 1)
            tick = eng.st.count
        else:
            tick = eng.st.count + 1
        self._record(eng.st, tick, reads, writes)
        return ins

    def dma(self, eng, stream, out, in_, reads=(), writes=()):
        self._sync(eng, reads, writes, is_dma=True)
        ins = eng.h.dma_start(out=out, in_=in_)
        stream.count += 16
        ins.then_inc(stream.sem, 16)
        self._record(stream, stream.count, reads, writes)
        return ins


def build(plan, final_norm=True):
    k = K()
    nc = k.nc
    pe, act, dve, pool, sp = k.pe, k.act, k.dve, k.pool, k.sp

    x_d = nc.dram_tensor("x", [S, D], F32, kind="ExternalInput").ap()
    pc_d = nc.dram_tensor("pc", [128, NPC], F32, kind="ExternalInput").ap()
    awin_d = nc.dram_tensor("awin", [2, 128, KC * 1792], F32, kind="ExternalInput").ap()
    awout_d = nc.dram_tensor("awout", [2, 128, KC * D], F32, kind="ExternalInput").ap()
    hwin_d = nc.dram_tensor("hwin", [2, 8, 128, KC * 512], F32, kind="ExternalInput").ap()
    hwout_d = nc.dram_tensor("hwout", [2, 8, 128, D], F32, kind="ExternalInput").ap()
    fup_d = nc.dram_tensor("fup", [DEPTH, NG, 128, KC * 2 * GMAX * 128], F32, kind="ExternalInput").ap()
    fdn_d = nc.dram_tensor("fdn", [DEPTH, NG, 128, GMAX * D], F32, kind="ExternalInput").ap()
    out_d = nc.dram_tensor("out", [S, D], F32, kind="ExternalOutput").ap()

    h_t = k.sb("h", [128, KC, T], F32)
    h_b = [[Buf("h%d_%d" % (c, s)) for s in range(NSUB)] for c in range(KC)]
    hn_t = k.sb("hn", [128, KC, T], BF16)
    hn_b = [Buf("hn%d" % s) for s in range(NSUB)]
    pc_t = k.sb("pc", [128, NPC], F32)
    pc_b = Buf("pc")
    WSLOT = 15360
    w_t = [k.sb("w%d" % i, [128, WSLOT], BF16) for i in range(2)]
    w_b = [Buf("w%d" % i) for i in range(2)]
    w_st = [k.stream("s_w%d" % i, 16) for i in range(2)]
    wcount = [0]
    idb_t = k.sb("idb", [128, 128], BF16)
    idf_t = k.sb("idf", [128, 128], F32)
    ones_t = k.sb("ones", [128, 128], BF16)
    cm4_t = k.sb("cm4", [128, 512], BF16)
    mall_t = k.sb("mall", [128, 8, 512], BF16)
    const_b = Buf("const")
    lb_t = k.sb("lb", [128, 2, 8], F32)
    c1_t = k.sb("c1", [128, 2, 8], F32)
    esk_t = k.sb("esk", [128, 32], F32)
    S_t = k.sb("S", [128, 16, 128], F32)
    S_b = [Buf("S%d" % i) for i in range(16)]
    ktc_t = k.sb("ktc", [128, 2, 4, 128], BF16)
    vc_t = k.sb("vc", [128, 2, 256], BF16)
    kvc_b = [Buf("kvc%d" % i) for i in range(2)]
    halo_t = k.sb("halo", [128, DEPTH, 44, 2], F32)
    halo_b = [Buf("halo%d" % l) for l in range(DEPTH)]
    sq_t = [k.sb("sq%d" % i, [128, ST], BF16) for i in range(2)]
    sq_b = [Buf("sq%d" % i) for i in range(2)]
    rstd_t = k.sb("rstd", [128, ST], F32)
    rstd_b = Buf("rstd")
    TMPW = 5200
    tmp_t = k.sb("tmp", [128, TMPW], F32)
    tmp_live = []

    pb_t = [k.ps("pb%d" % i, [128, 512], F32) for i in range(8)]
    pb_b = [Buf("pb%d" % i) for i in range(8)]

    class TmpAlloc:
        def __init__(self):
            self.deps = merge_deps(tmp_live)
            del tmp_live[:]
            self.off = 0

        def get(self, name, shape_free, dt):
            n = int(np.prod(shape_free))
            words = n if dt == F32 else (n + 1) // 2
            assert self.off + words <= TMPW, (name, self.off, words)
            ap = tmp_t[:, self.off:self.off + words]
            if dt != F32:
                ap = ap.bitcast(dt)[:, 0:n]
            self.off += words
            b = Buf(name, self.deps)
            tmp_live.append(b)
            return ap, b

    TAIL0 = 5120
    tail_live = [[], []]

    class TailAlloc:
        def __init__(self, i):
            self.i = i
            self.deps = merge_deps(tail_live[i] + [w_b[i]])
            del tail_live[i][:]
            self.off = TAIL0

        def get(self, name, shape_free, dt):
            n = int(np.prod(shape_free))
            el = n if dt == BF16 else 2 * n
            assert self.off + el <= WSLOT, (name, self.off, el)
            ap = w_t[self.i][:, self.off:self.off + el]
            if dt != BF16:
                ap = ap.bitcast(dt)
            self.off += el
            b = Buf(name, self.deps)
            tail_live[self.i].append(b)
            return ap, b

    def pcol(j):
        return pc_t[:, j:j + 1]

    st_pc = k.stream("s_pc", 16)
    k.dma(sp, st_pc, pc_t[:], pc_d, writes=[pc_b])
    k.op(pool, lambda e: e.memset(ones_t[:], 1.0), writes=[const_b])
    k.op(pool, lambda e: e.memset(idb_t[:], 1.0), writes=[const_b])
    k.op(pool, lambda e: e.affine_select(out=idb_t[:], in_=idb_t[:], pattern=[[-1, 128]], compare_op=ALU.is_equal,
                                         fill=0.0, base=0, channel_multiplier=1), reads=[const_b], writes=[const_b])
    k.op(pool, lambda e: e.memset(idf_t[:], 1.0), writes=[const_b])
    k.op(pool, lambda e: e.affine_select(out=idf_t[:], in_=idf_t[:], pattern=[[-1, 128]], compare_op=ALU.is_equal,
                                         fill=0.0, base=0, channel_multiplier=1), reads=[const_b], writes=[const_b])
    k.op(pool, lambda e: e.memset(cm4_t[:], 1.0), writes=[const_b])
    for blk in range(4):
        k.op(pool, lambda e, blk=blk: e.affine_select(out=cm4_t[:, blk * 128:(blk + 1) * 128],
                                                      in_=cm4_t[:, blk * 128:(blk + 1) * 128],
                                                      pattern=[[1, 128]], compare_op=ALU.is_ge, fill=0.0, base=0,
                                                      channel_multiplier=-1), reads=[const_b], writes=[const_b])
    k.op(pool, lambda e: e.memset(halo_t[:].rearrange("p a b c -> p (a b c)"), 0.0), writes=halo_b)
    k.op(pool, lambda e: e.memset(S_t[:].rearrange("p a b -> p (a b)"), 0.0), writes=S_b)
    k.op(pool, lambda e: e.memset(ktc_t[:].rearrange("p a b c -> p (a b c)"), 0.0), writes=kvc_b)
    k.op(pool, lambda e: e.memset(vc_t[:].rearrange("p a b -> p (a b)"), 0.0), writes=kvc_b)

    has_attn = any(p[0] == "attn" for p in plan)
    has_hgrn = any(p[0] == "hgrn" for p in plan)

    if has_attn:
        ta = TmpAlloc()
        d1, d1_b = ta.get("d1", [128], F32)
        dpos, dpos_b = ta.get("dpos", [128], F32)
        dneg, dneg_b = ta.get("dneg", [128], F32)
        mtmp, mtmp_b = ta.get("mtmp", [256], F32)
        k.op(pool, lambda e: e.iota(d1, pattern=[[1, 128]], base=0, channel_multiplier=-1,
                                    allow_small_or_imprecise_dtypes=True), writes=[d1_b])
        k.op(dve, lambda e: e.tensor_scalar_max(out=dpos, in0=d1, scalar1=0.0), reads=[d1_b], writes=[dpos_b])
        k.op(dve, lambda e: e.tensor_scalar_min(out=dneg, in0=d1, scalar1=0.0), reads=[d1_b], writes=[dneg_b])
        for gp in range(8):
            g, p = gp // 2, gp % 2
            for hh in range(2):
                hq = 4 * g + p + 2 * hh
                slope = float(2.0 ** (-8.0 * (hq + 1) / NQH))
                k.op(act, lambda e, slope=slope: e.activation(out=mtmp[:, 0:128], in_=dneg, func=AF.Exp,
                                                              bias=-slope * 128.0, scale=-slope),
                     reads=[dneg_b], writes=[mtmp_b])
                k.op(act, lambda e, slope=slope: e.activation(out=mtmp[:, 128:256], in_=dpos, func=AF.Exp,
                                                              scale=-slope),
                     reads=[dpos_b], writes=[mtmp_b])
                k.op(pool, lambda e, gp=gp, hh=hh: e.affine_select(
                    out=mall_t[:, gp, hh * 128:(hh + 1) * 128], in_=mtmp[:, 0:128], pattern=[[-1, 128]],
                    compare_op=ALU.is_gt, fill=0.0, base=0, channel_multiplier=1),
                    reads=[mtmp_b], writes=[const_b])
                k.op(pool, lambda e, gp=gp, hh=hh: e.affine_select(
                    out=mall_t[:, gp, 256 + hh * 128:256 + (hh + 1) * 128], in_=mtmp[:, 128:256],
                    pattern=[[1, 128]], compare_op=ALU.is_ge, fill=0.0, base=0, channel_multiplier=-1),
                    reads=[mtmp_b], writes=[const_b])
        k.op(act, lambda e: e.activation(out=esk_t[:, 0:16], in_=pc_t[:, SK0:SK0 + 16], func=AF.Exp),
             reads=[pc_b], writes=[const_b])

    if has_hgrn:
        ta = TmpAlloc()
        lg = pc_t[:, LG0:LG0 + 32].rearrange("p (h l) -> p h l", l=4)
        mx, mx_b = ta.get("mx", [8], F32)
        ex, ex_b = ta.get("ex", [32], F32)
        sm, sm_b = ta.get("sm", [8], F32)
        ex3 = ex.rearrange("p (h l) -> p h l", l=4)
        k.op(dve, lambda e: e.tensor_reduce(out=mx, in_=lg, axis=AX.X, op=ALU.max), reads=[pc_b], writes=[mx_b])
        for li in range(4):
            k.op(dve, lambda e, li=li: e.tensor_tensor(out=ex3[:, :, li], in0=lg[:, :, li], in1=mx, op=ALU.subtract),
                 reads=[pc_b, mx_b, ex_b], writes=[ex_b])
        k.op(act, lambda e: e.activation(out=ex, in_=ex, func=AF.Exp), reads=[ex_b], writes=[ex_b])
        k.op(dve, lambda e: e.tensor_reduce(out=sm, in_=ex3, axis=AX.X, op=ALU.add), reads=[ex_b], writes=[sm_b])
        k.op(dve, lambda e: e.reciprocal(out=sm, in_=sm), reads=[sm_b], writes=[sm_b])
        for li in range(4):
            k.op(dve, lambda e, li=li: e.tensor_tensor(out=ex3[:, :, li], in0=ex3[:, :, li], in1=sm, op=ALU.mult),
                 reads=[ex_b, sm_b], writes=[ex_b])
        k.op(dve, lambda e: e.tensor_copy(out=lb_t[:, 0, :], in_=ex3[:, :, 1]), reads=[ex_b], writes=[const_b])
        k.op(dve, lambda e: e.tensor_tensor(out=lb_t[:, 1, :], in0=ex3[:, :, 1], in1=ex3[:, :, 2], op=ALU.add),
             reads=[ex_b], writes=[const_b])
        k.op(dve, lambda e: e.tensor_tensor(out=lb_t[:, 1, :], in0=lb_t[:, 1, :], in1=ex3[:, :, 3], op=ALU.add),
             reads=[ex_b, const_b], writes=[const_b])
        k.op(act, lambda e: e.activation(out=c1_t[:].rearrange("p a b -> p (a b)"),
                                         in_=lb_t[:].rearrange("p a b -> p (a b)"), func=AF.Ln, bias=1.0, scale=-1.0),
             reads=[const_b], writes=[const_b])

    def next_slot():
        i = wcount[0] % 2
        wcount[0] += 1
        return i

    evac_rr = [0]

    def evac_copy(out, in_, reads, writes, force_act=False):
        evac_rr[0] += 1
        if force_act or evac_rr[0] % 2 == 0:
            k.op(act, lambda e: e.copy(out=out, in_=in_), reads=reads, writes=writes)
        else:
            k.op(dve, lambda e: e.tensor_copy(out=out, in_=in_), reads=reads, writes=writes)

    def rmsnorm_sub(s, gain_idx, nbank, in_place=False):
        sl = slice(s * ST, (s + 1) * ST)
        for c in range(KC):
            i = c % 2
            k.op(act, lambda e, c=c, i=i: e.activation(out=sq_t[i][:], in_=h_t[:, c, sl], func=AF.Square),
                 reads=[h_b[c][s]], writes=[sq_b[i]])
            k.op(pe, lambda e, c=c, i=i: e.matmul(pb_t[nbank][:], lhsT=ones_t[:], rhs=sq_t[i][:],
                                                   start=(c == 0), stop=(c == KC - 1)),
                 reads=[sq_b[i], const_b], writes=[pb_b[nbank]], inc=True)
        k.op(act, lambda e: e.activation(out=rstd_t[:], in_=pb_t[nbank][:], func=AF.Ln, bias=EPS, scale=1.0 / D),
             reads=[pb_b[nbank]], writes=[rstd_b])
        k.op(act, lambda e: e.activation(out=rstd_t[:], in_=rstd_t[:], func=AF.Exp, scale=-0.5),
             reads=[rstd_b], writes=[rstd_b])
        for c in range(KC):
            if in_place:
                k.op(dve, lambda e, c=c: e.scalar_tensor_tensor(out=h_t[:, c, sl], in0=h_t[:, c, sl],
                                                                scalar=pcol(GAIN0 + gain_idx * 8 + c),
                                                                in1=rstd_t[:], op0=ALU.mult, op1=ALU.mult),
                     reads=[h_b[c][s], rstd_b, pc_b], writes=[h_b[c][s]])
            else:
                k.op(dve, lambda e, c=c: e.scalar_tensor_tensor(out=hn_t[:, c, sl], in0=h_t[:, c, sl],
                                                                scalar=pcol(GAIN0 + gain_idx * 8 + c),
                                                                in1=rstd_t[:], op0=ALU.mult, op1=ALU.mult),
                     reads=[h_b[c][s], rstd_b, pc_b], writes=[hn_b[s]])

    def load_half(hf):
        ta = TmpAlloc()
        xin = [ta.get("xin%d" % i, [D], F32) for i in range(2)]
        xst = [k.stream("s_xin%d_%d" % (hf, i), 16) for i in range(2)]
        for blk in range(T // 128):
            i = blk % 2
            t0 = hf * T + blk * 128
            s = blk // 4
            col = blk * 128
            k.dma(sp, xst[i], xin[i][0], x_d[t0:t0 + 128, :], writes=[xin[i][1]])
            for half in range(2):
                bank = half
                for cc in range(4):
                    c = half * 4 + cc
                    k.op(pe, lambda e, c=c, cc=cc, bank=bank, i=i: e.transpose(
                        pb_t[bank][:, cc * 128:(cc + 1) * 128], xin[i][0][:, c * 128:(c + 1) * 128], idf_t[:]),
                        reads=[xin[i][1], const_b], writes=[pb_b[bank]], inc=(cc == 3))
                evac_copy(h_t[:, half * 4:half * 4 + 4, col:col + 128],
                          pb_t[bank][:].rearrange("p (c t) -> p c t", c=4),
                          reads=[pb_b[bank]], writes=[h_b[half * 4 + cc][s] for cc in range(4)])

    def store_half(hf):
        ta = TmpAlloc()
        yo = [ta.get("yo%d" % i, [D], F32) for i in range(2)]
        yst = [k.stream("s_yo%d_%d" % (hf, i), 16) for i in range(2)]
        for s in range(NSUB):
            if final_norm:
                rmsnorm_sub(s, 8, 6, in_place=True)
            for bb in range(4):
                blk = s * 4 + bb
                i = blk % 2
                t0 = hf * T + blk * 128
                col = blk * 128
                for half in range(2):
                    bank = half
                    for cc in range(4):
                        c = half * 4 + cc
                        k.op(pe, lambda e, c=c, cc=cc, bank=bank: e.transpose(
                            pb_t[bank][:, cc * 128:(cc + 1) * 128], h_t[:, c, col:col + 128], idf_t[:]),
                            reads=[h_b[c][s], const_b], writes=[pb_b[bank]], inc=(cc == 3))
                    evac_copy(yo[i][0][:, half * 512:(half + 1) * 512], pb_t[bank][:],
                              reads=[pb_b[bank]], writes=[yo[i][1]])
                k.dma(sp, yst[i], out_d[t0:t0 + 128, :], yo[i][0], reads=[yo[i][1]])
        return yst

    def ffn_phase(l, hf):
        for s in range(NSUB):
            rmsnorm_sub(s, 4 + l, 6)
        ta = TmpAlloc()
        cg = [ta.get("cg%d" % i, [ST], F32) for i in range(2)]
        cv = [ta.get("cv%d" % i, [ST], F32) for i in range(2)]
        sg = [ta.get("sg%d" % i, [ST], BF16) for i in range(2)]
        actb = [ta.get("act%d" % i, [GMAX, ST], BF16) for i in range(2)]
        items = []
        gs_list = []
        for g in range(NG):
            n = G_SIZES[g]
            slot = next_slot()
            for s in range(NSUB):
                gs_list.append((g, s, slot))
                for jj in range(n):
                    items.append((g, s, jj, slot, len(gs_list) - 1))

        def wup(slot, kc, col0):
            base = kc * 2 * GMAX * 128 + col0
            return w_t[slot][:, base:base + 128]

        def wdn(slot, jj, dc):
            base = KC * 2 * GMAX * 128 + jj * D + dc * 128
            return w_t[slot][:, base:base + 128]

        def stageA(it, idx):
            g, s, jj, slot, gsi = it
            par = idx % 2
            sl = slice(s * ST, (s + 1) * ST)
            for which in range(2):
                bank = par * 2 + which
                col0 = which * GMAX * 128 + jj * 128
                for kc in range(KC):
                    k.op(pe, lambda e, kc=kc, bank=bank, col0=col0: e.matmul(
                        pb_t[bank][:], lhsT=wup(slot, kc, col0), rhs=hn_t[:, kc, sl],
                        start=(kc == 0), stop=(kc == KC - 1)),
                        reads=[w_b[slot], hn_b[s]], writes=[pb_b[bank]], inc=(kc == KC - 1))

        def stageB(it, idx):
            g, s, jj, slot, gsi = it
            par = idx % 2
            first_tok = (hf == 0 and s == 0)
            for which in range(2):
                bank = par * 2 + which
                P = pb_t[bank]
                C, Cb = (cg, cv)[which][par]
                jc = which * NFC + G_OFF[g] + jj
                cb = CONV0 + (l * 44 + jc) * 4
                hl = halo_t[:, l, jc, :]
                k.op(act, lambda e, P=P, C=C, cb=cb: e.activation(out=C, in_=P[:], func=AF.Identity,
                                                                  bias=pcol(cb + 3), scale=pcol(cb + 2)),
                     reads=[pb_b[bank], pc_b], writes=[Cb])
                k.op(dve, lambda e, P=P, C=C, cb=cb: e.scalar_tensor_tensor(
                    out=C[:, 1:ST], in0=P[:, 0:ST - 1], scalar=pcol(cb + 1), in1=C[:, 1:ST],
                    op0=ALU.mult, op1=ALU.add), reads=[pb_b[bank], Cb, pc_b], writes=[Cb])
                k.op(dve, lambda e, P=P, C=C, cb=cb: e.scalar_tensor_tensor(
                    out=C[:, 2:ST], in0=P[:, 0:ST - 2], scalar=pcol(cb + 0), in1=C[:, 2:ST],
                    op0=ALU.mult, op1=ALU.add), reads=[pb_b[bank], Cb, pc_b], writes=[Cb])
                if not first_tok:
                    k.op(dve, lambda e, C=C, cb=cb, hl=hl: e.scalar_tensor_tensor(
                        out=C[:, 0:1], in0=hl[:, 1:2], scalar=pcol(cb + 1), in1=C[:, 0:1],
                        op0=ALU.mult, op1=ALU.add), reads=[halo_b[l], Cb, pc_b], writes=[Cb])
                    k.op(dve, lambda e, C=C, cb=cb, hl=hl: e.scalar_tensor_tensor(
                        out=C[:, 0:2], in0=hl[:, 0:2], scalar=pcol(cb + 0), in1=C[:, 0:2],
                        op0=ALU.mult, op1=ALU.add), reads=[halo_b[l], Cb, pc_b], writes=[Cb])
                k.op(act, lambda e, P=P, hl=hl: e.copy(out=hl, in_=P[:, ST - 2:ST]),
                     reads=[pb_b[bank]], writes=[halo_b[l]])

        def stageC(it, idx):
            g, s, jj, slot, gsi = it
            par = idx % 2
            Cg, Cgb = cg[par]
            Cv, Cvb = cv[par]
            Sg, Sgb = sg[par]
            A, Ab = actb[gsi % 2]
            k.op(act, lambda e: e.activation(out=Sg, in_=Cg, func=AF.Silu), reads=[Cgb], writes=[Sgb])
            k.op(dve, lambda e: e.tensor_tensor(out=A[:, jj * ST:(jj + 1) * ST], in0=Sg, in1=Cv, op=ALU.mult),
                 reads=[Sgb, Cvb], writes=[Ab])

        def stageD(gsi):
            g, s, slot = gs_list[gsi]
            n = G_SIZES[g]
            A, Ab = actb[gsi % 2]
            sl = slice(s * ST, (s + 1) * ST)
            for dc in range(KC):
                bank = 4 + dc % 2
                for jj in range(n):
                    k.op(pe, lambda e, jj=jj, dc=dc, bank=bank: e.matmul(
                        pb_t[bank][:], lhsT=wdn(slot, jj, dc), rhs=A[:, jj * ST:(jj + 1) * ST],
                        start=(jj == 0), stop=(jj == n - 1)),
                        reads=[w_b[slot], Ab], writes=[pb_b[bank]], inc=(jj == n - 1))
                k.op(dve, lambda e, dc=dc, bank=bank: e.tensor_tensor(out=h_t[:, dc, sl], in0=pb_t[bank][:],
                                                                       in1=h_t[:, dc, sl], op=ALU.add),
                     reads=[pb_b[bank], h_b[dc][s]], writes=[h_b[dc][s]])

        def load_w(g, slot):
            wt, wb, wst = w_t[slot], w_b[slot], w_st[slot]
            k.dma(pool, wst, wt[:, 0:KC * 2 * GMAX * 128], fup_d[l, g], writes=[wb] + tail_live[slot])
            k.dma(pool, wst, wt[:, KC * 2 * GMAX * 128:KC * 2 * GMAX * 128 + GMAX * D], fdn_d[l, g], writes=[wb] + tail_live[slot])

        nI = len(items)
        for step in range(nI + 2):
            if step < nI:
                g_, s_, jj_, slot_, _ = items[step]
                if s_ == 0 and jj_ == 0:
                    load_w(g_, slot_)
                stageA(items[step], step)
            if 0 <= step - 1 < nI:
                stageB(items[step - 1], step - 1)
            if 0 <= step - 2 < nI:
                it = items[step - 2]
                stageC(it, step - 2)
                g, s, jj, slot, gsi = it
                if jj == G_SIZES[g] - 1:
                    stageD(gsi)

    def attn_phase(l, hf):
        idx = l // 2
        sA = next_slot()
        k.dma(pool, w_st[sA], w_t[sA][:, 0:KC * 1792], awin_d[idx], writes=[w_b[sA]] + tail_live[sA])
        sB = next_slot()
        k.dma(pool, w_st[sB], w_t[sB][:, 0:KC * D], awout_d[idx], writes=[w_b[sB]] + tail_live[sB])
        ta = TmpAlloc()
        kt, kt_b = ta.get("kt", [4, 640], BF16)
        vb, vb_b = ta.get("vb", [5, 256], BF16)
        et = [ta.get("e%d" % i, [512], BF16) for i in range(2)]
        pt = [ta.get("p%d" % i, [512], BF16) for i in range(4)]
        rd = [ta.get("rd%d" % i, [256], F32) for i in range(2)]
        kt3 = kt.rearrange("p (g t) -> p g t", g=4)
        vb3 = vb.rearrange("p (b d) -> p b d", b=5)
        hnS_b, qt_b, ot_b = hn_b[0], hn_b[1], hn_b[2]

        def hnS(kc, a, b):
            return hn_t[:, kc, a:b]

        QT0, OT0 = ST, 2 * ST
        k.op(dve, lambda e: e.tensor_copy(out=kt3[:, :, 0:128], in_=ktc_t[:, idx]), reads=[kvc_b[idx]], writes=[kt_b])
        k.op(dve, lambda e: e.tensor_copy(out=vb3[:, 0, :], in_=vc_t[:, idx]), reads=[kvc_b[idx]], writes=[vb_b])

        def win(kc, c0, n):
            base = kc * 1792 + c0
            return w_t[sA][:, base:base + n]

        def wout(c, dc):
            base = c * D + dc * 128
            return w_t[sB][:, base:base + 128]

        pcnt = [0]
        for s in range(NSUB):
            sl = slice(s * ST, (s + 1) * ST)
            for c in range(KC):
                i = c % 2
                k.op(act, lambda e, c=c, i=i: e.activation(out=sq_t[i][:], in_=h_t[:, c, sl], func=AF.Square),
                     reads=[h_b[c][s]], writes=[sq_b[i]])
                k.op(pe, lambda e, c=c, i=i: e.matmul(pb_t[6][:], lhsT=ones_t[:], rhs=sq_t[i][:],
                                                       start=(c == 0), stop=(c == KC - 1)),
                     reads=[sq_b[i], const_b], writes=[pb_b[6]], inc=True)
            k.op(act, lambda e: e.activation(out=rstd_t[:], in_=pb_t[6][:], func=AF.Ln, bias=EPS, scale=1.0 / D),
                 reads=[pb_b[6]], writes=[rstd_b])
            k.op(act, lambda e: e.activation(out=rstd_t[:], in_=rstd_t[:], func=AF.Exp, scale=-0.5),
                 reads=[rstd_b], writes=[rstd_b])
            for c in range(KC):
                k.op(dve, lambda e, c=c: e.scalar_tensor_tensor(out=hn_t[:, c, 0:ST], in0=h_t[:, c, sl],
                                                                scalar=pcol(GAIN0 + l * 8 + c),
                                                                in1=rstd_t[:], op0=ALU.mult, op1=ALU.mult),
                     reads=[h_b[c][s], rstd_b, pc_b], writes=[hnS_b])
            for c in range(KC + 4):
                bank = pcnt[0] % 2
                pcnt[0] += 1
                for kc in range(KC):
                    k.op(pe, lambda e, c=c, kc=kc, bank=bank: e.matmul(
                        pb_t[bank][:], lhsT=win(kc, c * 128, 128), rhs=hnS(kc, 0, ST),
                        start=(kc == 0), stop=(kc == KC - 1)),
                        reads=[w_b[sA], hnS_b], writes=[pb_b[bank]], inc=(kc == KC - 1))
                if c < KC:
                    evac_copy(hn_t[:, c, QT0:QT0 + ST], pb_t[bank][:], reads=[pb_b[bank]], writes=[qt_b], force_act=True)
                else:
                    evac_copy(kt3[:, c - KC, 128:640], pb_t[bank][:], reads=[pb_b[bank]], writes=[kt_b], force_act=True)
            for pair in range(2):
                bank = pcnt[0] % 2
                pcnt[0] += 1
                for bb in range(2):
                    blk = pair * 2 + bb
                    for kc in range(KC):
                        k.op(pe, lambda e, kc=kc, blk=blk, bb=bb, bank=bank: e.matmul(
                            pb_t[bank][:, bb * 256:(bb + 1) * 256], lhsT=hnS(kc, blk * 128, (blk + 1) * 128),
                            rhs=win(kc, 1536, 256), start=(kc == 0), stop=(kc == KC - 1)),
                            reads=[w_b[sA], hnS_b], writes=[pb_b[bank]], inc=(kc == KC - 1))
                evac_copy(vb[:, (1 + pair * 2) * 256:(3 + pair * 2) * 256], pb_t[bank][:],
                          reads=[pb_b[bank]], writes=[vb_b], force_act=True)
            steps = [(bq, g) for bq in range(4) for g in range(4)]

            def scores(bq, g):
                nglob = hf * (T // 128) + s * 4 + bq
                has_prev = nglob > 0
                for p in range(2):
                    gp = g * 2 + p
                    bank = 2 + gp % 2
                    X = pb_t[bank]
                    rows = slice(p * 64, (p + 1) * 64)
                    qrhs = hn_t[rows, 2 * g:2 * g + 2, QT0 + bq * 128:QT0 + (bq + 1) * 128]
                    if has_prev:
                        k.op(pe, lambda e, X=X, rows=rows, qrhs=qrhs: e.matmul(
                            X[:, 0:256].rearrange("p (h q) -> p h q", h=2),
                            lhsT=kt3[rows, g, bq * 128:(bq + 1) * 128], rhs=qrhs, start=True, stop=True),
                            reads=[kt_b, qt_b], writes=[pb_b[bank]], inc=False)
                    k.op(pe, lambda e, X=X, rows=rows, qrhs=qrhs: e.matmul(
                        X[:, 256:512].rearrange("p (h q) -> p h q", h=2),
                        lhsT=kt3[rows, g, (bq + 1) * 128:(bq + 2) * 128], rhs=qrhs, start=True, stop=True),
                        reads=[kt_b, qt_b], writes=[pb_b[bank]], inc=True)
                    E, Eb = et[gp % 2]
                    Pm, Pb = pt[gp % 4]
                    lo = 0 if has_prev else 256
                    k.op(act, lambda e, X=X, E=E, lo=lo: e.activation(out=E[:, lo:512], in_=X[:, lo:512],
                                                                       func=AF.Exp, scale=HD ** -0.5),
                         reads=[pb_b[bank]], writes=[Eb])
                    k.op(dve, lambda e, E=E, Pm=Pm, gp=gp, lo=lo: e.tensor_tensor(
                        out=Pm[:, lo:512], in0=E[:, lo:512], in1=mall_t[:, gp, lo:512], op=ALU.mult),
                        reads=[Eb, const_b], writes=[Pb])

            def pv(bq, g):
                nglob = hf * (T // 128) + s * 4 + bq
                has_prev = nglob > 0
                bank = (4, 5, 7)[(bq * 4 + g) % 3]
                Y = pb_t[bank]
                vcols = slice(g * 64, (g + 1) * 64)
                for p in range(2):
                    gp = g * 2 + p
                    Pm, Pb = pt[gp % 4]
                    rows = slice(p * 64, (p + 1) * 64)
                    if has_prev:
                        k.op(pe, lambda e, rows=rows, Pm=Pm: e.matmul(
                            Y[rows, 0:256], lhsT=vb3[:, bq, vcols], rhs=Pm[:, 0:256], start=True, stop=False),
                            reads=[vb_b, Pb], writes=[pb_b[bank]], inc=False)
                    k.op(pe, lambda e, rows=rows, Pm=Pm: e.matmul(
                        Y[rows, 0:256], lhsT=vb3[:, bq + 1, vcols], rhs=Pm[:, 256:512],
                        start=(not has_prev), stop=True),
                        reads=[vb_b, Pb], writes=[pb_b[bank]], inc=False)
                    if has_prev:
                        k.op(pe, lambda e, rows=rows, Pm=Pm: e.matmul(
                            Y[rows, 256:512], lhsT=ones_t[:, 0:64], rhs=Pm[:, 0:256], start=True, stop=False),
                            reads=[const_b, Pb], writes=[pb_b[bank]], inc=False)
                    k.op(pe, lambda e, rows=rows, Pm=Pm: e.matmul(
                        Y[rows, 256:512], lhsT=ones_t[:, 0:64], rhs=Pm[:, 256:512],
                        start=(not has_prev), stop=True),
                        reads=[const_b, Pb], writes=[pb_b[bank]], inc=(p == 1))
                R, Rb = rd[(bq * 4 + g) % 2]
                for hh in range(2):
                    col = idx * 8 + g * 2 + hh
                    k.op(act, lambda e, hh=hh, col=col: e.activation(
                        out=R[:, hh * 128:(hh + 1) * 128], in_=Y[:, 256 + hh * 128:256 + (hh + 1) * 128],
                        func=AF.Ln, bias=esk_t[:, col:col + 1], scale=1.0),
                        reads=[pb_b[bank], const_b], writes=[Rb])
                k.op(act, lambda e: e.activation(out=R, in_=R, func=AF.Exp, scale=-1.0), reads=[Rb], writes=[Rb])
                k.op(dve, lambda e: e.tensor_tensor(
                    out=hn_t[:, 2 * g:2 * g + 2, OT0 + bq * 128:OT0 + (bq + 1) * 128],
                    in0=Y[:, 0:256].rearrange("p (h q) -> p h q", h=2),
                    in1=R.rearrange("p (h q) -> p h q", h=2), op=ALU.mult),
                    reads=[pb_b[bank], Rb], writes=[ot_b])

            scores(*steps[0])
            for si in range(len(steps)):
                if si + 1 < len(steps):
                    scores(*steps[si + 1])
                pv(*steps[si])
            for dc in range(KC):
                bank = pcnt[0] % 2
                pcnt[0] += 1
                for c in range(KC):
                    k.op(pe, lambda e, c=c, dc=dc, bank=bank: e.matmul(
                        pb_t[bank][:], lhsT=wout(c, dc), rhs=hn_t[:, c, OT0:OT0 + ST],
                        start=(c == 0), stop=(c == KC - 1)),
                        reads=[w_b[sB], ot_b], writes=[pb_b[bank]], inc=(c == KC - 1))
                k.op(dve, lambda e, dc=dc, bank=bank: e.tensor_tensor(out=h_t[:, dc, sl], in0=pb_t[bank][:],
                                                                       in1=h_t[:, dc, sl], op=ALU.add),
                     reads=[pb_b[bank], h_b[dc][s]], writes=[h_b[dc][s]])
            k.op(dve, lambda e: e.tensor_copy(out=kt3[:, :, 0:128], in_=kt3[:, :, 512:640]),
                 reads=[kt_b], writes=[kt_b])
            k.op(dve, lambda e: e.tensor_copy(out=vb3[:, 0, :], in_=vb3[:, 4, :]), reads=[vb_b], writes=[vb_b])
        k.op(dve, lambda e: e.tensor_copy(out=ktc_t[:, idx], in_=kt3[:, :, 0:128]), reads=[kt_b], writes=[kvc_b[idx]])
        k.op(dve, lambda e: e.tensor_copy(out=vc_t[:, idx], in_=vb3[:, 0, :]), reads=[vb_b], writes=[kvc_b[idx]])

    def hgrn_phase(l, hf):
        idx = (l - 1) // 2
        for s in range(NSUB):
            rmsnorm_sub(s, l, 4)
        ta = TmpAlloc()
        tl = [TailAlloc(0), TailAlloc(1)]
        Te, Te_b = ta.get("Te", [ST], F32)
        TL1, TL1_b = ta.get("TL1", [ST], F32)
        TL2, TL2_b = ta.get("TL2", [ST], F32)
        Tb, Tb_b = ta.get("Tb", [ST], F32)
        Tt, Tt_b = ta.get("Tt", [ST], F32)
        KD, KD_b = ta.get("KD", [ST], BF16)
        cbm, cbm_b = ta.get("cbm", [4], F32)
        nbm, nbm_b = ta.get("nbm", [4], F32)
        HB = []
        for i in range(2):
            d = {}
            for nm in ("QD", "KDT", "VT", "AT"):
                d[nm] = tl[i].get(nm + str(i), [ST], BF16)
            d["Tgs"] = tl[i].get("Tgs" + str(i), [ST], F32)
            for nm in ("ebl", "ebm", "eblm"):
                d[nm] = ta.get(nm + str(i), [4], F32)
            HB.append(d)
        To, To_b = tl[0].get("To", [ST], F32)
        OSQ, OSQ_b = tl[0].get("OSQ", [ST], BF16)
        OG, OG_b = tl[0].get("OG", [ST], BF16)
        Tr, Tr_b = tl[1].get("Tr", [ST], F32)
        Sp, Sp_b = tl[1].get("Sp", [128], F32)
        Sbf, Sbf_b = tl[1].get("Sbf", [128], BF16)
        Bf, Bq, Bv, Bg, Ba, Bt, Bo, Bp = range(8)
        BtB = pb_t[Bt][:].bitcast(BF16)

        units = []
        for head in range(8):
            slot = next_slot()
            for s in range(NSUB):
                units.append((head, s, slot))

        def stage1(u, ui):
            head, s, slot = u
            wt = w_t[slot]
            if s == 0:
                k.dma(pool, w_st[slot], wt[:, 0:KC * 512], hwin_d[idx, head], writes=[w_b[slot]])
                k.dma(pool, w_st[slot], wt[:, KC * 512:KC * 512 + D], hwout_d[idx, head], writes=[w_b[slot]])
            hb = HB[ui % 2]
            QD, QD_b = hb["QD"]
            KDT, KDT_b = hb["KDT"]
            VT, VT_b = hb["VT"]
            AT, AT_b = hb["AT"]
            Tgs, Tgs_b = hb["Tgs"]
            ebl, ebl_b = hb["ebl"]
            ebm, ebm_b = hb["ebm"]
            eblm, eblm_b = hb["eblm"]
            lbc = lb_t[:, idx, head:head + 1]
            c1c = c1_t[:, idx, head:head + 1]
            sl = slice(s * ST, (s + 1) * ST)

            def wi(kc, which):
                base = kc * 512 + which * 128
                return wt[:, base:base + 128]

            for which, bank in ((1, Bf), (0, Bq)):
                for kc in range(KC):
                    k.op(pe, lambda e, kc=kc, which=which, bank=bank: e.matmul(
                        pb_t[bank][:], lhsT=wi(kc, which), rhs=hn_t[:, kc, sl],
                        start=(kc == 0), stop=(kc == KC - 1)),
                        reads=[w_b[slot], hn_b[s]], writes=[pb_b[bank]], inc=(kc == KC - 1))
            yield
            for blk in range(4):
                for kc in range(KC):
                    k.op(pe, lambda e, kc=kc, blk=blk: e.matmul(
                        pb_t[Bv][:, blk * 128:(blk + 1) * 128],
                        lhsT=hn_t[:, kc, s * ST + blk * 128:s * ST + (blk + 1) * 128], rhs=wi(kc, 2),
                        start=(kc == 0), stop=(kc == KC - 1)),
                        reads=[w_b[slot], hn_b[s]], writes=[pb_b[Bv]], inc=(kc == KC - 1))
            yield
            for kc in range(KC):
                k.op(pe, lambda e, kc=kc: e.matmul(
                    pb_t[Bg][:], lhsT=wi(kc, 3), rhs=hn_t[:, kc, sl],
                    start=(kc == 0), stop=(kc == KC - 1)),
                    reads=[w_b[slot], hn_b[s]], writes=[pb_b[Bg]], inc=(kc == KC - 1))
            yield
            k.op(act, lambda e: e.activation(out=Te, in_=pb_t[Bf][:], func=AF.Exp, scale=-1.0),
                 reads=[pb_b[Bf]], writes=[Te_b])
            k.op(act, lambda e: e.activation(out=TL1, in_=Te, func=AF.Ln, bias=1.0, scale=1.0),
                 reads=[Te_b], writes=[TL1_b])
            k.op(act, lambda e: e.activation(out=TL2, in_=Te, func=AF.Ln, bias=1.0, scale=lbc),
                 reads=[Te_b, const_b], writes=[TL2_b])
            for blk in range(4):
                cs = slice(blk * 128, (blk + 1) * 128)
                k.op(dve, lambda e, cs=cs: e.tensor_tensor_scan(out=Tb[:, cs], data0=TL2[:, cs], data1=TL1[:, cs],
                                                                initial=0.0, op0=ALU.add, op1=ALU.subtract),
                     reads=[TL1_b, TL2_b], writes=[Tb_b])
            k.op(dve, lambda e: e.tensor_tensor(out=Tt, in0=pb_t[Bf][:], in1=TL1, op=ALU.add),
                 reads=[pb_b[Bf], TL1_b], writes=[Tt_b])
            k.op(dve, lambda e: e.tensor_tensor(out=Tt, in0=Tt, in1=Tb, op=ALU.add),
                 reads=[Tt_b, Tb_b], writes=[Tt_b])
            yield
            Tb4 = Tb.rearrange("p (b t) -> p b t", b=4)
            bm_v, bl_v = Tb4[:, :, 63], Tb4[:, :, 127]
            k.op(act, lambda e: e.activation(out=ebl, in_=bl_v, func=AF.Exp), reads=[Tb_b], writes=[ebl_b])
            k.op(act, lambda e: e.activation(out=ebm, in_=bm_v, func=AF.Exp), reads=[Tb_b], writes=[ebm_b])
            k.op(dve, lambda e: e.tensor_tensor(out=eblm, in0=bl_v, in1=bm_v, op=ALU.subtract),
                 reads=[Tb_b], writes=[eblm_b])
            k.op(act, lambda e: e.activation(out=eblm, in_=eblm, func=AF.Exp), reads=[eblm_b], writes=[eblm_b])
            k.op(dve, lambda e: e.tensor_scalar(out=cbm, in0=bm_v, scalar1=c1c, scalar2=None, op0=ALU.add),
                 reads=[Tb_b, const_b], writes=[cbm_b])
            k.op(dve, lambda e: e.tensor_scalar(out=nbm, in0=bm_v, scalar1=-1.0, scalar2=None, op0=ALU.mult),
                 reads=[Tb_b], writes=[nbm_b])
            for blk in range(4):
                cs = slice(blk * 128, (blk + 1) * 128)
                k.op(act, lambda e, cs=cs, blk=blk: e.activation(out=KD[:, cs], in_=Tt[:, cs], func=AF.Exp,
                                                                 bias=cbm[:, blk:blk + 1], scale=-1.0),
                     reads=[Tt_b, cbm_b], writes=[KD_b])
            yield
            k.op(act, lambda e: e.activation(out=Te, in_=pb_t[Bq][:], func=AF.Exp, scale=-1.0),
                 reads=[pb_b[Bq]], writes=[Te_b])
            k.op(act, lambda e: e.activation(out=TL1, in_=Te, func=AF.Ln, bias=1.0, scale=1.0),
                 reads=[Te_b], writes=[TL1_b])
            k.op(dve, lambda e: e.tensor_tensor(out=TL2, in0=Tb, in1=TL1, op=ALU.subtract),
                 reads=[Tb_b, TL1_b], writes=[TL2_b])
            for blk in range(4):
                cs = slice(blk * 128, (blk + 1) * 128)
                k.op(act, lambda e, cs=cs, blk=blk: e.activation(out=TL2[:, cs], in_=TL2[:, cs], func=AF.Exp,
                                                                 bias=nbm[:, blk:blk + 1], scale=1.0),
                     reads=[TL2_b, nbm_b], writes=[TL2_b])
            k.op(dve, lambda e: e.tensor_tensor(out=QD, in0=pb_t[Bq][:], in1=TL2, op=ALU.mult),
                 reads=[pb_b[Bq], TL2_b], writes=[QD_b])
            yield
            for blk in range(4):
                cs = slice(blk * 128, (blk + 1) * 128)
                k.op(pe, lambda e, cs=cs: e.transpose(BtB[:, cs], KD[:, cs], idb_t[:]),
                     reads=[KD_b, const_b], writes=[pb_b[Bt]], inc=(blk == 3))
            k.op(dve, lambda e: e.tensor_copy(out=KDT, in_=BtB[:, 0:ST]), reads=[pb_b[Bt]], writes=[KDT_b])
            k.op(act, lambda e: e.copy(out=VT, in_=pb_t[Bv][:]), reads=[pb_b[Bv]], writes=[VT_b])
            yield
            k.op(act, lambda e: e.activation(out=Te, in_=pb_t[Bg][:], func=AF.Exp, scale=-1.0),
                 reads=[pb_b[Bg]], writes=[Te_b])
            k.op(act, lambda e: e.activation(out=TL1, in_=Te, func=AF.Ln, bias=1.0, scale=1.0),
                 reads=[Te_b], writes=[TL1_b])
            k.op(act, lambda e: e.activation(out=TL1, in_=TL1, func=AF.Exp, scale=-1.0),
                 reads=[TL1_b], writes=[TL1_b])
            k.op(dve, lambda e: e.tensor_tensor(out=Tgs, in0=pb_t[Bg][:], in1=TL1, op=ALU.mult),
                 reads=[pb_b[Bg], TL1_b], writes=[Tgs_b])
            yield
            for blk in range(4):
                cs = slice(blk * 128, (blk + 1) * 128)
                k.op(pe, lambda e, cs=cs: e.matmul(pb_t[Ba][:, cs], lhsT=KD[:, cs], rhs=QD[:, cs],
                                                   start=True, stop=True),
                     reads=[KD_b, QD_b], writes=[pb_b[Ba]], inc=(blk == 3))
            k.op(dve, lambda e: e.tensor_tensor(out=AT, in0=pb_t[Ba][:], in1=cm4_t[:], op=ALU.mult),
                 reads=[pb_b[Ba], const_b], writes=[AT_b])

        def stage2(u, ui):
            head, s, slot = u
            wt = w_t[slot]
            hb = HB[ui % 2]
            QD, QD_b = hb["QD"]
            KDT, KDT_b = hb["KDT"]
            VT, VT_b = hb["VT"]
            AT, AT_b = hb["AT"]
            Tgs, Tgs_b = hb["Tgs"]
            ebl, ebl_b = hb["ebl"]
            ebm, ebm_b = hb["ebm"]
            eblm, eblm_b = hb["eblm"]
            Sst = S_t[:, idx * 8 + head, :]
            Sb_ = S_b[idx * 8 + head]
            sl = slice(s * ST, (s + 1) * ST)
            for blk in range(4):
                cs = slice(blk * 128, (blk + 1) * 128)
                k.op(act, lambda e, blk=blk: e.activation(out=Sbf, in_=Sst, func=AF.Identity,
                                                          scale=ebm[:, blk:blk + 1]),
                     reads=[Sb_, ebm_b], writes=[Sbf_b])
                k.op(pe, lambda e, cs=cs: e.matmul(pb_t[Bo][:, cs], lhsT=VT[:, cs], rhs=AT[:, cs],
                                                   start=True, stop=False),
                     reads=[VT_b, AT_b], writes=[pb_b[Bo]], inc=False)
                k.op(pe, lambda e, cs=cs: e.matmul(pb_t[Bo][:, cs], lhsT=Sbf, rhs=QD[:, cs],
                                                   start=False, stop=True),
                     reads=[Sbf_b, QD_b], writes=[pb_b[Bo]], inc=True)
                k.op(pe, lambda e, cs=cs: e.matmul(pb_t[Bp][:, 0:128], lhsT=KDT[:, cs], rhs=VT[:, cs],
                                                   start=True, stop=True),
                     reads=[KDT_b, VT_b], writes=[pb_b[Bp]], inc=True)
                k.op(act, lambda e, blk=blk: e.activation(out=Sp, in_=Sst, func=AF.Identity,
                                                          scale=ebl[:, blk:blk + 1]),
                     reads=[Sb_, ebl_b], writes=[Sp_b])
                k.op(dve, lambda e, blk=blk: e.scalar_tensor_tensor(
                    out=Sst, in0=pb_t[Bp][:, 0:128], scalar=eblm[:, blk:blk + 1], in1=Sp,
                    op0=ALU.mult, op1=ALU.add), reads=[pb_b[Bp], eblm_b, Sp_b], writes=[Sb_])
                yield
            k.op(act, lambda e: e.activation(out=OSQ, in_=pb_t[Bo][:], func=AF.Square),
                 reads=[pb_b[Bo]], writes=[OSQ_b])
            k.op(pe, lambda e: e.matmul(pb_t[Bp][:], lhsT=ones_t[:], rhs=OSQ, start=True, stop=True),
                 reads=[OSQ_b, const_b], writes=[pb_b[Bp]], inc=True)
            k.op(act, lambda e: e.activation(out=Tr, in_=pb_t[Bp][:], func=AF.Ln, bias=EPS, scale=1.0 / 128),
                 reads=[pb_b[Bp]], writes=[Tr_b])
            k.op(act, lambda e: e.activation(out=Tr, in_=Tr, func=AF.Exp, scale=-0.5), reads=[Tr_b], writes=[Tr_b])
            k.op(dve, lambda e: e.scalar_tensor_tensor(out=To, in0=pb_t[Bo][:], scalar=pcol(HN0 + idx), in1=Tr,
                                                       op0=ALU.mult, op1=ALU.mult),
                 reads=[pb_b[Bo], Tr_b, pc_b], writes=[To_b])
            k.op(dve, lambda e: e.tensor_tensor(out=OG, in0=To, in1=Tgs, op=ALU.mult),
                 reads=[To_b, Tgs_b], writes=[OG_b])
            yield
            for dc in range(KC):
                ob = (Bp, Bo)[dc % 2]
                k.op(pe, lambda e, dc=dc, ob=ob: e.matmul(pb_t[ob][:], lhsT=wt[:, KC * 512 + dc * 128:KC * 512 + (dc + 1) * 128],
                                                          rhs=OG, start=True, stop=True),
                     reads=[w_b[slot], OG_b], writes=[pb_b[ob]], inc=True)
                k.op(dve, lambda e, dc=dc, ob=ob: e.tensor_tensor(out=h_t[:, dc, sl], in0=pb_t[ob][:],
                                                                  in1=h_t[:, dc, sl], op=ALU.add),
                     reads=[pb_b[ob], h_b[dc][s]], writes=[h_b[dc][s]])
                if dc % 2 == 1:
                    yield

        nU = len(units)
        for _ in stage1(units[0], 0):
            pass
        for ui in range(nU):
            g2 = stage2(units[ui], ui)
            g1 = stage1(units[ui + 1], ui + 1) if ui + 1 < nU else iter(())
            d1 = d2 = False
            while not (d1 and d2):
                if not d1:
                    try:
                        next(g1)
                    except StopIteration:
                        d1 = True
                if not d2:
                    try:
                        next(g2)
                    except StopIteration:
                        d2 = True

    out_streams = []
    for hf in range(NHALF):
        load_half(hf)
        for kind, l in plan:
            if kind == "attn":
                attn_phase(l, hf)
            elif kind == "hgrn":
                hgrn_phase(l, hf)
            else:
                ffn_phase(l, hf)
        out_streams += store_half(hf)
    for st in out_streams:
        sp.h.wait_ge(st.sem, st.count)
    return k


def pack_inputs(inp):
    f = np.float32
    pc = np.zeros((128, NPC), f)
    norms = [inp["norm_mix"][i] for i in range(4)] + [inp["norm_ffn"][i] for i in range(4)] + [inp["norm_final"]]
    for n, gvec in enumerate(norms):
        pc[:, GAIN0 + n * 8:GAIN0 + (n + 1) * 8] = np.asarray(gvec, f).reshape(8, 128).T
    cw = np.asarray(inp["ffn_conv_w"], f)
    cb = np.asarray(inp["ffn_conv_b"], f)
    for l in range(DEPTH):
        blk = np.concatenate([cw[l], cb[l][None]], axis=0)
        blk = blk.reshape(4, 44, 128).transpose(2, 1, 0)
        pc[:, CONV0 + l * 176:CONV0 + (l + 1) * 176] = blk.reshape(128, 176)
    lg = np.asarray(inp["hgrn_lb_logits"], f)
    pc[:, LG0:LG0 + 32] = lg.reshape(4, 8, 128).transpose(2, 1, 0).reshape(128, 32)
    pc[:, HN0:HN0 + 2] = np.asarray(inp["hgrn_norm"], f).T
    sk = np.asarray(inp["attn_sinks"], f)
    for idx in range(2):
        for g in range(4):
            for hh in range(2):
                col = SK0 + idx * 8 + g * 2 + hh
                pc[0:64, col] = sk[idx, 4 * g + 0 + 2 * hh]
                pc[64:128, col] = sk[idx, 4 * g + 1 + 2 * hh]

    awi = np.asarray(inp["attn_w_in"], f)
    q = awi[:, :, 0:1024]
    kk = awi[:, :, 1024:1280].reshape(2, D, 4, 64)
    kdup = np.concatenate([kk, kk], axis=3).reshape(2, D, 512)
    v = awi[:, :, 1280:1536]
    win = np.concatenate([q, kdup, v], axis=2)
    awin = np.ascontiguousarray(win.reshape(2, KC, 128, 1792).transpose(0, 2, 1, 3)).reshape(2, 128, KC * 1792)
    awo = np.asarray(inp["attn_w_out"], f)
    awout = np.ascontiguousarray(awo.reshape(2, KC, 128, D).transpose(0, 2, 1, 3)).reshape(2, 128, KC * D)

    hwi = np.asarray(inp["hgrn_w_in"], f)
    hwi = hwi.reshape(2, KC, 128, 4, 8, 128)
    hwin = np.ascontiguousarray(hwi.transpose(0, 4, 2, 1, 3, 5)).reshape(2, 8, 128, KC * 512)
    hwo = np.asarray(inp["hgrn_w_out"], f)
    hwout = np.ascontiguousarray(hwo.reshape(2, 8, 128, D))

    fu = np.asarray(inp["ffn_w_up"], f)
    fd = np.asarray(inp["ffn_w_down"], f)
    fup = np.zeros((DEPTH, NG, 128, KC, 2, GMAX * 128), f)
    fdn = np.zeros((DEPTH, NG, 128, GMAX, D), f)
    fu_r = fu.reshape(DEPTH, KC, 128, 2, D_FF)
    for g in range(NG):
        n = G_SIZES[g]
        c0 = G_OFF[g] * 128
        fup[:, g, :, :, :, 0:n * 128] = fu_r[:, :, :, :, c0:c0 + n * 128].transpose(0, 2, 1, 3, 4)
        fdn[:, g, :, 0:n, :] = fd[:, c0:c0 + n * 128, :].reshape(DEPTH, n, 128, D).transpose(0, 2, 1, 3)
    fup = fup.reshape(DEPTH, NG, 128, KC * 2 * GMAX * 128)
    fdn = fdn.reshape(DEPTH, NG, 128, GMAX * D)
    return dict(pc=pc, awin=awin, awout=awout, hwin=hwin, hwout=hwout, fup=fup, fdn=fdn)


FULL_PLAN = [("attn", 0), ("ffn", 0), ("hgrn", 1), ("ffn", 1), ("attn", 2), ("ffn", 2), ("hgrn", 3), ("ffn", 3)]
_CACHE = {}


def run_plan(inputs, plan, final_norm=True, x_override=None, trace=False, n_cores=8):
    key = (tuple(plan), final_norm)
    if key not in _CACHE:
        _CACHE[key] = build(plan, final_norm)
    k = _CACHE[key]
    shared = pack_inputs(inputs)
    x = np.asarray(inputs["x"] if x_override is None else x_override, np.float32)
    in_maps = []
    for b in range(n_cores):
        m = dict(shared)
        m["x"] = np.ascontiguousarray(x[b])
        in_maps.append(m)
    res = run_bass_kernel_spmd(k.nc, in_maps, core_ids=list(range(n_cores)), trace=trace)
    out = np.stack([np.asarray(r["out"], np.float32) for r in res.results], axis=0)
    return out, res


def kernel(**inputs):
    out, _ = run_plan(inputs, FULL_PLAN, True)
    return out
```

```python
import numpy as np
from contextlib import ExitStack
import concourse.bass as bass
import concourse.mybir as mybir
from concourse.bass_utils import run_bass_kernel_spmd

F32 = mybir.dt.float32
BF16 = mybir.dt.bfloat16
AF = mybir.ActivationFunctionType
ALU = mybir.AluOpType
AX = mybir.AxisListType

S = 4096
D = 1024
DEPTH = 4
T = 2048
NHALF = S // T
ST = 512
NSUB = T // ST
KC = D // 128
D_FF = 2816
NFC = D_FF // 128
G_SIZES = [5, 5, 4, 4, 4]
G_OFF = [0, 5, 10, 14, 18]
NG = len(G_SIZES)
GMAX = 5
EPS = 1e-6
NQH = 16
HD = 64

GAIN0 = 0
CONV0 = GAIN0 + 9 * 8
LG0 = CONV0 + DEPTH * 44 * 4
HN0 = LG0 + 32
SK0 = HN0 + 2
NPC = SK0 + 32


class Stream:
    def __init__(self, sem, step):
        self.sem = sem
        self.count = 0
        self.step = step


class Eng:
    def __init__(self, name, h, st):
        self.name = name
        self.h = h
        self.st = st
        self.seen = {}


class Buf:
    __slots__ = ("name", "w", "r")

    def __init__(self, name, deps=None):
        self.name = name
        self.w = None
        self.r = dict(deps) if deps else {}


def merge_deps(bufs):
    d = {}
    for b in bufs:
        if b.w is not None:
            s, v = b.w
            if d.get(s, 0) < v:
                d[s] = v
        for s, v in b.r.items():
            if d.get(s, 0) < v:
                d[s] = v
    return d


class K:
    def __init__(self):
        self.nc = bass.Bass("TRN2", target_bir_lowering=False)
        self.es = ExitStack()
        nc = self.nc
        self.pe = Eng("pe", nc.tensor, self.stream("s_pe", 1))
        self.act = Eng("act", nc.scalar, self.stream("s_act", 1))
        self.dve = Eng("dve", nc.vector, self.stream("s_dve", 1))
        self.pool = Eng("pool", nc.gpsimd, self.stream("s_pool", 1))
        self.sp = Eng("sp", nc.sync, self.stream("s_sp", 1))

    def stream(self, name, step):
        sem = self.es.enter_context(self.nc.semaphore(name))
        return Stream(sem, step)

    def sb(self, name, shape, dt):
        return self.es.enter_context(self.nc.sbuf_tensor("sb_" + name, shape, dt))

    def ps(self, name, shape, dt):
        return self.es.enter_context(self.nc.psum_tensor("ps_" + name, shape, dt))

    def _sync(self, eng, reads, writes, is_dma=False):
        need = {}
        for b in reads:
            if b.w is not None:
                s, v = b.w
                if need.get(s, 0) < v:
                    need[s] = v
        same_ok = (not is_dma) and eng.name == "pe"
        for b in writes:
            if b.w is not None:
                s, v = b.w
                if not (same_ok and s is eng.st):
                    if need.get(s, 0) < v:
                        need[s] = v
            for s, v in b.r.items():
                if not (same_ok and s is eng.st):
                    if need.get(s, 0) < v:
                        need[s] = v
        for s, v in need.items():
            if eng.seen.get(s, 0) < v:
                eng.h.wait_ge(s.sem, v)
                eng.seen[s] = v

    @staticmethod
    def _record(stream, tick, reads, writes):
        for b in reads:
            if b.r.get(stream, 0) < tick:
                b.r[stream] = tick
        for b in writes:
            b.w = (stream, tick)
            b.r = {}

    def op(self, eng, fn, reads=(), writes=(), inc=True):
        self._sync(eng, reads, writes)
        ins = fn(eng.h)
        if inc:
            eng.st.count += 1
            ins.then_inc(eng.st.sem, 1)
            tick = eng.st.count
        else:
            tick = eng.st.count + 1
        self._record(eng.st, tick, reads, writes)
        return ins

    def dma(self, eng, stream, out, in_, reads=(), writes=()):
        self._sync(eng, reads, writes, is_dma=True)
        ins = eng.h.dma_start(out=out, in_=in_)
        stream.count += 16
        ins.then_inc(stream.sem, 16)
        self._record(stream, stream.count, reads, writes)
        return ins


def build(plan, final_norm=True):
    k = K()
    nc = k.nc
    pe, act, dve, pool, sp = k.pe, k.act, k.dve, k.pool, k.sp

    x_d = nc.dram_tensor("x", [S, D], F32, kind="ExternalInput").ap()
    pc_d = nc.dram_tensor("pc", [128, NPC], F32, kind="ExternalInput").ap()
    awin_d = nc.dram_tensor("awin", [2, 128, KC * 1792], F32, kind="ExternalInput").ap()
    awout_d = nc.dram_tensor("awout", [2, 128, KC * D], F32, kind="ExternalInput").ap()
    hwin_d = nc.dram_tensor("hwin", [2, 8, 128, KC * 512], F32, kind="ExternalInput").ap()
    hwout_d = nc.dram_tensor("hwout", [2, 8, 128, D], F32, kind="ExternalInput").ap()
    fup_d = nc.dram_tensor("fup", [DEPTH, NG, 128, KC * 2 * GMAX * 128], F32, kind="ExternalInput").ap()
    fdn_d = nc.dram_tensor("fdn", [DEPTH, NG, 128, GMAX * D], F32, kind="ExternalInput").ap()
    out_d = nc.dram_tensor("out", [S, D], F32, kind="ExternalOutput").ap()

    h_t = k.sb("h", [128, KC, T], F32)
    h_b = [[Buf("h%d_%d" % (c, s)) for s in range(NSUB)] for c in range(KC)]
    hn_t = k.sb("hn", [128, KC, T], BF16)
    hn_b = [Buf("hn%d" % s) for s in range(NSUB)]
    pc_t = k.sb("pc", [128, NPC], F32)
    pc_b = Buf("pc")
    WSLOT = 15360
    w_t = [k.sb("w%d" % i, [128, WSLOT], BF16) for i in range(2)]
    w_b = [Buf("w%d" % i) for i in range(2)]
    w_st = [k.stream("s_w%d" % i, 16) for i in range(2)]
    wcount = [0]
    idb_t = k.sb("idb", [128, 128], BF16)
    idf_t = k.sb("idf", [128, 128], F32)
    ones_t = k.sb("ones", [128, 128], BF16)
    cm4_t = k.sb("cm4", [128, 512], BF16)
    mall_t = k.sb("mall", [128, 8, 512], BF16)
    const_b = Buf("const")
    lb_t = k.sb("lb", [128, 2, 8], F32)
    c1_t = k.sb("c1", [128, 2, 8], F32)
    esk_t = k.sb("esk", [128, 32], F32)
    S_t = k.sb("S", [128, 16, 128], F32)
    S_b = [Buf("S%d" % i) for i in range(16)]
    ktc_t = k.sb("ktc", [128, 2, 4, 128], BF16)
    vc_t = k.sb("vc", [128, 2, 256], BF16)
    kvc_b = [Buf("kvc%d" % i) for i in range(2)]
    halo_t = k.sb("halo", [128, DEPTH, 2, 44, 2], F32)
    halo_b = [[[Buf("halo%d_%d_%d" % (l, q, j)) for j in range(44)] for q in range(2)] for l in range(DEPTH)]
    sq_t = [k.sb("sq%d" % i, [128, ST], BF16) for i in range(2)]
    sq_b = [Buf("sq%d" % i) for i in range(2)]
    rstd_t = k.sb("rstd", [128, ST], F32)
    rstd_b = Buf("rstd")
    TMPW = 5200
    tmp_t = k.sb("tmp", [128, TMPW], F32)
    tmp_live = []

    pb_t = [k.ps("pb%d" % i, [128, 512], F32) for i in range(8)]
    pb_b = [Buf("pb%d" % i) for i in range(8)]

    class TmpAlloc:
        def __init__(self):
            self.deps = merge_deps(tmp_live)
            del tmp_live[:]
            self.off = 0

        def get(self, name, shape_free, dt):
            n = int(np.prod(shape_free))
            words = n if dt == F32 else (n + 1) // 2
            assert self.off + words <= TMPW, (name, self.off, words)
            ap = tmp_t[:, self.off:self.off + words]
            if dt != F32:
                ap = ap.bitcast(dt)[:, 0:n]
            self.off += words
            b = Buf(name, self.deps)
            tmp_live.append(b)
            return ap, b

    TAIL0 = 5120
    tail_live = [[], []]

    class TailAlloc:
        def __init__(self, i):
            self.i = i
            self.deps = merge_deps(tail_live[i] + [w_b[i]])
            del tail_live[i][:]
            self.off = TAIL0

        def get(self, name, shape_free, dt):
            n = int(np.prod(shape_free))
            el = n if dt == BF16 else 2 * n
            assert self.off + el <= WSLOT, (name, self.off, el)
            ap = w_t[self.i][:, self.off:self.off + el]
            if dt != BF16:
                ap = ap.bitcast(dt)
            self.off += el
            b = Buf(name, self.deps)
            tail_live[self.i].append(b)
            return ap, b

    def pcol(j):
        return pc_t[:, j:j + 1]

    st_pc = k.stream("s_pc", 16)
    k.dma(sp, st_pc, pc_t[:], pc_d, writes=[pc_b])
    k.op(pool, lambda e: e.memset(ones_t[:], 1.0), writes=[const_b])
    k.op(pool, lambda e: e.memset(idb_t[:], 1.0), writes=[const_b])
    k.op(pool, lambda e: e.affine_select(out=idb_t[:], in_=idb_t[:], pattern=[[-1, 128]], compare_op=ALU.is_equal,
                                         fill=0.0, base=0, channel_multiplier=1), reads=[const_b], writes=[const_b])
    k.op(pool, lambda e: e.memset(idf_t[:], 1.0), writes=[const_b])
    k.op(pool, lambda e: e.affine_select(out=idf_t[:], in_=idf_t[:], pattern=[[-1, 128]], compare_op=ALU.is_equal,
                                         fill=0.0, base=0, channel_multiplier=1), reads=[const_b], writes=[const_b])
    k.op(pool, lambda e: e.memset(cm4_t[:], 1.0), writes=[const_b])
    for blk in range(4):
        k.op(pool, lambda e, blk=blk: e.affine_select(out=cm4_t[:, blk * 128:(blk + 1) * 128],
                                                      in_=cm4_t[:, blk * 128:(blk + 1) * 128],
                                                      pattern=[[1, 128]], compare_op=ALU.is_ge, fill=0.0, base=0,
                                                      channel_multiplier=-1), reads=[const_b], writes=[const_b])
    k.op(pool, lambda e: e.memset(halo_t[:].rearrange("p a q b c -> p (a q b c)"), 0.0), writes=[b for h1 in halo_b for h2 in h1 for b in h2])
    k.op(pool, lambda e: e.memset(S_t[:].rearrange("p a b -> p (a b)"), 0.0), writes=S_b)
    k.op(pool, lambda e: e.memset(ktc_t[:].rearrange("p a b c -> p (a b c)"), 0.0), writes=kvc_b)
    k.op(pool, lambda e: e.memset(vc_t[:].rearrange("p a b -> p (a b)"), 0.0), writes=kvc_b)

    has_attn = any(p[0] == "attn" for p in plan)
    has_hgrn = any(p[0] == "hgrn" for p in plan)

    if has_attn:
        ta = TmpAlloc()
        d1, d1_b = ta.get("d1", [128], F32)
        dpos, dpos_b = ta.get("dpos", [128], F32)
        dneg, dneg_b = ta.get("dneg", [128], F32)
        mtmp, mtmp_b = ta.get("mtmp", [256], F32)
        k.op(pool, lambda e: e.iota(d1, pattern=[[1, 128]], base=0, channel_multiplier=-1,
                                    allow_small_or_imprecise_dtypes=True), writes=[d1_b])
        k.op(dve, lambda e: e.tensor_scalar_max(out=dpos, in0=d1, scalar1=0.0), reads=[d1_b], writes=[dpos_b])
        k.op(dve, lambda e: e.tensor_scalar_min(out=dneg, in0=d1, scalar1=0.0), reads=[d1_b], writes=[dneg_b])
        for gp in range(8):
            g, p = gp // 2, gp % 2
            for hh in range(2):
                hq = 4 * g + p + 2 * hh
                slope = float(2.0 ** (-8.0 * (hq + 1) / NQH))
                k.op(act, lambda e, slope=slope: e.activation(out=mtmp[:, 0:128], in_=dneg, func=AF.Exp,
                                                              bias=-slope * 128.0, scale=-slope),
                     reads=[dneg_b], writes=[mtmp_b])
                k.op(act, lambda e, slope=slope: e.activation(out=mtmp[:, 128:256], in_=dpos, func=AF.Exp,
                                                              scale=-slope),
                     reads=[dpos_b], writes=[mtmp_b])
                k.op(pool, lambda e, gp=gp, hh=hh: e.affine_select(
                    out=mall_t[:, gp, hh * 128:(hh + 1) * 128], in_=mtmp[:, 0:128], pattern=[[-1, 128]],
                    compare_op=ALU.is_gt, fill=0.0, base=0, channel_multiplier=1),
                    reads=[mtmp_b], writes=[const_b])
                k.op(pool, lambda e, gp=gp, hh=hh: e.affine_select(
                    out=mall_t[:, gp, 256 + hh * 128:256 + (hh + 1) * 128], in_=mtmp[:, 128:256],
                    pattern=[[1, 128]], compare_op=ALU.is_ge, fill=0.0, base=0, channel_multiplier=-1),
                    reads=[mtmp_b], writes=[const_b])
        k.op(act, lambda e: e.activation(out=esk_t[:, 0:16], in_=pc_t[:, SK0:SK0 + 16], func=AF.Exp),
             reads=[pc_b], writes=[const_b])

    if has_hgrn:
        ta = TmpAlloc()
        lg = pc_t[:, LG0:LG0 + 32].rearrange("p (h l) -> p h l", l=4)
        mx, mx_b = ta.get("mx", [8], F32)
        ex, ex_b = ta.get("ex", [32], F32)
        sm, sm_b = ta.get("sm", [8], F32)
        ex3 = ex.rearrange("p (h l) -> p h l", l=4)
        k.op(dve, lambda e: e.tensor_reduce(out=mx, in_=lg, axis=AX.X, op=ALU.max), reads=[pc_b], writes=[mx_b])
        for li in range(4):
            k.op(dve, lambda e, li=li: e.tensor_tensor(out=ex3[:, :, li], in0=lg[:, :, li], in1=mx, op=ALU.subtract),
                 reads=[pc_b, mx_b, ex_b], writes=[ex_b])
        k.op(act, lambda e: e.activation(out=ex, in_=ex, func=AF.Exp), reads=[ex_b], writes=[ex_b])
        k.op(dve, lambda e: e.tensor_reduce(out=sm, in_=ex3, axis=AX.X, op=ALU.add), reads=[ex_b], writes=[sm_b])
        k.op(dve, lambda e: e.reciprocal(out=sm, in_=sm), reads=[sm_b], writes=[sm_b])
        for li in range(4):
            k.op(dve, lambda e, li=li: e.tensor_tensor(out=ex3[:, :, li], in0=ex3[:, :, li], in1=sm, op=ALU.mult),
                 reads=[ex_b, sm_b], writes=[ex_b])
        k.op(dve, lambda e: e.tensor_copy(out=lb_t[:, 0, :], in_=ex3[:, :, 1]), reads=[ex_b], writes=[const_b])
        k.op(dve, lambda e: e.tensor_tensor(out=lb_t[:, 1, :], in0=ex3[:, :, 1], in1=ex3[:, :, 2], op=ALU.add),
             reads=[ex_b], writes=[const_b])
        k.op(dve, lambda e: e.tensor_tensor(out=lb_t[:, 1, :], in0=lb_t[:, 1, :], in1=ex3[:, :, 3], op=ALU.add),
             reads=[ex_b, const_b], writes=[const_b])
        k.op(act, lambda e: e.activation(out=c1_t[:].rearrange("p a b -> p (a b)"),
                                         in_=lb_t[:].rearrange("p a b -> p (a b)"), func=AF.Ln, bias=1.0, scale=-1.0),
             reads=[const_b], writes=[const_b])

    def next_slot():
        i = wcount[0] % 2
        wcount[0] += 1
        return i

    evac_rr = [0]

    def evac_copy(out, in_, reads, writes, force_act=False):
        evac_rr[0] += 1
        if force_act or evac_rr[0] % 2 == 0:
            k.op(act, lambda e: e.copy(out=out, in_=in_), reads=reads, writes=writes)
        else:
            k.op(dve, lambda e: e.tensor_copy(out=out, in_=in_), reads=reads, writes=writes)

    def rmsnorm_sub(s, gain_idx, nbank, in_place=False):
        sl = slice(s * ST, (s + 1) * ST)
        for c in range(KC):
            i = c % 2
            k.op(act, lambda e, c=c, i=i: e.activation(out=sq_t[i][:], in_=h_t[:, c, sl], func=AF.Square),
                 reads=[h_b[c][s]], writes=[sq_b[i]])
            k.op(pe, lambda e, c=c, i=i: e.matmul(pb_t[nbank][:], lhsT=ones_t[:], rhs=sq_t[i][:],
                                                   start=(c == 0), stop=(c == KC - 1)),
                 reads=[sq_b[i], const_b], writes=[pb_b[nbank]], inc=True)
        k.op(act, lambda e: e.activation(out=rstd_t[:], in_=pb_t[nbank][:], func=AF.Ln, bias=EPS, scale=1.0 / D),
             reads=[pb_b[nbank]], writes=[rstd_b])
        k.op(act, lambda e: e.activation(out=rstd_t[:], in_=rstd_t[:], func=AF.Exp, scale=-0.5),
             reads=[rstd_b], writes=[rstd_b])
        for c in range(KC):
            if in_place:
                k.op(dve, lambda e, c=c: e.scalar_tensor_tensor(out=h_t[:, c, sl], in0=h_t[:, c, sl],
                                                                scalar=pcol(GAIN0 + gain_idx * 8 + c),
                                                                in1=rstd_t[:], op0=ALU.mult, op1=ALU.mult),
                     reads=[h_b[c][s], rstd_b, pc_b], writes=[h_b[c][s]])
            else:
                k.op(dve, lambda e, c=c: e.scalar_tensor_tensor(out=hn_t[:, c, sl], in0=h_t[:, c, sl],
                                                                scalar=pcol(GAIN0 + gain_idx * 8 + c),
                                                                in1=rstd_t[:], op0=ALU.mult, op1=ALU.mult),
                     reads=[h_b[c][s], rstd_b, pc_b], writes=[hn_b[s]])

    def load_half(hf):
        ta = TmpAlloc()
        xin = [ta.get("xin%d" % i, [D], F32) for i in range(2)]
        xst = [k.stream("s_xin%d_%d" % (hf, i), 16) for i in range(2)]
        for blk in range(T // 128):
            i = blk % 2
            t0 = hf * T + blk * 128
            s = blk // 4
            col = blk * 128
            k.dma(sp, xst[i], xin[i][0], x_d[t0:t0 + 128, :], writes=[xin[i][1]])
            for half in range(2):
                bank = half
                for cc in range(4):
                    c = half * 4 + cc
                    k.op(pe, lambda e, c=c, cc=cc, bank=bank, i=i: e.transpose(
                        pb_t[bank][:, cc * 128:(cc + 1) * 128], xin[i][0][:, c * 128:(c + 1) * 128], idf_t[:]),
                        reads=[xin[i][1], const_b], writes=[pb_b[bank]], inc=(cc == 3))
                evac_copy(h_t[:, half * 4:half * 4 + 4, col:col + 128],
                          pb_t[bank][:].rearrange("p (c t) -> p c t", c=4),
                          reads=[pb_b[bank]], writes=[h_b[half * 4 + cc][s] for cc in range(4)])

    def store_half(hf):
        ta = TmpAlloc()
        yo = [ta.get("yo%d" % i, [D], F32) for i in range(2)]
        yst = [k.stream("s_yo%d_%d" % (hf, i), 16) for i in range(2)]
        for s in range(NSUB):
            if final_norm:
                rmsnorm_sub(s, 8, 6, in_place=True)
            for bb in range(4):
                blk = s * 4 + bb
                i = blk % 2
                t0 = hf * T + blk * 128
                col = blk * 128
                for half in range(2):
                    bank = half
                    for cc in range(4):
                        c = half * 4 + cc
                        k.op(pe, lambda e, c=c, cc=cc, bank=bank: e.transpose(
                            pb_t[bank][:, cc * 128:(cc + 1) * 128], h_t[:, c, col:col + 128], idf_t[:]),
                            reads=[h_b[c][s], const_b], writes=[pb_b[bank]], inc=(cc == 3))
                    evac_copy(yo[i][0][:, half * 512:(half + 1) * 512], pb_t[bank][:],
                              reads=[pb_b[bank]], writes=[yo[i][1]])
                k.dma(sp, yst[i], out_d[t0:t0 + 128, :], yo[i][0], reads=[yo[i][1]])
        return yst

    def ffn_phase(l, hf):
        for s in range(NSUB):
            rmsnorm_sub(s, 4 + l, 6)
        ta = TmpAlloc()
        cg = [ta.get("cg%d" % i, [ST], F32) for i in range(2)]
        cv = [ta.get("cv%d" % i, [ST], F32) for i in range(2)]
        sg = [ta.get("sg%d" % i, [ST], BF16) for i in range(2)]
        actb = [ta.get("act%d" % i, [GMAX, ST], BF16) for i in range(2)]
        items = []
        gs_list = []
        for g in range(NG):
            n = G_SIZES[g]
            slot = next_slot()
            for s in range(NSUB):
                gs_list.append((g, s, slot))
                for jj in range(n):
                    items.append((g, s, jj, slot, len(gs_list) - 1))

        def wup(slot, kc, col0):
            base = kc * 2 * GMAX * 128 + col0
            return w_t[slot][:, base:base + 128]

        def wdn(slot, jj, dc):
            base = KC * 2 * GMAX * 128 + jj * D + dc * 128
            return w_t[slot][:, base:base + 128]

        def stageA(it, idx):
            g, s, jj, slot, gsi = it
            par = idx % 2
            sl = slice(s * ST, (s + 1) * ST)
            for which in range(2):
                bank = par * 2 + which
                col0 = which * GMAX * 128 + jj * 128
                for kc in range(KC):
                    k.op(pe, lambda e, kc=kc, bank=bank, col0=col0: e.matmul(
                        pb_t[bank][:], lhsT=wup(slot, kc, col0), rhs=hn_t[:, kc, sl],
                        start=(kc == 0), stop=(kc == KC - 1)),
                        reads=[w_b[slot], hn_b[s]], writes=[pb_b[bank]], inc=(kc == KC - 1))

        def stageB(it, idx):
            g, s, jj, slot, gsi = it
            par = idx % 2
            first_tok = (hf == 0 and s == 0)
            sg_ = hf * NSUB + s
            hq_r, hq_w = (sg_ - 1) % 2, sg_ % 2
            info = []
            for which in range(2):
                bank = par * 2 + which
                P = pb_t[bank]
                C, Cb = (cg, cv)[which][par]
                jc = which * NFC + G_OFF[g] + jj
                cb = CONV0 + (l * 44 + jc) * 4
                info.append((bank, P, C, Cb, jc, cb))
                k.op(act, lambda e, P=P, C=C, cb=cb: e.activation(out=C, in_=P[:], func=AF.Identity,
                                                                  bias=pcol(cb + 3), scale=pcol(cb + 2)),
                     reads=[pb_b[bank], pc_b], writes=[Cb])
                k.op(act, lambda e, P=P, jc=jc: e.copy(out=halo_t[:, l, hq_w, jc, :], in_=P[:, ST - 2:ST]),
                     reads=[pb_b[bank]], writes=[halo_b[l][hq_w][jc]])
            for bank, P, C, Cb, jc, cb in info:
                hl = halo_t[:, l, hq_r, jc, :]
                hb_ = halo_b[l][hq_r][jc]
                k.op(dve, lambda e, P=P, C=C, cb=cb: e.scalar_tensor_tensor(
                    out=C[:, 1:ST], in0=P[:, 0:ST - 1], scalar=pcol(cb + 1), in1=C[:, 1:ST],
                    op0=ALU.mult, op1=ALU.add), reads=[pb_b[bank], Cb, pc_b], writes=[Cb])
                k.op(dve, lambda e, P=P, C=C, cb=cb: e.scalar_tensor_tensor(
                    out=C[:, 2:ST], in0=P[:, 0:ST - 2], scalar=pcol(cb + 0), in1=C[:, 2:ST],
                    op0=ALU.mult, op1=ALU.add), reads=[pb_b[bank], Cb, pc_b], writes=[Cb])
                if not first_tok:
                    k.op(dve, lambda e, C=C, cb=cb, hl=hl: e.scalar_tensor_tensor(
                        out=C[:, 0:1], in0=hl[:, 1:2], scalar=pcol(cb + 1), in1=C[:, 0:1],
                        op0=ALU.mult, op1=ALU.add), reads=[hb_, Cb, pc_b], writes=[Cb])
                    k.op(dve, lambda e, C=C, cb=cb, hl=hl: e.scalar_tensor_tensor(
                        out=C[:, 0:2], in0=hl[:, 0:2], scalar=pcol(cb + 0), in1=C[:, 0:2],
                        op0=ALU.mult, op1=ALU.add), reads=[hb_, Cb, pc_b], writes=[Cb])

        def stageC(it, idx):
            g, s, jj, slot, gsi = it
            par = idx % 2
            Cg, Cgb = cg[par]
            Cv, Cvb = cv[par]
            Sg, Sgb = sg[par]
            A, Ab = actb[gsi % 2]
            k.op(act, lambda e: e.activation(out=Sg, in_=Cg, func=AF.Silu), reads=[Cgb], writes=[Sgb])
            k.op(dve, lambda e: e.tensor_tensor(out=A[:, jj * ST:(jj + 1) * ST], in0=Sg, in1=Cv, op=ALU.mult),
                 reads=[Sgb, Cvb], writes=[Ab])

        def D_mm(gsi, dc):
            g, s, slot = gs_list[gsi]
            n = G_SIZES[g]
            A, Ab = actb[gsi % 2]
            bank = 4 + dc % 2
            for jj in range(n):
                k.op(pe, lambda e, jj=jj: e.matmul(
                    pb_t[bank][:], lhsT=wdn(slot, jj, dc), rhs=A[:, jj * ST:(jj + 1) * ST],
                    start=(jj == 0), stop=(jj == n - 1)),
                    reads=[w_b[slot], Ab], writes=[pb_b[bank]], inc=(jj == n - 1))

        def D_add(gsi, dc):
            g, s, slot = gs_list[gsi]
            bank = 4 + dc % 2
            sl = slice(s * ST, (s + 1) * ST)
            k.op(dve, lambda e: e.tensor_tensor(out=h_t[:, dc, sl], in0=pb_t[bank][:],
                                                in1=h_t[:, dc, sl], op=ALU.add),
                 reads=[pb_b[bank], h_b[dc][s]], writes=[h_b[dc][s]])

        def load_w(g, slot):
            wt, wb, wst = w_t[slot], w_b[slot], w_st[slot]
            k.dma(pool, wst, wt[:, 0:KC * 2 * GMAX * 128], fup_d[l, g], writes=[wb] + tail_live[slot])
            k.dma(pool, wst, wt[:, KC * 2 * GMAX * 128:KC * 2 * GMAX * 128 + GMAX * D], fdn_d[l, g], writes=[wb] + tail_live[slot])

        nI = len(items)
        pend = []
        step = 0
        while step < nI + 2 or pend:
            tasks = pend[:2]
            del pend[:2]
            for tk in tasks:
                D_mm(*tk)
            if step < nI:
                g_, s_, jj_, slot_, _ = items[step]
                if s_ == 0 and jj_ == 0:
                    load_w(g_, slot_)
                stageA(items[step], step)
            if 0 <= step - 1 < nI:
                stageB(items[step - 1], step - 1)
            if 0 <= step - 2 < nI:
                it = items[step - 2]
                stageC(it, step - 2)
                g, s, jj, slot, gsi = it
                if jj == G_SIZES[g] - 1:
                    pend.extend((gsi, dc) for dc in range(KC))
            for tk in tasks:
                D_add(*tk)
            step += 1

    def attn_phase(l, hf):
        idx = l // 2
        sA = next_slot()
        k.dma(pool, w_st[sA], w_t[sA][:, 0:KC * 1792], awin_d[idx], writes=[w_b[sA]] + tail_live[sA])
        sB = next_slot()
        k.dma(pool, w_st[sB], w_t[sB][:, 0:KC * D], awout_d[idx], writes=[w_b[sB]] + tail_live[sB])
        ta = TmpAlloc()
        kt, kt_b = ta.get("kt", [4, 640], BF16)
        vb, vb_b = ta.get("vb", [5, 256], BF16)
        et = [ta.get("e%d" % i, [512], BF16) for i in range(2)]
        pt = [ta.get("p%d" % i, [512], BF16) for i in range(4)]
        rd = [ta.get("rd%d" % i, [256], F32) for i in range(2)]
        kt3 = kt.rearrange("p (g t) -> p g t", g=4)
        vb3 = vb.rearrange("p (b d) -> p b d", b=5)
        hnS_b, qt_b, ot_b = hn_b[0], hn_b[1], hn_b[2]

        def hnS(kc, a, b):
            return hn_t[:, kc, a:b]

        QT0, OT0 = ST, 2 * ST
        k.op(dve, lambda e: e.tensor_copy(out=kt3[:, :, 0:128], in_=ktc_t[:, idx]), reads=[kvc_b[idx]], writes=[kt_b])
        k.op(dve, lambda e: e.tensor_copy(out=vb3[:, 0, :], in_=vc_t[:, idx]), reads=[kvc_b[idx]], writes=[vb_b])

        def win(kc, c0, n):
            base = kc * 1792 + c0
            return w_t[sA][:, base:base + n]

        def wout(c, dc):
            base = c * D + dc * 128
            return w_t[sB][:, base:base + 128]

        pcnt = [0]
        for s in range(NSUB):
            sl = slice(s * ST, (s + 1) * ST)
            for c in range(KC):
                i = c % 2
                k.op(act, lambda e, c=c, i=i: e.activation(out=sq_t[i][:], in_=h_t[:, c, sl], func=AF.Square),
                     reads=[h_b[c][s]], writes=[sq_b[i]])
                k.op(pe, lambda e, c=c, i=i: e.matmul(pb_t[6][:], lhsT=ones_t[:], rhs=sq_t[i][:],
                                                       start=(c == 0), stop=(c == KC - 1)),
                     reads=[sq_b[i], const_b], writes=[pb_b[6]], inc=True)
            k.op(act, lambda e: e.activation(out=rstd_t[:], in_=pb_t[6][:], func=AF.Ln, bias=EPS, scale=1.0 / D),
                 reads=[pb_b[6]], writes=[rstd_b])
            k.op(act, lambda e: e.activation(out=rstd_t[:], in_=rstd_t[:], func=AF.Exp, scale=-0.5),
                 reads=[rstd_b], writes=[rstd_b])
            for c in range(KC):
                k.op(dve, lambda e, c=c: e.scalar_tensor_tensor(out=hn_t[:, c, 0:ST], in0=h_t[:, c, sl],
                                                                scalar=pcol(GAIN0 + l * 8 + c),
                                                                in1=rstd_t[:], op0=ALU.mult, op1=ALU.mult),
                     reads=[h_b[c][s], rstd_b, pc_b], writes=[hnS_b])
            for c in range(KC + 4):
                bank = pcnt[0] % 2
                pcnt[0] += 1
                for kc in range(KC):
                    k.op(pe, lambda e, c=c, kc=kc, bank=bank: e.matmul(
                        pb_t[bank][:], lhsT=win(kc, c * 128, 128), rhs=hnS(kc, 0, ST),
                        start=(kc == 0), stop=(kc == KC - 1)),
                        reads=[w_b[sA], hnS_b], writes=[pb_b[bank]], inc=(kc == KC - 1))
                if c < KC:
                    evac_copy(hn_t[:, c, QT0:QT0 + ST], pb_t[bank][:], reads=[pb_b[bank]], writes=[qt_b], force_act=True)
                else:
                    evac_copy(kt3[:, c - KC, 128:640], pb_t[bank][:], reads=[pb_b[bank]], writes=[kt_b], force_act=True)
            for pair in range(2):
                bank = pcnt[0] % 2
                pcnt[0] += 1
                for bb in range(2):
                    blk = pair * 2 + bb
                    for kc in range(KC):
                        k.op(pe, lambda e, kc=kc, blk=blk, bb=bb, bank=bank: e.matmul(
                            pb_t[bank][:, bb * 256:(bb + 1) * 256], lhsT=hnS(kc, blk * 128, (blk + 1) * 128),
                            rhs=win(kc, 1536, 256), start=(kc == 0), stop=(kc == KC - 1)),
                            reads=[w_b[sA], hnS_b], writes=[pb_b[bank]], inc=(kc == KC - 1))
                evac_copy(vb[:, (1 + pair * 2) * 256:(3 + pair * 2) * 256], pb_t[bank][:],
                          reads=[pb_b[bank]], writes=[vb_b], force_act=True)
            steps = [(bq, g) for bq in range(4) for g in range(4)]

            def scores(bq, g):
                nglob = hf * (T // 128) + s * 4 + bq
                has_prev = nglob > 0
                for p in range(2):
                    gp = g * 2 + p
                    bank = 2 + gp % 2
                    X = pb_t[bank]
                    rows = slice(p * 64, (p + 1) * 64)
                    qrhs = hn_t[rows, 2 * g:2 * g + 2, QT0 + bq * 128:QT0 + (bq + 1) * 128]
                    if has_prev:
                        k.op(pe, lambda e, X=X, rows=rows, qrhs=qrhs: e.matmul(
                            X[:, 0:256].rearrange("p (h q) -> p h q", h=2),
                            lhsT=kt3[rows, g, bq * 128:(bq + 1) * 128], rhs=qrhs, start=True, stop=True),
                            reads=[kt_b, qt_b], writes=[pb_b[bank]], inc=False)
                    k.op(pe, lambda e, X=X, rows=rows, qrhs=qrhs: e.matmul(
                        X[:, 256:512].rearrange("p (h q) -> p h q", h=2),
                        lhsT=kt3[rows, g, (bq + 1) * 128:(bq + 2) * 128], rhs=qrhs, start=True, stop=True),
                        reads=[kt_b, qt_b], writes=[pb_b[bank]], inc=True)
                    E, Eb = et[gp % 2]
                    Pm, Pb = pt[gp % 4]
                    lo = 0 if has_prev else 256
                    k.op(act, lambda e, X=X, E=E, lo=lo: e.activation(out=E[:, lo:512], in_=X[:, lo:512],
                                                                       func=AF.Exp, scale=HD ** -0.5),
                         reads=[pb_b[bank]], writes=[Eb])
                    k.op(dve, lambda e, E=E, Pm=Pm, gp=gp, lo=lo: e.tensor_tensor(
                        out=Pm[:, lo:512], in0=E[:, lo:512], in1=mall_t[:, gp, lo:512], op=ALU.mult),
                        reads=[Eb, const_b], writes=[Pb])

            def pv(bq, g):
                nglob = hf * (T // 128) + s * 4 + bq
                has_prev = nglob > 0
                bank = (4, 5, 7)[(bq * 4 + g) % 3]
                Y = pb_t[bank]
                vcols = slice(g * 64, (g + 1) * 64)
                for p in range(2):
                    gp = g * 2 + p
                    Pm, Pb = pt[gp % 4]
                    rows = slice(p * 64, (p + 1) * 64)
                    if has_prev:
                        k.op(pe, lambda e, rows=rows, Pm=Pm: e.matmul(
                            Y[rows, 0:256], lhsT=vb3[:, bq, vcols], rhs=Pm[:, 0:256], start=True, stop=False),
                            reads=[vb_b, Pb], writes=[pb_b[bank]], inc=False)
                    k.op(pe, lambda e, rows=rows, Pm=Pm: e.matmul(
                        Y[rows, 0:256], lhsT=vb3[:, bq + 1, vcols], rhs=Pm[:, 256:512],
                        start=(not has_prev), stop=True),
                        reads=[vb_b, Pb], writes=[pb_b[bank]], inc=False)
                    if has_prev:
                        k.op(pe, lambda e, rows=rows, Pm=Pm: e.matmul(
                            Y[rows, 256:512], lhsT=ones_t[:, 0:64], rhs=Pm[:, 0:256], start=True, stop=False),
                            reads=[const_b, Pb], writes=[pb_b[bank]], inc=False)
                    k.op(pe, lambda e, rows=rows, Pm=Pm: e.matmul(
                        Y[rows, 256:512], lhsT=ones_t[:, 0:64], rhs=Pm[:, 256:512],
                        start=(not has_prev), stop=True),
                        reads=[const_b, Pb], writes=[pb_b[bank]], inc=(p == 1))
                R, Rb = rd[(bq * 4 + g) % 2]
                for hh in range(2):
                    col = idx * 8 + g * 2 + hh
                    k.op(act, lambda e, hh=hh, col=col: e.activation(
                        out=R[:, hh * 128:(hh + 1) * 128], in_=Y[:, 256 + hh * 128:256 + (hh + 1) * 128],
                        func=AF.Ln, bias=esk_t[:, col:col + 1], scale=1.0),
                        reads=[pb_b[bank], const_b], writes=[Rb])
                k.op(act, lambda e: e.activation(out=R, in_=R, func=AF.Exp, scale=-1.0), reads=[Rb], writes=[Rb])
                k.op(dve, lambda e: e.tensor_tensor(
                    out=hn_t[:, 2 * g:2 * g + 2, OT0 + bq * 128:OT0 + (bq + 1) * 128],
                    in0=Y[:, 0:256].rearrange("p (h q) -> p h q", h=2),
                    in1=R.rearrange("p (h q) -> p h q", h=2), op=ALU.mult),
                    reads=[pb_b[bank], Rb], writes=[ot_b])

            scores(*steps[0])
            for si in range(len(steps)):
                if si + 1 < len(steps):
                    scores(*steps[si + 1])
                pv(*steps[si])
            for dc in range(KC):
                bank = pcnt[0] % 2
                pcnt[0] += 1
                for c in range(KC):
                    k.op(pe, lambda e, c=c, dc=dc, bank=bank: e.matmul(
                        pb_t[bank][:], lhsT=wout(c, dc), rhs=hn_t[:, c, OT0:OT0 + ST],
                        start=(c == 0), stop=(c == KC - 1)),
                        reads=[w_b[sB], ot_b], writes=[pb_b[bank]], inc=(c == KC - 1))
                k.op(dve, lambda e, dc=dc, bank=bank: e.tensor_tensor(out=h_t[:, dc, sl], in0=pb_t[bank][:],
                                                                       in1=h_t[:, dc, sl], op=ALU.add),
                     reads=[pb_b[bank], h_b[dc][s]], writes=[h_b[dc][s]])
            k.op(dve, lambda e: e.tensor_copy(out=kt3[:, :, 0:128], in_=kt3[:, :, 512:640]),
                 reads=[kt_b], writes=[kt_b])
            k.op(dve, lambda e: e.tensor_copy(out=vb3[:, 0, :], in_=vb3[:, 4, :]), reads=[vb_b], writes=[vb_b])
        k.op(dve, lambda e: e.tensor_copy(out=ktc_t[:, idx], in_=kt3[:, :, 0:128]), reads=[kt_b], writes=[kvc_b[idx]])
        k.op(dve, lambda e: e.tensor_copy(out=vc_t[:, idx], in_=vb3[:, 0, :]), reads=[vb_b], writes=[kvc_b[idx]])

    def hgrn_phase(l, hf):
        idx = (l - 1) // 2
        for s in range(NSUB):
            rmsnorm_sub(s, l, 4)
        ta = TmpAlloc()
        tl = [TailAlloc(0), TailAlloc(1)]
        Te, Te_b = ta.get("Te", [ST], F32)
        TL1, TL1_b = ta.get("TL1", [ST], F32)
        TL2, TL2_b = ta.get("TL2", [ST], F32)
        Tb, Tb_b = ta.get("Tb", [ST], F32)
        Tt, Tt_b = ta.get("Tt", [ST], F32)
        KD, KD_b = ta.get("KD", [ST], BF16)
        cbm, cbm_b = ta.get("cbm", [4], F32)
        nbm, nbm_b = ta.get("nbm", [4], F32)
        HB = []
        for i in range(2):
            d = {}
            for nm in ("QD", "KDT", "VT", "AT"):
                d[nm] = tl[i].get(nm + str(i), [ST], BF16)
            d["Tgs"] = tl[i].get("Tgs" + str(i), [ST], F32)
            for nm in ("ebl", "ebm", "eblm"):
                d[nm] = ta.get(nm + str(i), [4], F32)
            HB.append(d)
        To, To_b = tl[0].get("To", [ST], F32)
        OSQ, OSQ_b = tl[0].get("OSQ", [ST], BF16)
        OG, OG_b = tl[0].get("OG", [ST], BF16)
        Tr, Tr_b = tl[1].get("Tr", [ST], F32)
        Sp, Sp_b = tl[1].get("Sp", [128], F32)
        Sbf, Sbf_b = tl[1].get("Sbf", [128], BF16)
        Bf, Bq, Bv, Bg, Ba, Bt, Bo, Bp = range(8)
        BtB = pb_t[Bt][:].bitcast(BF16)

        units = []
        for head in range(8):
            slot = next_slot()
            for s in range(NSUB):
                units.append((head, s, slot))

        def stage1(u, ui):
            head, s, slot = u
            wt = w_t[slot]
            if s == 0:
                k.dma(pool, w_st[slot], wt[:, 0:KC * 512], hwin_d[idx, head], writes=[w_b[slot]])
                k.dma(pool, w_st[slot], wt[:, KC * 512:KC * 512 + D], hwout_d[idx, head], writes=[w_b[slot]])
            hb = HB[ui % 2]
            QD, QD_b = hb["QD"]
            KDT, KDT_b = hb["KDT"]
            VT, VT_b = hb["VT"]
            AT, AT_b = hb["AT"]
            Tgs, Tgs_b = hb["Tgs"]
            ebl, ebl_b = hb["ebl"]
            ebm, ebm_b = hb["ebm"]
            eblm, eblm_b = hb["eblm"]
            lbc = lb_t[:, idx, head:head + 1]
            c1c = c1_t[:, idx, head:head + 1]
            sl = slice(s * ST, (s + 1) * ST)

            def wi(kc, which):
                base = kc * 512 + which * 128
                return wt[:, base:base + 128]

            for which, bank in ((1, Bf), (0, Bq)):
                for kc in range(KC):
                    k.op(pe, lambda e, kc=kc, which=which, bank=bank: e.matmul(
                        pb_t[bank][:], lhsT=wi(kc, which), rhs=hn_t[:, kc, sl],
                        start=(kc == 0), stop=(kc == KC - 1)),
                        reads=[w_b[slot], hn_b[s]], writes=[pb_b[bank]], inc=(kc == KC - 1))
            yield
            for blk in range(4):
                for kc in range(KC):
                    k.op(pe, lambda e, kc=kc, blk=blk: e.matmul(
                        pb_t[Bv][:, blk * 128:(blk + 1) * 128],
                        lhsT=hn_t[:, kc, s * ST + blk * 128:s * ST + (blk + 1) * 128], rhs=wi(kc, 2),
                        start=(kc == 0), stop=(kc == KC - 1)),
                        reads=[w_b[slot], hn_b[s]], writes=[pb_b[Bv]], inc=(kc == KC - 1))
            yield
            for kc in range(KC):
                k.op(pe, lambda e, kc=kc: e.matmul(
                    pb_t[Bg][:], lhsT=wi(kc, 3), rhs=hn_t[:, kc, sl],
                    start=(kc == 0), stop=(kc == KC - 1)),
                    reads=[w_b[slot], hn_b[s]], writes=[pb_b[Bg]], inc=(kc == KC - 1))
            yield
            k.op(act, lambda e: e.activation(out=Te, in_=pb_t[Bf][:], func=AF.Exp, scale=-1.0),
                 reads=[pb_b[Bf]], writes=[Te_b])
            k.op(act, lambda e: e.activation(out=TL1, in_=Te, func=AF.Ln, bias=1.0, scale=1.0),
                 reads=[Te_b], writes=[TL1_b])
            k.op(act, lambda e: e.activation(out=TL2, in_=Te, func=AF.Ln, bias=1.0, scale=lbc),
                 reads=[Te_b, const_b], writes=[TL2_b])
            for blk in range(4):
                cs = slice(blk * 128, (blk + 1) * 128)
                k.op(dve, lambda e, cs=cs: e.tensor_tensor_scan(out=Tb[:, cs], data0=TL2[:, cs], data1=TL1[:, cs],
                                                                initial=0.0, op0=ALU.add, op1=ALU.subtract),
                     reads=[TL1_b, TL2_b], writes=[Tb_b])
            k.op(dve, lambda e: e.tensor_tensor(out=Tt, in0=pb_t[Bf][:], in1=TL1, op=ALU.add),
                 reads=[pb_b[Bf], TL1_b], writes=[Tt_b])
            k.op(dve, lambda e: e.tensor_tensor(out=Tt, in0=Tt, in1=Tb, op=ALU.add),
                 reads=[Tt_b, Tb_b], writes=[Tt_b])
            yield
            Tb4 = Tb.rearrange("p (b t) -> p b t", b=4)
            bm_v, bl_v = Tb4[:, :, 63], Tb4[:, :, 127]
            k.op(act, lambda e: e.activation(out=ebl, in_=bl_v, func=AF.Exp), reads=[Tb_b], writes=[ebl_b])
            k.op(act, lambda e: e.activation(out=ebm, in_=bm_v, func=AF.Exp), reads=[Tb_b], writes=[ebm_b])
            k.op(dve, lambda e: e.tensor_tensor(out=eblm, in0=bl_v, in1=bm_v, op=ALU.subtract),
                 reads=[Tb_b], writes=[eblm_b])
            k.op(act, lambda e: e.activation(out=eblm, in_=eblm, func=AF.Exp), reads=[eblm_b], writes=[eblm_b])
            k.op(dve, lambda e: e.tensor_scalar(out=cbm, in0=bm_v, scalar1=c1c, scalar2=None, op0=ALU.add),
                 reads=[Tb_b, const_b], writes=[cbm_b])
            k.op(dve, lambda e: e.tensor_scalar(out=nbm, in0=bm_v, scalar1=-1.0, scalar2=None, op0=ALU.mult),
                 reads=[Tb_b], writes=[nbm_b])
            for blk in range(4):
                cs = slice(blk * 128, (blk + 1) * 128)
                k.op(act, lambda e, cs=cs, blk=blk: e.activation(out=KD[:, cs], in_=Tt[:, cs], func=AF.Exp,
                                                                 bias=cbm[:, blk:blk + 1], scale=-1.0),
                     reads=[Tt_b, cbm_b], writes=[KD_b])
            yield
            k.op(act, lambda e: e.activation(out=Te, in_=pb_t[Bq][:], func=AF.Exp, scale=-1.0),
                 reads=[pb_b[Bq]], writes=[Te_b])
            k.op(act, lambda e: e.activation(out=TL1, in_=Te, func=AF.Ln, bias=1.0, scale=1.0),
                 reads=[Te_b], writes=[TL1_b])
            k.op(dve, lambda e: e.tensor_tensor(out=TL2, in0=Tb, in1=TL1, op=ALU.subtract),
                 reads=[Tb_b, TL1_b], writes=[TL2_b])
            for blk in range(4):
                cs = slice(blk * 128, (blk + 1) * 128)
                k.op(act, lambda e, cs=cs, blk=blk: e.activation(out=TL2[:, cs], in_=TL2[:, cs], func=AF.Exp,
                                                                 bias=nbm[:, blk:blk + 1], scale=1.0),
                     reads=[TL2_b, nbm_b], writes=[TL2_b])
            k.op(dve, lambda e: e.tensor_tensor(out=QD, in0=pb_t[Bq][:], in1=TL2, op=ALU.mult),
                 reads=[pb_b[Bq], TL2_b], writes=[QD_b])
            yield
            for blk in range(4):
                cs = slice(blk * 128, (blk + 1) * 128)
                k.op(pe, lambda e, cs=cs: e.transpose(BtB[:, cs], KD[:, cs], idb_t[:]),
                     reads=[KD_b, const_b], writes=[pb_b[Bt]], inc=(blk == 3))
            k.op(dve, lambda e: e.tensor_copy(out=KDT, in_=BtB[:, 0:ST]), reads=[pb_b[Bt]], writes=[KDT_b])
            k.op(act, lambda e: e.copy(out=VT, in_=pb_t[Bv][:]), reads=[pb_b[Bv]], writes=[VT_b])
            yield
            k.op(act, lambda e: e.activation(out=Te, in_=pb_t[Bg][:], func=AF.Exp, scale=-1.0),
                 reads=[pb_b[Bg]], writes=[Te_b])
            k.op(act, lambda e: e.activation(out=TL1, in_=Te, func=AF.Ln, bias=1.0, scale=1.0),
                 reads=[Te_b], writes=[TL1_b])
            k.op(act, lambda e: e.activation(out=TL1, in_=TL1, func=AF.Exp, scale=-1.0),
                 reads=[TL1_b], writes=[TL1_b])
            k.op(dve, lambda e: e.tensor_tensor(out=Tgs, in0=pb_t[Bg][:], in1=TL1, op=ALU.mult),
                 reads=[pb_b[Bg], TL1_b], writes=[Tgs_b])
            yield
            for blk in range(4):
                cs = slice(blk * 128, (blk + 1) * 128)
                k.op(pe, lambda e, cs=cs: e.matmul(pb_t[Ba][:, cs], lhsT=KD[:, cs], rhs=QD[:, cs],
                                                   start=True, stop=True),
                     reads=[KD_b, QD_b], writes=[pb_b[Ba]], inc=(blk == 3))
            k.op(dve, lambda e: e.tensor_tensor(out=AT, in0=pb_t[Ba][:], in1=cm4_t[:], op=ALU.mult),
                 reads=[pb_b[Ba], const_b], writes=[AT_b])

        def stage2(u, ui):
            head, s, slot = u
            wt = w_t[slot]
            hb = HB[ui % 2]
            QD, QD_b = hb["QD"]
            KDT, KDT_b = hb["KDT"]
            VT, VT_b = hb["VT"]
            AT, AT_b = hb["AT"]
            Tgs, Tgs_b = hb["Tgs"]
            ebl, ebl_b = hb["ebl"]
            ebm, ebm_b = hb["ebm"]
            eblm, eblm_b = hb["eblm"]
            Sst = S_t[:, idx * 8 + head, :]
            Sb_ = S_b[idx * 8 + head]
            sl = slice(s * ST, (s + 1) * ST)
            for blk in range(4):
                cs = slice(blk * 128, (blk + 1) * 128)
                k.op(act, lambda e, blk=blk: e.activation(out=Sbf, in_=Sst, func=AF.Identity,
                                                          scale=ebm[:, blk:blk + 1]),
                     reads=[Sb_, ebm_b], writes=[Sbf_b])
                k.op(pe, lambda e, cs=cs: e.matmul(pb_t[Bo][:, cs], lhsT=VT[:, cs], rhs=AT[:, cs],
                                                   start=True, stop=False),
                     reads=[VT_b, AT_b], writes=[pb_b[Bo]], inc=False)
                k.op(pe, lambda e, cs=cs: e.matmul(pb_t[Bo][:, cs], lhsT=Sbf, rhs=QD[:, cs],
                                                   start=False, stop=True),
                     reads=[Sbf_b, QD_b], writes=[pb_b[Bo]], inc=True)
                k.op(pe, lambda e, cs=cs: e.matmul(pb_t[Bp][:, 0:128], lhsT=KDT[:, cs], rhs=VT[:, cs],
                                                   start=True, stop=True),
                     reads=[KDT_b, VT_b], writes=[pb_b[Bp]], inc=True)
                k.op(act, lambda e, blk=blk: e.activation(out=Sp, in_=Sst, func=AF.Identity,
                                                          scale=ebl[:, blk:blk + 1]),
                     reads=[Sb_, ebl_b], writes=[Sp_b])
                k.op(dve, lambda e, blk=blk: e.scalar_tensor_tensor(
                    out=Sst, in0=pb_t[Bp][:, 0:128], scalar=eblm[:, blk:blk + 1], in1=Sp,
                    op0=ALU.mult, op1=ALU.add), reads=[pb_b[Bp], eblm_b, Sp_b], writes=[Sb_])
                yield
            k.op(act, lambda e: e.activation(out=OSQ, in_=pb_t[Bo][:], func=AF.Square),
                 reads=[pb_b[Bo]], writes=[OSQ_b])
            k.op(pe, lambda e: e.matmul(pb_t[Bp][:], lhsT=ones_t[:], rhs=OSQ, start=True, stop=True),
                 reads=[OSQ_b, const_b], writes=[pb_b[Bp]], inc=True)
            k.op(act, lambda e: e.activation(out=Tr, in_=pb_t[Bp][:], func=AF.Ln, bias=EPS, scale=1.0 / 128),
                 reads=[pb_b[Bp]], writes=[Tr_b])
            k.op(act, lambda e: e.activation(out=Tr, in_=Tr, func=AF.Exp, scale=-0.5), reads=[Tr_b], writes=[Tr_b])
            k.op(dve, lambda e: e.scalar_tensor_tensor(out=To, in0=pb_t[Bo][:], scalar=pcol(HN0 + idx), in1=Tr,
                                                       op0=ALU.mult, op1=ALU.mult),
                 reads=[pb_b[Bo], Tr_b, pc_b], writes=[To_b])
            k.op(dve, lambda e: e.tensor_tensor(out=OG, in0=To, in1=Tgs, op=ALU.mult),
                 reads=[To_b, Tgs_b], writes=[OG_b])
            yield
            for dc in range(KC):
                ob = (Bp, Bo)[dc % 2]
                k.op(pe, lambda e, dc=dc, ob=ob: e.matmul(pb_t[ob][:], lhsT=wt[:, KC * 512 + dc * 128:KC * 512 + (dc + 1) * 128],
                                                          rhs=OG, start=True, stop=True),
                     reads=[w_b[slot], OG_b], writes=[pb_b[ob]], inc=True)
                k.op(dve, lambda e, dc=dc, ob=ob: e.tensor_tensor(out=h_t[:, dc, sl], in0=pb_t[ob][:],
                                                                  in1=h_t[:, dc, sl], op=ALU.add),
                     reads=[pb_b[ob], h_b[dc][s]], writes=[h_b[dc][s]])
                if dc % 2 == 1:
                    yield

        nU = len(units)
        for _ in stage1(units[0], 0):
            pass
        for ui in range(nU):
            g2 = stage2(units[ui], ui)
            g1 = stage1(units[ui + 1], ui + 1) if ui + 1 < nU else iter(())
            d1 = d2 = False
            while not (d1 and d2):
                if not d1:
                    try:
                        next(g1)
                    except StopIteration:
                        d1 = True
                if not d2:
                    try:
                        next(g2)
                    except StopIteration:
                        d2 = True

    out_streams = []
    for hf in range(NHALF):
        load_half(hf)
        for kind, l in plan:
            if kind == "attn":
                attn_phase(l, hf)
            elif kind == "hgrn":
                hgrn_phase(l, hf)
            else:
                ffn_phase(l, hf)
        out_streams += store_half(hf)
    for st in out_streams:
        sp.h.wait_ge(st.sem, st.count)
    return k


def pack_inputs(inp):
    f = np.float32
    pc = np.zeros((128, NPC), f)
    norms = [inp["norm_mix"][i] for i in range(4)] + [inp["norm_ffn"][i] for i in range(4)] + [inp["norm_final"]]
    for n, gvec in enumerate(norms):
        pc[:, GAIN0 + n * 8:GAIN0 + (n + 1) * 8] = np.asarray(gvec, f).reshape(8, 128).T
    cw = np.asarray(inp["ffn_conv_w"], f)
    cb = np.asarray(inp["ffn_conv_b"], f)
    for l in range(DEPTH):
        blk = np.concatenate([cw[l], cb[l][None]], axis=0)
        blk = blk.reshape(4, 44, 128).transpose(2, 1, 0)
        pc[:, CONV0 + l * 176:CONV0 + (l + 1) * 176] = blk.reshape(128, 176)
    lg = np.asarray(inp["hgrn_lb_logits"], f)
    pc[:, LG0:LG0 + 32] = lg.reshape(4, 8, 128).transpose(2, 1, 0).reshape(128, 32)
    pc[:, HN0:HN0 + 2] = np.asarray(inp["hgrn_norm"], f).T
    sk = np.asarray(inp["attn_sinks"], f)
    for idx in range(2):
        for g in range(4):
            for hh in range(2):
                col = SK0 + idx * 8 + g * 2 + hh
                pc[0:64, col] = sk[idx, 4 * g + 0 + 2 * hh]
                pc[64:128, col] = sk[idx, 4 * g + 1 + 2 * hh]

    awi = np.asarray(inp["attn_w_in"], f)
    q = awi[:, :, 0:1024]
    kk = awi[:, :, 1024:1280].reshape(2, D, 4, 64)
    kdup = np.concatenate([kk, kk], axis=3).reshape(2, D, 512)
    v = awi[:, :, 1280:1536]
    win = np.concatenate([q, kdup, v], axis=2)
    awin = np.ascontiguousarray(win.reshape(2, KC, 128, 1792).transpose(0, 2, 1, 3)).reshape(2, 128, KC * 1792)
    awo = np.asarray(inp["attn_w_out"], f)
    awout = np.ascontiguousarray(awo.reshape(2, KC, 128, D).transpose(0, 2, 1, 3)).reshape(2, 128, KC * D)

    hwi = np.asarray(inp["hgrn_w_in"], f)
    hwi = hwi.reshape(2, KC, 128, 4, 8, 128)
    hwin = np.ascontiguousarray(hwi.transpose(0, 4, 2, 1, 3, 5)).reshape(2, 8, 128, KC * 512)
    hwo = np.asarray(inp["hgrn_w_out"], f)
    hwout = np.ascontiguousarray(hwo.reshape(2, 8, 128, D))

    fu = np.asarray(inp["ffn_w_up"], f)
    fd = np.asarray(inp["ffn_w_down"], f)
    fup = np.zeros((DEPTH, NG, 128, KC, 2, GMAX * 128), f)
    fdn = np.zeros((DEPTH, NG, 128, GMAX, D), f)
    fu_r = fu.reshape(DEPTH, KC, 128, 2, D_FF)
    for g in range(NG):
        n = G_SIZES[g]
        c0 = G_OFF[g] * 128
        fup[:, g, :, :, :, 0:n * 128] = fu_r[:, :, :, :, c0:c0 + n * 128].transpose(0, 2, 1, 3, 4)
        fdn[:, g, :, 0:n, :] = fd[:, c0:c0 + n * 128, :].reshape(DEPTH, n, 128, D).transpose(0, 2, 1, 3)
    fup = fup.reshape(DEPTH, NG, 128, KC * 2 * GMAX * 128)
    fdn = fdn.reshape(DEPTH, NG, 128, GMAX * D)
    return dict(pc=pc, awin=awin, awout=awout, hwin=hwin, hwout=hwout, fup=fup, fdn=fdn)


FULL_PLAN = [("attn", 0), ("ffn", 0), ("hgrn", 1), ("ffn", 1), ("attn", 2), ("ffn", 2), ("hgrn", 3), ("ffn", 3)]
_CACHE = {}


def run_plan(inputs, plan, final_norm=True, x_override=None, trace=False, n_cores=8):
    key = (tuple(plan), final_norm)
    if key not in _CACHE:
        _CACHE[key] = build(plan, final_norm)
    k = _CACHE[key]
    shared = pack_inputs(inputs)
    x = np.asarray(inputs["x"] if x_override is None else x_override, np.float32)
    in_maps = []
    for b in range(n_cores):
        m = dict(shared)
        m["x"] = np.ascontiguousarray(x[b])
        in_maps.append(m)
    res = run_bass_kernel_spmd(k.nc, in_maps, core_ids=list(range(n_cores)), trace=trace)
    out = np.stack([np.asarray(r["out"], np.float32) for r in res.results], axis=0)
    return out, res


def kernel(**inputs):
    out, _ = run_plan(inputs, FULL_PLAN, True)
    return out
```

```python
import numpy as np
from contextlib import ExitStack
import concourse.bass as bass
import concourse.mybir as mybir
from concourse.bass_utils import run_bass_kernel_spmd

F32 = mybir.dt.float32
BF16 = mybir.dt.bfloat16
AF = mybir.ActivationFunctionType
ALU = mybir.AluOpType
AX = mybir.AxisListType

S = 4096
D = 1024
DEPTH = 4
T = 2048
NHALF = S // T
ST = 512
NSUB = T // ST
KC = D // 128
D_FF = 2816
NFC = D_FF // 128
G_SIZES = [5, 5, 4, 4, 4]
G_OFF = [0, 5, 10, 14, 18]
NG = len(G_SIZES)
GMAX = 5
EPS = 1e-6
NQH = 16
HG_LEAD = 0
HD = 64

GAIN0 = 0
CONV0 = GAIN0 + 9 * 8
LG0 = CONV0 + DEPTH * 44 * 4
HN0 = LG0 + 32
SK0 = HN0 + 2
NPC = SK0 + 32


class Stream:
    def __init__(self, sem, step):
        self.sem = sem
        self.count = 0
        self.step = step


class Eng:
    def __init__(self, name, h, st):
        self.name = name
        self.h = h
        self.st = st
        self.seen = {}


class Buf:
    __slots__ = ("name", "w", "r")

    def __init__(self, name, deps=None):
        self.name = name
        self.w = None
        self.r = dict(deps) if deps else {}


def merge_deps(bufs):
    d = {}
    for b in bufs:
        if b.w is not None:
            s, v = b.w
            if d.get(s, 0) < v:
                d[s] = v
        for s, v in b.r.items():
            if d.get(s, 0) < v:
                d[s] = v
    return d


class K:
    def __init__(self):
        self.nc = bass.Bass("TRN2", target_bir_lowering=False)
        self.es = ExitStack()
        nc = self.nc
        self.pe = Eng("pe", nc.tensor, self.stream("s_pe", 1))
        self.act = Eng("act", nc.scalar, self.stream("s_act", 1))
        self.dve = Eng("dve", nc.vector, self.stream("s_dve", 1))
        self.pool = Eng("pool", nc.gpsimd, self.stream("s_pool", 1))
        self.sp = Eng("sp", nc.sync, self.stream("s_sp", 1))

    def stream(self, name, step):
        sem = self.es.enter_context(self.nc.semaphore(name))
        return Stream(sem, step)

    def sb(self, name, shape, dt):
        return self.es.enter_context(self.nc.sbuf_tensor("sb_" + name, shape, dt))

    def ps(self, name, shape, dt):
        return self.es.enter_context(self.nc.psum_tensor("ps_" + name, shape, dt))

    def _sync(self, eng, reads, writes, is_dma=False):
        need = {}
        for b in reads:
            if b.w is not None:
                s, v = b.w
                if need.get(s, 0) < v:
                    need[s] = v
        same_ok = (not is_dma) and eng.name == "pe"
        for b in writes:
            if b.w is not None:
                s, v = b.w
                if not (same_ok and s is eng.st):
                    if need.get(s, 0) < v:
                        need[s] = v
            for s, v in b.r.items():
                if not (same_ok and s is eng.st):
                    if need.get(s, 0) < v:
                        need[s] = v
        for s, v in need.items():
            if eng.seen.get(s, 0) < v:
                eng.h.wait_ge(s.sem, v)
                eng.seen[s] = v

    @staticmethod
    def _record(stream, tick, reads, writes):
        for b in reads:
            if b.r.get(stream, 0) < tick:
                b.r[stream] = tick
        for b in writes:
            b.w = (stream, tick)
            b.r = {}

    def op(self, eng, fn, reads=(), writes=(), inc=True):
        self._sync(eng, reads, writes)
        ins = fn(eng.h)
        if inc:
            eng.st.count += 1
            ins.then_inc(eng.st.sem, 1)
            tick = eng.st.count
        else:
            tick = eng.st.count + 1
        self._record(eng.st, tick, reads, writes)
        return ins

    def dma(self, eng, stream, out, in_, reads=(), writes=()):
        self._sync(eng, reads, writes, is_dma=True)
        ins = eng.h.dma_start(out=out, in_=in_)
        stream.count += 16
        ins.then_inc(stream.sem, 16)
        self._record(stream, stream.count, reads, writes)
        return ins


def build(plan, final_norm=True):
    k = K()
    nc = k.nc
    pe, act, dve, pool, sp = k.pe, k.act, k.dve, k.pool, k.sp

    x_d = nc.dram_tensor("x", [S, D], F32, kind="ExternalInput").ap()
    pc_d = nc.dram_tensor("pc", [128, NPC], F32, kind="ExternalInput").ap()
    awin_d = nc.dram_tensor("awin", [2, 128, KC * 1792], F32, kind="ExternalInput").ap()
    awout_d = nc.dram_tensor("awout", [2, 128, KC * D], F32, kind="ExternalInput").ap()
    hwin_d = nc.dram_tensor("hwin", [2, 8, 128, KC * 512], F32, kind="ExternalInput").ap()
    hwout_d = nc.dram_tensor("hwout", [2, 8, 128, D], F32, kind="ExternalInput").ap()
    fup_d = nc.dram_tensor("fup", [DEPTH, NG, 128, KC * 2 * GMAX * 128], F32, kind="ExternalInput").ap()
    fdn_d = nc.dram_tensor("fdn", [DEPTH, NG, 128, GMAX * D], F32, kind="ExternalInput").ap()
    out_d = nc.dram_tensor("out", [S, D], F32, kind="ExternalOutput").ap()

    h_t = k.sb("h", [128, KC, T], F32)
    h_b = [[Buf("h%d_%d" % (c, s)) for s in range(NSUB)] for c in range(KC)]
    hn_t = k.sb("hn", [128, KC, T], BF16)
    hn_b = [Buf("hn%d" % s) for s in range(NSUB)]
    pc_t = k.sb("pc", [128, NPC], F32)
    pc_b = Buf("pc")
    WSLOT = 15360
    w_t = [k.sb("w%d" % i, [128, WSLOT], BF16) for i in range(2)]
    w_b = [Buf("w%d" % i) for i in range(2)]
    w_st = [k.stream("s_w%d" % i, 16) for i in range(2)]
    wcount = [0]
    idb_t = k.sb("idb", [128, 128], BF16)
    idf_t = k.sb("idf", [128, 128], F32)
    ones_t = k.sb("ones", [128, 128], BF16)
    cm4_t = k.sb("cm4", [128, 512], BF16)
    mall_t = k.sb("mall", [128, 8, 512], BF16)
    const_b = Buf("const")
    lb_t = k.sb("lb", [128, 2, 8], F32)
    c1_t = k.sb("c1", [128, 2, 8], F32)
    esk_t = k.sb("esk", [128, 32], F32)
    S_t = k.sb("S", [128, 16, 128], F32)
    S_b = [Buf("S%d" % i) for i in range(16)]
    ktc_t = k.sb("ktc", [128, 2, 4, 128], BF16)
    vc_t = k.sb("vc", [128, 2, 256], BF16)
    kvc_b = [Buf("kvc%d" % i) for i in range(2)]
    halo_t = k.sb("halo", [128, DEPTH, 2, 44, 2], F32)
    halo_b = [[[Buf("halo%d_%d_%d" % (l, q, j)) for j in range(44)] for q in range(2)] for l in range(DEPTH)]
    sq_t = [k.sb("sq%d" % i, [128, ST], BF16) for i in range(2)]
    sq_b = [Buf("sq%d" % i) for i in range(2)]
    rstd_t = k.sb("rstd", [128, ST], F32)
    rstd_b = Buf("rstd")
    TMPW = 5200
    tmp_t = k.sb("tmp", [128, TMPW], F32)
    tmp_live = []

    pb_t = [k.ps("pb%d" % i, [128, 512], F32) for i in range(8)]
    pb_b = [Buf("pb%d" % i) for i in range(8)]

    class TmpAlloc:
        def __init__(self):
            self.deps = merge_deps(tmp_live)
            del tmp_live[:]
            self.off = 0

        def get(self, name, shape_free, dt):
            n = int(np.prod(shape_free))
            words = n if dt == F32 else (n + 1) // 2
            assert self.off + words <= TMPW, (name, self.off, words)
            ap = tmp_t[:, self.off:self.off + words]
            if dt != F32:
                ap = ap.bitcast(dt)[:, 0:n]
            self.off += words
            b = Buf(name, self.deps)
            tmp_live.append(b)
            return ap, b

    TAIL0 = 5120
    tail_live = [[], []]

    class TailAlloc:
        def __init__(self, i):
            self.i = i
            self.deps = merge_deps(tail_live[i] + [w_b[i]])
            del tail_live[i][:]
            self.off = TAIL0

        def get(self, name, shape_free, dt):
            n = int(np.prod(shape_free))
            el = n if dt == BF16 else 2 * n
            assert self.off + el <= WSLOT, (name, self.off, el)
            ap = w_t[self.i][:, self.off:self.off + el]
            if dt != BF16:
                ap = ap.bitcast(dt)
            self.off += el
            b = Buf(name, self.deps)
            tail_live[self.i].append(b)
            return ap, b

    def pcol(j):
        return pc_t[:, j:j + 1]

    st_pc = k.stream("s_pc", 16)
    k.dma(sp, st_pc, pc_t[:], pc_d, writes=[pc_b])
    k.op(pool, lambda e: e.memset(ones_t[:], 1.0), writes=[const_b])
    k.op(pool, lambda e: e.memset(idb_t[:], 1.0), writes=[const_b])
    k.op(pool, lambda e: e.affine_select(out=idb_t[:], in_=idb_t[:], pattern=[[-1, 128]], compare_op=ALU.is_equal,
                                         fill=0.0, base=0, channel_multiplier=1), reads=[const_b], writes=[const_b])
    k.op(pool, lambda e: e.memset(idf_t[:], 1.0), writes=[const_b])
    k.op(pool, lambda e: e.affine_select(out=idf_t[:], in_=idf_t[:], pattern=[[-1, 128]], compare_op=ALU.is_equal,
                                         fill=0.0, base=0, channel_multiplier=1), reads=[const_b], writes=[const_b])
    k.op(pool, lambda e: e.memset(cm4_t[:], 1.0), writes=[const_b])
    for blk in range(4):
        k.op(pool, lambda e, blk=blk: e.affine_select(out=cm4_t[:, blk * 128:(blk + 1) * 128],
                                                      in_=cm4_t[:, blk * 128:(blk + 1) * 128],
                                                      pattern=[[1, 128]], compare_op=ALU.is_ge, fill=0.0, base=0,
                                                      channel_multiplier=-1), reads=[const_b], writes=[const_b])
    k.op(pool, lambda e: e.memset(halo_t[:].rearrange("p a q b c -> p (a q b c)"), 0.0), writes=[b for h1 in halo_b for h2 in h1 for b in h2])
    k.op(pool, lambda e: e.memset(S_t[:].rearrange("p a b -> p (a b)"), 0.0), writes=S_b)
    k.op(pool, lambda e: e.memset(ktc_t[:].rearrange("p a b c -> p (a b c)"), 0.0), writes=kvc_b)
    k.op(pool, lambda e: e.memset(vc_t[:].rearrange("p a b -> p (a b)"), 0.0), writes=kvc_b)

    has_attn = any(p[0] == "attn" for p in plan)
    has_hgrn = any(p[0] == "hgrn" for p in plan)

    if has_attn:
        ta = TmpAlloc()
        d1, d1_b = ta.get("d1", [128], F32)
        dpos, dpos_b = ta.get("dpos", [128], F32)
        dneg, dneg_b = ta.get("dneg", [128], F32)
        mtmp, mtmp_b = ta.get("mtmp", [256], F32)
        k.op(pool, lambda e: e.iota(d1, pattern=[[1, 128]], base=0, channel_multiplier=-1,
                                    allow_small_or_imprecise_dtypes=True), writes=[d1_b])
        k.op(dve, lambda e: e.tensor_scalar_max(out=dpos, in0=d1, scalar1=0.0), reads=[d1_b], writes=[dpos_b])
        k.op(dve, lambda e: e.tensor_scalar_min(out=dneg, in0=d1, scalar1=0.0), reads=[d1_b], writes=[dneg_b])
        for gp in range(8):
            g, p = gp // 2, gp % 2
            for hh in range(2):
                hq = 4 * g + p + 2 * hh
                slope = float(2.0 ** (-8.0 * (hq + 1) / NQH))
                k.op(act, lambda e, slope=slope: e.activation(out=mtmp[:, 0:128], in_=dneg, func=AF.Exp,
                                                              bias=-slope * 128.0, scale=-slope),
                     reads=[dneg_b], writes=[mtmp_b])
                k.op(act, lambda e, slope=slope: e.activation(out=mtmp[:, 128:256], in_=dpos, func=AF.Exp,
                                                              scale=-slope),
                     reads=[dpos_b], writes=[mtmp_b])
                k.op(pool, lambda e, gp=gp, hh=hh: e.affine_select(
                    out=mall_t[:, gp, hh * 128:(hh + 1) * 128], in_=mtmp[:, 0:128], pattern=[[-1, 128]],
                    compare_op=ALU.is_gt, fill=0.0, base=0, channel_multiplier=1),
                    reads=[mtmp_b], writes=[const_b])
                k.op(pool, lambda e, gp=gp, hh=hh: e.affine_select(
                    out=mall_t[:, gp, 256 + hh * 128:256 + (hh + 1) * 128], in_=mtmp[:, 128:256],
                    pattern=[[1, 128]], compare_op=ALU.is_ge, fill=0.0, base=0, channel_multiplier=-1),
                    reads=[mtmp_b], writes=[const_b])
        k.op(act, lambda e: e.activation(out=esk_t[:, 0:16], in_=pc_t[:, SK0:SK0 + 16], func=AF.Exp),
             reads=[pc_b], writes=[const_b])

    if has_hgrn:
        ta = TmpAlloc()
        lg = pc_t[:, LG0:LG0 + 32].rearrange("p (h l) -> p h l", l=4)
        mx, mx_b = ta.get("mx", [8], F32)
        ex, ex_b = ta.get("ex", [32], F32)
        sm, sm_b = ta.get("sm", [8], F32)
        ex3 = ex.rearrange("p (h l) -> p h l", l=4)
        k.op(dve, lambda e: e.tensor_reduce(out=mx, in_=lg, axis=AX.X, op=ALU.max), reads=[pc_b], writes=[mx_b])
        for li in range(4):
            k.op(dve, lambda e, li=li: e.tensor_tensor(out=ex3[:, :, li], in0=lg[:, :, li], in1=mx, op=ALU.subtract),
                 reads=[pc_b, mx_b, ex_b], writes=[ex_b])
        k.op(act, lambda e: e.activation(out=ex, in_=ex, func=AF.Exp), reads=[ex_b], writes=[ex_b])
        k.op(dve, lambda e: e.tensor_reduce(out=sm, in_=ex3, axis=AX.X, op=ALU.add), reads=[ex_b], writes=[sm_b])
        k.op(dve, lambda e: e.reciprocal(out=sm, in_=sm), reads=[sm_b], writes=[sm_b])
        for li in range(4):
            k.op(dve, lambda e, li=li: e.tensor_tensor(out=ex3[:, :, li], in0=ex3[:, :, li], in1=sm, op=ALU.mult),
                 reads=[ex_b, sm_b], writes=[ex_b])
        k.op(dve, lambda e: e.tensor_copy(out=lb_t[:, 0, :], in_=ex3[:, :, 1]), reads=[ex_b], writes=[const_b])
        k.op(dve, lambda e: e.tensor_tensor(out=lb_t[:, 1, :], in0=ex3[:, :, 1], in1=ex3[:, :, 2], op=ALU.add),
             reads=[ex_b], writes=[const_b])
        k.op(dve, lambda e: e.tensor_tensor(out=lb_t[:, 1, :], in0=lb_t[:, 1, :], in1=ex3[:, :, 3], op=ALU.add),
             reads=[ex_b, const_b], writes=[const_b])
        k.op(act, lambda e: e.activation(out=c1_t[:].rearrange("p a b -> p (a b)"),
                                         in_=lb_t[:].rearrange("p a b -> p (a b)"), func=AF.Ln, bias=1.0, scale=-1.0),
             reads=[const_b], writes=[const_b])

    def next_slot():
        i = wcount[0] % 2
        wcount[0] += 1
        return i

    evac_rr = [0]

    def evac_copy(out, in_, reads, writes, force_act=False):
        evac_rr[0] += 1
        if force_act or evac_rr[0] % 2 == 0:
            k.op(act, lambda e: e.copy(out=out, in_=in_), reads=reads, writes=writes)
        else:
            k.op(dve, lambda e: e.tensor_copy(out=out, in_=in_), reads=reads, writes=writes)

    def rmsnorm_sub(s, gain_idx, nbank, in_place=False):
        sl = slice(s * ST, (s + 1) * ST)
        for c in range(KC):
            i = c % 2
            k.op(act, lambda e, c=c, i=i: e.activation(out=sq_t[i][:], in_=h_t[:, c, sl], func=AF.Square),
                 reads=[h_b[c][s]], writes=[sq_b[i]])
            k.op(pe, lambda e, c=c, i=i: e.matmul(pb_t[nbank][:], lhsT=ones_t[:], rhs=sq_t[i][:],
                                                   start=(c == 0), stop=(c == KC - 1)),
                 reads=[sq_b[i], const_b], writes=[pb_b[nbank]], inc=True)
        k.op(act, lambda e: e.activation(out=rstd_t[:], in_=pb_t[nbank][:], func=AF.Ln, bias=EPS, scale=1.0 / D),
             reads=[pb_b[nbank]], writes=[rstd_b])
        k.op(act, lambda e: e.activation(out=rstd_t[:], in_=rstd_t[:], func=AF.Exp, scale=-0.5),
             reads=[rstd_b], writes=[rstd_b])
        for c in range(KC):
            if in_place:
                k.op(dve, lambda e, c=c: e.scalar_tensor_tensor(out=h_t[:, c, sl], in0=h_t[:, c, sl],
                                                                scalar=pcol(GAIN0 + gain_idx * 8 + c),
                                                                in1=rstd_t[:], op0=ALU.mult, op1=ALU.mult),
                     reads=[h_b[c][s], rstd_b, pc_b], writes=[h_b[c][s]])
            else:
                k.op(dve, lambda e, c=c: e.scalar_tensor_tensor(out=hn_t[:, c, sl], in0=h_t[:, c, sl],
                                                                scalar=pcol(GAIN0 + gain_idx * 8 + c),
                                                                in1=rstd_t[:], op0=ALU.mult, op1=ALU.mult),
                     reads=[h_b[c][s], rstd_b, pc_b], writes=[hn_b[s]])

    def load_half(hf):
        ta = TmpAlloc()
        xin = [ta.get("xin%d" % i, [D], F32) for i in range(2)]
        xst = [k.stream("s_xin%d_%d" % (hf, i), 16) for i in range(2)]
        for blk in range(T // 128):
            i = blk % 2
            t0 = hf * T + blk * 128
            s = blk // 4
            col = blk * 128
            k.dma(sp, xst[i], xin[i][0], x_d[t0:t0 + 128, :], writes=[xin[i][1]])
            for half in range(2):
                bank = half
                for cc in range(4):
                    c = half * 4 + cc
                    k.op(pe, lambda e, c=c, cc=cc, bank=bank, i=i: e.transpose(
                        pb_t[bank][:, cc * 128:(cc + 1) * 128], xin[i][0][:, c * 128:(c + 1) * 128], idf_t[:]),
                        reads=[xin[i][1], const_b], writes=[pb_b[bank]], inc=(cc == 3))
                evac_copy(h_t[:, half * 4:half * 4 + 4, col:col + 128],
                          pb_t[bank][:].rearrange("p (c t) -> p c t", c=4),
                          reads=[pb_b[bank]], writes=[h_b[half * 4 + cc][s] for cc in range(4)])

    def store_half(hf):
        ta = TmpAlloc()
        yo = [ta.get("yo%d" % i, [D], F32) for i in range(2)]
        yst = [k.stream("s_yo%d_%d" % (hf, i), 16) for i in range(2)]
        for s in range(NSUB):
            if final_norm:
                rmsnorm_sub(s, 8, 6, in_place=True)
            for bb in range(4):
                blk = s * 4 + bb
                i = blk % 2
                t0 = hf * T + blk * 128
                col = blk * 128
                for half in range(2):
                    bank = half
                    for cc in range(4):
                        c = half * 4 + cc
                        k.op(pe, lambda e, c=c, cc=cc, bank=bank: e.transpose(
                            pb_t[bank][:, cc * 128:(cc + 1) * 128], h_t[:, c, col:col + 128], idf_t[:]),
                            reads=[h_b[c][s], const_b], writes=[pb_b[bank]], inc=(cc == 3))
                    evac_copy(yo[i][0][:, half * 512:(half + 1) * 512], pb_t[bank][:],
                              reads=[pb_b[bank]], writes=[yo[i][1]])
                k.dma(sp, yst[i], out_d[t0:t0 + 128, :], yo[i][0], reads=[yo[i][1]])
        return yst

    def ffn_phase(l, hf):
        for s in range(NSUB):
            rmsnorm_sub(s, 4 + l, 6)
        ta = TmpAlloc()
        cg = [ta.get("cg%d" % i, [ST], F32) for i in range(2)]
        cv = [ta.get("cv%d" % i, [ST], F32) for i in range(2)]
        sg = [ta.get("sg%d" % i, [ST], BF16) for i in range(2)]
        actb = [ta.get("act%d" % i, [GMAX, ST], BF16) for i in range(2)]
        items = []
        gs_list = []
        for g in range(NG):
            n = G_SIZES[g]
            slot = next_slot()
            for s in range(NSUB):
                gs_list.append((g, s, slot))
                for jj in range(n):
                    items.append((g, s, jj, slot, len(gs_list) - 1))

        def wup(slot, kc, col0):
            base = kc * 2 * GMAX * 128 + col0
            return w_t[slot][:, base:base + 128]

        def wdn(slot, jj, dc):
            base = KC * 2 * GMAX * 128 + jj * D + dc * 128
            return w_t[slot][:, base:base + 128]

        def stageA(it, idx):
            g, s, jj, slot, gsi = it
            par = idx % 2
            sl = slice(s * ST, (s + 1) * ST)
            for which in range(2):
                bank = par * 2 + which
                col0 = which * GMAX * 128 + jj * 128
                for kc in range(KC):
                    k.op(pe, lambda e, kc=kc, bank=bank, col0=col0: e.matmul(
                        pb_t[bank][:], lhsT=wup(slot, kc, col0), rhs=hn_t[:, kc, sl],
                        start=(kc == 0), stop=(kc == KC - 1)),
                        reads=[w_b[slot], hn_b[s]], writes=[pb_b[bank]], inc=(kc == KC - 1))

        def stageB(it, idx):
            g, s, jj, slot, gsi = it
            par = idx % 2
            first_tok = (hf == 0 and s == 0)
            sg_ = hf * NSUB + s
            hq_r, hq_w = (sg_ - 1) % 2, sg_ % 2
            info = []
            for which in range(2):
                bank = par * 2 + which
                P = pb_t[bank]
                C, Cb = (cg, cv)[which][par]
                jc = which * NFC + G_OFF[g] + jj
                cb = CONV0 + (l * 44 + jc) * 4
                info.append((bank, P, C, Cb, jc, cb))
                k.op(act, lambda e, P=P, C=C, cb=cb: e.activation(out=C, in_=P[:], func=AF.Identity,
                                                                  bias=pcol(cb + 3), scale=pcol(cb + 2)),
                     reads=[pb_b[bank], pc_b], writes=[Cb])
                k.op(act, lambda e, P=P, jc=jc: e.copy(out=halo_t[:, l, hq_w, jc, :], in_=P[:, ST - 2:ST]),
                     reads=[pb_b[bank]], writes=[halo_b[l][hq_w][jc]])
            for bank, P, C, Cb, jc, cb in info:
                hl = halo_t[:, l, hq_r, jc, :]
                hb_ = halo_b[l][hq_r][jc]
                k.op(dve, lambda e, P=P, C=C, cb=cb: e.scalar_tensor_tensor(
                    out=C[:, 1:ST], in0=P[:, 0:ST - 1], scalar=pcol(cb + 1), in1=C[:, 1:ST],
                    op0=ALU.mult, op1=ALU.add), reads=[pb_b[bank], Cb, pc_b], writes=[Cb])
                k.op(dve, lambda e, P=P, C=C, cb=cb: e.scalar_tensor_tensor(
                    out=C[:, 2:ST], in0=P[:, 0:ST - 2], scalar=pcol(cb + 0), in1=C[:, 2:ST],
                    op0=ALU.mult, op1=ALU.add), reads=[pb_b[bank], Cb, pc_b], writes=[Cb])
                if not first_tok:
                    k.op(dve, lambda e, C=C, cb=cb, hl=hl: e.scalar_tensor_tensor(
                        out=C[:, 0:1], in0=hl[:, 1:2], scalar=pcol(cb + 1), in1=C[:, 0:1],
                        op0=ALU.mult, op1=ALU.add), reads=[hb_, Cb, pc_b], writes=[Cb])
                    k.op(dve, lambda e, C=C, cb=cb, hl=hl: e.scalar_tensor_tensor(
                        out=C[:, 0:2], in0=hl[:, 0:2], scalar=pcol(cb + 0), in1=C[:, 0:2],
                        op0=ALU.mult, op1=ALU.add), reads=[hb_, Cb, pc_b], writes=[Cb])

        def stageC(it, idx):
            g, s, jj, slot, gsi = it
            par = idx % 2
            Cg, Cgb = cg[par]
            Cv, Cvb = cv[par]
            Sg, Sgb = sg[par]
            A, Ab = actb[gsi % 2]
            k.op(act, lambda e: e.activation(out=Sg, in_=Cg, func=AF.Silu), reads=[Cgb], writes=[Sgb])
            k.op(dve, lambda e: e.tensor_tensor(out=A[:, jj * ST:(jj + 1) * ST], in0=Sg, in1=Cv, op=ALU.mult),
                 reads=[Sgb, Cvb], writes=[Ab])

        def D_mm(gsi, dc):
            g, s, slot = gs_list[gsi]
            n = G_SIZES[g]
            A, Ab = actb[gsi % 2]
            bank = 4 + dc % 2
            for jj in range(n):
                k.op(pe, lambda e, jj=jj: e.matmul(
                    pb_t[bank][:], lhsT=wdn(slot, jj, dc), rhs=A[:, jj * ST:(jj + 1) * ST],
                    start=(jj == 0), stop=(jj == n - 1)),
                    reads=[w_b[slot], Ab], writes=[pb_b[bank]], inc=(jj == n - 1))

        def D_add(gsi, dc):
            g, s, slot = gs_list[gsi]
            bank = 4 + dc % 2
            sl = slice(s * ST, (s + 1) * ST)
            k.op(dve, lambda e: e.tensor_tensor(out=h_t[:, dc, sl], in0=pb_t[bank][:],
                                                in1=h_t[:, dc, sl], op=ALU.add),
                 reads=[pb_b[bank], h_b[dc][s]], writes=[h_b[dc][s]])

        def load_w(g, slot):
            wt, wb, wst = w_t[slot], w_b[slot], w_st[slot]
            k.dma(pool, wst, wt[:, 0:KC * 2 * GMAX * 128], fup_d[l, g], writes=[wb] + tail_live[slot])
            k.dma(pool, wst, wt[:, KC * 2 * GMAX * 128:KC * 2 * GMAX * 128 + GMAX * D], fdn_d[l, g], writes=[wb] + tail_live[slot])

        nI = len(items)
        pend = []
        step = 0
        while step < nI + 2 or pend:
            tasks = pend[:2]
            del pend[:2]
            for tk in tasks:
                D_mm(*tk)
            if step < nI:
                g_, s_, jj_, slot_, _ = items[step]
                if s_ == 0 and jj_ == 0:
                    load_w(g_, slot_)
                stageA(items[step], step)
            if 0 <= step - 1 < nI:
                stageB(items[step - 1], step - 1)
            if 0 <= step - 2 < nI:
                it = items[step - 2]
                stageC(it, step - 2)
                g, s, jj, slot, gsi = it
                if jj == G_SIZES[g] - 1:
                    pend.extend((gsi, dc) for dc in range(KC))
            for tk in tasks:
                D_add(*tk)
            step += 1

    def attn_phase(l, hf):
        idx = l // 2
        sA = next_slot()
        k.dma(pool, w_st[sA], w_t[sA][:, 0:KC * 1792], awin_d[idx], writes=[w_b[sA]] + tail_live[sA])
        sB = next_slot()
        k.dma(pool, w_st[sB], w_t[sB][:, 0:KC * D], awout_d[idx], writes=[w_b[sB]] + tail_live[sB])
        ta = TmpAlloc()
        kt, kt_b = ta.get("kt", [4, 640], BF16)
        vb, vb_b = ta.get("vb", [5, 256], BF16)
        et = [ta.get("e%d" % i, [512], BF16) for i in range(2)]
        pt = [ta.get("p%d" % i, [512], BF16) for i in range(4)]
        rd = [ta.get("rd%d" % i, [256], F32) for i in range(2)]
        kt3 = kt.rearrange("p (g t) -> p g t", g=4)
        vb3 = vb.rearrange("p (b d) -> p b d", b=5)
        hnS_b, qt_b, ot_b = hn_b[0], hn_b[1], hn_b[2]

        def hnS(kc, a, b):
            return hn_t[:, kc, a:b]

        QT0, OT0 = ST, 2 * ST
        k.op(dve, lambda e: e.tensor_copy(out=kt3[:, :, 0:128], in_=ktc_t[:, idx]), reads=[kvc_b[idx]], writes=[kt_b])
        k.op(dve, lambda e: e.tensor_copy(out=vb3[:, 0, :], in_=vc_t[:, idx]), reads=[kvc_b[idx]], writes=[vb_b])

        def win(kc, c0, n):
            base = kc * 1792 + c0
            return w_t[sA][:, base:base + n]

        def wout(c, dc):
            base = c * D + dc * 128
            return w_t[sB][:, base:base + 128]

        pcnt = [0]
        for s in range(NSUB):
            sl = slice(s * ST, (s + 1) * ST)
            for c in range(KC):
                i = c % 2
                k.op(act, lambda e, c=c, i=i: e.activation(out=sq_t[i][:], in_=h_t[:, c, sl], func=AF.Square),
                     reads=[h_b[c][s]], writes=[sq_b[i]])
                k.op(pe, lambda e, c=c, i=i: e.matmul(pb_t[6][:], lhsT=ones_t[:], rhs=sq_t[i][:],
                                                       start=(c == 0), stop=(c == KC - 1)),
                     reads=[sq_b[i], const_b], writes=[pb_b[6]], inc=True)
            k.op(act, lambda e: e.activation(out=rstd_t[:], in_=pb_t[6][:], func=AF.Ln, bias=EPS, scale=1.0 / D),
                 reads=[pb_b[6]], writes=[rstd_b])
            k.op(act, lambda e: e.activation(out=rstd_t[:], in_=rstd_t[:], func=AF.Exp, scale=-0.5),
                 reads=[rstd_b], writes=[rstd_b])
            for c in range(KC):
                k.op(dve, lambda e, c=c: e.scalar_tensor_tensor(out=hn_t[:, c, 0:ST], in0=h_t[:, c, sl],
                                                                scalar=pcol(GAIN0 + l * 8 + c),
                                                                in1=rstd_t[:], op0=ALU.mult, op1=ALU.mult),
                     reads=[h_b[c][s], rstd_b, pc_b], writes=[hnS_b])
            for c in range(KC + 4):
                bank = pcnt[0] % 2
                pcnt[0] += 1
                for kc in range(KC):
                    k.op(pe, lambda e, c=c, kc=kc, bank=bank: e.matmul(
                        pb_t[bank][:], lhsT=win(kc, c * 128, 128), rhs=hnS(kc, 0, ST),
                        start=(kc == 0), stop=(kc == KC - 1)),
                        reads=[w_b[sA], hnS_b], writes=[pb_b[bank]], inc=(kc == KC - 1))
                if c < KC:
                    evac_copy(hn_t[:, c, QT0:QT0 + ST], pb_t[bank][:], reads=[pb_b[bank]], writes=[qt_b], force_act=True)
                else:
                    evac_copy(kt3[:, c - KC, 128:640], pb_t[bank][:], reads=[pb_b[bank]], writes=[kt_b], force_act=True)
            for pair in range(2):
                bank = pcnt[0] % 2
                pcnt[0] += 1
                for bb in range(2):
                    blk = pair * 2 + bb
                    for kc in range(KC):
                        k.op(pe, lambda e, kc=kc, blk=blk, bb=bb, bank=bank: e.matmul(
                            pb_t[bank][:, bb * 256:(bb + 1) * 256], lhsT=hnS(kc, blk * 128, (blk + 1) * 128),
                            rhs=win(kc, 1536, 256), start=(kc == 0), stop=(kc == KC - 1)),
                            reads=[w_b[sA], hnS_b], writes=[pb_b[bank]], inc=(kc == KC - 1))
                evac_copy(vb[:, (1 + pair * 2) * 256:(3 + pair * 2) * 256], pb_t[bank][:],
                          reads=[pb_b[bank]], writes=[vb_b], force_act=True)
            steps = [(bq, g) for bq in range(4) for g in range(4)]

            def scores(bq, g):
                nglob = hf * (T // 128) + s * 4 + bq
                has_prev = nglob > 0
                for p in range(2):
                    gp = g * 2 + p
                    bank = 2 + gp % 2
                    X = pb_t[bank]
                    rows = slice(p * 64, (p + 1) * 64)
                    qrhs = hn_t[rows, 2 * g:2 * g + 2, QT0 + bq * 128:QT0 + (bq + 1) * 128]
                    if has_prev:
                        k.op(pe, lambda e, X=X, rows=rows, qrhs=qrhs: e.matmul(
                            X[:, 0:256].rearrange("p (h q) -> p h q", h=2),
                            lhsT=kt3[rows, g, bq * 128:(bq + 1) * 128], rhs=qrhs, start=True, stop=True),
                            reads=[kt_b, qt_b], writes=[pb_b[bank]], inc=False)
                    k.op(pe, lambda e, X=X, rows=rows, qrhs=qrhs: e.matmul(
                        X[:, 256:512].rearrange("p (h q) -> p h q", h=2),
                        lhsT=kt3[rows, g, (bq + 1) * 128:(bq + 2) * 128], rhs=qrhs, start=True, stop=True),
                        reads=[kt_b, qt_b], writes=[pb_b[bank]], inc=True)
                    E, Eb = et[gp % 2]
                    Pm, Pb = pt[gp % 4]
                    lo = 0 if has_prev else 256
                    k.op(act, lambda e, X=X, E=E, lo=lo: e.activation(out=E[:, lo:512], in_=X[:, lo:512],
                                                                       func=AF.Exp, scale=HD ** -0.5),
                         reads=[pb_b[bank]], writes=[Eb])
                    k.op(dve, lambda e, E=E, Pm=Pm, gp=gp, lo=lo: e.tensor_tensor(
                        out=Pm[:, lo:512], in0=E[:, lo:512], in1=mall_t[:, gp, lo:512], op=ALU.mult),
                        reads=[Eb, const_b], writes=[Pb])

            def pv(bq, g):
                nglob = hf * (T // 128) + s * 4 + bq
                has_prev = nglob > 0
                bank = (4, 5, 7)[(bq * 4 + g) % 3]
                Y = pb_t[bank]
                vcols = slice(g * 64, (g + 1) * 64)
                for p in range(2):
                    gp = g * 2 + p
                    Pm, Pb = pt[gp % 4]
                    rows = slice(p * 64, (p + 1) * 64)
                    if has_prev:
                        k.op(pe, lambda e, rows=rows, Pm=Pm: e.matmul(
                            Y[rows, 0:256], lhsT=vb3[:, bq, vcols], rhs=Pm[:, 0:256], start=True, stop=False),
                            reads=[vb_b, Pb], writes=[pb_b[bank]], inc=False)
                    k.op(pe, lambda e, rows=rows, Pm=Pm: e.matmul(
                        Y[rows, 0:256], lhsT=vb3[:, bq + 1, vcols], rhs=Pm[:, 256:512],
                        start=(not has_prev), stop=True),
                        reads=[vb_b, Pb], writes=[pb_b[bank]], inc=False)
                    if has_prev:
                        k.op(pe, lambda e, rows=rows, Pm=Pm: e.matmul(
                            Y[rows, 256:512], lhsT=ones_t[:, 0:64], rhs=Pm[:, 0:256], start=True, stop=False),
                            reads=[const_b, Pb], writes=[pb_b[bank]], inc=False)
                    k.op(pe, lambda e, rows=rows, Pm=Pm: e.matmul(
                        Y[rows, 256:512], lhsT=ones_t[:, 0:64], rhs=Pm[:, 256:512],
                        start=(not has_prev), stop=True),
                        reads=[const_b, Pb], writes=[pb_b[bank]], inc=(p == 1))
                R, Rb = rd[(bq * 4 + g) % 2]
                for hh in range(2):
                    col = idx * 8 + g * 2 + hh
                    k.op(act, lambda e, hh=hh, col=col: e.activation(
                        out=R[:, hh * 128:(hh + 1) * 128], in_=Y[:, 256 + hh * 128:256 + (hh + 1) * 128],
                        func=AF.Ln, bias=esk_t[:, col:col + 1], scale=1.0),
                        reads=[pb_b[bank], const_b], writes=[Rb])
                k.op(act, lambda e: e.activation(out=R, in_=R, func=AF.Exp, scale=-1.0), reads=[Rb], writes=[Rb])
                k.op(dve, lambda e: e.tensor_tensor(
                    out=hn_t[:, 2 * g:2 * g + 2, OT0 + bq * 128:OT0 + (bq + 1) * 128],
                    in0=Y[:, 0:256].rearrange("p (h q) -> p h q", h=2),
                    in1=R.rearrange("p (h q) -> p h q", h=2), op=ALU.mult),
                    reads=[pb_b[bank], Rb], writes=[ot_b])

            scores(*steps[0])
            for si in range(len(steps)):
                if si + 1 < len(steps):
                    scores(*steps[si + 1])
                pv(*steps[si])
            for dc in range(KC):
                bank = pcnt[0] % 2
                pcnt[0] += 1
                for c in range(KC):
                    k.op(pe, lambda e, c=c, dc=dc, bank=bank: e.matmul(
                        pb_t[bank][:], lhsT=wout(c, dc), rhs=hn_t[:, c, OT0:OT0 + ST],
                        start=(c == 0), stop=(c == KC - 1)),
                        reads=[w_b[sB], ot_b], writes=[pb_b[bank]], inc=(c == KC - 1))
                k.op(dve, lambda e, dc=dc, bank=bank: e.tensor_tensor(out=h_t[:, dc, sl], in0=pb_t[bank][:],
                                                                       in1=h_t[:, dc, sl], op=ALU.add),
                     reads=[pb_b[bank], h_b[dc][s]], writes=[h_b[dc][s]])
            k.op(dve, lambda e: e.tensor_copy(out=kt3[:, :, 0:128], in_=kt3[:, :, 512:640]),
                 reads=[kt_b], writes=[kt_b])
            k.op(dve, lambda e: e.tensor_copy(out=vb3[:, 0, :], in_=vb3[:, 4, :]), reads=[vb_b], writes=[vb_b])
        k.op(dve, lambda e: e.tensor_copy(out=ktc_t[:, idx], in_=kt3[:, :, 0:128]), reads=[kt_b], writes=[kvc_b[idx]])
        k.op(dve, lambda e: e.tensor_copy(out=vc_t[:, idx], in_=vb3[:, 0, :]), reads=[vb_b], writes=[kvc_b[idx]])

    def hgrn_phase(l, hf):
        idx = (l - 1) // 2
        for s in range(NSUB):
            rmsnorm_sub(s, l, 4)
        ta = TmpAlloc()
        tl = [TailAlloc(0), TailAlloc(1)]
        Te, Te_b = ta.get("Te", [ST], F32)
        TL1, TL1_b = ta.get("TL1", [ST], F32)
        TL2, TL2_b = ta.get("TL2", [ST], F32)
        Tb, Tb_b = ta.get("Tb", [ST], F32)
        Tt, Tt_b = ta.get("Tt", [ST], F32)
        KD, KD_b = ta.get("KD", [ST], BF16)
        cbm, cbm_b = ta.get("cbm", [4], F32)
        nbm, nbm_b = ta.get("nbm", [4], F32)
        HB = []
        for i in range(2):
            d = {}
            for nm in ("QD", "KDT", "VT", "AT"):
                d[nm] = tl[i].get(nm + str(i), [ST], BF16)
            d["Tgs"] = tl[i].get("Tgs" + str(i), [ST], F32)
            for nm in ("ebl", "ebm", "eblm"):
                d[nm] = ta.get(nm + str(i), [4], F32)
            HB.append(d)
        Tqe, Tqe_b = tl[0].get("Tqe", [ST], F32)
        TqL, TqL_b = tl[0].get("TqL", [ST], F32)
        Tqd, Tqd_b = tl[0].get("Tqd", [ST], F32)
        Tge, Tge_b = tl[1].get("Tge", [ST], F32)
        TgL, TgL_b = tl[1].get("TgL", [ST], F32)
        To, To_b = tl[0].get("To", [ST], F32)
        OSQ, OSQ_b = tl[0].get("OSQ", [ST], BF16)
        OG, OG_b = tl[0].get("OG", [ST], BF16)
        Tr, Tr_b = tl[1].get("Tr", [ST], F32)
        Sp, Sp_b = tl[1].get("Sp", [128], F32)
        Sbf, Sbf_b = tl[1].get("Sbf", [128], BF16)
        Bf, Bq, Bv, Bg, Ba, Bt, Bo, Bp = range(8)
        BtB = pb_t[Bt][:].bitcast(BF16)

        units = []
        for head in range(8):
            slot = next_slot()
            for s in range(NSUB):
                units.append((head, s, slot))

        def stage1(u, ui):
            head, s, slot = u
            wt = w_t[slot]
            if s == 0:
                k.dma(pool, w_st[slot], wt[:, 0:KC * 512], hwin_d[idx, head], writes=[w_b[slot]])
                k.dma(pool, w_st[slot], wt[:, KC * 512:KC * 512 + D], hwout_d[idx, head], writes=[w_b[slot]])
            hb = HB[ui % 2]
            QD, QD_b = hb["QD"]
            KDT, KDT_b = hb["KDT"]
            VT, VT_b = hb["VT"]
            AT, AT_b = hb["AT"]
            Tgs, Tgs_b = hb["Tgs"]
            ebl, ebl_b = hb["ebl"]
            ebm, ebm_b = hb["ebm"]
            eblm, eblm_b = hb["eblm"]
            lbc = lb_t[:, idx, head:head + 1]
            c1c = c1_t[:, idx, head:head + 1]
            sl = slice(s * ST, (s + 1) * ST)

            def wi(kc, which):
                base = kc * 512 + which * 128
                return wt[:, base:base + 128]

            for which, bank in ((1, Bf), (0, Bq)):
                for kc in range(KC):
                    k.op(pe, lambda e, kc=kc, which=which, bank=bank: e.matmul(
                        pb_t[bank][:], lhsT=wi(kc, which), rhs=hn_t[:, kc, sl],
                        start=(kc == 0), stop=(kc == KC - 1)),
                        reads=[w_b[slot], hn_b[s]], writes=[pb_b[bank]], inc=(kc == KC - 1))
            yield
            for blk in range(4):
                for kc in range(KC):
                    k.op(pe, lambda e, kc=kc, blk=blk: e.matmul(
                        pb_t[Bv][:, blk * 128:(blk + 1) * 128],
                        lhsT=hn_t[:, kc, s * ST + blk * 128:s * ST + (blk + 1) * 128], rhs=wi(kc, 2),
                        start=(kc == 0), stop=(kc == KC - 1)),
                        reads=[w_b[slot], hn_b[s]], writes=[pb_b[Bv]], inc=(kc == KC - 1))
            yield
            for kc in range(KC):
                k.op(pe, lambda e, kc=kc: e.matmul(
                    pb_t[Bg][:], lhsT=wi(kc, 3), rhs=hn_t[:, kc, sl],
                    start=(kc == 0), stop=(kc == KC - 1)),
                    reads=[w_b[slot], hn_b[s]], writes=[pb_b[Bg]], inc=(kc == KC - 1))
            yield
            k.op(act, lambda e: e.copy(out=VT, in_=pb_t[Bv][:]), reads=[pb_b[Bv]], writes=[VT_b])
            k.op(act, lambda e: e.activation(out=Te, in_=pb_t[Bf][:], func=AF.Exp, scale=-1.0),
                 reads=[pb_b[Bf]], writes=[Te_b])
            k.op(act, lambda e: e.activation(out=TL1, in_=Te, func=AF.Ln, bias=1.0, scale=1.0),
                 reads=[Te_b], writes=[TL1_b])
            k.op(act, lambda e: e.activation(out=TL2, in_=Te, func=AF.Ln, bias=1.0, scale=lbc),
                 reads=[Te_b, const_b], writes=[TL2_b])
            yield
            k.op(act, lambda e: e.activation(out=Tqe, in_=pb_t[Bq][:], func=AF.Exp, scale=-1.0),
                 reads=[pb_b[Bq]], writes=[Tqe_b])
            k.op(act, lambda e: e.activation(out=TqL, in_=Tqe, func=AF.Ln, bias=1.0, scale=1.0),
                 reads=[Tqe_b], writes=[TqL_b])
            for blk in range(4):
                cs = slice(blk * 128, (blk + 1) * 128)
                k.op(dve, lambda e, cs=cs: e.tensor_tensor_scan(out=Tb[:, cs], data0=TL2[:, cs], data1=TL1[:, cs],
                                                                initial=0.0, op0=ALU.add, op1=ALU.subtract),
                     reads=[TL1_b, TL2_b], writes=[Tb_b])
            yield
            k.op(act, lambda e: e.activation(out=Tge, in_=pb_t[Bg][:], func=AF.Exp, scale=-1.0),
                 reads=[pb_b[Bg]], writes=[Tge_b])
            k.op(act, lambda e: e.activation(out=TgL, in_=Tge, func=AF.Ln, bias=1.0, scale=1.0),
                 reads=[Tge_b], writes=[TgL_b])
            k.op(act, lambda e: e.activation(out=TgL, in_=TgL, func=AF.Exp, scale=-1.0),
                 reads=[TgL_b], writes=[TgL_b])
            k.op(dve, lambda e: e.tensor_tensor(out=Tt, in0=pb_t[Bf][:], in1=TL1, op=ALU.add),
                 reads=[pb_b[Bf], TL1_b], writes=[Tt_b])
            k.op(dve, lambda e: e.tensor_tensor(out=Tt, in0=Tt, in1=Tb, op=ALU.add),
                 reads=[Tt_b, Tb_b], writes=[Tt_b])
            yield
            Tb4 = Tb.rearrange("p (b t) -> p b t", b=4)
            bm_v, bl_v = Tb4[:, :, 63], Tb4[:, :, 127]
            k.op(dve, lambda e: e.tensor_scalar(out=cbm, in0=bm_v, scalar1=c1c, scalar2=None, op0=ALU.add),
                 reads=[Tb_b, const_b], writes=[cbm_b])
            k.op(dve, lambda e: e.tensor_scalar(out=nbm, in0=bm_v, scalar1=-1.0, scalar2=None, op0=ALU.mult),
                 reads=[Tb_b], writes=[nbm_b])
            k.op(dve, lambda e: e.tensor_tensor(out=eblm, in0=bl_v, in1=bm_v, op=ALU.subtract),
                 reads=[Tb_b], writes=[eblm_b])
            k.op(dve, lambda e: e.tensor_tensor(out=Tqd, in0=Tb, in1=TqL, op=ALU.subtract),
                 reads=[Tb_b, TqL_b], writes=[Tqd_b])
            for blk in range(4):
                cs = slice(blk * 128, (blk + 1) * 128)
                k.op(act, lambda e, cs=cs, blk=blk: e.activation(out=KD[:, cs], in_=Tt[:, cs], func=AF.Exp,
                                                                 bias=cbm[:, blk:blk + 1], scale=-1.0),
                     reads=[Tt_b, cbm_b], writes=[KD_b])
            yield
            for blk in range(4):
                cs = slice(blk * 128, (blk + 1) * 128)
                k.op(pe, lambda e, cs=cs: e.transpose(BtB[:, cs], KD[:, cs], idb_t[:]),
                     reads=[KD_b, const_b], writes=[pb_b[Bt]], inc=(blk == 3))
            for blk in range(4):
                cs = slice(blk * 128, (blk + 1) * 128)
                k.op(act, lambda e, cs=cs, blk=blk: e.activation(out=Tqd[:, cs], in_=Tqd[:, cs], func=AF.Exp,
                                                                 bias=nbm[:, blk:blk + 1], scale=1.0),
                     reads=[Tqd_b, nbm_b], writes=[Tqd_b])
            k.op(dve, lambda e: e.tensor_tensor(out=Tgs, in0=pb_t[Bg][:], in1=TgL, op=ALU.mult),
                 reads=[pb_b[Bg], TgL_b], writes=[Tgs_b])
            k.op(dve, lambda e: e.tensor_copy(out=KDT, in_=BtB[:, 0:ST]), reads=[pb_b[Bt]], writes=[KDT_b])
            yield
            k.op(act, lambda e: e.activation(out=ebl, in_=bl_v, func=AF.Exp), reads=[Tb_b], writes=[ebl_b])
            k.op(act, lambda e: e.activation(out=ebm, in_=bm_v, func=AF.Exp), reads=[Tb_b], writes=[ebm_b])
            k.op(act, lambda e: e.activation(out=eblm, in_=eblm, func=AF.Exp), reads=[eblm_b], writes=[eblm_b])
            k.op(dve, lambda e: e.tensor_tensor(out=QD, in0=pb_t[Bq][:], in1=Tqd, op=ALU.mult),
                 reads=[pb_b[Bq], Tqd_b], writes=[QD_b])
            yield
            for blk in range(4):
                cs = slice(blk * 128, (blk + 1) * 128)
                k.op(pe, lambda e, cs=cs: e.matmul(pb_t[Ba][:, cs], lhsT=KD[:, cs], rhs=QD[:, cs],
                                                   start=True, stop=True),
                     reads=[KD_b, QD_b], writes=[pb_b[Ba]], inc=(blk == 3))
            k.op(dve, lambda e: e.tensor_tensor(out=AT, in0=pb_t[Ba][:], in1=cm4_t[:], op=ALU.mult),
                 reads=[pb_b[Ba], const_b], writes=[AT_b])

        def stage2(u, ui):
            head, s, slot = u
            wt = w_t[slot]
            hb = HB[ui % 2]
            QD, QD_b = hb["QD"]
            KDT, KDT_b = hb["KDT"]
            VT, VT_b = hb["VT"]
            AT, AT_b = hb["AT"]
            Tgs, Tgs_b = hb["Tgs"]
            ebl, ebl_b = hb["ebl"]
            ebm, ebm_b = hb["ebm"]
            eblm, eblm_b = hb["eblm"]
            Sst = S_t[:, idx * 8 + head, :]
            Sb_ = S_b[idx * 8 + head]
            sl = slice(s * ST, (s + 1) * ST)
            for blk in range(4):
                cs = slice(blk * 128, (blk + 1) * 128)
                k.op(act, lambda e, blk=blk: e.activation(out=Sbf, in_=Sst, func=AF.Identity,
                                                          scale=ebm[:, blk:blk + 1]),
                     reads=[Sb_, ebm_b], writes=[Sbf_b])
                k.op(pe, lambda e, cs=cs: e.matmul(pb_t[Bo][:, cs], lhsT=VT[:, cs], rhs=AT[:, cs],
                                                   start=True, stop=False),
                     reads=[VT_b, AT_b], writes=[pb_b[Bo]], inc=False)
                k.op(pe, lambda e, cs=cs: e.matmul(pb_t[Bo][:, cs], lhsT=Sbf, rhs=QD[:, cs],
                                                   start=False, stop=True),
                     reads=[Sbf_b, QD_b], writes=[pb_b[Bo]], inc=True)
                k.op(pe, lambda e, cs=cs: e.matmul(pb_t[Bp][:, 0:128], lhsT=KDT[:, cs], rhs=VT[:, cs],
                                                   start=True, stop=True),
                     reads=[KDT_b, VT_b], writes=[pb_b[Bp]], inc=True)
                k.op(act, lambda e, blk=blk: e.activation(out=Sp, in_=Sst, func=AF.Identity,
                                                          scale=ebl[:, blk:blk + 1]),
                     reads=[Sb_, ebl_b], writes=[Sp_b])
                k.op(dve, lambda e, blk=blk: e.scalar_tensor_tensor(
                    out=Sst, in0=pb_t[Bp][:, 0:128], scalar=eblm[:, blk:blk + 1], in1=Sp,
                    op0=ALU.mult, op1=ALU.add), reads=[pb_b[Bp], eblm_b, Sp_b], writes=[Sb_])
                yield
            k.op(act, lambda e: e.activation(out=OSQ, in_=pb_t[Bo][:], func=AF.Square),
                 reads=[pb_b[Bo]], writes=[OSQ_b])
            k.op(pe, lambda e: e.matmul(pb_t[Bp][:], lhsT=ones_t[:], rhs=OSQ, start=True, stop=True),
                 reads=[OSQ_b, const_b], writes=[pb_b[Bp]], inc=True)
            k.op(act, lambda e: e.activation(out=Tr, in_=pb_t[Bp][:], func=AF.Ln, bias=EPS, scale=1.0 / 128),
                 reads=[pb_b[Bp]], writes=[Tr_b])
            k.op(act, lambda e: e.activation(out=Tr, in_=Tr, func=AF.Exp, scale=-0.5), reads=[Tr_b], writes=[Tr_b])
            k.op(dve, lambda e: e.scalar_tensor_tensor(out=To, in0=pb_t[Bo][:], scalar=pcol(HN0 + idx), in1=Tr,
                                                       op0=ALU.mult, op1=ALU.mult),
                 reads=[pb_b[Bo], Tr_b, pc_b], writes=[To_b])
            k.op(dve, lambda e: e.tensor_tensor(out=OG, in0=To, in1=Tgs, op=ALU.mult),
                 reads=[To_b, Tgs_b], writes=[OG_b])
            yield
            for dc in range(KC):
                ob = (Bp, Bo)[dc % 2]
                k.op(pe, lambda e, dc=dc, ob=ob: e.matmul(pb_t[ob][:], lhsT=wt[:, KC * 512 + dc * 128:KC * 512 + (dc + 1) * 128],
                                                          rhs=OG, start=True, stop=True),
                     reads=[w_b[slot], OG_b], writes=[pb_b[ob]], inc=True)
                k.op(dve, lambda e, dc=dc, ob=ob: e.tensor_tensor(out=h_t[:, dc, sl], in0=pb_t[ob][:],
                                                                  in1=h_t[:, dc, sl], op=ALU.add),
                     reads=[pb_b[ob], h_b[dc][s]], writes=[h_b[dc][s]])
                if dc % 2 == 1:
                    yield

        nU = len(units)
        for _ in stage1(units[0], 0):
            pass
        for ui in range(nU):
            g2 = stage2(units[ui], ui)
            g1 = stage1(units[ui + 1], ui + 1) if ui + 1 < nU else iter(())
            d1 = d2 = False
            for _ in range(HG_LEAD):
                try:
                    next(g1)
                except StopIteration:
                    d1 = True
            while not (d1 and d2):
                if not d1:
                    try:
                        next(g1)
                    except StopIteration:
                        d1 = True
                if not d2:
                    try:
                        next(g2)
                    except StopIteration:
                        d2 = True

    out_streams = []
    for hf in range(NHALF):
        load_half(hf)
        for kind, l in plan:
            if kind == "attn":
                attn_phase(l, hf)
            elif kind == "hgrn":
                hgrn_phase(l, hf)
            else:
                ffn_phase(l, hf)
        out_streams += store_half(hf)
    for st in out_streams:
        sp.h.wait_ge(st.sem, st.count)
    return k


def pack_inputs(inp):
    f = np.float32
    pc = np.zeros((128, NPC), f)
    norms = [inp["norm_mix"][i] for i in range(4)] + [inp["norm_ffn"][i] for i in range(4)] + [inp["norm_final"]]
    for n, gvec in enumerate(norms):
        pc[:, GAIN0 + n * 8:GAIN0 + (n + 1) * 8] = np.asarray(gvec, f).reshape(8, 128).T
    cw = np.asarray(inp["ffn_conv_w"], f)
    cb = np.asarray(inp["ffn_conv_b"], f)
    for l in range(DEPTH):
        blk = np.concatenate([cw[l], cb[l][None]], axis=0)
        blk = blk.reshape(4, 44, 128).transpose(2, 1, 0)
        pc[:, CONV0 + l * 176:CONV0 + (l + 1) * 176] = blk.reshape(128, 176)
    lg = np.asarray(inp["hgrn_lb_logits"], f)
    pc[:, LG0:LG0 + 32] = lg.reshape(4, 8, 128).transpose(2, 1, 0).reshape(128, 32)
    pc[:, HN0:HN0 + 2] = np.asarray(inp["hgrn_norm"], f).T
    sk = np.asarray(inp["attn_sinks"], f)
    for idx in range(2):
        for g in range(4):
            for hh in range(2):
                col = SK0 + idx * 8 + g * 2 + hh
                pc[0:64, col] = sk[idx, 4 * g + 0 + 2 * hh]
                pc[64:128, col] = sk[idx, 4 * g + 1 + 2 * hh]

    awi = np.asarray(inp["attn_w_in"], f)
    q = awi[:, :, 0:1024]
    kk = awi[:, :, 1024:1280].reshape(2, D, 4, 64)
    kdup = np.concatenate([kk, kk], axis=3).reshape(2, D, 512)
    v = awi[:, :, 1280:1536]
    win = np.concatenate([q, kdup, v], axis=2)
    awin = np.ascontiguousarray(win.reshape(2, KC, 128, 1792).transpose(0, 2, 1, 3)).reshape(2, 128, KC * 1792)
    awo = np.asarray(inp["attn_w_out"], f)
    awout = np.ascontiguousarray(awo.reshape(2, KC, 128, D).transpose(0, 2, 1, 3)).reshape(2, 128, KC * D)

    hwi = np.asarray(inp["hgrn_w_in"], f)
    hwi = hwi.reshape(2, KC, 128, 4, 8, 128)
    hwin = np.ascontiguousarray(hwi.transpose(0, 4, 2, 1, 3, 5)).reshape(2, 8, 128, KC * 512)
    hwo = np.asarray(inp["hgrn_w_out"], f)
    hwout = np.ascontiguousarray(hwo.reshape(2, 8, 128, D))

    fu = np.asarray(inp["ffn_w_up"], f)
    fd = np.asarray(inp["ffn_w_down"], f)
    fup = np.zeros((DEPTH, NG, 128, KC, 2, GMAX * 128), f)
    fdn = np.zeros((DEPTH, NG, 128, GMAX, D), f)
    fu_r = fu.reshape(DEPTH, KC, 128, 2, D_FF)
    for g in range(NG):
        n = G_SIZES[g]
        c0 = G_OFF[g] * 128
        fup[:, g, :, :, :, 0:n * 128] = fu_r[:, :, :, :, c0:c0 + n * 128].transpose(0, 2, 1, 3, 4)
        fdn[:, g, :, 0:n, :] = fd[:, c0:c0 + n * 128, :].reshape(DEPTH, n, 128, D).transpose(0, 2, 1, 3)
    fup = fup.reshape(DEPTH, NG, 128, KC * 2 * GMAX * 128)
    fdn = fdn.reshape(DEPTH, NG, 128, GMAX * D)
    return dict(pc=pc, awin=awin, awout=awout, hwin=hwin, hwout=hwout, fup=fup, fdn=fdn)


FULL_PLAN = [("attn", 0), ("ffn", 0), ("hgrn", 1), ("ffn", 1), ("attn", 2), ("ffn", 2), ("hgrn", 3), ("ffn", 3)]
_CACHE = {}


def run_plan(inputs, plan, final_norm=True, x_override=None, trace=False, n_cores=8):
    key = (tuple(plan), final_norm)
    if key not in _CACHE:
        _CACHE[key] = build(plan, final_norm)
    k = _CACHE[key]
    shared = pack_inputs(inputs)
    x = np.asarray(inputs["x"] if x_override is None else x_override, np.float32)
    in_maps = []
    for b in range(n_cores):
        m = dict(shared)
        m["x"] = np.ascontiguousarray(x[b])
        in_maps.append(m)
    res = run_bass_kernel_spmd(k.nc, in_maps, core_ids=list(range(n_cores)), trace=trace)
    out = np.stack([np.asarray(r["out"], np.float32) for r in res.results], axis=0)
    return out, res


def kernel(**inputs):
    out, _ = run_plan(inputs, FULL_PLAN, True)
    return out
```

```python
import numpy as np
from contextlib import ExitStack
import concourse.bass as bass
import concourse.mybir as mybir
from concourse.bass_utils import run_bass_kernel_spmd

F32 = mybir.dt.float32
BF16 = mybir.dt.bfloat16
AF = mybir.ActivationFunctionType
ALU = mybir.AluOpType
AX = mybir.AxisListType

S = 4096
D = 1024
DEPTH = 4
T = 2048
NHALF = S // T
ST = 512
NSUB = T // ST
KC = D // 128
D_FF = 2816
NFC = D_FF // 128
G_SIZES = [5, 5, 4, 4, 4]
G_OFF = [0, 5, 10, 14, 18]
NG = len(G_SIZES)
GMAX = 5
EPS = 1e-6
NQH = 16
HG_LEAD = 0
HD = 64

GAIN0 = 0
CONV0 = GAIN0 + 9 * 8
LG0 = CONV0 + DEPTH * 44 * 4
HN0 = LG0 + 32
SK0 = HN0 + 2
NPC = SK0 + 32


class Stream:
    def __init__(self, sem, step):
        self.sem = sem
        self.count = 0
        self.step = step


class Eng:
    def __init__(self, name, h, st):
        self.name = name
        self.h = h
        self.st = st
        self.seen = {}


class Buf:
    __slots__ = ("name", "w", "r", "excl")

    def __init__(self, name, deps=None, excl=False):
        self.name = name
        self.w = None
        self.r = dict(deps) if deps else {}
        self.excl = excl


def merge_deps(bufs):
    d = {}
    for b in bufs:
        if b.w is not None:
            s, v = b.w
            if d.get(s, 0) < v:
                d[s] = v
        for s, v in b.r.items():
            if d.get(s, 0) < v:
                d[s] = v
    return d


class K:
    def __init__(self):
        self.nc = bass.Bass("TRN2", target_bir_lowering=False)
        self.es = ExitStack()
        nc = self.nc
        self.pe = Eng("pe", nc.tensor, self.stream("s_pe", 1))
        self.act = Eng("act", nc.scalar, self.stream("s_act", 1))
        self.dve = Eng("dve", nc.vector, self.stream("s_dve", 1))
        self.pool = Eng("pool", nc.gpsimd, self.stream("s_pool", 1))
        self.sp = Eng("sp", nc.sync, self.stream("s_sp", 1))

    def stream(self, name, step):
        sem = self.es.enter_context(self.nc.semaphore(name))
        return Stream(sem, step)

    def sb(self, name, shape, dt):
        return self.es.enter_context(self.nc.sbuf_tensor("sb_" + name, shape, dt))

    def ps(self, name, shape, dt):
        return self.es.enter_context(self.nc.psum_tensor("ps_" + name, shape, dt))

    def _sync(self, eng, reads, writes, is_dma=False):
        need = {}
        for b in reads:
            if b.w is not None:
                s, v = b.w
                if need.get(s, 0) < v:
                    need[s] = v
            if b.excl:
                for s, v in b.r.items():
                    if s is not eng.st and need.get(s, 0) < v:
                        need[s] = v
        same_ok = (not is_dma) and eng.name == "pe"
        for b in writes:
            if b.w is not None:
                s, v = b.w
                if not (same_ok and s is eng.st):
                    if need.get(s, 0) < v:
                        need[s] = v
            for s, v in b.r.items():
                if not (same_ok and s is eng.st):
                    if need.get(s, 0) < v:
                        need[s] = v
        for s, v in need.items():
            if eng.seen.get(s, 0) < v:
                eng.h.wait_ge(s.sem, v)
                eng.seen[s] = v

    @staticmethod
    def _record(stream, tick, reads, writes):
        for b in reads:
            if b.r.get(stream, 0) < tick:
                b.r[stream] = tick
        for b in writes:
            b.w = (stream, tick)
            b.r = {}

    def op(self, eng, fn, reads=(), writes=(), inc=True):
        self._sync(eng, reads, writes)
        ins = fn(eng.h)
        if inc:
            eng.st.count += 1
            ins.then_inc(eng.st.sem, 1)
            tick = eng.st.count
        else:
            tick = eng.st.count + 1
        self._record(eng.st, tick, reads, writes)
        return ins

    def dma(self, eng, stream, out, in_, reads=(), writes=()):
        self._sync(eng, reads, writes, is_dma=True)
        ins = eng.h.dma_start(out=out, in_=in_)
        stream.count += 16
        ins.then_inc(stream.sem, 16)
        self._record(stream, stream.count, reads, writes)
        return ins


def build(plan, final_norm=True):
    k = K()
    nc = k.nc
    pe, act, dve, pool, sp = k.pe, k.act, k.dve, k.pool, k.sp

    x_d = nc.dram_tensor("x", [S, D], F32, kind="ExternalInput").ap()
    pc_d = nc.dram_tensor("pc", [128, NPC], F32, kind="ExternalInput").ap()
    awin_d = nc.dram_tensor("awin", [2, 128, KC * 1792], F32, kind="ExternalInput").ap()
    awout_d = nc.dram_tensor("awout", [2, 128, KC * D], F32, kind="ExternalInput").ap()
    hwin_d = nc.dram_tensor("hwin", [2, 8, 128, KC * 512], F32, kind="ExternalInput").ap()
    hwout_d = nc.dram_tensor("hwout", [2, 8, 128, D], F32, kind="ExternalInput").ap()
    fup_d = nc.dram_tensor("fup", [DEPTH, NG, 128, KC * 2 * GMAX * 128], F32, kind="ExternalInput").ap()
    fdn_d = nc.dram_tensor("fdn", [DEPTH, NG, 128, GMAX * D], F32, kind="ExternalInput").ap()
    out_d = nc.dram_tensor("out", [S, D], F32, kind="ExternalOutput").ap()

    h_t = k.sb("h", [128, KC, T], F32)
    h_b = [[Buf("h%d_%d" % (c, s)) for s in range(NSUB)] for c in range(KC)]
    hn_t = k.sb("hn", [128, KC, T], BF16)
    hn_b = [Buf("hn%d" % s) for s in range(NSUB)]
    pc_t = k.sb("pc", [128, NPC], F32)
    pc_b = Buf("pc")
    WSLOT = 15360
    w_t = [k.sb("w%d" % i, [128, WSLOT], BF16) for i in range(2)]
    w_b = [Buf("w%d" % i) for i in range(2)]
    w_st = [k.stream("s_w%d" % i, 16) for i in range(2)]
    wcount = [0]
    idb_t = k.sb("idb", [128, 128], BF16)
    idf_t = k.sb("idf", [128, 128], F32)
    ones_t = k.sb("ones", [128, 128], BF16)
    cm4_t = k.sb("cm4", [128, 512], BF16)
    mall_t = k.sb("mall", [128, 8, 512], BF16)
    const_b = Buf("const")
    lb_t = k.sb("lb", [128, 2, 8], F32)
    c1_t = k.sb("c1", [128, 2, 8], F32)
    esk_t = k.sb("esk", [128, 32], F32)
    S_t = k.sb("S", [128, 16, 128], F32)
    S_b = [Buf("S%d" % i) for i in range(16)]
    ktc_t = k.sb("ktc", [128, 2, 4, 128], BF16)
    vc_t = k.sb("vc", [128, 2, 256], BF16)
    kvc_b = [Buf("kvc%d" % i) for i in range(2)]
    halo_t = k.sb("halo", [128, DEPTH, 2, 44, 2], F32)
    halo_b = [[[Buf("halo%d_%d_%d" % (l, q, j)) for j in range(44)] for q in range(2)] for l in range(DEPTH)]
    sq_t = [k.sb("sq%d" % i, [128, ST], BF16) for i in range(2)]
    sq_b = [Buf("sq%d" % i) for i in range(2)]
    rstd_t = k.sb("rstd", [128, ST], F32)
    rstd_b = Buf("rstd")
    TMPW = 5200
    tmp_t = k.sb("tmp", [128, TMPW], F32)
    tmp_live = []

    pb_t = [k.ps("pb%d" % i, [128, 512], F32) for i in range(8)]
    pb_b = [Buf("pb%d" % i, excl=True) for i in range(8)]

    class TmpAlloc:
        def __init__(self):
            self.deps = merge_deps(tmp_live)
            del tmp_live[:]
            self.off = 0

        def get(self, name, shape_free, dt):
            n = int(np.prod(shape_free))
            words = n if dt == F32 else (n + 1) // 2
            assert self.off + words <= TMPW, (name, self.off, words)
            ap = tmp_t[:, self.off:self.off + words]
            if dt != F32:
                ap = ap.bitcast(dt)[:, 0:n]
            self.off += words
            b = Buf(name, self.deps)
            tmp_live.append(b)
            return ap, b

    TAIL0 = 5120
    tail_live = [[], []]

    class TailAlloc:
        def __init__(self, i):
            self.i = i
            self.deps = merge_deps(tail_live[i] + [w_b[i]])
            del tail_live[i][:]
            self.off = TAIL0

        def get(self, name, shape_free, dt):
            n = int(np.prod(shape_free))
            el = n if dt == BF16 else 2 * n
            assert self.off + el <= WSLOT, (name, self.off, el)
            ap = w_t[self.i][:, self.off:self.off + el]
            if dt != BF16:
                ap = ap.bitcast(dt)
            self.off += el
            b = Buf(name, self.deps)
            tail_live[self.i].append(b)
            return ap, b

    def pcol(j):
        return pc_t[:, j:j + 1]

    st_pc = k.stream("s_pc", 16)
    k.dma(sp, st_pc, pc_t[:], pc_d, writes=[pc_b])
    k.op(pool, lambda e: e.memset(ones_t[:], 1.0), writes=[const_b])
    k.op(pool, lambda e: e.memset(idb_t[:], 1.0), writes=[const_b])
    k.op(pool, lambda e: e.affine_select(out=idb_t[:], in_=idb_t[:], pattern=[[-1, 128]], compare_op=ALU.is_equal,
                                         fill=0.0, base=0, channel_multiplier=1), reads=[const_b], writes=[const_b])
    k.op(pool, lambda e: e.memset(idf_t[:], 1.0), writes=[const_b])
    k.op(pool, lambda e: e.affine_select(out=idf_t[:], in_=idf_t[:], pattern=[[-1, 128]], compare_op=ALU.is_equal,
                                         fill=0.0, base=0, channel_multiplier=1), reads=[const_b], writes=[const_b])
    k.op(pool, lambda e: e.memset(cm4_t[:], 1.0), writes=[const_b])
    for blk in range(4):
        k.op(pool, lambda e, blk=blk: e.affine_select(out=cm4_t[:, blk * 128:(blk + 1) * 128],
                                                      in_=cm4_t[:, blk * 128:(blk + 1) * 128],
                                                      pattern=[[1, 128]], compare_op=ALU.is_ge, fill=0.0, base=0,
                                                      channel_multiplier=-1), reads=[const_b], writes=[const_b])
    k.op(pool, lambda e: e.memset(halo_t[:].rearrange("p a q b c -> p (a q b c)"), 0.0), writes=[b for h1 in halo_b for h2 in h1 for b in h2])
    k.op(pool, lambda e: e.memset(S_t[:].rearrange("p a b -> p (a b)"), 0.0), writes=S_b)
    k.op(pool, lambda e: e.memset(ktc_t[:].rearrange("p a b c -> p (a b c)"), 0.0), writes=kvc_b)
    k.op(pool, lambda e: e.memset(vc_t[:].rearrange("p a b -> p (a b)"), 0.0), writes=kvc_b)

    has_attn = any(p[0] == "attn" for p in plan)
    has_hgrn = any(p[0] == "hgrn" for p in plan)

    if has_attn:
        ta = TmpAlloc()
        d1, d1_b = ta.get("d1", [128], F32)
        dpos, dpos_b = ta.get("dpos", [128], F32)
        dneg, dneg_b = ta.get("dneg", [128], F32)
        mtmp, mtmp_b = ta.get("mtmp", [256], F32)
        k.op(pool, lambda e: e.iota(d1, pattern=[[1, 128]], base=0, channel_multiplier=-1,
                                    allow_small_or_imprecise_dtypes=True), writes=[d1_b])
        k.op(dve, lambda e: e.tensor_scalar_max(out=dpos, in0=d1, scalar1=0.0), reads=[d1_b], writes=[dpos_b])
        k.op(dve, lambda e: e.tensor_scalar_min(out=dneg, in0=d1, scalar1=0.0), reads=[d1_b], writes=[dneg_b])
        for gp in range(8):
            g, p = gp // 2, gp % 2
            for hh in range(2):
                hq = 4 * g + p + 2 * hh
                slope = float(2.0 ** (-8.0 * (hq + 1) / NQH))
                k.op(act, lambda e, slope=slope: e.activation(out=mtmp[:, 0:128], in_=dneg, func=AF.Exp,
                                                              bias=-slope * 128.0, scale=-slope),
                     reads=[dneg_b], writes=[mtmp_b])
                k.op(act, lambda e, slope=slope: e.activation(out=mtmp[:, 128:256], in_=dpos, func=AF.Exp,
                                                              scale=-slope),
                     reads=[dpos_b], writes=[mtmp_b])
                k.op(pool, lambda e, gp=gp, hh=hh: e.affine_select(
                    out=mall_t[:, gp, hh * 128:(hh + 1) * 128], in_=mtmp[:, 0:128], pattern=[[-1, 128]],
                    compare_op=ALU.is_gt, fill=0.0, base=0, channel_multiplier=1),
                    reads=[mtmp_b], writes=[const_b])
                k.op(pool, lambda e, gp=gp, hh=hh: e.affine_select(
                    out=mall_t[:, gp, 256 + hh * 128:256 + (hh + 1) * 128], in_=mtmp[:, 128:256],
                    pattern=[[1, 128]], compare_op=ALU.is_ge, fill=0.0, base=0, channel_multiplier=-1),
                    reads=[mtmp_b], writes=[const_b])
        k.op(act, lambda e: e.activation(out=esk_t[:, 0:16], in_=pc_t[:, SK0:SK0 + 16], func=AF.Exp),
             reads=[pc_b], writes=[const_b])

    if has_hgrn:
        ta = TmpAlloc()
        lg = pc_t[:, LG0:LG0 + 32].rearrange("p (h l) -> p h l", l=4)
        mx, mx_b = ta.get("mx", [8], F32)
        ex, ex_b = ta.get("ex", [32], F32)
        sm, sm_b = ta.get("sm", [8], F32)
        ex3 = ex.rearrange("p (h l) -> p h l", l=4)
        k.op(dve, lambda e: e.tensor_reduce(out=mx, in_=lg, axis=AX.X, op=ALU.max), reads=[pc_b], writes=[mx_b])
        for li in range(4):
            k.op(dve, lambda e, li=li: e.tensor_tensor(out=ex3[:, :, li], in0=lg[:, :, li], in1=mx, op=ALU.subtract),
                 reads=[pc_b, mx_b, ex_b], writes=[ex_b])
        k.op(act, lambda e: e.activation(out=ex, in_=ex, func=AF.Exp), reads=[ex_b], writes=[ex_b])
        k.op(dve, lambda e: e.tensor_reduce(out=sm, in_=ex3, axis=AX.X, op=ALU.add), reads=[ex_b], writes=[sm_b])
        k.op(dve, lambda e: e.reciprocal(out=sm, in_=sm), reads=[sm_b], writes=[sm_b])
        for li in range(4):
            k.op(dve, lambda e, li=li: e.tensor_tensor(out=ex3[:, :, li], in0=ex3[:, :, li], in1=sm, op=ALU.mult),
                 reads=[ex_b, sm_b], writes=[ex_b])
        k.op(dve, lambda e: e.tensor_copy(out=lb_t[:, 0, :], in_=ex3[:, :, 1]), reads=[ex_b], writes=[const_b])
        k.op(dve, lambda e: e.tensor_tensor(out=lb_t[:, 1, :], in0=ex3[:, :, 1], in1=ex3[:, :, 2], op=ALU.add),
             reads=[ex_b], writes=[const_b])
        k.op(dve, lambda e: e.tensor_tensor(out=lb_t[:, 1, :], in0=lb_t[:, 1, :], in1=ex3[:, :, 3], op=ALU.add),
             reads=[ex_b, const_b], writes=[const_b])
        k.op(act, lambda e: e.activation(out=c1_t[:].rearrange("p a b -> p (a b)"),
                                         in_=lb_t[:].rearrange("p a b -> p (a b)"), func=AF.Ln, bias=1.0, scale=-1.0),
             reads=[const_b], writes=[const_b])

    def next_slot():
        i = wcount[0] % 2
        wcount[0] += 1
        return i

    evac_rr = [0]

    def evac_copy(out, in_, reads, writes, force_act=False):
        evac_rr[0] += 1
        if force_act or evac_rr[0] % 2 == 0:
            k.op(act, lambda e: e.copy(out=out, in_=in_), reads=reads, writes=writes)
        else:
            k.op(dve, lambda e: e.tensor_copy(out=out, in_=in_), reads=reads, writes=writes)

    def rmsnorm_sub(s, gain_idx, nbank, in_place=False):
        sl = slice(s * ST, (s + 1) * ST)
        for c in range(KC):
            i = c % 2
            k.op(act, lambda e, c=c, i=i: e.activation(out=sq_t[i][:], in_=h_t[:, c, sl], func=AF.Square),
                 reads=[h_b[c][s]], writes=[sq_b[i]])
            k.op(pe, lambda e, c=c, i=i: e.matmul(pb_t[nbank][:], lhsT=ones_t[:], rhs=sq_t[i][:],
                                                   start=(c == 0), stop=(c == KC - 1)),
                 reads=[sq_b[i], const_b], writes=[pb_b[nbank]], inc=True)
        k.op(act, lambda e: e.activation(out=rstd_t[:], in_=pb_t[nbank][:], func=AF.Ln, bias=EPS, scale=1.0 / D),
             reads=[pb_b[nbank]], writes=[rstd_b])
        k.op(act, lambda e: e.activation(out=rstd_t[:], in_=rstd_t[:], func=AF.Exp, scale=-0.5),
             reads=[rstd_b], writes=[rstd_b])
        for c in range(KC):
            if in_place:
                k.op(dve, lambda e, c=c: e.scalar_tensor_tensor(out=h_t[:, c, sl], in0=h_t[:, c, sl],
                                                                scalar=pcol(GAIN0 + gain_idx * 8 + c),
                                                                in1=rstd_t[:], op0=ALU.mult, op1=ALU.mult),
                     reads=[h_b[c][s], rstd_b, pc_b], writes=[h_b[c][s]])
            else:
                k.op(dve, lambda e, c=c: e.scalar_tensor_tensor(out=hn_t[:, c, sl], in0=h_t[:, c, sl],
                                                                scalar=pcol(GAIN0 + gain_idx * 8 + c),
                                                                in1=rstd_t[:], op0=ALU.mult, op1=ALU.mult),
                     reads=[h_b[c][s], rstd_b, pc_b], writes=[hn_b[s]])

    def load_half(hf):
        ta = TmpAlloc()
        xin = [ta.get("xin%d" % i, [D], F32) for i in range(2)]
        xst = [k.stream("s_xin%d_%d" % (hf, i), 16) for i in range(2)]
        for blk in range(T // 128):
            i = blk % 2
            t0 = hf * T + blk * 128
            s = blk // 4
            col = blk * 128
            k.dma(sp, xst[i], xin[i][0], x_d[t0:t0 + 128, :], writes=[xin[i][1]])
            for half in range(2):
                bank = half
                for cc in range(4):
                    c = half * 4 + cc
                    k.op(pe, lambda e, c=c, cc=cc, bank=bank, i=i: e.transpose(
                        pb_t[bank][:, cc * 128:(cc + 1) * 128], xin[i][0][:, c * 128:(c + 1) * 128], idf_t[:]),
                        reads=[xin[i][1], const_b], writes=[pb_b[bank]], inc=(cc == 3))
                evac_copy(h_t[:, half * 4:half * 4 + 4, col:col + 128],
                          pb_t[bank][:].rearrange("p (c t) -> p c t", c=4),
                          reads=[pb_b[bank]], writes=[h_b[half * 4 + cc][s] for cc in range(4)])

    def store_half(hf):
        ta = TmpAlloc()
        yo = [ta.get("yo%d" % i, [D], F32) for i in range(2)]
        yst = [k.stream("s_yo%d_%d" % (hf, i), 16) for i in range(2)]
        for s in range(NSUB):
            if final_norm:
                rmsnorm_sub(s, 8, 6, in_place=True)
            for bb in range(4):
                blk = s * 4 + bb
                i = blk % 2
                t0 = hf * T + blk * 128
                col = blk * 128
                for half in range(2):
                    bank = half
                    for cc in range(4):
                        c = half * 4 + cc
                        k.op(pe, lambda e, c=c, cc=cc, bank=bank: e.transpose(
                            pb_t[bank][:, cc * 128:(cc + 1) * 128], h_t[:, c, col:col + 128], idf_t[:]),
                            reads=[h_b[c][s], const_b], writes=[pb_b[bank]], inc=(cc == 3))
                    evac_copy(yo[i][0][:, half * 512:(half + 1) * 512], pb_t[bank][:],
                              reads=[pb_b[bank]], writes=[yo[i][1]])
                k.dma(sp, yst[i], out_d[t0:t0 + 128, :], yo[i][0], reads=[yo[i][1]])
        return yst

    def ffn_phase(l, hf):
        for s in range(NSUB):
            rmsnorm_sub(s, 4 + l, 6)
        ta = TmpAlloc()
        cg = [ta.get("cg%d" % i, [ST], F32) for i in range(2)]
        cv = [ta.get("cv%d" % i, [ST], F32) for i in range(2)]
        sg = [ta.get("sg%d" % i, [ST], BF16) for i in range(2)]
        actb = [ta.get("act%d" % i, [GMAX, ST], BF16) for i in range(2)]
        items = []
        gs_list = []
        for g in range(NG):
            n = G_SIZES[g]
            slot = next_slot()
            for s in range(NSUB):
                gs_list.append((g, s, slot))
                for jj in range(n):
                    items.append((g, s, jj, slot, len(gs_list) - 1))

        def wup(slot, kc, col0):
            base = kc * 2 * GMAX * 128 + col0
            return w_t[slot][:, base:base + 128]

        def wdn(slot, jj, dc):
            base = KC * 2 * GMAX * 128 + jj * D + dc * 128
            return w_t[slot][:, base:base + 128]

        def stageA(it, idx):
            g, s, jj, slot, gsi = it
            par = idx % 2
            sl = slice(s * ST, (s + 1) * ST)
            for which in range(2):
                bank = par * 2 + which
                col0 = which * GMAX * 128 + jj * 128
                for kc in range(KC):
                    k.op(pe, lambda e, kc=kc, bank=bank, col0=col0: e.matmul(
                        pb_t[bank][:], lhsT=wup(slot, kc, col0), rhs=hn_t[:, kc, sl],
                        start=(kc == 0), stop=(kc == KC - 1)),
                        reads=[w_b[slot], hn_b[s]], writes=[pb_b[bank]], inc=(kc == KC - 1))

        def stageB(it, idx):
            g, s, jj, slot, gsi = it
            par = idx % 2
            first_tok = (hf == 0 and s == 0)
            sg_ = hf * NSUB + s
            hq_r, hq_w = (sg_ - 1) % 2, sg_ % 2
            info = []
            for which in range(2):
                bank = par * 2 + which
                P = pb_t[bank]
                C, Cb = (cg, cv)[which][par]
                jc = which * NFC + G_OFF[g] + jj
                cb = CONV0 + (l * 44 + jc) * 4
                info.append((bank, P, C, Cb, jc, cb))
                k.op(act, lambda e, P=P, C=C, cb=cb: e.activation(out=C, in_=P[:], func=AF.Identity,
                                                                  bias=pcol(cb + 3), scale=pcol(cb + 2)),
                     reads=[pb_b[bank], pc_b], writes=[Cb])
                k.op(act, lambda e, P=P, jc=jc: e.copy(out=halo_t[:, l, hq_w, jc, :], in_=P[:, ST - 2:ST]),
                     reads=[pb_b[bank]], writes=[halo_b[l][hq_w][jc]])
            for bank, P, C, Cb, jc, cb in info:
                hl = halo_t[:, l, hq_r, jc, :]
                hb_ = halo_b[l][hq_r][jc]
                k.op(dve, lambda e, P=P, C=C, cb=cb: e.scalar_tensor_tensor(
                    out=C[:, 1:ST], in0=P[:, 0:ST - 1], scalar=pcol(cb + 1), in1=C[:, 1:ST],
                    op0=ALU.mult, op1=ALU.add), reads=[pb_b[bank], Cb, pc_b], writes=[Cb])
                k.op(dve, lambda e, P=P, C=C, cb=cb: e.scalar_tensor_tensor(
                    out=C[:, 2:ST], in0=P[:, 0:ST - 2], scalar=pcol(cb + 0), in1=C[:, 2:ST],
                    op0=ALU.mult, op1=ALU.add), reads=[pb_b[bank], Cb, pc_b], writes=[Cb])
                if not first_tok:
                    k.op(dve, lambda e, C=C, cb=cb, hl=hl: e.scalar_tensor_tensor(
                        out=C[:, 0:1], in0=hl[:, 1:2], scalar=pcol(cb + 1), in1=C[:, 0:1],
                        op0=ALU.mult, op1=ALU.add), reads=[hb_, Cb, pc_b], writes=[Cb])
                    k.op(dve, lambda e, C=C, cb=cb, hl=hl: e.scalar_tensor_tensor(
                        out=C[:, 0:2], in0=hl[:, 0:2], scalar=pcol(cb + 0), in1=C[:, 0:2],
                        op0=ALU.mult, op1=ALU.add), reads=[hb_, Cb, pc_b], writes=[Cb])

        def stageC(it, idx):
            g, s, jj, slot, gsi = it
            par = idx % 2
            Cg, Cgb = cg[par]
            Cv, Cvb = cv[par]
            Sg, Sgb = sg[par]
            A, Ab = actb[gsi % 2]
            k.op(act, lambda e: e.activation(out=Sg, in_=Cg, func=AF.Silu), reads=[Cgb], writes=[Sgb])
            k.op(dve, lambda e: e.tensor_tensor(out=A[:, jj * ST:(jj + 1) * ST], in0=Sg, in1=Cv, op=ALU.mult),
                 reads=[Sgb, Cvb], writes=[Ab])

        def D_mm(gsi, dc):
            g, s, slot = gs_list[gsi]
            n = G_SIZES[g]
            A, Ab = actb[gsi % 2]
            bank = 4 + dc % 2
            for jj in range(n):
                k.op(pe, lambda e, jj=jj: e.matmul(
                    pb_t[bank][:], lhsT=wdn(slot, jj, dc), rhs=A[:, jj * ST:(jj + 1) * ST],
                    start=(jj == 0), stop=(jj == n - 1)),
                    reads=[w_b[slot], Ab], writes=[pb_b[bank]], inc=(jj == n - 1))

        def D_add(gsi, dc):
            g, s, slot = gs_list[gsi]
            bank = 4 + dc % 2
            sl = slice(s * ST, (s + 1) * ST)
            k.op(dve, lambda e: e.tensor_tensor(out=h_t[:, dc, sl], in0=pb_t[bank][:],
                                                in1=h_t[:, dc, sl], op=ALU.add),
                 reads=[pb_b[bank], h_b[dc][s]], writes=[h_b[dc][s]])

        def load_w(g, slot):
            wt, wb, wst = w_t[slot], w_b[slot], w_st[slot]
            k.dma(pool, wst, wt[:, 0:KC * 2 * GMAX * 128], fup_d[l, g], writes=[wb] + tail_live[slot])
            k.dma(pool, wst, wt[:, KC * 2 * GMAX * 128:KC * 2 * GMAX * 128 + GMAX * D], fdn_d[l, g], writes=[wb] + tail_live[slot])

        nI = len(items)
        pend = []
        step = 0
        while step < nI + 2 or pend:
            tasks = pend[:2]
            del pend[:2]
            for tk in tasks:
                D_mm(*tk)
            if step < nI:
                g_, s_, jj_, slot_, _ = items[step]
                if s_ == 0 and jj_ == 0:
                    load_w(g_, slot_)
                stageA(items[step], step)
            if 0 <= step - 1 < nI:
                stageB(items[step - 1], step - 1)
            if 0 <= step - 2 < nI:
                it = items[step - 2]
                stageC(it, step - 2)
                g, s, jj, slot, gsi = it
                if jj == G_SIZES[g] - 1:
                    pend.extend((gsi, dc) for dc in range(KC))
            for tk in tasks:
                D_add(*tk)
            step += 1

    def attn_phase(l, hf):
        idx = l // 2
        sA = next_slot()
        k.dma(pool, w_st[sA], w_t[sA][:, 0:KC * 1792], awin_d[idx], writes=[w_b[sA]] + tail_live[sA])
        sB = next_slot()
        k.dma(pool, w_st[sB], w_t[sB][:, 0:KC * D], awout_d[idx], writes=[w_b[sB]] + tail_live[sB])
        ta = TmpAlloc()
        kt, kt_b = ta.get("kt", [4, 640], BF16)
        vb, vb_b = ta.get("vb", [5, 256], BF16)
        et = [ta.get("e%d" % i, [512], BF16) for i in range(2)]
        pt = [ta.get("p%d" % i, [512], BF16) for i in range(4)]
        rd = [ta.get("rd%d" % i, [256], F32) for i in range(2)]
        kt3 = kt.rearrange("p (g t) -> p g t", g=4)
        vb3 = vb.rearrange("p (b d) -> p b d", b=5)
        hnS_b, qt_b, ot_b = hn_b[0], hn_b[1], hn_b[2]

        def hnS(kc, a, b):
            return hn_t[:, kc, a:b]

        QT0, OT0 = ST, 2 * ST
        k.op(dve, lambda e: e.tensor_copy(out=kt3[:, :, 0:128], in_=ktc_t[:, idx]), reads=[kvc_b[idx]], writes=[kt_b])
        k.op(dve, lambda e: e.tensor_copy(out=vb3[:, 0, :], in_=vc_t[:, idx]), reads=[kvc_b[idx]], writes=[vb_b])

        def win(kc, c0, n):
            base = kc * 1792 + c0
            return w_t[sA][:, base:base + n]

        def wout(c, dc):
            base = c * D + dc * 128
            return w_t[sB][:, base:base + 128]

        pcnt = [0]
        for s in range(NSUB):
            sl = slice(s * ST, (s + 1) * ST)
            for c in range(KC):
                i = c % 2
                k.op(act, lambda e, c=c, i=i: e.activation(out=sq_t[i][:], in_=h_t[:, c, sl], func=AF.Square),
                     reads=[h_b[c][s]], writes=[sq_b[i]])
                k.op(pe, lambda e, c=c, i=i: e.matmul(pb_t[6][:], lhsT=ones_t[:], rhs=sq_t[i][:],
                                                       start=(c == 0), stop=(c == KC - 1)),
                     reads=[sq_b[i], const_b], writes=[pb_b[6]], inc=True)
            k.op(act, lambda e: e.activation(out=rstd_t[:], in_=pb_t[6][:], func=AF.Ln, bias=EPS, scale=1.0 / D),
                 reads=[pb_b[6]], writes=[rstd_b])
            k.op(act, lambda e: e.activation(out=rstd_t[:], in_=rstd_t[:], func=AF.Exp, scale=-0.5),
                 reads=[rstd_b], writes=[rstd_b])
            for c in range(KC):
                k.op(dve, lambda e, c=c: e.scalar_tensor_tensor(out=hn_t[:, c, 0:ST], in0=h_t[:, c, sl],
                                                                scalar=pcol(GAIN0 + l * 8 + c),
                                                                in1=rstd_t[:], op0=ALU.mult, op1=ALU.mult),
                     reads=[h_b[c][s], rstd_b, pc_b], writes=[hnS_b])
            for c in range(KC + 4):
                bank = pcnt[0] % 2
                pcnt[0] += 1
                for kc in range(KC):
                    k.op(pe, lambda e, c=c, kc=kc, bank=bank: e.matmul(
                        pb_t[bank][:], lhsT=win(kc, c * 128, 128), rhs=hnS(kc, 0, ST),
                        start=(kc == 0), stop=(kc == KC - 1)),
                        reads=[w_b[sA], hnS_b], writes=[pb_b[bank]], inc=(kc == KC - 1))
                if c < KC:
                    evac_copy(hn_t[:, c, QT0:QT0 + ST], pb_t[bank][:], reads=[pb_b[bank]], writes=[qt_b], force_act=True)
                else:
                    evac_copy(kt3[:, c - KC, 128:640], pb_t[bank][:], reads=[pb_b[bank]], writes=[kt_b], force_act=True)
            for pair in range(2):
                bank = pcnt[0] % 2
                pcnt[0] += 1
                for bb in range(2):
                    blk = pair * 2 + bb
                    for kc in range(KC):
                        k.op(pe, lambda e, kc=kc, blk=blk, bb=bb, bank=bank: e.matmul(
                            pb_t[bank][:, bb * 256:(bb + 1) * 256], lhsT=hnS(kc, blk * 128, (blk + 1) * 128),
                            rhs=win(kc, 1536, 256), start=(kc == 0), stop=(kc == KC - 1)),
                            reads=[w_b[sA], hnS_b], writes=[pb_b[bank]], inc=(kc == KC - 1))
                evac_copy(vb[:, (1 + pair * 2) * 256:(3 + pair * 2) * 256], pb_t[bank][:],
                          reads=[pb_b[bank]], writes=[vb_b], force_act=True)
            steps = [(bq, g) for bq in range(4) for g in range(4)]

            def scores(bq, g):
                nglob = hf * (T // 128) + s * 4 + bq
                has_prev = nglob > 0
                for p in range(2):
                    gp = g * 2 + p
                    bank = 2 + gp % 2
                    X = pb_t[bank]
                    rows = slice(p * 64, (p + 1) * 64)
                    qrhs = hn_t[rows, 2 * g:2 * g + 2, QT0 + bq * 128:QT0 + (bq + 1) * 128]
                    if has_prev:
                        k.op(pe, lambda e, X=X, rows=rows, qrhs=qrhs: e.matmul(
                            X[:, 0:256].rearrange("p (h q) -> p h q", h=2),
                            lhsT=kt3[rows, g, bq * 128:(bq + 1) * 128], rhs=qrhs, start=True, stop=True),
                            reads=[kt_b, qt_b], writes=[pb_b[bank]], inc=False)
                    k.op(pe, lambda e, X=X, rows=rows, qrhs=qrhs: e.matmul(
                        X[:, 256:512].rearrange("p (h q) -> p h q", h=2),
                        lhsT=kt3[rows, g, (bq + 1) * 128:(bq + 2) * 128], rhs=qrhs, start=True, stop=True),
                        reads=[kt_b, qt_b], writes=[pb_b[bank]], inc=True)
                    E, Eb = et[gp % 2]
                    Pm, Pb = pt[gp % 4]
                    lo = 0 if has_prev else 256
                    k.op(act, lambda e, X=X, E=E, lo=lo: e.activation(out=E[:, lo:512], in_=X[:, lo:512],
                                                                       func=AF.Exp, scale=HD ** -0.5),
                         reads=[pb_b[bank]], writes=[Eb])
                    k.op(dve, lambda e, E=E, Pm=Pm, gp=gp, lo=lo: e.tensor_tensor(
                        out=Pm[:, lo:512], in0=E[:, lo:512], in1=mall_t[:, gp, lo:512], op=ALU.mult),
                        reads=[Eb, const_b], writes=[Pb])

            def pv(bq, g):
                nglob = hf * (T // 128) + s * 4 + bq
                has_prev = nglob > 0
                bank = (4, 5, 7)[(bq * 4 + g) % 3]
                Y = pb_t[bank]
                vcols = slice(g * 64, (g + 1) * 64)
                for p in range(2):
                    gp = g * 2 + p
                    Pm, Pb = pt[gp % 4]
                    rows = slice(p * 64, (p + 1) * 64)
                    if has_prev:
                        k.op(pe, lambda e, rows=rows, Pm=Pm: e.matmul(
                            Y[rows, 0:256], lhsT=vb3[:, bq, vcols], rhs=Pm[:, 0:256], start=True, stop=False),
                            reads=[vb_b, Pb], writes=[pb_b[bank]], inc=False)
                    k.op(pe, lambda e, rows=rows, Pm=Pm: e.matmul(
                        Y[rows, 0:256], lhsT=vb3[:, bq + 1, vcols], rhs=Pm[:, 256:512],
                        start=(not has_prev), stop=True),
                        reads=[vb_b, Pb], writes=[pb_b[bank]], inc=False)
                    if has_prev:
                        k.op(pe, lambda e, rows=rows, Pm=Pm: e.matmul(
                            Y[rows, 256:512], lhsT=ones_t[:, 0:64], rhs=Pm[:, 0:256], start=True, stop=False),
                            reads=[const_b, Pb], writes=[pb_b[bank]], inc=False)
                    k.op(pe, lambda e, rows=rows, Pm=Pm: e.matmul(
                        Y[rows, 256:512], lhsT=ones_t[:, 0:64], rhs=Pm[:, 256:512],
                        start=(not has_prev), stop=True),
                        reads=[const_b, Pb], writes=[pb_b[bank]], inc=(p == 1))
                R, Rb = rd[(bq * 4 + g) % 2]
                for hh in range(2):
                    col = idx * 8 + g * 2 + hh
                    k.op(act, lambda e, hh=hh, col=col: e.activation(
                        out=R[:, hh * 128:(hh + 1) * 128], in_=Y[:, 256 + hh * 128:256 + (hh + 1) * 128],
                        func=AF.Ln, bias=esk_t[:, col:col + 1], scale=1.0),
                        reads=[pb_b[bank], const_b], writes=[Rb])
                k.op(act, lambda e: e.activation(out=R, in_=R, func=AF.Exp, scale=-1.0), reads=[Rb], writes=[Rb])
                k.op(dve, lambda e: e.tensor_tensor(
                    out=hn_t[:, 2 * g:2 * g + 2, OT0 + bq * 128:OT0 + (bq + 1) * 128],
                    in0=Y[:, 0:256].rearrange("p (h q) -> p h q", h=2),
                    in1=R.rearrange("p (h q) -> p h q", h=2), op=ALU.mult),
                    reads=[pb_b[bank], Rb], writes=[ot_b])

            scores(*steps[0])
            for si in range(len(steps)):
                if si + 1 < len(steps):
                    scores(*steps[si + 1])
                pv(*steps[si])
            for dc in range(KC):
                bank = pcnt[0] % 2
                pcnt[0] += 1
                for c in range(KC):
                    k.op(pe, lambda e, c=c, dc=dc, bank=bank: e.matmul(
                        pb_t[bank][:], lhsT=wout(c, dc), rhs=hn_t[:, c, OT0:OT0 + ST],
                        start=(c == 0), stop=(c == KC - 1)),
                        reads=[w_b[sB], ot_b], writes=[pb_b[bank]], inc=(c == KC - 1))
                k.op(dve, lambda e, dc=dc, bank=bank: e.tensor_tensor(out=h_t[:, dc, sl], in0=pb_t[bank][:],
                                                                       in1=h_t[:, dc, sl], op=ALU.add),
                     reads=[pb_b[bank], h_b[dc][s]], writes=[h_b[dc][s]])
            k.op(dve, lambda e: e.tensor_copy(out=kt3[:, :, 0:128], in_=kt3[:, :, 512:640]),
                 reads=[kt_b], writes=[kt_b])
            k.op(dve, lambda e: e.tensor_copy(out=vb3[:, 0, :], in_=vb3[:, 4, :]), reads=[vb_b], writes=[vb_b])
        k.op(dve, lambda e: e.tensor_copy(out=ktc_t[:, idx], in_=kt3[:, :, 0:128]), reads=[kt_b], writes=[kvc_b[idx]])
        k.op(dve, lambda e: e.tensor_copy(out=vc_t[:, idx], in_=vb3[:, 0, :]), reads=[vb_b], writes=[kvc_b[idx]])

    def hgrn_phase(l, hf):
        idx = (l - 1) // 2
        for s in range(NSUB):
            rmsnorm_sub(s, l, 4)
        ta = TmpAlloc()
        tl = [TailAlloc(0), TailAlloc(1)]
        Te, Te_b = ta.get("Te", [ST], F32)
        TL1, TL1_b = ta.get("TL1", [ST], F32)
        TL2, TL2_b = ta.get("TL2", [ST], F32)
        Tb, Tb_b = ta.get("Tb", [ST], F32)
        Tt, Tt_b = ta.get("Tt", [ST], F32)
        KD, KD_b = ta.get("KD", [ST], BF16)
        cbm, cbm_b = ta.get("cbm", [4], F32)
        nbm, nbm_b = ta.get("nbm", [4], F32)
        HB = []
        for i in range(2):
            d = {}
            for nm in ("QD", "KDT", "VT", "AT"):
                d[nm] = tl[i].get(nm + str(i), [ST], BF16)
            d["Tgs"] = tl[i].get("Tgs" + str(i), [ST], F32)
            for nm in ("ebl", "ebm", "eblm"):
                d[nm] = ta.get(nm + str(i), [4], F32)
            HB.append(d)
        Tqe, Tqe_b = tl[0].get("Tqe", [ST], F32)
        TqL, TqL_b = tl[0].get("TqL", [ST], F32)
        Tqd, Tqd_b = tl[0].get("Tqd", [ST], F32)
        Tqq, Tqq_b = tl[1].get("Tqq", [ST], F32)
        Tgg, Tgg_b = tl[1].get("Tgg", [ST], F32)
        Tge, Tge_b = tl[1].get("Tge", [ST], F32)
        TgL, TgL_b = tl[1].get("TgL", [ST], F32)
        To, To_b = tl[0].get("To", [ST], F32)
        OSQ, OSQ_b = tl[0].get("OSQ", [ST], BF16)
        OG, OG_b = tl[0].get("OG", [ST], BF16)
        Tr, Tr_b = tl[1].get("Tr", [ST], F32)
        Sp, Sp_b = tl[1].get("Sp", [128], F32)
        Sbf, Sbf_b = tl[1].get("Sbf", [128], BF16)
        Bf, Bq, Bv, Bg, Ba, Bt, Bo, Bp = range(8)
        BtB = pb_t[Bt][:].bitcast(BF16)

        units = []
        for head in range(8):
            slot = next_slot()
            for s in range(NSUB):
                units.append((head, s, slot))

        def stageP(u, ui):
            head, s, slot = u
            wt = w_t[slot]
            if s == 0:
                k.dma(pool, w_st[slot], wt[:, 0:KC * 512], hwin_d[idx, head], writes=[w_b[slot]])
                k.dma(pool, w_st[slot], wt[:, KC * 512:KC * 512 + D], hwout_d[idx, head], writes=[w_b[slot]])
            hb = HB[ui % 2]
            QD, QD_b = hb["QD"]
            KDT, KDT_b = hb["KDT"]
            VT, VT_b = hb["VT"]
            AT, AT_b = hb["AT"]
            Tgs, Tgs_b = hb["Tgs"]
            ebl, ebl_b = hb["ebl"]
            ebm, ebm_b = hb["ebm"]
            eblm, eblm_b = hb["eblm"]
            lbc = lb_t[:, idx, head:head + 1]
            c1c = c1_t[:, idx, head:head + 1]
            sl = slice(s * ST, (s + 1) * ST)

            def wi(kc, which):
                base = kc * 512 + which * 128
                return wt[:, base:base + 128]

            for which, bank in ((1, Bf), (0, Bq)):
                for kc in range(KC):
                    k.op(pe, lambda e, kc=kc, which=which, bank=bank: e.matmul(
                        pb_t[bank][:], lhsT=wi(kc, which), rhs=hn_t[:, kc, sl],
                        start=(kc == 0), stop=(kc == KC - 1)),
                        reads=[w_b[slot], hn_b[s]], writes=[pb_b[bank]], inc=(kc == KC - 1))
            yield
            for blk in range(4):
                for kc in range(KC):
                    k.op(pe, lambda e, kc=kc, blk=blk: e.matmul(
                        pb_t[Bv][:, blk * 128:(blk + 1) * 128],
                        lhsT=hn_t[:, kc, s * ST + blk * 128:s * ST + (blk + 1) * 128], rhs=wi(kc, 2),
                        start=(kc == 0), stop=(kc == KC - 1)),
                        reads=[w_b[slot], hn_b[s]], writes=[pb_b[Bv]], inc=(kc == KC - 1))
            yield
            for kc in range(KC):
                k.op(pe, lambda e, kc=kc: e.matmul(
                    pb_t[Bg][:], lhsT=wi(kc, 3), rhs=hn_t[:, kc, sl],
                    start=(kc == 0), stop=(kc == KC - 1)),
                    reads=[w_b[slot], hn_b[s]], writes=[pb_b[Bg]], inc=(kc == KC - 1))
        def stageC(u, ui):
            head, s, slot = u
            hb = HB[ui % 2]
            QD, QD_b = hb["QD"]
            KDT, KDT_b = hb["KDT"]
            VT, VT_b = hb["VT"]
            AT, AT_b = hb["AT"]
            Tgs, Tgs_b = hb["Tgs"]
            ebl, ebl_b = hb["ebl"]
            ebm, ebm_b = hb["ebm"]
            eblm, eblm_b = hb["eblm"]
            lbc = lb_t[:, idx, head:head + 1]
            c1c = c1_t[:, idx, head:head + 1]
            k.op(act, lambda e: e.copy(out=VT, in_=pb_t[Bv][:]), reads=[pb_b[Bv]], writes=[VT_b])
            k.op(act, lambda e: e.activation(out=Te, in_=pb_t[Bf][:], func=AF.Exp, scale=-1.0),
                 reads=[pb_b[Bf]], writes=[Te_b])
            k.op(act, lambda e: e.activation(out=Tt, in_=pb_t[Bf][:], func=AF.Exp, scale=1.0),
                 reads=[pb_b[Bf]], writes=[Tt_b])
            k.op(act, lambda e: e.activation(out=Tqe, in_=pb_t[Bq][:], func=AF.Exp, scale=-1.0),
                 reads=[pb_b[Bq]], writes=[Tqe_b])
            k.op(dve, lambda e: e.tensor_copy(out=Tqq, in_=pb_t[Bq][:]), reads=[pb_b[Bq]], writes=[Tqq_b])
            k.op(act, lambda e: e.activation(out=Tge, in_=pb_t[Bg][:], func=AF.Exp, scale=-1.0),
                 reads=[pb_b[Bg]], writes=[Tge_b])
            k.op(dve, lambda e: e.tensor_copy(out=Tgg, in_=pb_t[Bg][:]), reads=[pb_b[Bg]], writes=[Tgg_b])
            yield
            k.op(act, lambda e: e.activation(out=TL1, in_=Te, func=AF.Ln, bias=1.0, scale=1.0),
                 reads=[Te_b], writes=[TL1_b])
            k.op(act, lambda e: e.activation(out=TL2, in_=Te, func=AF.Ln, bias=1.0, scale=lbc),
                 reads=[Te_b, const_b], writes=[TL2_b])
            k.op(act, lambda e: e.activation(out=Tt, in_=Tt, func=AF.Ln, bias=1.0, scale=1.0),
                 reads=[Tt_b], writes=[Tt_b])
            yield
            k.op(act, lambda e: e.activation(out=TqL, in_=Tqe, func=AF.Ln, bias=1.0, scale=1.0),
                 reads=[Tqe_b], writes=[TqL_b])
            for blk in range(4):
                cs = slice(blk * 128, (blk + 1) * 128)
                k.op(dve, lambda e, cs=cs: e.tensor_tensor_scan(out=Tb[:, cs], data0=TL2[:, cs], data1=TL1[:, cs],
                                                                initial=0.0, op0=ALU.add, op1=ALU.subtract),
                     reads=[TL1_b, TL2_b], writes=[Tb_b])
            yield
            k.op(act, lambda e: e.activation(out=TgL, in_=Tge, func=AF.Ln, bias=1.0, scale=1.0),
                 reads=[Tge_b], writes=[TgL_b])
            k.op(act, lambda e: e.activation(out=TgL, in_=TgL, func=AF.Exp, scale=-1.0),
                 reads=[TgL_b], writes=[TgL_b])
            k.op(dve, lambda e: e.tensor_tensor(out=Tt, in0=Tt, in1=Tb, op=ALU.add),
                 reads=[Tt_b, Tb_b], writes=[Tt_b])
            yield
            Tb4 = Tb.rearrange("p (b t) -> p b t", b=4)
            bm_v, bl_v = Tb4[:, :, 63], Tb4[:, :, 127]
            k.op(dve, lambda e: e.tensor_scalar(out=cbm, in0=bm_v, scalar1=c1c, scalar2=None, op0=ALU.add),
                 reads=[Tb_b, const_b], writes=[cbm_b])
            k.op(dve, lambda e: e.tensor_scalar(out=nbm, in0=bm_v, scalar1=-1.0, scalar2=None, op0=ALU.mult),
                 reads=[Tb_b], writes=[nbm_b])
            k.op(dve, lambda e: e.tensor_tensor(out=eblm, in0=bl_v, in1=bm_v, op=ALU.subtract),
                 reads=[Tb_b], writes=[eblm_b])
            k.op(dve, lambda e: e.tensor_tensor(out=Tqd, in0=Tb, in1=TqL, op=ALU.subtract),
                 reads=[Tb_b, TqL_b], writes=[Tqd_b])
            for blk in range(4):
                cs = slice(blk * 128, (blk + 1) * 128)
                k.op(act, lambda e, cs=cs, blk=blk: e.activation(out=KD[:, cs], in_=Tt[:, cs], func=AF.Exp,
                                                                 bias=cbm[:, blk:blk + 1], scale=-1.0),
                     reads=[Tt_b, cbm_b], writes=[KD_b])
            yield
            for blk in range(4):
                cs = slice(blk * 128, (blk + 1) * 128)
                k.op(pe, lambda e, cs=cs: e.transpose(BtB[:, cs], KD[:, cs], idb_t[:]),
                     reads=[KD_b, const_b], writes=[pb_b[Bt]], inc=(blk == 3))
            for blk in range(4):
                cs = slice(blk * 128, (blk + 1) * 128)
                k.op(act, lambda e, cs=cs, blk=blk: e.activation(out=Tqd[:, cs], in_=Tqd[:, cs], func=AF.Exp,
                                                                 bias=nbm[:, blk:blk + 1], scale=1.0),
                     reads=[Tqd_b, nbm_b], writes=[Tqd_b])
            k.op(dve, lambda e: e.tensor_tensor(out=Tgs, in0=Tgg, in1=TgL, op=ALU.mult),
                 reads=[Tgg_b, TgL_b], writes=[Tgs_b])
            k.op(dve, lambda e: e.tensor_copy(out=KDT, in_=BtB[:, 0:ST]), reads=[pb_b[Bt]], writes=[KDT_b])
            yield
            k.op(act, lambda e: e.activation(out=ebl, in_=bl_v, func=AF.Exp), reads=[Tb_b], writes=[ebl_b])
            k.op(act, lambda e: e.activation(out=ebm, in_=bm_v, func=AF.Exp), reads=[Tb_b], writes=[ebm_b])
            k.op(act, lambda e: e.activation(out=eblm, in_=eblm, func=AF.Exp), reads=[eblm_b], writes=[eblm_b])
            k.op(dve, lambda e: e.tensor_tensor(out=QD, in0=Tqq, in1=Tqd, op=ALU.mult),
                 reads=[Tqq_b, Tqd_b], writes=[QD_b])
            yield
            for blk in range(4):
                cs = slice(blk * 128, (blk + 1) * 128)
                k.op(pe, lambda e, cs=cs: e.matmul(pb_t[Ba][:, cs], lhsT=KD[:, cs], rhs=QD[:, cs],
                                                   start=True, stop=True),
                     reads=[KD_b, QD_b], writes=[pb_b[Ba]], inc=(blk == 3))
            k.op(dve, lambda e: e.tensor_tensor(out=AT, in0=pb_t[Ba][:], in1=cm4_t[:], op=ALU.mult),
                 reads=[pb_b[Ba], const_b], writes=[AT_b])

        def stage2(u, ui):
            head, s, slot = u
            wt = w_t[slot]
            hb = HB[ui % 2]
            QD, QD_b = hb["QD"]
            KDT, KDT_b = hb["KDT"]
            VT, VT_b = hb["VT"]
            AT, AT_b = hb["AT"]
            Tgs, Tgs_b = hb["Tgs"]
            ebl, ebl_b = hb["ebl"]
            ebm, ebm_b = hb["ebm"]
            eblm, eblm_b = hb["eblm"]
            Sst = S_t[:, idx * 8 + head, :]
            Sb_ = S_b[idx * 8 + head]
            sl = slice(s * ST, (s + 1) * ST)
            for blk in range(4):
                cs = slice(blk * 128, (blk + 1) * 128)
                k.op(act, lambda e, blk=blk: e.activation(out=Sbf, in_=Sst, func=AF.Identity,
                                                          scale=ebm[:, blk:blk + 1]),
                     reads=[Sb_, ebm_b], writes=[Sbf_b])
                k.op(pe, lambda e, cs=cs: e.matmul(pb_t[Bo][:, cs], lhsT=VT[:, cs], rhs=AT[:, cs],
                                                   start=True, stop=False),
                     reads=[VT_b, AT_b], writes=[pb_b[Bo]], inc=False)
                k.op(pe, lambda e, cs=cs: e.matmul(pb_t[Bo][:, cs], lhsT=Sbf, rhs=QD[:, cs],
                                                   start=False, stop=True),
                     reads=[Sbf_b, QD_b], writes=[pb_b[Bo]], inc=True)
                k.op(pe, lambda e, cs=cs: e.matmul(pb_t[Bp][:, 0:128], lhsT=KDT[:, cs], rhs=VT[:, cs],
                                                   start=True, stop=True),
                     reads=[KDT_b, VT_b], writes=[pb_b[Bp]], inc=True)
                k.op(act, lambda e, blk=blk: e.activation(out=Sp, in_=Sst, func=AF.Identity,
                                                          scale=ebl[:, blk:blk + 1]),
                     reads=[Sb_, ebl_b], writes=[Sp_b])
                k.op(dve, lambda e, blk=blk: e.scalar_tensor_tensor(
                    out=Sst, in0=pb_t[Bp][:, 0:128], scalar=eblm[:, blk:blk + 1], in1=Sp,
                    op0=ALU.mult, op1=ALU.add), reads=[pb_b[Bp], eblm_b, Sp_b], writes=[Sb_])
                yield
            k.op(act, lambda e: e.activation(out=OSQ, in_=pb_t[Bo][:], func=AF.Square),
                 reads=[pb_b[Bo]], writes=[OSQ_b])
            k.op(pe, lambda e: e.matmul(pb_t[Bp][:], lhsT=ones_t[:], rhs=OSQ, start=True, stop=True),
                 reads=[OSQ_b, const_b], writes=[pb_b[Bp]], inc=True)
            k.op(act, lambda e: e.activation(out=Tr, in_=pb_t[Bp][:], func=AF.Ln, bias=EPS, scale=1.0 / 128),
                 reads=[pb_b[Bp]], writes=[Tr_b])
            k.op(act, lambda e: e.activation(out=Tr, in_=Tr, func=AF.Exp, scale=-0.5), reads=[Tr_b], writes=[Tr_b])
            k.op(dve, lambda e: e.scalar_tensor_tensor(out=To, in0=pb_t[Bo][:], scalar=pcol(HN0 + idx), in1=Tr,
                                                       op0=ALU.mult, op1=ALU.mult),
                 reads=[pb_b[Bo], Tr_b, pc_b], writes=[To_b])
            k.op(dve, lambda e: e.tensor_tensor(out=OG, in0=To, in1=Tgs, op=ALU.mult),
                 reads=[To_b, Tgs_b], writes=[OG_b])
            yield
            for dc in range(KC):
                ob = (Bp, Bo)[dc % 2]
                k.op(pe, lambda e, dc=dc, ob=ob: e.matmul(pb_t[ob][:], lhsT=wt[:, KC * 512 + dc * 128:KC * 512 + (dc + 1) * 128],
                                                          rhs=OG, start=True, stop=True),
                     reads=[w_b[slot], OG_b], writes=[pb_b[ob]], inc=True)
                k.op(dve, lambda e, dc=dc, ob=ob: e.tensor_tensor(out=h_t[:, dc, sl], in0=pb_t[ob][:],
                                                                  in1=h_t[:, dc, sl], op=ALU.add),
                     reads=[pb_b[ob], h_b[dc][s]], writes=[h_b[dc][s]])
                if dc % 2 == 1:
                    yield

        nU = len(units)

        def run_rr(gens):
            gens = [g for g in gens if g is not None]
            while gens:
                nxt = []
                for g in gens:
                    try:
                        next(g)
                        nxt.append(g)
                    except StopIteration:
                        pass
                gens = nxt

        run_rr([stageP(units[0], 0)])
        run_rr([stageC(units[0], 0), stageP(units[1], 1)])
        for ui in range(nU):
            run_rr([stageC(units[ui + 1], ui + 1) if ui + 1 < nU else None,
                    stage2(units[ui], ui),
                    stageP(units[ui + 2], ui + 2) if ui + 2 < nU else None])

    out_streams = []
    for hf in range(NHALF):
        load_half(hf)
        for kind, l in plan:
            if kind == "attn":
                attn_phase(l, hf)
            elif kind == "hgrn":
                hgrn_phase(l, hf)
            else:
                ffn_phase(l, hf)
        out_streams += store_half(hf)
    for st in out_streams:
        sp.h.wait_ge(st.sem, st.count)
    return k


def pack_inputs(inp):
    f = np.float32
    pc = np.zeros((128, NPC), f)
    norms = [inp["norm_mix"][i] for i in range(4)] + [inp["norm_ffn"][i] for i in range(4)] + [inp["norm_final"]]
    for n, gvec in enumerate(norms):
        pc[:, GAIN0 + n * 8:GAIN0 + (n + 1) * 8] = np.asarray(gvec, f).reshape(8, 128).T
    cw = np.asarray(inp["ffn_conv_w"], f)
    cb = np.asarray(inp["ffn_conv_b"], f)
    for l in range(DEPTH):
        blk = np.concatenate([cw[l], cb[l][None]], axis=0)
        blk = blk.reshape(4, 44, 128).transpose(2, 1, 0)
        pc[:, CONV0 + l * 176:CONV0 + (l + 1) * 176] = blk.reshape(128, 176)
    lg = np.asarray(inp["hgrn_lb_logits"], f)
    pc[:, LG0:LG0 + 32] = lg.reshape(4, 8, 128).transpose(2, 1, 0).reshape(128, 32)
    pc[:, HN0:HN0 + 2] = np.asarray(inp["hgrn_norm"], f).T
    sk = np.asarray(inp["attn_sinks"], f)
    for idx in range(2):
        for g in range(4):
            for hh in range(2):
                col = SK0 + idx * 8 + g * 2 + hh
                pc[0:64, col] = sk[idx, 4 * g + 0 + 2 * hh]
                pc[64:128, col] = sk[idx, 4 * g + 1 + 2 * hh]

    awi = np.asarray(inp["attn_w_in"], f)
    q = awi[:, :, 0:1024]
    kk = awi[:, :, 1024:1280].reshape(2, D, 4, 64)
    kdup = np.concatenate([kk, kk], axis=3).reshape(2, D, 512)
    v = awi[:, :, 1280:1536]
    win = np.concatenate([q, kdup, v], axis=2)
    awin = np.ascontiguousarray(win.reshape(2, KC, 128, 1792).transpose(0, 2, 1, 3)).reshape(2, 128, KC * 1792)
    awo = np.asarray(inp["attn_w_out"], f)
    awout = np.ascontiguousarray(awo.reshape(2, KC, 128, D).transpose(0, 2, 1, 3)).reshape(2, 128, KC * D)

    hwi = np.asarray(inp["hgrn_w_in"], f)
    hwi = hwi.reshape(2, KC, 128, 4, 8, 128)
    hwin = np.ascontiguousarray(hwi.transpose(0, 4, 2, 1, 3, 5)).reshape(2, 8, 128, KC * 512)
    hwo = np.asarray(inp["hgrn_w_out"], f)
    hwout = np.ascontiguousarray(hwo.reshape(2, 8, 128, D))

    fu = np.asarray(inp["ffn_w_up"], f)
    fd = np.asarray(inp["ffn_w_down"], f)
    fup = np.zeros((DEPTH, NG, 128, KC, 2, GMAX * 128), f)
    fdn = np.zeros((DEPTH, NG, 128, GMAX, D), f)
    fu_r = fu.reshape(DEPTH, KC, 128, 2, D_FF)
    for g in range(NG):
        n = G_SIZES[g]
        c0 = G_OFF[g] * 128
        fup[:, g, :, :, :, 0:n * 128] = fu_r[:, :, :, :, c0:c0 + n * 128].transpose(0, 2, 1, 3, 4)
        fdn[:, g, :, 0:n, :] = fd[:, c0:c0 + n * 128, :].reshape(DEPTH, n, 128, D).transpose(0, 2, 1, 3)
    fup = fup.reshape(DEPTH, NG, 128, KC * 2 * GMAX * 128)
    fdn = fdn.reshape(DEPTH, NG, 128, GMAX * D)
    return dict(pc=pc, awin=awin, awout=awout, hwin=hwin, hwout=hwout, fup=fup, fdn=fdn)


FULL_PLAN = [("attn", 0), ("ffn", 0), ("hgrn", 1), ("ffn", 1), ("attn", 2), ("ffn", 2), ("hgrn", 3), ("ffn", 3)]
_CACHE = {}


def run_plan(inputs, plan, final_norm=True, x_override=None, trace=False, n_cores=8):
    key = (tuple(plan), final_norm)
    if key not in _CACHE:
        _CACHE[key] = build(plan, final_norm)
    k = _CACHE[key]
    shared = pack_inputs(inputs)
    x = np.asarray(inputs["x"] if x_override is None else x_override, np.float32)
    in_maps = []
    for b in range(n_cores):
        m = dict(shared)
        m["x"] = np.ascontiguousarray(x[b])
        in_maps.append(m)
    res = run_bass_kernel_spmd(k.nc, in_maps, core_ids=list(range(n_cores)), trace=trace)
    out = np.stack([np.asarray(r["out"], np.float32) for r in res.results], axis=0)
    return out, res


def kernel(**inputs):
    out, _ = run_plan(inputs, FULL_PLAN, True)
    return out
```

```python
import numpy as np
from contextlib import ExitStack
import concourse.bass as bass
import concourse.mybir as mybir
from concourse.bass_utils import run_bass_kernel_spmd

F32 = mybir.dt.float32
BF16 = mybir.dt.bfloat16
AF = mybir.ActivationFunctionType
ALU = mybir.AluOpType
AX = mybir.AxisListType

S = 4096
D = 1024
DEPTH = 4
T = 2048
NHALF = S // T
ST = 512
NSUB = T // ST
KC = D // 128
D_FF = 2816
NFC = D_FF // 128
G_SIZES = [5, 5, 4, 4, 4]
G_OFF = [0, 5, 10, 14, 18]
NG = len(G_SIZES)
GMAX = 5
EPS = 1e-6
NQH = 16
HG_LEAD = 0
HD = 64

GAIN0 = 0
CONV0 = GAIN0 + 9 * 8
LG0 = CONV0 + DEPTH * 44 * 4
HN0 = LG0 + 32
SK0 = HN0 + 2
NPC = SK0 + 32


class Stream:
    def __init__(self, sem, step):
        self.sem = sem
        self.count = 0
        self.step = step


class Eng:
    def __init__(self, name, h, st):
        self.name = name
        self.h = h
        self.st = st
        self.seen = {}


class Buf:
    __slots__ = ("name", "w", "r", "excl")

    def __init__(self, name, deps=None, excl=False):
        self.name = name
        self.w = None
        self.r = dict(deps) if deps else {}
        self.excl = excl


def merge_deps(bufs):
    d = {}
    for b in bufs:
        if b.w is not None:
            s, v = b.w
            if d.get(s, 0) < v:
                d[s] = v
        for s, v in b.r.items():
            if d.get(s, 0) < v:
                d[s] = v
    return d


class K:
    def __init__(self):
        self.nc = bass.Bass("TRN2", target_bir_lowering=False)
        self.es = ExitStack()
        nc = self.nc
        self.pe = Eng("pe", nc.tensor, self.stream("s_pe", 1))
        self.act = Eng("act", nc.scalar, self.stream("s_act", 1))
        self.dve = Eng("dve", nc.vector, self.stream("s_dve", 1))
        self.pool = Eng("pool", nc.gpsimd, self.stream("s_pool", 1))
        self.sp = Eng("sp", nc.sync, self.stream("s_sp", 1))

    def stream(self, name, step):
        sem = self.es.enter_context(self.nc.semaphore(name))
        return Stream(sem, step)

    def sb(self, name, shape, dt):
        return self.es.enter_context(self.nc.sbuf_tensor("sb_" + name, shape, dt))

    def ps(self, name, shape, dt):
        return self.es.enter_context(self.nc.psum_tensor("ps_" + name, shape, dt))

    def _sync(self, eng, reads, writes, is_dma=False):
        need = {}
        for b in reads:
            if b.w is not None:
                s, v = b.w
                if need.get(s, 0) < v:
                    need[s] = v
            if b.excl:
                for s, v in b.r.items():
                    if s is not eng.st and need.get(s, 0) < v:
                        need[s] = v
        same_ok = (not is_dma) and eng.name == "pe"
        for b in writes:
            if b.w is not None:
                s, v = b.w
                if not (same_ok and s is eng.st):
                    if need.get(s, 0) < v:
                        need[s] = v
            for s, v in b.r.items():
                if not (same_ok and s is eng.st):
                    if need.get(s, 0) < v:
                        need[s] = v
        for s, v in need.items():
            if eng.seen.get(s, 0) < v:
                eng.h.wait_ge(s.sem, v)
                eng.seen[s] = v

    @staticmethod
    def _record(stream, tick, reads, writes):
        for b in reads:
            if b.r.get(stream, 0) < tick:
                b.r[stream] = tick
        for b in writes:
            b.w = (stream, tick)
            b.r = {}

    def op(self, eng, fn, reads=(), writes=(), inc=True):
        self._sync(eng, reads, writes)
        ins = fn(eng.h)
        if inc:
            eng.st.count += 1
            ins.then_inc(eng.st.sem, 1)
            tick = eng.st.count
        else:
            tick = eng.st.count + 1
        self._record(eng.st, tick, reads, writes)
        return ins

    def dma(self, eng, stream, out, in_, reads=(), writes=()):
        self._sync(eng, reads, writes, is_dma=True)
        ins = eng.h.dma_start(out=out, in_=in_)
        stream.count += 16
        ins.then_inc(stream.sem, 16)
        self._record(stream, stream.count, reads, writes)
        return ins


def build(plan, final_norm=True):
    k = K()
    nc = k.nc
    pe, act, dve, pool, sp = k.pe, k.act, k.dve, k.pool, k.sp

    x_d = nc.dram_tensor("x", [S, D], F32, kind="ExternalInput").ap()
    pc_d = nc.dram_tensor("pc", [128, NPC], F32, kind="ExternalInput").ap()
    awin_d = nc.dram_tensor("awin", [2, 128, KC * 1792], F32, kind="ExternalInput").ap()
    awout_d = nc.dram_tensor("awout", [2, 128, KC * D], F32, kind="ExternalInput").ap()
    hwin_d = nc.dram_tensor("hwin", [2, 8, 128, KC * 512], F32, kind="ExternalInput").ap()
    hwout_d = nc.dram_tensor("hwout", [2, 8, 128, D], F32, kind="ExternalInput").ap()
    fup_d = nc.dram_tensor("fup", [DEPTH, NG, 128, KC * 2 * GMAX * 128], F32, kind="ExternalInput").ap()
    fdn_d = nc.dram_tensor("fdn", [DEPTH, NG, 128, GMAX * D], F32, kind="ExternalInput").ap()
    out_d = nc.dram_tensor("out", [S, D], F32, kind="ExternalOutput").ap()

    h_t = k.sb("h", [128, KC, T], F32)
    h_b = [[Buf("h%d_%d" % (c, s)) for s in range(NSUB)] for c in range(KC)]
    hn_t = k.sb("hn", [128, KC, T], BF16)
    hn_b = [Buf("hn%d" % s) for s in range(NSUB)]
    pc_t = k.sb("pc", [128, NPC], F32)
    pc_b = Buf("pc")
    WSLOT = 15360
    w_t = [k.sb("w%d" % i, [128, WSLOT], BF16) for i in range(2)]
    w_b = [Buf("w%d" % i) for i in range(2)]
    w_st = [k.stream("s_w%d" % i, 16) for i in range(2)]
    wcount = [0]
    idb_t = k.sb("idb", [128, 128], BF16)
    idf_t = k.sb("idf", [128, 128], F32)
    ones_t = k.sb("ones", [128, 128], BF16)
    cm4_t = k.sb("cm4", [128, 512], BF16)
    mall_t = k.sb("mall", [128, 8, 512], BF16)
    const_b = Buf("const")
    lb_t = k.sb("lb", [128, 2, 8], F32)
    c1_t = k.sb("c1", [128, 2, 8], F32)
    esk_t = k.sb("esk", [128, 32], F32)
    S_t = k.sb("S", [128, 16, 128], F32)
    S_b = [Buf("S%d" % i) for i in range(16)]
    ktc_t = k.sb("ktc", [128, 2, 4, 128], BF16)
    vc_t = k.sb("vc", [128, 2, 256], BF16)
    kvc_b = [Buf("kvc%d" % i) for i in range(2)]
    halo_t = k.sb("halo", [128, DEPTH, 2, 44, 2], F32)
    halo_b = [[[Buf("halo%d_%d_%d" % (l, q, j)) for j in range(44)] for q in range(2)] for l in range(DEPTH)]
    sq_t = [k.sb("sq%d" % i, [128, ST], BF16) for i in range(2)]
    sq_b = [Buf("sq%d" % i) for i in range(2)]
    rstd_t = k.sb("rstd", [128, ST], F32)
    rstd_b = Buf("rstd")
    TMPW = 5200
    tmp_t = k.sb("tmp", [128, TMPW], F32)
    tmp_live = []

    pb_t = [k.ps("pb%d" % i, [128, 512], F32) for i in range(8)]
    pb_b = [Buf("pb%d" % i, excl=True) for i in range(8)]

    class TmpAlloc:
        def __init__(self):
            self.deps = merge_deps(tmp_live)
            del tmp_live[:]
            self.off = 0

        def get(self, name, shape_free, dt):
            n = int(np.prod(shape_free))
            words = n if dt == F32 else (n + 1) // 2
            assert self.off + words <= TMPW, (name, self.off, words)
            ap = tmp_t[:, self.off:self.off + words]
            if dt != F32:
                ap = ap.bitcast(dt)[:, 0:n]
            self.off += words
            b = Buf(name, self.deps)
            tmp_live.append(b)
            return ap, b

    TAIL0 = 5120
    tail_live = [[], []]

    class TailAlloc:
        def __init__(self, i):
            self.i = i
            self.deps = merge_deps(tail_live[i] + [w_b[i]])
            del tail_live[i][:]
            self.off = TAIL0

        def get(self, name, shape_free, dt):
            n = int(np.prod(shape_free))
            el = n if dt == BF16 else 2 * n
            assert self.off + el <= WSLOT, (name, self.off, el)
            ap = w_t[self.i][:, self.off:self.off + el]
            if dt != BF16:
                ap = ap.bitcast(dt)
            self.off += el
            b = Buf(name, self.deps)
            tail_live[self.i].append(b)
            return ap, b

    def pcol(j):
        return pc_t[:, j:j + 1]

    st_pc = k.stream("s_pc", 16)
    k.dma(sp, st_pc, pc_t[:], pc_d, writes=[pc_b])
    k.op(pool, lambda e: e.memset(ones_t[:], 1.0), writes=[const_b])
    k.op(pool, lambda e: e.memset(idb_t[:], 1.0), writes=[const_b])
    k.op(pool, lambda e: e.affine_select(out=idb_t[:], in_=idb_t[:], pattern=[[-1, 128]], compare_op=ALU.is_equal,
                                         fill=0.0, base=0, channel_multiplier=1), reads=[const_b], writes=[const_b])
    k.op(pool, lambda e: e.memset(idf_t[:], 1.0), writes=[const_b])
    k.op(pool, lambda e: e.affine_select(out=idf_t[:], in_=idf_t[:], pattern=[[-1, 128]], compare_op=ALU.is_equal,
                                         fill=0.0, base=0, channel_multiplier=1), reads=[const_b], writes=[const_b])
    k.op(pool, lambda e: e.memset(cm4_t[:], 1.0), writes=[const_b])
    for blk in range(4):
        k.op(pool, lambda e, blk=blk: e.affine_select(out=cm4_t[:, blk * 128:(blk + 1) * 128],
                                                      in_=cm4_t[:, blk * 128:(blk + 1) * 128],
                                                      pattern=[[1, 128]], compare_op=ALU.is_ge, fill=0.0, base=0,
                                                      channel_multiplier=-1), reads=[const_b], writes=[const_b])
    k.op(pool, lambda e: e.memset(halo_t[:].rearrange("p a q b c -> p (a q b c)"), 0.0), writes=[b for h1 in halo_b for h2 in h1 for b in h2])
    k.op(pool, lambda e: e.memset(S_t[:].rearrange("p a b -> p (a b)"), 0.0), writes=S_b)
    k.op(pool, lambda e: e.memset(ktc_t[:].rearrange("p a b c -> p (a b c)"), 0.0), writes=kvc_b)
    k.op(pool, lambda e: e.memset(vc_t[:].rearrange("p a b -> p (a b)"), 0.0), writes=kvc_b)

    has_attn = any(p[0] == "attn" for p in plan)
    has_hgrn = any(p[0] == "hgrn" for p in plan)

    if has_attn:
        ta = TmpAlloc()
        d1, d1_b = ta.get("d1", [128], F32)
        dpos, dpos_b = ta.get("dpos", [128], F32)
        dneg, dneg_b = ta.get("dneg", [128], F32)
        mtmp, mtmp_b = ta.get("mtmp", [256], F32)
        k.op(pool, lambda e: e.iota(d1, pattern=[[1, 128]], base=0, channel_multiplier=-1,
                                    allow_small_or_imprecise_dtypes=True), writes=[d1_b])
        k.op(dve, lambda e: e.tensor_scalar_max(out=dpos, in0=d1, scalar1=0.0), reads=[d1_b], writes=[dpos_b])
        k.op(dve, lambda e: e.tensor_scalar_min(out=dneg, in0=d1, scalar1=0.0), reads=[d1_b], writes=[dneg_b])
        for gp in range(8):
            g, p = gp // 2, gp % 2
            for hh in range(2):
                hq = 4 * g + p + 2 * hh
                slope = float(2.0 ** (-8.0 * (hq + 1) / NQH))
                k.op(act, lambda e, slope=slope: e.activation(out=mtmp[:, 0:128], in_=dneg, func=AF.Exp,
                                                              bias=-slope * 128.0, scale=-slope),
                     reads=[dneg_b], writes=[mtmp_b])
                k.op(act, lambda e, slope=slope: e.activation(out=mtmp[:, 128:256], in_=dpos, func=AF.Exp,
                                                              scale=-slope),
                     reads=[dpos_b], writes=[mtmp_b])
                k.op(pool, lambda e, gp=gp, hh=hh: e.affine_select(
                    out=mall_t[:, gp, hh * 128:(hh + 1) * 128], in_=mtmp[:, 0:128], pattern=[[-1, 128]],
                    compare_op=ALU.is_gt, fill=0.0, base=0, channel_multiplier=1),
                    reads=[mtmp_b], writes=[const_b])
                k.op(pool, lambda e, gp=gp, hh=hh: e.affine_select(
                    out=mall_t[:, gp, 256 + hh * 128:256 + (hh + 1) * 128], in_=mtmp[:, 128:256],
                    pattern=[[1, 128]], compare_op=ALU.is_ge, fill=0.0, base=0, channel_multiplier=-1),
                    reads=[mtmp_b], writes=[const_b])
        k.op(act, lambda e: e.activation(out=esk_t[:, 0:16], in_=pc_t[:, SK0:SK0 + 16], func=AF.Exp),
             reads=[pc_b], writes=[const_b])

    if has_hgrn:
        ta = TmpAlloc()
        lg = pc_t[:, LG0:LG0 + 32].rearrange("p (h l) -> p h l", l=4)
        mx, mx_b = ta.get("mx", [8], F32)
        ex, ex_b = ta.get("ex", [32], F32)
        sm, sm_b = ta.get("sm", [8], F32)
        ex3 = ex.rearrange("p (h l) -> p h l", l=4)
        k.op(dve, lambda e: e.tensor_reduce(out=mx, in_=lg, axis=AX.X, op=ALU.max), reads=[pc_b], writes=[mx_b])
        for li in range(4):
            k.op(dve, lambda e, li=li: e.tensor_tensor(out=ex3[:, :, li], in0=lg[:, :, li], in1=mx, op=ALU.subtract),
                 reads=[pc_b, mx_b, ex_b], writes=[ex_b])
        k.op(act, lambda e: e.activation(out=ex, in_=ex, func=AF.Exp), reads=[ex_b], writes=[ex_b])
        k.op(dve, lambda e: e.tensor_reduce(out=sm, in_=ex3, axis=AX.X, op=ALU.add), reads=[ex_b], writes=[sm_b])
        k.op(dve, lambda e: e.reciprocal(out=sm, in_=sm), reads=[sm_b], writes=[sm_b])
        for li in range(4):
            k.op(dve, lambda e, li=li: e.tensor_tensor(out=ex3[:, :, li], in0=ex3[:, :, li], in1=sm, op=ALU.mult),
                 reads=[ex_b, sm_b], writes=[ex_b])
        k.op(dve, lambda e: e.tensor_copy(out=lb_t[:, 0, :], in_=ex3[:, :, 1]), reads=[ex_b], writes=[const_b])
        k.op(dve, lambda e: e.tensor_tensor(out=lb_t[:, 1, :], in0=ex3[:, :, 1], in1=ex3[:, :, 2], op=ALU.add),
             reads=[ex_b], writes=[const_b])
        k.op(dve, lambda e: e.tensor_tensor(out=lb_t[:, 1, :], in0=lb_t[:, 1, :], in1=ex3[:, :, 3], op=ALU.add),
             reads=[ex_b, const_b], writes=[const_b])
        k.op(act, lambda e: e.activation(out=c1_t[:].rearrange("p a b -> p (a b)"),
                                         in_=lb_t[:].rearrange("p a b -> p (a b)"), func=AF.Ln, bias=1.0, scale=-1.0),
             reads=[const_b], writes=[const_b])

    def next_slot():
        i = wcount[0] % 2
        wcount[0] += 1
        return i

    evac_rr = [0]

    def evac_copy(out, in_, reads, writes, force_act=False):
        evac_rr[0] += 1
        if force_act or evac_rr[0] % 2 == 0:
            k.op(act, lambda e: e.copy(out=out, in_=in_), reads=reads, writes=writes)
        else:
            k.op(dve, lambda e: e.tensor_copy(out=out, in_=in_), reads=reads, writes=writes)

    def rmsnorm_sub(s, gain_idx, nbank, in_place=False):
        sl = slice(s * ST, (s + 1) * ST)
        for c in range(KC):
            i = c % 2
            k.op(act, lambda e, c=c, i=i: e.activation(out=sq_t[i][:], in_=h_t[:, c, sl], func=AF.Square),
                 reads=[h_b[c][s]], writes=[sq_b[i]])
            k.op(pe, lambda e, c=c, i=i: e.matmul(pb_t[nbank][:], lhsT=ones_t[:], rhs=sq_t[i][:],
                                                   start=(c == 0), stop=(c == KC - 1)),
                 reads=[sq_b[i], const_b], writes=[pb_b[nbank]], inc=True)
        k.op(act, lambda e: e.activation(out=rstd_t[:], in_=pb_t[nbank][:], func=AF.Ln, bias=EPS, scale=1.0 / D),
             reads=[pb_b[nbank]], writes=[rstd_b])
        k.op(act, lambda e: e.activation(out=rstd_t[:], in_=rstd_t[:], func=AF.Exp, scale=-0.5),
             reads=[rstd_b], writes=[rstd_b])
        for c in range(KC):
            if in_place:
                k.op(dve, lambda e, c=c: e.scalar_tensor_tensor(out=h_t[:, c, sl], in0=h_t[:, c, sl],
                                                                scalar=pcol(GAIN0 + gain_idx * 8 + c),
                                                                in1=rstd_t[:], op0=ALU.mult, op1=ALU.mult),
                     reads=[h_b[c][s], rstd_b, pc_b], writes=[h_b[c][s]])
            else:
                k.op(dve, lambda e, c=c: e.scalar_tensor_tensor(out=hn_t[:, c, sl], in0=h_t[:, c, sl],
                                                                scalar=pcol(GAIN0 + gain_idx * 8 + c),
                                                                in1=rstd_t[:], op0=ALU.mult, op1=ALU.mult),
                     reads=[h_b[c][s], rstd_b, pc_b], writes=[hn_b[s]])

    def load_half(hf):
        ta = TmpAlloc()
        xin = [ta.get("xin%d" % i, [D], F32) for i in range(2)]
        xst = [k.stream("s_xin%d_%d" % (hf, i), 16) for i in range(2)]
        for blk in range(T // 128):
            i = blk % 2
            t0 = hf * T + blk * 128
            s = blk // 4
            col = blk * 128
            k.dma(sp, xst[i], xin[i][0], x_d[t0:t0 + 128, :], writes=[xin[i][1]])
            for half in range(2):
                bank = half
                for cc in range(4):
                    c = half * 4 + cc
                    k.op(pe, lambda e, c=c, cc=cc, bank=bank, i=i: e.transpose(
                        pb_t[bank][:, cc * 128:(cc + 1) * 128], xin[i][0][:, c * 128:(c + 1) * 128], idf_t[:]),
                        reads=[xin[i][1], const_b], writes=[pb_b[bank]], inc=(cc == 3))
                evac_copy(h_t[:, half * 4:half * 4 + 4, col:col + 128],
                          pb_t[bank][:].rearrange("p (c t) -> p c t", c=4),
                          reads=[pb_b[bank]], writes=[h_b[half * 4 + cc][s] for cc in range(4)])

    def store_half(hf):
        ta = TmpAlloc()
        yo = [ta.get("yo%d" % i, [D], F32) for i in range(2)]
        yst = [k.stream("s_yo%d_%d" % (hf, i), 16) for i in range(2)]
        for s in range(NSUB):
            if final_norm:
                rmsnorm_sub(s, 8, 6, in_place=True)
            for bb in range(4):
                blk = s * 4 + bb
                i = blk % 2
                t0 = hf * T + blk * 128
                col = blk * 128
                for half in range(2):
                    bank = half
                    for cc in range(4):
                        c = half * 4 + cc
                        k.op(pe, lambda e, c=c, cc=cc, bank=bank: e.transpose(
                            pb_t[bank][:, cc * 128:(cc + 1) * 128], h_t[:, c, col:col + 128], idf_t[:]),
                            reads=[h_b[c][s], const_b], writes=[pb_b[bank]], inc=(cc == 3))
                    evac_copy(yo[i][0][:, half * 512:(half + 1) * 512], pb_t[bank][:],
                              reads=[pb_b[bank]], writes=[yo[i][1]])
                k.dma(sp, yst[i], out_d[t0:t0 + 128, :], yo[i][0], reads=[yo[i][1]])
        return yst

    def ffn_phase(l, hf):
        for s in range(NSUB):
            rmsnorm_sub(s, 4 + l, 6)
        ta = TmpAlloc()
        cg = [ta.get("cg%d" % i, [ST], F32) for i in range(2)]
        cv = [ta.get("cv%d" % i, [ST], F32) for i in range(2)]
        sg = [ta.get("sg%d" % i, [ST], BF16) for i in range(2)]
        actb = [ta.get("act%d" % i, [GMAX, ST], BF16) for i in range(2)]
        items = []
        gs_list = []
        for g in range(NG):
            n = G_SIZES[g]
            slot = next_slot()
            for s in range(NSUB):
                gs_list.append((g, s, slot))
                for jj in range(n):
                    items.append((g, s, jj, slot, len(gs_list) - 1))

        def wup(slot, kc, col0):
            base = kc * 2 * GMAX * 128 + col0
            return w_t[slot][:, base:base + 128]

        def wdn(slot, jj, dc):
            base = KC * 2 * GMAX * 128 + jj * D + dc * 128
            return w_t[slot][:, base:base + 128]

        def stageA(it, idx):
            g, s, jj, slot, gsi = it
            par = idx % 2
            sl = slice(s * ST, (s + 1) * ST)
            for which in range(2):
                bank = par * 2 + which
                col0 = which * GMAX * 128 + jj * 128
                for kc in range(KC):
                    k.op(pe, lambda e, kc=kc, bank=bank, col0=col0: e.matmul(
                        pb_t[bank][:], lhsT=wup(slot, kc, col0), rhs=hn_t[:, kc, sl],
                        start=(kc == 0), stop=(kc == KC - 1)),
                        reads=[w_b[slot], hn_b[s]], writes=[pb_b[bank]], inc=(kc == KC - 1))

        def stageB(it, idx):
            g, s, jj, slot, gsi = it
            par = idx % 2
            first_tok = (hf == 0 and s == 0)
            sg_ = hf * NSUB + s
            hq_r, hq_w = (sg_ - 1) % 2, sg_ % 2
            info = []
            for which in range(2):
                bank = par * 2 + which
                P = pb_t[bank]
                C, Cb = (cg, cv)[which][par]
                jc = which * NFC + G_OFF[g] + jj
                cb = CONV0 + (l * 44 + jc) * 4
                info.append((bank, P, C, Cb, jc, cb))
                k.op(act, lambda e, P=P, C=C, cb=cb: e.activation(out=C, in_=P[:], func=AF.Identity,
                                                                  bias=pcol(cb + 3), scale=pcol(cb + 2)),
                     reads=[pb_b[bank], pc_b], writes=[Cb])
                k.op(act, lambda e, P=P, jc=jc: e.copy(out=halo_t[:, l, hq_w, jc, :], in_=P[:, ST - 2:ST]),
                     reads=[pb_b[bank]], writes=[halo_b[l][hq_w][jc]])
            for bank, P, C, Cb, jc, cb in info:
                hl = halo_t[:, l, hq_r, jc, :]
                hb_ = halo_b[l][hq_r][jc]
                k.op(dve, lambda e, P=P, C=C, cb=cb: e.scalar_tensor_tensor(
                    out=C[:, 1:ST], in0=P[:, 0:ST - 1], scalar=pcol(cb + 1), in1=C[:, 1:ST],
                    op0=ALU.mult, op1=ALU.add), reads=[pb_b[bank], Cb, pc_b], writes=[Cb])
                k.op(dve, lambda e, P=P, C=C, cb=cb: e.scalar_tensor_tensor(
                    out=C[:, 2:ST], in0=P[:, 0:ST - 2], scalar=pcol(cb + 0), in1=C[:, 2:ST],
                    op0=ALU.mult, op1=ALU.add), reads=[pb_b[bank], Cb, pc_b], writes=[Cb])
                if not first_tok:
                    k.op(dve, lambda e, C=C, cb=cb, hl=hl: e.scalar_tensor_tensor(
                        out=C[:, 0:1], in0=hl[:, 1:2], scalar=pcol(cb + 1), in1=C[:, 0:1],
                        op0=ALU.mult, op1=ALU.add), reads=[hb_, Cb, pc_b], writes=[Cb])
                    k.op(dve, lambda e, C=C, cb=cb, hl=hl: e.scalar_tensor_tensor(
                        out=C[:, 0:2], in0=hl[:, 0:2], scalar=pcol(cb + 0), in1=C[:, 0:2],
                        op0=ALU.mult, op1=ALU.add), reads=[hb_, Cb, pc_b], writes=[Cb])

        def stageC(it, idx):
            g, s, jj, slot, gsi = it
            par = idx % 2
            Cg, Cgb = cg[par]
            Cv, Cvb = cv[par]
            Sg, Sgb = sg[par]
            A, Ab = actb[gsi % 2]
            k.op(act, lambda e: e.activation(out=Sg, in_=Cg, func=AF.Silu), reads=[Cgb], writes=[Sgb])
            k.op(dve, lambda e: e.tensor_tensor(out=A[:, jj * ST:(jj + 1) * ST], in0=Sg, in1=Cv, op=ALU.mult),
                 reads=[Sgb, Cvb], writes=[Ab])

        def D_mm(gsi, dc):
            g, s, slot = gs_list[gsi]
            n = G_SIZES[g]
            A, Ab = actb[gsi % 2]
            bank = 4 + dc % 2
            for jj in range(n):
                k.op(pe, lambda e, jj=jj: e.matmul(
                    pb_t[bank][:], lhsT=wdn(slot, jj, dc), rhs=A[:, jj * ST:(jj + 1) * ST],
                    start=(jj == 0), stop=(jj == n - 1)),
                    reads=[w_b[slot], Ab], writes=[pb_b[bank]], inc=(jj == n - 1))

        def D_add(gsi, dc):
            g, s, slot = gs_list[gsi]
            bank = 4 + dc % 2
            sl = slice(s * ST, (s + 1) * ST)
            k.op(dve, lambda e: e.tensor_tensor(out=h_t[:, dc, sl], in0=pb_t[bank][:],
                                                in1=h_t[:, dc, sl], op=ALU.add),
                 reads=[pb_b[bank], h_b[dc][s]], writes=[h_b[dc][s]])

        def load_w(g, slot):
            wt, wb, wst = w_t[slot], w_b[slot], w_st[slot]
            k.dma(pool, wst, wt[:, 0:KC * 2 * GMAX * 128], fup_d[l, g], writes=[wb] + tail_live[slot])
            k.dma(pool, wst, wt[:, KC * 2 * GMAX * 128:KC * 2 * GMAX * 128 + GMAX * D], fdn_d[l, g], writes=[wb] + tail_live[slot])

        nI = len(items)
        pend = []
        step = 0
        while step < nI + 2 or pend:
            tasks = pend[:2]
            del pend[:2]
            for tk in tasks:
                D_mm(*tk)
            if step < nI:
                g_, s_, jj_, slot_, _ = items[step]
                if s_ == 0 and jj_ == 0:
                    load_w(g_, slot_)
                stageA(items[step], step)
            if 0 <= step - 1 < nI:
                stageB(items[step - 1], step - 1)
            if 0 <= step - 2 < nI:
                it = items[step - 2]
                stageC(it, step - 2)
                g, s, jj, slot, gsi = it
                if jj == G_SIZES[g] - 1:
                    pend.extend((gsi, dc) for dc in range(KC))
            for tk in tasks:
                D_add(*tk)
            step += 1

    def attn_phase(l, hf):
        idx = l // 2
        sA = next_slot()
        k.dma(pool, w_st[sA], w_t[sA][:, 0:KC * 1792], awin_d[idx], writes=[w_b[sA]] + tail_live[sA])
        sB = next_slot()
        k.dma(pool, w_st[sB], w_t[sB][:, 0:KC * D], awout_d[idx], writes=[w_b[sB]] + tail_live[sB])
        ta = TmpAlloc()
        kt, kt_b = ta.get("kt", [4, 640], BF16)
        vb, vb_b = ta.get("vb", [5, 256], BF16)
        et = [ta.get("e%d" % i, [512], BF16) for i in range(2)]
        pt = [ta.get("p%d" % i, [512], BF16) for i in range(4)]
        rd = [ta.get("rd%d" % i, [256], F32) for i in range(2)]
        kt3 = kt.rearrange("p (g t) -> p g t", g=4)
        vb3 = vb.rearrange("p (b d) -> p b d", b=5)
        hnS_b, qt_b, ot_b = hn_b[0], hn_b[1], hn_b[2]

        def hnS(kc, a, b):
            return hn_t[:, kc, a:b]

        QT0, OT0 = ST, 2 * ST
        k.op(dve, lambda e: e.tensor_copy(out=kt3[:, :, 0:128], in_=ktc_t[:, idx]), reads=[kvc_b[idx]], writes=[kt_b])
        k.op(dve, lambda e: e.tensor_copy(out=vb3[:, 0, :], in_=vc_t[:, idx]), reads=[kvc_b[idx]], writes=[vb_b])

        def win(kc, c0, n):
            base = kc * 1792 + c0
            return w_t[sA][:, base:base + n]

        def wout(c, dc):
            base = c * D + dc * 128
            return w_t[sB][:, base:base + 128]

        pcnt = [0]
        for s in range(NSUB):
            sl = slice(s * ST, (s + 1) * ST)
            for c in range(KC):
                i = c % 2
                k.op(act, lambda e, c=c, i=i: e.activation(out=sq_t[i][:], in_=h_t[:, c, sl], func=AF.Square),
                     reads=[h_b[c][s]], writes=[sq_b[i]])
                k.op(pe, lambda e, c=c, i=i: e.matmul(pb_t[6][:], lhsT=ones_t[:], rhs=sq_t[i][:],
                                                       start=(c == 0), stop=(c == KC - 1)),
                     reads=[sq_b[i], const_b], writes=[pb_b[6]], inc=True)
            k.op(act, lambda e: e.activation(out=rstd_t[:], in_=pb_t[6][:], func=AF.Ln, bias=EPS, scale=1.0 / D),
                 reads=[pb_b[6]], writes=[rstd_b])
            k.op(act, lambda e: e.activation(out=rstd_t[:], in_=rstd_t[:], func=AF.Exp, scale=-0.5),
                 reads=[rstd_b], writes=[rstd_b])
            for c in range(KC):
                k.op(dve, lambda e, c=c: e.scalar_tensor_tensor(out=hn_t[:, c, 0:ST], in0=h_t[:, c, sl],
                                                                scalar=pcol(GAIN0 + l * 8 + c),
                                                                in1=rstd_t[:], op0=ALU.mult, op1=ALU.mult),
                     reads=[h_b[c][s], rstd_b, pc_b], writes=[hnS_b])
            for c in range(KC + 4):
                bank = pcnt[0] % 2
                pcnt[0] += 1
                for kc in range(KC):
                    k.op(pe, lambda e, c=c, kc=kc, bank=bank: e.matmul(
                        pb_t[bank][:], lhsT=win(kc, c * 128, 128), rhs=hnS(kc, 0, ST),
                        start=(kc == 0), stop=(kc == KC - 1)),
                        reads=[w_b[sA], hnS_b], writes=[pb_b[bank]], inc=(kc == KC - 1))
                if c < KC:
                    evac_copy(hn_t[:, c, QT0:QT0 + ST], pb_t[bank][:], reads=[pb_b[bank]], writes=[qt_b], force_act=True)
                else:
                    evac_copy(kt3[:, c - KC, 128:640], pb_t[bank][:], reads=[pb_b[bank]], writes=[kt_b], force_act=True)
            for pair in range(2):
                bank = pcnt[0] % 2
                pcnt[0] += 1
                for bb in range(2):
                    blk = pair * 2 + bb
                    for kc in range(KC):
                        k.op(pe, lambda e, kc=kc, blk=blk, bb=bb, bank=bank: e.matmul(
                            pb_t[bank][:, bb * 256:(bb + 1) * 256], lhsT=hnS(kc, blk * 128, (blk + 1) * 128),
                            rhs=win(kc, 1536, 256), start=(kc == 0), stop=(kc == KC - 1)),
                            reads=[w_b[sA], hnS_b], writes=[pb_b[bank]], inc=(kc == KC - 1))
                evac_copy(vb[:, (1 + pair * 2) * 256:(3 + pair * 2) * 256], pb_t[bank][:],
                          reads=[pb_b[bank]], writes=[vb_b], force_act=True)
            steps = [(bq, g) for bq in range(4) for g in range(4)]

            def scores(bq, g):
                nglob = hf * (T // 128) + s * 4 + bq
                has_prev = nglob > 0
                for p in range(2):
                    gp = g * 2 + p
                    bank = 2 + gp % 2
                    X = pb_t[bank]
                    rows = slice(p * 64, (p + 1) * 64)
                    qrhs = hn_t[rows, 2 * g:2 * g + 2, QT0 + bq * 128:QT0 + (bq + 1) * 128]
                    if has_prev:
                        k.op(pe, lambda e, X=X, rows=rows, qrhs=qrhs: e.matmul(
                            X[:, 0:256].rearrange("p (h q) -> p h q", h=2),
                            lhsT=kt3[rows, g, bq * 128:(bq + 1) * 128], rhs=qrhs, start=True, stop=True),
                            reads=[kt_b, qt_b], writes=[pb_b[bank]], inc=False)
                    k.op(pe, lambda e, X=X, rows=rows, qrhs=qrhs: e.matmul(
                        X[:, 256:512].rearrange("p (h q) -> p h q", h=2),
                        lhsT=kt3[rows, g, (bq + 1) * 128:(bq + 2) * 128], rhs=qrhs, start=True, stop=True),
                        reads=[kt_b, qt_b], writes=[pb_b[bank]], inc=True)
                    E, Eb = et[gp % 2]
                    Pm, Pb = pt[gp % 4]
                    lo = 0 if has_prev else 256
                    k.op(act, lambda e, X=X, E=E, lo=lo: e.activation(out=E[:, lo:512], in_=X[:, lo:512],
                                                                       func=AF.Exp, scale=HD ** -0.5),
                         reads=[pb_b[bank]], writes=[Eb])
                    k.op(dve, lambda e, E=E, Pm=Pm, gp=gp, lo=lo: e.tensor_tensor(
                        out=Pm[:, lo:512], in0=E[:, lo:512], in1=mall_t[:, gp, lo:512], op=ALU.mult),
                        reads=[Eb, const_b], writes=[Pb])

            def pv(bq, g):
                nglob = hf * (T // 128) + s * 4 + bq
                has_prev = nglob > 0
                bank = (4, 5, 7)[(bq * 4 + g) % 3]
                Y = pb_t[bank]
                vcols = slice(g * 64, (g + 1) * 64)
                for p in range(2):
                    gp = g * 2 + p
                    Pm, Pb = pt[gp % 4]
                    rows = slice(p * 64, (p + 1) * 64)
                    if has_prev:
                        k.op(pe, lambda e, rows=rows, Pm=Pm: e.matmul(
                            Y[rows, 0:256], lhsT=vb3[:, bq, vcols], rhs=Pm[:, 0:256], start=True, stop=False),
                            reads=[vb_b, Pb], writes=[pb_b[bank]], inc=False)
                    k.op(pe, lambda e, rows=rows, Pm=Pm: e.matmul(
                        Y[rows, 0:256], lhsT=vb3[:, bq + 1, vcols], rhs=Pm[:, 256:512],
                        start=(not has_prev), stop=True),
                        reads=[vb_b, Pb], writes=[pb_b[bank]], inc=False)
                    if has_prev:
                        k.op(pe, lambda e, rows=rows, Pm=Pm: e.matmul(
                            Y[rows, 256:512], lhsT=ones_t[:, 0:64], rhs=Pm[:, 0:256], start=True, stop=False),
                            reads=[const_b, Pb], writes=[pb_b[bank]], inc=False)
                    k.op(pe, lambda e, rows=rows, Pm=Pm: e.matmul(
                        Y[rows, 256:512], lhsT=ones_t[:, 0:64], rhs=Pm[:, 256:512],
                        start=(not has_prev), stop=True),
                        reads=[const_b, Pb], writes=[pb_b[bank]], inc=(p == 1))
                R, Rb = rd[(bq * 4 + g) % 2]
                for hh in range(2):
                    col = idx * 8 + g * 2 + hh
                    k.op(act, lambda e, hh=hh, col=col: e.activation(
                        out=R[:, hh * 128:(hh + 1) * 128], in_=Y[:, 256 + hh * 128:256 + (hh + 1) * 128],
                        func=AF.Ln, bias=esk_t[:, col:col + 1], scale=1.0),
                        reads=[pb_b[bank], const_b], writes=[Rb])
                k.op(act, lambda e: e.activation(out=R, in_=R, func=AF.Exp, scale=-1.0), reads=[Rb], writes=[Rb])
                k.op(dve, lambda e: e.tensor_tensor(
                    out=hn_t[:, 2 * g:2 * g + 2, OT0 + bq * 128:OT0 + (bq + 1) * 128],
                    in0=Y[:, 0:256].rearrange("p (h q) -> p h q", h=2),
                    in1=R.rearrange("p (h q) -> p h q", h=2), op=ALU.mult),
                    reads=[pb_b[bank], Rb], writes=[ot_b])

            scores(*steps[0])
            for si in range(len(steps)):
                if si + 1 < len(steps):
                    scores(*steps[si + 1])
                pv(*steps[si])
            for dc in range(KC):
                bank = pcnt[0] % 2
                pcnt[0] += 1
                for c in range(KC):
                    k.op(pe, lambda e, c=c, dc=dc, bank=bank: e.matmul(
                        pb_t[bank][:], lhsT=wout(c, dc), rhs=hn_t[:, c, OT0:OT0 + ST],
                        start=(c == 0), stop=(c == KC - 1)),
                        reads=[w_b[sB], ot_b], writes=[pb_b[bank]], inc=(c == KC - 1))
                k.op(dve, lambda e, dc=dc, bank=bank: e.tensor_tensor(out=h_t[:, dc, sl], in0=pb_t[bank][:],
                                                                       in1=h_t[:, dc, sl], op=ALU.add),
                     reads=[pb_b[bank], h_b[dc][s]], writes=[h_b[dc][s]])
            k.op(dve, lambda e: e.tensor_copy(out=kt3[:, :, 0:128], in_=kt3[:, :, 512:640]),
                 reads=[kt_b], writes=[kt_b])
            k.op(dve, lambda e: e.tensor_copy(out=vb3[:, 0, :], in_=vb3[:, 4, :]), reads=[vb_b], writes=[vb_b])
        k.op(dve, lambda e: e.tensor_copy(out=ktc_t[:, idx], in_=kt3[:, :, 0:128]), reads=[kt_b], writes=[kvc_b[idx]])
        k.op(dve, lambda e: e.tensor_copy(out=vc_t[:, idx], in_=vb3[:, 0, :]), reads=[vb_b], writes=[kvc_b[idx]])

    def hgrn_phase(l, hf):
        idx = (l - 1) // 2
        for s in range(NSUB):
            rmsnorm_sub(s, l, 4)
        ta = TmpAlloc()
        tl = [TailAlloc(0), TailAlloc(1)]
        Te, Te_b = ta.get("Te", [ST], F32)
        TL1, TL1_b = ta.get("TL1", [ST], F32)
        TL2, TL2_b = ta.get("TL2", [ST], F32)
        Tb, Tb_b = ta.get("Tb", [ST], F32)
        Tt, Tt_b = ta.get("Tt", [ST], F32)
        KD, KD_b = ta.get("KD", [ST], BF16)
        cbm, cbm_b = ta.get("cbm", [4], F32)
        nbm, nbm_b = ta.get("nbm", [4], F32)
        HB = []
        for i in range(2):
            d = {}
            for nm in ("QD", "KDT", "VT", "AT"):
                d[nm] = tl[i].get(nm + str(i), [ST], BF16)
            d["Tgs"] = tl[i].get("Tgs" + str(i), [ST], F32)
            for nm in ("ebl", "ebm", "eblm"):
                d[nm] = ta.get(nm + str(i), [4], F32)
            HB.append(d)
        Tqe, Tqe_b = tl[0].get("Tqe", [ST], F32)
        TqL, TqL_b = tl[0].get("TqL", [ST], F32)
        Tqd, Tqd_b = tl[0].get("Tqd", [ST], F32)
        Tqq, Tqq_b = tl[1].get("Tqq", [ST], F32)
        Tgg, Tgg_b = tl[1].get("Tgg", [ST], F32)
        Tge, Tge_b = tl[1].get("Tge", [ST], F32)
        TgL, TgL_b = tl[1].get("TgL", [ST], F32)
        To, To_b = tl[0].get("To", [ST], F32)
        OSQ, OSQ_b = tl[0].get("OSQ", [ST], BF16)
        OG, OG_b = tl[0].get("OG", [ST], BF16)
        Tr, Tr_b = tl[1].get("Tr", [ST], F32)
        Sp, Sp_b = tl[1].get("Sp", [128], F32)
        Sbf, Sbf_b = tl[1].get("Sbf", [128], BF16)
        Bf, Bq, Bv, Bg, Ba, Bt, Bo, Bp = range(8)
        BtB = pb_t[Bt][:].bitcast(BF16)

        units = []
        for head in range(8):
            slot = next_slot()
            for s in range(NSUB):
                units.append((head, s, slot))

        def stageP(u, ui):
            head, s, slot = u
            wt = w_t[slot]
            if s == 0:
                k.dma(pool, w_st[slot], wt[:, 0:KC * 512], hwin_d[idx, head], writes=[w_b[slot]])
                k.dma(pool, w_st[slot], wt[:, KC * 512:KC * 512 + D], hwout_d[idx, head], writes=[w_b[slot]])
            hb = HB[ui % 2]
            QD, QD_b = hb["QD"]
            KDT, KDT_b = hb["KDT"]
            VT, VT_b = hb["VT"]
            AT, AT_b = hb["AT"]
            Tgs, Tgs_b = hb["Tgs"]
            ebl, ebl_b = hb["ebl"]
            ebm, ebm_b = hb["ebm"]
            eblm, eblm_b = hb["eblm"]
            lbc = lb_t[:, idx, head:head + 1]
            c1c = c1_t[:, idx, head:head + 1]
            sl = slice(s * ST, (s + 1) * ST)

            def wi(kc, which):
                base = kc * 512 + which * 128
                return wt[:, base:base + 128]

            for which, bank in ((1, Bf), (0, Bq)):
                for kc in range(KC):
                    k.op(pe, lambda e, kc=kc, which=which, bank=bank: e.matmul(
                        pb_t[bank][:], lhsT=wi(kc, which), rhs=hn_t[:, kc, sl],
                        start=(kc == 0), stop=(kc == KC - 1)),
                        reads=[w_b[slot], hn_b[s]], writes=[pb_b[bank]], inc=(kc == KC - 1))
                if which == 1:
                    yield
            yield
            for blk in range(4):
                for kc in range(KC):
                    k.op(pe, lambda e, kc=kc, blk=blk: e.matmul(
                        pb_t[Bv][:, blk * 128:(blk + 1) * 128],
                        lhsT=hn_t[:, kc, s * ST + blk * 128:s * ST + (blk + 1) * 128], rhs=wi(kc, 2),
                        start=(kc == 0), stop=(kc == KC - 1)),
                        reads=[w_b[slot], hn_b[s]], writes=[pb_b[Bv]], inc=(kc == KC - 1))
                if blk == 1:
                    yield
            yield
            for kc in range(KC):
                k.op(pe, lambda e, kc=kc: e.matmul(
                    pb_t[Bg][:], lhsT=wi(kc, 3), rhs=hn_t[:, kc, sl],
                    start=(kc == 0), stop=(kc == KC - 1)),
                    reads=[w_b[slot], hn_b[s]], writes=[pb_b[Bg]], inc=(kc == KC - 1))
        def stageC(u, ui):
            head, s, slot = u
            hb = HB[ui % 2]
            QD, QD_b = hb["QD"]
            KDT, KDT_b = hb["KDT"]
            VT, VT_b = hb["VT"]
            AT, AT_b = hb["AT"]
            Tgs, Tgs_b = hb["Tgs"]
            ebl, ebl_b = hb["ebl"]
            ebm, ebm_b = hb["ebm"]
            eblm, eblm_b = hb["eblm"]
            lbc = lb_t[:, idx, head:head + 1]
            c1c = c1_t[:, idx, head:head + 1]
            k.op(act, lambda e: e.copy(out=VT, in_=pb_t[Bv][:]), reads=[pb_b[Bv]], writes=[VT_b])
            k.op(act, lambda e: e.activation(out=Te, in_=pb_t[Bf][:], func=AF.Exp, scale=-1.0),
                 reads=[pb_b[Bf]], writes=[Te_b])
            k.op(act, lambda e: e.activation(out=Tt, in_=pb_t[Bf][:], func=AF.Exp, scale=1.0),
                 reads=[pb_b[Bf]], writes=[Tt_b])
            k.op(act, lambda e: e.activation(out=Tqe, in_=pb_t[Bq][:], func=AF.Exp, scale=-1.0),
                 reads=[pb_b[Bq]], writes=[Tqe_b])
            k.op(dve, lambda e: e.tensor_copy(out=Tqq, in_=pb_t[Bq][:]), reads=[pb_b[Bq]], writes=[Tqq_b])
            k.op(act, lambda e: e.activation(out=Tge, in_=pb_t[Bg][:], func=AF.Exp, scale=-1.0),
                 reads=[pb_b[Bg]], writes=[Tge_b])
            k.op(dve, lambda e: e.tensor_copy(out=Tgg, in_=pb_t[Bg][:]), reads=[pb_b[Bg]], writes=[Tgg_b])
            yield
            k.op(act, lambda e: e.activation(out=TL1, in_=Te, func=AF.Ln, bias=1.0, scale=1.0),
                 reads=[Te_b], writes=[TL1_b])
            k.op(act, lambda e: e.activation(out=TL2, in_=Te, func=AF.Ln, bias=1.0, scale=lbc),
                 reads=[Te_b, const_b], writes=[TL2_b])
            k.op(act, lambda e: e.activation(out=Tt, in_=Tt, func=AF.Ln, bias=1.0, scale=1.0),
                 reads=[Tt_b], writes=[Tt_b])
            yield
            k.op(act, lambda e: e.activation(out=TqL, in_=Tqe, func=AF.Ln, bias=1.0, scale=1.0),
                 reads=[Tqe_b], writes=[TqL_b])
            for blk in range(4):
                cs = slice(blk * 128, (blk + 1) * 128)
                k.op(dve, lambda e, cs=cs: e.tensor_tensor_scan(out=Tb[:, cs], data0=TL2[:, cs], data1=TL1[:, cs],
                                                                initial=0.0, op0=ALU.add, op1=ALU.subtract),
                     reads=[TL1_b, TL2_b], writes=[Tb_b])
            yield
            k.op(act, lambda e: e.activation(out=TgL, in_=Tge, func=AF.Ln, bias=1.0, scale=1.0),
                 reads=[Tge_b], writes=[TgL_b])
            k.op(act, lambda e: e.activation(out=TgL, in_=TgL, func=AF.Exp, scale=-1.0),
                 reads=[TgL_b], writes=[TgL_b])
            k.op(dve, lambda e: e.tensor_tensor(out=Tt, in0=Tt, in1=Tb, op=ALU.add),
                 reads=[Tt_b, Tb_b], writes=[Tt_b])
            yield
            Tb4 = Tb.rearrange("p (b t) -> p b t", b=4)
            bm_v, bl_v = Tb4[:, :, 63], Tb4[:, :, 127]
            k.op(dve, lambda e: e.tensor_scalar(out=cbm, in0=bm_v, scalar1=c1c, scalar2=None, op0=ALU.add),
                 reads=[Tb_b, const_b], writes=[cbm_b])
            k.op(dve, lambda e: e.tensor_scalar(out=nbm, in0=bm_v, scalar1=-1.0, scalar2=None, op0=ALU.mult),
                 reads=[Tb_b], writes=[nbm_b])
            k.op(dve, lambda e: e.tensor_tensor(out=eblm, in0=bl_v, in1=bm_v, op=ALU.subtract),
                 reads=[Tb_b], writes=[eblm_b])
            k.op(dve, lambda e: e.tensor_tensor(out=Tqd, in0=Tb, in1=TqL, op=ALU.subtract),
                 reads=[Tb_b, TqL_b], writes=[Tqd_b])
            for blk in range(4):
                cs = slice(blk * 128, (blk + 1) * 128)
                k.op(act, lambda e, cs=cs, blk=blk: e.activation(out=KD[:, cs], in_=Tt[:, cs], func=AF.Exp,
                                                                 bias=cbm[:, blk:blk + 1], scale=-1.0),
                     reads=[Tt_b, cbm_b], writes=[KD_b])
            yield
            for blk in range(4):
                cs = slice(blk * 128, (blk + 1) * 128)
                k.op(pe, lambda e, cs=cs: e.transpose(BtB[:, cs], KD[:, cs], idb_t[:]),
                     reads=[KD_b, const_b], writes=[pb_b[Bt]], inc=(blk == 3))
            for blk in range(4):
                cs = slice(blk * 128, (blk + 1) * 128)
                k.op(act, lambda e, cs=cs, blk=blk: e.activation(out=Tqd[:, cs], in_=Tqd[:, cs], func=AF.Exp,
                                                                 bias=nbm[:, blk:blk + 1], scale=1.0),
                     reads=[Tqd_b, nbm_b], writes=[Tqd_b])
            k.op(dve, lambda e: e.tensor_tensor(out=Tgs, in0=Tgg, in1=TgL, op=ALU.mult),
                 reads=[Tgg_b, TgL_b], writes=[Tgs_b])
            k.op(dve, lambda e: e.tensor_copy(out=KDT, in_=BtB[:, 0:ST]), reads=[pb_b[Bt]], writes=[KDT_b])
            yield
            k.op(act, lambda e: e.activation(out=ebl, in_=bl_v, func=AF.Exp), reads=[Tb_b], writes=[ebl_b])
            k.op(act, lambda e: e.activation(out=ebm, in_=bm_v, func=AF.Exp), reads=[Tb_b], writes=[ebm_b])
            k.op(act, lambda e: e.activation(out=eblm, in_=eblm, func=AF.Exp), reads=[eblm_b], writes=[eblm_b])
            k.op(dve, lambda e: e.tensor_tensor(out=QD, in0=Tqq, in1=Tqd, op=ALU.mult),
                 reads=[Tqq_b, Tqd_b], writes=[QD_b])
            yield
            for blk in range(4):
                cs = slice(blk * 128, (blk + 1) * 128)
                k.op(pe, lambda e, cs=cs: e.matmul(pb_t[Ba][:, cs], lhsT=KD[:, cs], rhs=QD[:, cs],
                                                   start=True, stop=True),
                     reads=[KD_b, QD_b], writes=[pb_b[Ba]], inc=(blk == 3))
            k.op(dve, lambda e: e.tensor_tensor(out=AT, in0=pb_t[Ba][:], in1=cm4_t[:], op=ALU.mult),
                 reads=[pb_b[Ba], const_b], writes=[AT_b])

        def stage2(u, ui):
            head, s, slot = u
            wt = w_t[slot]
            hb = HB[ui % 2]
            QD, QD_b = hb["QD"]
            KDT, KDT_b = hb["KDT"]
            VT, VT_b = hb["VT"]
            AT, AT_b = hb["AT"]
            Tgs, Tgs_b = hb["Tgs"]
            ebl, ebl_b = hb["ebl"]
            ebm, ebm_b = hb["ebm"]
            eblm, eblm_b = hb["eblm"]
            Sst = S_t[:, idx * 8 + head, :]
            Sb_ = S_b[idx * 8 + head]
            sl = slice(s * ST, (s + 1) * ST)
            for blk in range(4):
                cs = slice(blk * 128, (blk + 1) * 128)
                k.op(act, lambda e, blk=blk: e.activation(out=Sbf, in_=Sst, func=AF.Identity,
                                                          scale=ebm[:, blk:blk + 1]),
                     reads=[Sb_, ebm_b], writes=[Sbf_b])
                k.op(pe, lambda e, cs=cs: e.matmul(pb_t[Bo][:, cs], lhsT=VT[:, cs], rhs=AT[:, cs],
                                                   start=True, stop=False),
                     reads=[VT_b, AT_b], writes=[pb_b[Bo]], inc=False)
                k.op(pe, lambda e, cs=cs: e.matmul(pb_t[Bo][:, cs], lhsT=Sbf, rhs=QD[:, cs],
                                                   start=False, stop=True),
                     reads=[Sbf_b, QD_b], writes=[pb_b[Bo]], inc=True)
                k.op(pe, lambda e, cs=cs: e.matmul(pb_t[Bp][:, 0:128], lhsT=KDT[:, cs], rhs=VT[:, cs],
                                                   start=True, stop=True),
                     reads=[KDT_b, VT_b], writes=[pb_b[Bp]], inc=True)
                k.op(act, lambda e, blk=blk: e.activation(out=Sp, in_=Sst, func=AF.Identity,
                                                          scale=ebl[:, blk:blk + 1]),
                     reads=[Sb_, ebl_b], writes=[Sp_b])
                k.op(dve, lambda e, blk=blk: e.scalar_tensor_tensor(
                    out=Sst, in0=pb_t[Bp][:, 0:128], scalar=eblm[:, blk:blk + 1], in1=Sp,
                    op0=ALU.mult, op1=ALU.add), reads=[pb_b[Bp], eblm_b, Sp_b], writes=[Sb_])
                yield
            k.op(act, lambda e: e.activation(out=OSQ, in_=pb_t[Bo][:], func=AF.Square),
                 reads=[pb_b[Bo]], writes=[OSQ_b])
            k.op(pe, lambda e: e.matmul(pb_t[Bp][:], lhsT=ones_t[:], rhs=OSQ, start=True, stop=True),
                 reads=[OSQ_b, const_b], writes=[pb_b[Bp]], inc=True)
            k.op(act, lambda e: e.activation(out=Tr, in_=pb_t[Bp][:], func=AF.Ln, bias=EPS, scale=1.0 / 128),
                 reads=[pb_b[Bp]], writes=[Tr_b])
            k.op(act, lambda e: e.activation(out=Tr, in_=Tr, func=AF.Exp, scale=-0.5), reads=[Tr_b], writes=[Tr_b])
            k.op(dve, lambda e: e.scalar_tensor_tensor(out=To, in0=pb_t[Bo][:], scalar=pcol(HN0 + idx), in1=Tr,
                                                       op0=ALU.mult, op1=ALU.mult),
                 reads=[pb_b[Bo], Tr_b, pc_b], writes=[To_b])
            k.op(dve, lambda e: e.tensor_tensor(out=OG, in0=To, in1=Tgs, op=ALU.mult),
                 reads=[To_b, Tgs_b], writes=[OG_b])
            yield
            for dc in range(KC):
                ob = (Bp, Bo)[dc % 2]
                k.op(pe, lambda e, dc=dc, ob=ob: e.matmul(pb_t[ob][:], lhsT=wt[:, KC * 512 + dc * 128:KC * 512 + (dc + 1) * 128],
                                                          rhs=OG, start=True, stop=True),
                     reads=[w_b[slot], OG_b], writes=[pb_b[ob]], inc=True)
                k.op(dve, lambda e, dc=dc, ob=ob: e.tensor_tensor(out=h_t[:, dc, sl], in0=pb_t[ob][:],
                                                                  in1=h_t[:, dc, sl], op=ALU.add),
                     reads=[pb_b[ob], h_b[dc][s]], writes=[h_b[dc][s]])
                if dc % 2 == 1:
                    yield

        nU = len(units)

        def run_rr(gens):
            gens = [g for g in gens if g is not None]
            while gens:
                nxt = []
                for g in gens:
                    try:
                        next(g)
                        nxt.append(g)
                    except StopIteration:
                        pass
                gens = nxt

        run_rr([stageP(units[0], 0)])
        run_rr([stageC(units[0], 0), stageP(units[1], 1)])
        for ui in range(nU):
            run_rr([stageC(units[ui + 1], ui + 1) if ui + 1 < nU else None,
                    stage2(units[ui], ui),
                    stageP(units[ui + 2], ui + 2) if ui + 2 < nU else None])

    out_streams = []
    for hf in range(NHALF):
        load_half(hf)
        for kind, l in plan:
            if kind == "attn":
                attn_phase(l, hf)
            elif kind == "hgrn":
                hgrn_phase(l, hf)
            else:
                ffn_phase(l, hf)
        out_streams += store_half(hf)
    for st in out_streams:
        sp.h.wait_ge(st.sem, st.count)
    return k


def pack_inputs(inp):
    f = np.float32
    pc = np.zeros((128, NPC), f)
    norms = [inp["norm_mix"][i] for i in range(4)] + [inp["norm_ffn"][i] for i in range(4)] + [inp["norm_final"]]
    for n, gvec in enumerate(norms):
        pc[:, GAIN0 + n * 8:GAIN0 + (n + 1) * 8] = np.asarray(gvec, f).reshape(8, 128).T
    cw = np.asarray(inp["ffn_conv_w"], f)
    cb = np.asarray(inp["ffn_conv_b"], f)
    for l in range(DEPTH):
        blk = np.concatenate([cw[l], cb[l][None]], axis=0)
        blk = blk.reshape(4, 44, 128).transpose(2, 1, 0)
        pc[:, CONV0 + l * 176:CONV0 + (l + 1) * 176] = blk.reshape(128, 176)
    lg = np.asarray(inp["hgrn_lb_logits"], f)
    pc[:, LG0:LG0 + 32] = lg.reshape(4, 8, 128).transpose(2, 1, 0).reshape(128, 32)
    pc[:, HN0:HN0 + 2] = np.asarray(inp["hgrn_norm"], f).T
    sk = np.asarray(inp["attn_sinks"], f)
    for idx in range(2):
        for g in range(4):
            for hh in range(2):
                col = SK0 + idx * 8 + g * 2 + hh
                pc[0:64, col] = sk[idx, 4 * g + 0 + 2 * hh]
                pc[64:128, col] = sk[idx, 4 * g + 1 + 2 * hh]

    awi = np.asarray(inp["attn_w_in"], f)
    q = awi[:, :, 0:1024]
    kk = awi[:, :, 1024:1280].reshape(2, D, 4, 64)
    kdup = np.concatenate([kk, kk], axis=3).reshape(2, D, 512)
    v = awi[:, :, 1280:1536]
    win = np.concatenate([q, kdup, v], axis=2)
    awin = np.ascontiguousarray(win.reshape(2, KC, 128, 1792).transpose(0, 2, 1, 3)).reshape(2, 128, KC * 1792)
    awo = np.asarray(inp["attn_w_out"], f)
    awout = np.ascontiguousarray(awo.reshape(2, KC, 128, D).transpose(0, 2, 1, 3)).reshape(2, 128, KC * D)

    hwi = np.asarray(inp["hgrn_w_in"], f)
    hwi = hwi.reshape(2, KC, 128, 4, 8, 128)
    hwin = np.ascontiguousarray(hwi.transpose(0, 4, 2, 1, 3, 5)).reshape(2, 8, 128, KC * 512)
    hwo = np.asarray(inp["hgrn_w_out"], f)
    hwout = np.ascontiguousarray(hwo.reshape(2, 8, 128, D))

    fu = np.asarray(inp["ffn_w_up"], f)
    fd = np.asarray(inp["ffn_w_down"], f)
    fup = np.zeros((DEPTH, NG, 128, KC, 2, GMAX * 128), f)
    fdn = np.zeros((DEPTH, NG, 128, GMAX, D), f)
    fu_r = fu.reshape(DEPTH, KC, 128, 2, D_FF)
    for g in range(NG):
        n = G_SIZES[g]
        c0 = G_OFF[g] * 128
        fup[:, g, :, :, :, 0:n * 128] = fu_r[:, :, :, :, c0:c0 + n * 128].transpose(0, 2, 1, 3, 4)
        fdn[:, g, :, 0:n, :] = fd[:, c0:c0 + n * 128, :].reshape(DEPTH, n, 128, D).transpose(0, 2, 1, 3)
    fup = fup.reshape(DEPTH, NG, 128, KC * 2 * GMAX * 128)
    fdn = fdn.reshape(DEPTH, NG, 128, GMAX * D)
    return dict(pc=pc, awin=awin, awout=awout, hwin=hwin, hwout=hwout, fup=fup, fdn=fdn)


FULL_PLAN = [("attn", 0), ("ffn", 0), ("hgrn", 1), ("ffn", 1), ("attn", 2), ("ffn", 2), ("hgrn", 3), ("ffn", 3)]
_CACHE = {}


def run_plan(inputs, plan, final_norm=True, x_override=None, trace=False, n_cores=8):
    key = (tuple(plan), final_norm)
    if key not in _CACHE:
        _CACHE[key] = build(plan, final_norm)
    k = _CACHE[key]
    shared = pack_inputs(inputs)
    x = np.asarray(inputs["x"] if x_override is None else x_override, np.float32)
    in_maps = []
    for b in range(n_cores):
        m = dict(shared)
        m["x"] = np.ascontiguousarray(x[b])
        in_maps.append(m)
    res = run_bass_kernel_spmd(k.nc, in_maps, core_ids=list(range(n_cores)), trace=trace)
    out = np.stack([np.asarray(r["out"], np.float32) for r in res.results], axis=0)
    return out, res


def kernel(**inputs):
    out, _ = run_plan(inputs, FULL_PLAN, True)
    return out
```

```python
import numpy as np
from contextlib import ExitStack
import concourse.bass as bass
import concourse.mybir as mybir
from concourse.bass_utils import run_bass_kernel_spmd

F32 = mybir.dt.float32
BF16 = mybir.dt.bfloat16
AF = mybir.ActivationFunctionType
ALU = mybir.AluOpType
AX = mybir.AxisListType

S = 4096
D = 1024
DEPTH = 4
T = 2048
NHALF = S // T
ST = 512
NSUB = T // ST
KC = D // 128
D_FF = 2816
NFC = D_FF // 128
G_SIZES = [5, 5, 4, 4, 4]
G_OFF = [0, 5, 10, 14, 18]
NG = len(G_SIZES)
GMAX = 5
EPS = 1e-6
NQH = 16
HG_LEAD = 0
HD = 64

GAIN0 = 0
CONV0 = GAIN0 + 9 * 8
LG0 = CONV0 + DEPTH * 44 * 4
HN0 = LG0 + 32
SK0 = HN0 + 2
NPC = SK0 + 32


class Stream:
    def __init__(self, sem, step):
        self.sem = sem
        self.count = 0
        self.step = step


class Eng:
    def __init__(self, name, h, st):
        self.name = name
        self.h = h
        self.st = st
        self.seen = {}


class Buf:
    __slots__ = ("name", "w", "r", "excl")

    def __init__(self, name, deps=None, excl=False):
        self.name = name
        self.w = None
        self.r = dict(deps) if deps else {}
        self.excl = excl


def merge_deps(bufs):
    d = {}
    for b in bufs:
        if b.w is not None:
            s, v = b.w
            if d.get(s, 0) < v:
                d[s] = v
        for s, v in b.r.items():
            if d.get(s, 0) < v:
                d[s] = v
    return d


class K:
    def __init__(self):
        self.nc = bass.Bass("TRN2", target_bir_lowering=False)
        self.es = ExitStack()
        nc = self.nc
        self.pe = Eng("pe", nc.tensor, self.stream("s_pe", 1))
        self.act = Eng("act", nc.scalar, self.stream("s_act", 1))
        self.dve = Eng("dve", nc.vector, self.stream("s_dve", 1))
        self.pool = Eng("pool", nc.gpsimd, self.stream("s_pool", 1))
        self.sp = Eng("sp", nc.sync, self.stream("s_sp", 1))

    def stream(self, name, step):
        sem = self.es.enter_context(self.nc.semaphore(name))
        return Stream(sem, step)

    def sb(self, name, shape, dt):
        return self.es.enter_context(self.nc.sbuf_tensor("sb_" + name, shape, dt))

    def ps(self, name, shape, dt):
        return self.es.enter_context(self.nc.psum_tensor("ps_" + name, shape, dt))

    def _sync(self, eng, reads, writes, is_dma=False):
        need = {}
        for b in reads:
            if b.w is not None:
                s, v = b.w
                if need.get(s, 0) < v:
                    need[s] = v
            if b.excl:
                for s, v in b.r.items():
                    if s is not eng.st and need.get(s, 0) < v:
                        need[s] = v
        same_ok = (not is_dma) and eng.name == "pe"
        for b in writes:
            if b.w is not None:
                s, v = b.w
                if not (same_ok and s is eng.st):
                    if need.get(s, 0) < v:
                        need[s] = v
            for s, v in b.r.items():
                if not (same_ok and s is eng.st):
                    if need.get(s, 0) < v:
                        need[s] = v
        for s, v in need.items():
            if eng.seen.get(s, 0) < v:
                eng.h.wait_ge(s.sem, v)
                eng.seen[s] = v

    @staticmethod
    def _record(stream, tick, reads, writes):
        for b in reads:
            if b.r.get(stream, 0) < tick:
                b.r[stream] = tick
        for b in writes:
            b.w = (stream, tick)
            b.r = {}

    def op(self, eng, fn, reads=(), writes=(), inc=True):
        self._sync(eng, reads, writes)
        ins = fn(eng.h)
        if inc:
            eng.st.count += 1
            ins.then_inc(eng.st.sem, 1)
            tick = eng.st.count
        else:
            tick = eng.st.count + 1
        self._record(eng.st, tick, reads, writes)
        return ins

    def dma(self, eng, stream, out, in_, reads=(), writes=()):
        self._sync(eng, reads, writes, is_dma=True)
        ins = eng.h.dma_start(out=out, in_=in_)
        stream.count += 16
        ins.then_inc(stream.sem, 16)
        self._record(stream, stream.count, reads, writes)
        return ins


def build(plan, final_norm=True):
    k = K()
    nc = k.nc
    pe, act, dve, pool, sp = k.pe, k.act, k.dve, k.pool, k.sp

    x_d = nc.dram_tensor("x", [S, D], F32, kind="ExternalInput").ap()
    pc_d = nc.dram_tensor("pc", [128, NPC], F32, kind="ExternalInput").ap()
    awin_d = nc.dram_tensor("awin", [2, 128, KC * 1792], F32, kind="ExternalInput").ap()
    awout_d = nc.dram_tensor("awout", [2, 128, KC * D], F32, kind="ExternalInput").ap()
    hwin_d = nc.dram_tensor("hwin", [2, 8, 128, KC * 512], F32, kind="ExternalInput").ap()
    hwout_d = nc.dram_tensor("hwout", [2, 8, 128, D], F32, kind="ExternalInput").ap()
    fup_d = nc.dram_tensor("fup", [DEPTH, NG, 128, KC * 2 * GMAX * 128], F32, kind="ExternalInput").ap()
    fdn_d = nc.dram_tensor("fdn", [DEPTH, NG, 128, GMAX * D], F32, kind="ExternalInput").ap()
    out_d = nc.dram_tensor("out", [S, D], F32, kind="ExternalOutput").ap()

    h_t = k.sb("h", [128, KC, T], F32)
    h_b = [[Buf("h%d_%d" % (c, s)) for s in range(NSUB)] for c in range(KC)]
    hn_t = k.sb("hn", [128, KC, T], BF16)
    hn_b = [Buf("hn%d" % s) for s in range(NSUB)]
    pc_t = k.sb("pc", [128, NPC], F32)
    pc_b = Buf("pc")
    WSLOT = 15360
    w_t = [k.sb("w%d" % i, [128, WSLOT], BF16) for i in range(2)]
    w_b = [Buf("w%d" % i) for i in range(2)]
    w_st = [k.stream("s_w%d" % i, 16) for i in range(2)]
    wcount = [0]
    idb_t = k.sb("idb", [128, 128], BF16)
    idf_t = k.sb("idf", [128, 128], F32)
    ones_t = k.sb("ones", [128, 128], BF16)
    cm4_t = k.sb("cm4", [128, 512], BF16)
    mall_t = k.sb("mall", [128, 8, 512], BF16)
    const_b = Buf("const")
    lb_t = k.sb("lb", [128, 2, 8], F32)
    c1_t = k.sb("c1", [128, 2, 8], F32)
    esk_t = k.sb("esk", [128, 32], F32)
    S_t = k.sb("S", [128, 16, 128], F32)
    S_b = [Buf("S%d" % i) for i in range(16)]
    ktc_t = k.sb("ktc", [128, 2, 4, 128], BF16)
    vc_t = k.sb("vc", [128, 2, 256], BF16)
    kvc_b = [Buf("kvc%d" % i) for i in range(2)]
    halo_t = k.sb("halo", [128, DEPTH, 2, 44, 2], F32)
    halo_b = [[[Buf("halo%d_%d_%d" % (l, q, j)) for j in range(44)] for q in range(2)] for l in range(DEPTH)]
    sq_t = [k.sb("sq%d" % i, [128, ST], BF16) for i in range(2)]
    sq_b = [Buf("sq%d" % i) for i in range(2)]
    rstd_t = k.sb("rstd", [128, ST], F32)
    rstd_b = Buf("rstd")
    TMPW = 5200
    tmp_t = k.sb("tmp", [128, TMPW], F32)
    tmp_live = []

    pb_t = [k.ps("pb%d" % i, [128, 512], F32) for i in range(8)]
    pb_b = [Buf("pb%d" % i, excl=True) for i in range(8)]

    class TmpAlloc:
        def __init__(self):
            self.deps = merge_deps(tmp_live)
            del tmp_live[:]
            self.off = 0

        def get(self, name, shape_free, dt):
            n = int(np.prod(shape_free))
            words = n if dt == F32 else (n + 1) // 2
            assert self.off + words <= TMPW, (name, self.off, words)
            ap = tmp_t[:, self.off:self.off + words]
            if dt != F32:
                ap = ap.bitcast(dt)[:, 0:n]
            self.off += words
            b = Buf(name, self.deps)
            tmp_live.append(b)
            return ap, b

    TAIL0 = 5120
    tail_live = [[], []]

    class TailAlloc:
        def __init__(self, i):
            self.i = i
            self.deps = merge_deps(tail_live[i] + [w_b[i]])
            del tail_live[i][:]
            self.off = TAIL0

        def get(self, name, shape_free, dt):
            n = int(np.prod(shape_free))
            el = n if dt == BF16 else 2 * n
            assert self.off + el <= WSLOT, (name, self.off, el)
            ap = w_t[self.i][:, self.off:self.off + el]
            if dt != BF16:
                ap = ap.bitcast(dt)
            self.off += el
            b = Buf(name, self.deps)
            tail_live[self.i].append(b)
            return ap, b

    def pcol(j):
        return pc_t[:, j:j + 1]

    st_pc = k.stream("s_pc", 16)
    k.dma(sp, st_pc, pc_t[:], pc_d, writes=[pc_b])
    k.op(pool, lambda e: e.memset(ones_t[:], 1.0), writes=[const_b])
    k.op(pool, lambda e: e.memset(idb_t[:], 1.0), writes=[const_b])
    k.op(pool, lambda e: e.affine_select(out=idb_t[:], in_=idb_t[:], pattern=[[-1, 128]], compare_op=ALU.is_equal,
                                         fill=0.0, base=0, channel_multiplier=1), reads=[const_b], writes=[const_b])
    k.op(pool, lambda e: e.memset(idf_t[:], 1.0), writes=[const_b])
    k.op(pool, lambda e: e.affine_select(out=idf_t[:], in_=idf_t[:], pattern=[[-1, 128]], compare_op=ALU.is_equal,
                                         fill=0.0, base=0, channel_multiplier=1), reads=[const_b], writes=[const_b])
    k.op(pool, lambda e: e.memset(cm4_t[:], 1.0), writes=[const_b])
    for blk in range(4):
        k.op(pool, lambda e, blk=blk: e.affine_select(out=cm4_t[:, blk * 128:(blk + 1) * 128],
                                                      in_=cm4_t[:, blk * 128:(blk + 1) * 128],
                                                      pattern=[[1, 128]], compare_op=ALU.is_ge, fill=0.0, base=0,
                                                      channel_multiplier=-1), reads=[const_b], writes=[const_b])
    k.op(pool, lambda e: e.memset(halo_t[:].rearrange("p a q b c -> p (a q b c)"), 0.0), writes=[b for h1 in halo_b for h2 in h1 for b in h2])
    k.op(pool, lambda e: e.memset(S_t[:].rearrange("p a b -> p (a b)"), 0.0), writes=S_b)
    k.op(pool, lambda e: e.memset(ktc_t[:].rearrange("p a b c -> p (a b c)"), 0.0), writes=kvc_b)
    k.op(pool, lambda e: e.memset(vc_t[:].rearrange("p a b -> p (a b)"), 0.0), writes=kvc_b)

    has_attn = any(p[0] == "attn" for p in plan)
    has_hgrn = any(p[0] == "hgrn" for p in plan)

    if has_attn:
        ta = TmpAlloc()
        d1, d1_b = ta.get("d1", [128], F32)
        dpos, dpos_b = ta.get("dpos", [128], F32)
        dneg, dneg_b = ta.get("dneg", [128], F32)
        mtmp, mtmp_b = ta.get("mtmp", [256], F32)
        k.op(pool, lambda e: e.iota(d1, pattern=[[1, 128]], base=0, channel_multiplier=-1,
                                    allow_small_or_imprecise_dtypes=True), writes=[d1_b])
        k.op(dve, lambda e: e.tensor_scalar_max(out=dpos, in0=d1, scalar1=0.0), reads=[d1_b], writes=[dpos_b])
        k.op(dve, lambda e: e.tensor_scalar_min(out=dneg, in0=d1, scalar1=0.0), reads=[d1_b], writes=[dneg_b])
        for gp in range(8):
            g, p = gp // 2, gp % 2
            for hh in range(2):
                hq = 4 * g + p + 2 * hh
                slope = float(2.0 ** (-8.0 * (hq + 1) / NQH))
                k.op(act, lambda e, slope=slope: e.activation(out=mtmp[:, 0:128], in_=dneg, func=AF.Exp,
                                                              bias=-slope * 128.0, scale=-slope),
                     reads=[dneg_b], writes=[mtmp_b])
                k.op(act, lambda e, slope=slope: e.activation(out=mtmp[:, 128:256], in_=dpos, func=AF.Exp,
                                                              scale=-slope),
                     reads=[dpos_b], writes=[mtmp_b])
                k.op(pool, lambda e, gp=gp, hh=hh: e.affine_select(
                    out=mall_t[:, gp, hh * 128:(hh + 1) * 128], in_=mtmp[:, 0:128], pattern=[[-1, 128]],
                    compare_op=ALU.is_gt, fill=0.0, base=0, channel_multiplier=1),
                    reads=[mtmp_b], writes=[const_b])
                k.op(pool, lambda e, gp=gp, hh=hh: e.affine_select(
                    out=mall_t[:, gp, 256 + hh * 128:256 + (hh + 1) * 128], in_=mtmp[:, 128:256],
                    pattern=[[1, 128]], compare_op=ALU.is_ge, fill=0.0, base=0, channel_multiplier=-1),
                    reads=[mtmp_b], writes=[const_b])
        k.op(act, lambda e: e.activation(out=esk_t[:, 0:16], in_=pc_t[:, SK0:SK0 + 16], func=AF.Exp),
             reads=[pc_b], writes=[const_b])

    if has_hgrn:
        ta = TmpAlloc()
        lg = pc_t[:, LG0:LG0 + 32].rearrange("p (h l) -> p h l", l=4)
        mx, mx_b = ta.get("mx", [8], F32)
        ex, ex_b = ta.get("ex", [32], F32)
        sm, sm_b = ta.get("sm", [8], F32)
        ex3 = ex.rearrange("p (h l) -> p h l", l=4)
        k.op(dve, lambda e: e.tensor_reduce(out=mx, in_=lg, axis=AX.X, op=ALU.max), reads=[pc_b], writes=[mx_b])
        for li in range(4):
            k.op(dve, lambda e, li=li: e.tensor_tensor(out=ex3[:, :, li], in0=lg[:, :, li], in1=mx, op=ALU.subtract),
                 reads=[pc_b, mx_b, ex_b], writes=[ex_b])
        k.op(act, lambda e: e.activation(out=ex, in_=ex, func=AF.Exp), reads=[ex_b], writes=[ex_b])
        k.op(dve, lambda e: e.tensor_reduce(out=sm, in_=ex3, axis=AX.X, op=ALU.add), reads=[ex_b], writes=[sm_b])
        k.op(dve, lambda e: e.reciprocal(out=sm, in_=sm), reads=[sm_b], writes=[sm_b])
        for li in range(4):
            k.op(dve, lambda e, li=li: e.tensor_tensor(out=ex3[:, :, li], in0=ex3[:, :, li], in1=sm, op=ALU.mult),
                 reads=[ex_b, sm_b], writes=[ex_b])
        k.op(dve, lambda e: e.tensor_copy(out=lb_t[:, 0, :], in_=ex3[:, :, 1]), reads=[ex_b], writes=[const_b])
        k.op(dve, lambda e: e.tensor_tensor(out=lb_t[:, 1, :], in0=ex3[:, :, 1], in1=ex3[:, :, 2], op=ALU.add),
             reads=[ex_b], writes=[const_b])
        k.op(dve, lambda e: e.tensor_tensor(out=lb_t[:, 1, :], in0=lb_t[:, 1, :], in1=ex3[:, :, 3], op=ALU.add),
             reads=[ex_b, const_b], writes=[const_b])
        k.op(act, lambda e: e.activation(out=c1_t[:].rearrange("p a b -> p (a b)"),
                                         in_=lb_t[:].rearrange("p a b -> p (a b)"), func=AF.Ln, bias=1.0, scale=-1.0),
             reads=[const_b], writes=[const_b])

    def next_slot():
        i = wcount[0] % 2
        wcount[0] += 1
        return i

    evac_rr = [0]

    def evac_copy(out, in_, reads, writes, force_act=False):
        evac_rr[0] += 1
        if force_act or evac_rr[0] % 2 == 0:
            k.op(act, lambda e: e.copy(out=out, in_=in_), reads=reads, writes=writes)
        else:
            k.op(dve, lambda e: e.tensor_copy(out=out, in_=in_), reads=reads, writes=writes)

    def rmsnorm_sub(s, gain_idx, nbank, in_place=False):
        sl = slice(s * ST, (s + 1) * ST)
        for c in range(KC):
            i = c % 2
            k.op(act, lambda e, c=c, i=i: e.activation(out=sq_t[i][:], in_=h_t[:, c, sl], func=AF.Square),
                 reads=[h_b[c][s]], writes=[sq_b[i]])
            k.op(pe, lambda e, c=c, i=i: e.matmul(pb_t[nbank][:], lhsT=ones_t[:], rhs=sq_t[i][:],
                                                   start=(c == 0), stop=(c == KC - 1)),
                 reads=[sq_b[i], const_b], writes=[pb_b[nbank]], inc=True)
        k.op(act, lambda e: e.activation(out=rstd_t[:], in_=pb_t[nbank][:], func=AF.Ln, bias=EPS, scale=1.0 / D),
             reads=[pb_b[nbank]], writes=[rstd_b])
        k.op(act, lambda e: e.activation(out=rstd_t[:], in_=rstd_t[:], func=AF.Exp, scale=-0.5),
             reads=[rstd_b], writes=[rstd_b])
        for c in range(KC):
            if in_place:
                k.op(dve, lambda e, c=c: e.scalar_tensor_tensor(out=h_t[:, c, sl], in0=h_t[:, c, sl],
                                                                scalar=pcol(GAIN0 + gain_idx * 8 + c),
                                                                in1=rstd_t[:], op0=ALU.mult, op1=ALU.mult),
                     reads=[h_b[c][s], rstd_b, pc_b], writes=[h_b[c][s]])
            else:
                k.op(dve, lambda e, c=c: e.scalar_tensor_tensor(out=hn_t[:, c, sl], in0=h_t[:, c, sl],
                                                                scalar=pcol(GAIN0 + gain_idx * 8 + c),
                                                                in1=rstd_t[:], op0=ALU.mult, op1=ALU.mult),
                     reads=[h_b[c][s], rstd_b, pc_b], writes=[hn_b[s]])

    def load_half(hf):
        ta = TmpAlloc()
        xin = [ta.get("xin%d" % i, [D], F32) for i in range(2)]
        xst = [k.stream("s_xin%d_%d" % (hf, i), 16) for i in range(2)]
        for blk in range(T // 128):
            i = blk % 2
            t0 = hf * T + blk * 128
            s = blk // 4
            col = blk * 128
            k.dma(sp, xst[i], xin[i][0], x_d[t0:t0 + 128, :], writes=[xin[i][1]])
            for half in range(2):
                bank = half
                for cc in range(4):
                    c = half * 4 + cc
                    k.op(pe, lambda e, c=c, cc=cc, bank=bank, i=i: e.transpose(
                        pb_t[bank][:, cc * 128:(cc + 1) * 128], xin[i][0][:, c * 128:(c + 1) * 128], idf_t[:]),
                        reads=[xin[i][1], const_b], writes=[pb_b[bank]], inc=(cc == 3))
                evac_copy(h_t[:, half * 4:half * 4 + 4, col:col + 128],
                          pb_t[bank][:].rearrange("p (c t) -> p c t", c=4),
                          reads=[pb_b[bank]], writes=[h_b[half * 4 + cc][s] for cc in range(4)])

    def store_half(hf):
        ta = TmpAlloc()
        yo = [ta.get("yo%d" % i, [D], F32) for i in range(2)]
        yst = [k.stream("s_yo%d_%d" % (hf, i), 16) for i in range(2)]
        for s in range(NSUB):
            if final_norm:
                rmsnorm_sub(s, 8, 6, in_place=True)
            for bb in range(4):
                blk = s * 4 + bb
                i = blk % 2
                t0 = hf * T + blk * 128
                col = blk * 128
                for half in range(2):
                    bank = half
                    for cc in range(4):
                        c = half * 4 + cc
                        k.op(pe, lambda e, c=c, cc=cc, bank=bank: e.transpose(
                            pb_t[bank][:, cc * 128:(cc + 1) * 128], h_t[:, c, col:col + 128], idf_t[:]),
                            reads=[h_b[c][s], const_b], writes=[pb_b[bank]], inc=(cc == 3))
                    evac_copy(yo[i][0][:, half * 512:(half + 1) * 512], pb_t[bank][:],
                              reads=[pb_b[bank]], writes=[yo[i][1]])
                k.dma(sp, yst[i], out_d[t0:t0 + 128, :], yo[i][0], reads=[yo[i][1]])
        return yst

    def ffn_phase(l, hf):
        for s in range(NSUB):
            rmsnorm_sub(s, 4 + l, 6)
        ta = TmpAlloc()
        cg = [ta.get("cg%d" % i, [ST], F32) for i in range(2)]
        cv = [ta.get("cv%d" % i, [ST], F32) for i in range(2)]
        sg = [ta.get("sg%d" % i, [ST], BF16) for i in range(2)]
        actb = [ta.get("act%d" % i, [GMAX, ST], BF16) for i in range(2)]
        items = []
        gs_list = []
        for g in range(NG):
            n = G_SIZES[g]
            slot = next_slot()
            for s in range(NSUB):
                gs_list.append((g, s, slot))
                for jj in range(n):
                    items.append((g, s, jj, slot, len(gs_list) - 1))

        def wup(slot, kc, col0):
            base = kc * 2 * GMAX * 128 + col0
            return w_t[slot][:, base:base + 128]

        def wdn(slot, jj, dc):
            base = KC * 2 * GMAX * 128 + jj * D + dc * 128
            return w_t[slot][:, base:base + 128]

        def stageA(it, idx):
            g, s, jj, slot, gsi = it
            par = idx % 2
            sl = slice(s * ST, (s + 1) * ST)
            for which in range(2):
                bank = par * 2 + which
                col0 = which * GMAX * 128 + jj * 128
                for kc in range(KC):
                    k.op(pe, lambda e, kc=kc, bank=bank, col0=col0: e.matmul(
                        pb_t[bank][:], lhsT=wup(slot, kc, col0), rhs=hn_t[:, kc, sl],
                        start=(kc == 0), stop=(kc == KC - 1)),
                        reads=[w_b[slot], hn_b[s]], writes=[pb_b[bank]], inc=(kc == KC - 1))

        def stageB(it, idx):
            g, s, jj, slot, gsi = it
            par = idx % 2
            first_tok = (hf == 0 and s == 0)
            sg_ = hf * NSUB + s
            hq_r, hq_w = (sg_ - 1) % 2, sg_ % 2
            info = []
            for which in range(2):
                bank = par * 2 + which
                P = pb_t[bank]
                C, Cb = (cg, cv)[which][par]
                jc = which * NFC + G_OFF[g] + jj
                cb = CONV0 + (l * 44 + jc) * 4
                info.append((bank, P, C, Cb, jc, cb))
                k.op(act, lambda e, P=P, C=C, cb=cb: e.activation(out=C, in_=P[:], func=AF.Identity,
                                                                  bias=pcol(cb + 3), scale=pcol(cb + 2)),
                     reads=[pb_b[bank], pc_b], writes=[Cb])
                k.op(act, lambda e, P=P, jc=jc: e.copy(out=halo_t[:, l, hq_w, jc, :], in_=P[:, ST - 2:ST]),
                     reads=[pb_b[bank]], writes=[halo_b[l][hq_w][jc]])
            for bank, P, C, Cb, jc, cb in info:
                hl = halo_t[:, l, hq_r, jc, :]
                hb_ = halo_b[l][hq_r][jc]
                k.op(dve, lambda e, P=P, C=C, cb=cb: e.scalar_tensor_tensor(
                    out=C[:, 1:ST], in0=P[:, 0:ST - 1], scalar=pcol(cb + 1), in1=C[:, 1:ST],
                    op0=ALU.mult, op1=ALU.add), reads=[pb_b[bank], Cb, pc_b], writes=[Cb])
                k.op(dve, lambda e, P=P, C=C, cb=cb: e.scalar_tensor_tensor(
                    out=C[:, 2:ST], in0=P[:, 0:ST - 2], scalar=pcol(cb + 0), in1=C[:, 2:ST],
                    op0=ALU.mult, op1=ALU.add), reads=[pb_b[bank], Cb, pc_b], writes=[Cb])
                if not first_tok:
                    k.op(dve, lambda e, C=C, cb=cb, hl=hl: e.scalar_tensor_tensor(
                        out=C[:, 0:1], in0=hl[:, 1:2], scalar=pcol(cb + 1), in1=C[:, 0:1],
                        op0=ALU.mult, op1=ALU.add), reads=[hb_, Cb, pc_b], writes=[Cb])
                    k.op(dve, lambda e, C=C, cb=cb, hl=hl: e.scalar_tensor_tensor(
                        out=C[:, 0:2], in0=hl[:, 0:2], scalar=pcol(cb + 0), in1=C[:, 0:2],
                        op0=ALU.mult, op1=ALU.add), reads=[hb_, Cb, pc_b], writes=[Cb])

        def stageC(it, idx):
            g, s, jj, slot, gsi = it
            par = idx % 2
            Cg, Cgb = cg[par]
            Cv, Cvb = cv[par]
            Sg, Sgb = sg[par]
            A, Ab = actb[gsi % 2]
            k.op(act, lambda e: e.activation(out=Sg, in_=Cg, func=AF.Silu), reads=[Cgb], writes=[Sgb])
            k.op(dve, lambda e: e.tensor_tensor(out=A[:, jj * ST:(jj + 1) * ST], in0=Sg, in1=Cv, op=ALU.mult),
                 reads=[Sgb, Cvb], writes=[Ab])

        def D_mm(gsi, dc):
            g, s, slot = gs_list[gsi]
            n = G_SIZES[g]
            A, Ab = actb[gsi % 2]
            bank = 4 + dc % 2
            for jj in range(n):
                k.op(pe, lambda e, jj=jj: e.matmul(
                    pb_t[bank][:], lhsT=wdn(slot, jj, dc), rhs=A[:, jj * ST:(jj + 1) * ST],
                    start=(jj == 0), stop=(jj == n - 1)),
                    reads=[w_b[slot], Ab], writes=[pb_b[bank]], inc=(jj == n - 1))

        def D_add(gsi, dc):
            g, s, slot = gs_list[gsi]
            bank = 4 + dc % 2
            sl = slice(s * ST, (s + 1) * ST)
            k.op(dve, lambda e: e.tensor_tensor(out=h_t[:, dc, sl], in0=pb_t[bank][:],
                                                in1=h_t[:, dc, sl], op=ALU.add),
                 reads=[pb_b[bank], h_b[dc][s]], writes=[h_b[dc][s]])

        def load_w(g, slot):
            wt, wb, wst = w_t[slot], w_b[slot], w_st[slot]
            k.dma(pool, wst, wt[:, 0:KC * 2 * GMAX * 128], fup_d[l, g], writes=[wb] + tail_live[slot])
            k.dma(pool, wst, wt[:, KC * 2 * GMAX * 128:KC * 2 * GMAX * 128 + GMAX * D], fdn_d[l, g], writes=[wb] + tail_live[slot])

        nI = len(items)
        pend = []
        step = 0
        while step < nI + 2 or pend:
            tasks = pend[:2]
            del pend[:2]
            for tk in tasks:
                D_mm(*tk)
            if step < nI:
                g_, s_, jj_, slot_, _ = items[step]
                if s_ == 0 and jj_ == 0:
                    load_w(g_, slot_)
                stageA(items[step], step)
            if 0 <= step - 1 < nI:
                stageB(items[step - 1], step - 1)
            if 0 <= step - 2 < nI:
                it = items[step - 2]
                stageC(it, step - 2)
                g, s, jj, slot, gsi = it
                if jj == G_SIZES[g] - 1:
                    pend.extend((gsi, dc) for dc in range(KC))
            for tk in tasks:
                D_add(*tk)
            step += 1

    def attn_phase(l, hf):
        idx = l // 2
        sA = next_slot()
        k.dma(pool, w_st[sA], w_t[sA][:, 0:KC * 1792], awin_d[idx], writes=[w_b[sA]] + tail_live[sA])
        sB = next_slot()
        k.dma(pool, w_st[sB], w_t[sB][:, 0:KC * D], awout_d[idx], writes=[w_b[sB]] + tail_live[sB])
        ta = TmpAlloc()
        kt, kt_b = ta.get("kt", [4, 640], BF16)
        vb, vb_b = ta.get("vb", [5, 256], BF16)
        et = [ta.get("e%d" % i, [512], BF16) for i in range(2)]
        pt = [ta.get("p%d" % i, [512], BF16) for i in range(4)]
        rd = [ta.get("rd%d" % i, [256], F32) for i in range(2)]
        kt3 = kt.rearrange("p (g t) -> p g t", g=4)
        vb3 = vb.rearrange("p (b d) -> p b d", b=5)
        hnS_b, qt_b, ot_b = hn_b[0], hn_b[1], hn_b[2]

        def hnS(kc, a, b):
            return hn_t[:, kc, a:b]

        QT0, OT0 = ST, 2 * ST
        k.op(dve, lambda e: e.tensor_copy(out=kt3[:, :, 0:128], in_=ktc_t[:, idx]), reads=[kvc_b[idx]], writes=[kt_b])
        k.op(dve, lambda e: e.tensor_copy(out=vb3[:, 0, :], in_=vc_t[:, idx]), reads=[kvc_b[idx]], writes=[vb_b])

        def win(kc, c0, n):
            base = kc * 1792 + c0
            return w_t[sA][:, base:base + n]

        def wout(c, dc):
            base = c * D + dc * 128
            return w_t[sB][:, base:base + 128]

        pcnt = [0]
        for s in range(NSUB):
            sl = slice(s * ST, (s + 1) * ST)
            for c in range(KC):
                i = c % 2
                k.op(act, lambda e, c=c, i=i: e.activation(out=sq_t[i][:], in_=h_t[:, c, sl], func=AF.Square),
                     reads=[h_b[c][s]], writes=[sq_b[i]])
                k.op(pe, lambda e, c=c, i=i: e.matmul(pb_t[6][:], lhsT=ones_t[:], rhs=sq_t[i][:],
                                                       start=(c == 0), stop=(c == KC - 1)),
                     reads=[sq_b[i], const_b], writes=[pb_b[6]], inc=True)
            k.op(act, lambda e: e.activation(out=rstd_t[:], in_=pb_t[6][:], func=AF.Ln, bias=EPS, scale=1.0 / D),
                 reads=[pb_b[6]], writes=[rstd_b])
            k.op(act, lambda e: e.activation(out=rstd_t[:], in_=rstd_t[:], func=AF.Exp, scale=-0.5),
                 reads=[rstd_b], writes=[rstd_b])
            for c in range(KC):
                k.op(dve, lambda e, c=c: e.scalar_tensor_tensor(out=hn_t[:, c, 0:ST], in0=h_t[:, c, sl],
                                                                scalar=pcol(GAIN0 + l * 8 + c),
                                                                in1=rstd_t[:], op0=ALU.mult, op1=ALU.mult),
                     reads=[h_b[c][s], rstd_b, pc_b], writes=[hnS_b])
            for c in range(KC + 4):
                bank = pcnt[0] % 2
                pcnt[0] += 1
                for kc in range(KC):
                    k.op(pe, lambda e, c=c, kc=kc, bank=bank: e.matmul(
                        pb_t[bank][:], lhsT=win(kc, c * 128, 128), rhs=hnS(kc, 0, ST),
                        start=(kc == 0), stop=(kc == KC - 1)),
                        reads=[w_b[sA], hnS_b], writes=[pb_b[bank]], inc=(kc == KC - 1))
                if c < KC:
                    evac_copy(hn_t[:, c, QT0:QT0 + ST], pb_t[bank][:], reads=[pb_b[bank]], writes=[qt_b], force_act=True)
                else:
                    evac_copy(kt3[:, c - KC, 128:640], pb_t[bank][:], reads=[pb_b[bank]], writes=[kt_b], force_act=True)
            for pair in range(2):
                bank = pcnt[0] % 2
                pcnt[0] += 1
                for bb in range(2):
                    blk = pair * 2 + bb
                    for kc in range(KC):
                        k.op(pe, lambda e, kc=kc, blk=blk, bb=bb, bank=bank: e.matmul(
                            pb_t[bank][:, bb * 256:(bb + 1) * 256], lhsT=hnS(kc, blk * 128, (blk + 1) * 128),
                            rhs=win(kc, 1536, 256), start=(kc == 0), stop=(kc == KC - 1)),
                            reads=[w_b[sA], hnS_b], writes=[pb_b[bank]], inc=(kc == KC - 1))
                evac_copy(vb[:, (1 + pair * 2) * 256:(3 + pair * 2) * 256], pb_t[bank][:],
                          reads=[pb_b[bank]], writes=[vb_b], force_act=True)
            steps = [(bq, g) for bq in range(4) for g in range(4)]

            def scores(bq, g):
                nglob = hf * (T // 128) + s * 4 + bq
                has_prev = nglob > 0
                for p in range(2):
                    gp = g * 2 + p
                    bank = 2 + gp % 2
                    X = pb_t[bank]
                    rows = slice(p * 64, (p + 1) * 64)
                    qrhs = hn_t[rows, 2 * g:2 * g + 2, QT0 + bq * 128:QT0 + (bq + 1) * 128]
                    if has_prev:
                        k.op(pe, lambda e, X=X, rows=rows, qrhs=qrhs: e.matmul(
                            X[:, 0:256].rearrange("p (h q) -> p h q", h=2),
                            lhsT=kt3[rows, g, bq * 128:(bq + 1) * 128], rhs=qrhs, start=True, stop=True),
                            reads=[kt_b, qt_b], writes=[pb_b[bank]], inc=False)
                    k.op(pe, lambda e, X=X, rows=rows, qrhs=qrhs: e.matmul(
                        X[:, 256:512].rearrange("p (h q) -> p h q", h=2),
                        lhsT=kt3[rows, g, (bq + 1) * 128:(bq + 2) * 128], rhs=qrhs, start=True, stop=True),
                        reads=[kt_b, qt_b], writes=[pb_b[bank]], inc=True)
                    E, Eb = et[gp % 2]
                    Pm, Pb = pt[gp % 4]
                    lo = 0 if has_prev else 256
                    k.op(act, lambda e, X=X, E=E, lo=lo: e.activation(out=E[:, lo:512], in_=X[:, lo:512],
                                                                       func=AF.Exp, scale=HD ** -0.5),
                         reads=[pb_b[bank]], writes=[Eb])
                    k.op(dve, lambda e, E=E, Pm=Pm, gp=gp, lo=lo: e.tensor_tensor(
                        out=Pm[:, lo:512], in0=E[:, lo:512], in1=mall_t[:, gp, lo:512], op=ALU.mult),
                        reads=[Eb, const_b], writes=[Pb])

            def pv(bq, g):
                nglob = hf * (T // 128) + s * 4 + bq
                has_prev = nglob > 0
                bank = (4, 5, 7)[(bq * 4 + g) % 3]
                Y = pb_t[bank]
                vcols = slice(g * 64, (g + 1) * 64)
                for p in range(2):
                    gp = g * 2 + p
                    Pm, Pb = pt[gp % 4]
                    rows = slice(p * 64, (p + 1) * 64)
                    if has_prev:
                        k.op(pe, lambda e, rows=rows, Pm=Pm: e.matmul(
                            Y[rows, 0:256], lhsT=vb3[:, bq, vcols], rhs=Pm[:, 0:256], start=True, stop=False),
                            reads=[vb_b, Pb], writes=[pb_b[bank]], inc=False)
                    k.op(pe, lambda e, rows=rows, Pm=Pm: e.matmul(
                        Y[rows, 0:256], lhsT=vb3[:, bq + 1, vcols], rhs=Pm[:, 256:512],
                        start=(not has_prev), stop=True),
                        reads=[vb_b, Pb], writes=[pb_b[bank]], inc=False)
                    if has_prev:
                        k.op(pe, lambda e, rows=rows, Pm=Pm: e.matmul(
                            Y[rows, 256:512], lhsT=ones_t[:, 0:64], rhs=Pm[:, 0:256], start=True, stop=False),
                            reads=[const_b, Pb], writes=[pb_b[bank]], inc=False)
                    k.op(pe, lambda e, rows=rows, Pm=Pm: e.matmul(
                        Y[rows, 256:512], lhsT=ones_t[:, 0:64], rhs=Pm[:, 256:512],
                        start=(not has_prev), stop=True),
                        reads=[const_b, Pb], writes=[pb_b[bank]], inc=(p == 1))
                R, Rb = rd[(bq * 4 + g) % 2]
                for hh in range(2):
                    col = idx * 8 + g * 2 + hh
                    k.op(act, lambda e, hh=hh, col=col: e.activation(
                        out=R[:, hh * 128:(hh + 1) * 128], in_=Y[:, 256 + hh * 128:256 + (hh + 1) * 128],
                        func=AF.Ln, bias=esk_t[:, col:col + 1], scale=1.0),
                        reads=[pb_b[bank], const_b], writes=[Rb])
                k.op(act, lambda e: e.activation(out=R, in_=R, func=AF.Exp, scale=-1.0), reads=[Rb], writes=[Rb])
                k.op(dve, lambda e: e.tensor_tensor(
                    out=hn_t[:, 2 * g:2 * g + 2, OT0 + bq * 128:OT0 + (bq + 1) * 128],
                    in0=Y[:, 0:256].rearrange("p (h q) -> p h q", h=2),
                    in1=R.rearrange("p (h q) -> p h q", h=2), op=ALU.mult),
                    reads=[pb_b[bank], Rb], writes=[ot_b])

            scores(*steps[0])
            for si in range(len(steps)):
                if si + 1 < len(steps):
                    scores(*steps[si + 1])
                pv(*steps[si])
            for dc in range(KC):
                bank = pcnt[0] % 2
                pcnt[0] += 1
                for c in range(KC):
                    k.op(pe, lambda e, c=c, dc=dc, bank=bank: e.matmul(
                        pb_t[bank][:], lhsT=wout(c, dc), rhs=hn_t[:, c, OT0:OT0 + ST],
                        start=(c == 0), stop=(c == KC - 1)),
                        reads=[w_b[sB], ot_b], writes=[pb_b[bank]], inc=(c == KC - 1))
                k.op(dve, lambda e, dc=dc, bank=bank: e.tensor_tensor(out=h_t[:, dc, sl], in0=pb_t[bank][:],
                                                                       in1=h_t[:, dc, sl], op=ALU.add),
                     reads=[pb_b[bank], h_b[dc][s]], writes=[h_b[dc][s]])
            k.op(dve, lambda e: e.tensor_copy(out=kt3[:, :, 0:128], in_=kt3[:, :, 512:640]),
                 reads=[kt_b], writes=[kt_b])
            k.op(dve, lambda e: e.tensor_copy(out=vb3[:, 0, :], in_=vb3[:, 4, :]), reads=[vb_b], writes=[vb_b])
        k.op(dve, lambda e: e.tensor_copy(out=ktc_t[:, idx], in_=kt3[:, :, 0:128]), reads=[kt_b], writes=[kvc_b[idx]])
        k.op(dve, lambda e: e.tensor_copy(out=vc_t[:, idx], in_=vb3[:, 0, :]), reads=[vb_b], writes=[kvc_b[idx]])

    def hgrn_phase(l, hf):
        idx = (l - 1) // 2
        for s in range(NSUB):
            rmsnorm_sub(s, l, 4)
        ta = TmpAlloc()
        tl = [TailAlloc(0), TailAlloc(1)]
        Te, Te_b = ta.get("Te", [ST], F32)
        TL1, TL1_b = ta.get("TL1", [ST], F32)
        TL2, TL2_b = ta.get("TL2", [ST], F32)
        Tb, Tb_b = ta.get("Tb", [ST], F32)
        Tt, Tt_b = ta.get("Tt", [ST], F32)
        KD, KD_b = ta.get("KD", [ST], BF16)
        cbm, cbm_b = ta.get("cbm", [4], F32)
        nbm, nbm_b = ta.get("nbm", [4], F32)
        HB = []
        for i in range(2):
            d = {}
            for nm in ("QD", "KDT", "VT", "AT"):
                d[nm] = tl[i].get(nm + str(i), [ST], BF16)
            d["Tgs"] = tl[i].get("Tgs" + str(i), [ST], F32)
            for nm in ("ebl", "ebm", "eblm"):
                d[nm] = ta.get(nm + str(i), [4], F32)
            HB.append(d)
        Tqe, Tqe_b = tl[0].get("Tqe", [ST], F32)
        TqL, TqL_b = tl[0].get("TqL", [ST], F32)
        Tqd, Tqd_b = tl[0].get("Tqd", [ST], F32)
        Tqq, Tqq_b = tl[1].get("Tqq", [ST], F32)
        Tgg, Tgg_b = tl[1].get("Tgg", [ST], F32)
        Tge, Tge_b = tl[1].get("Tge", [ST], F32)
        TgL, TgL_b = tl[1].get("TgL", [ST], F32)
        To, To_b = tl[0].get("To", [ST], F32)
        OSQ, OSQ_b = tl[0].get("OSQ", [ST], BF16)
        OG, OG_b = tl[0].get("OG", [ST], BF16)
        Tr, Tr_b = tl[1].get("Tr", [ST], F32)
        Sp, Sp_b = tl[1].get("Sp", [128], F32)
        Sbf, Sbf_b = tl[1].get("Sbf", [128], BF16)
        Bf, Bq, Bv, Bg, Ba, Bt, Bo, Bp = range(8)
        BtB = pb_t[Bt][:].bitcast(BF16)

        units = []
        for head in range(8):
            slot = next_slot()
            for s in range(NSUB):
                units.append((head, s, slot))

        def stageP(u, ui):
            head, s, slot = u
            wt = w_t[slot]
            if s == 0:
                k.dma(pool, w_st[slot], wt[:, 0:KC * 512], hwin_d[idx, head], writes=[w_b[slot]])
                k.dma(pool, w_st[slot], wt[:, KC * 512:KC * 512 + D], hwout_d[idx, head], writes=[w_b[slot]])
            hb = HB[ui % 2]
            QD, QD_b = hb["QD"]
            KDT, KDT_b = hb["KDT"]
            VT, VT_b = hb["VT"]
            AT, AT_b = hb["AT"]
            Tgs, Tgs_b = hb["Tgs"]
            ebl, ebl_b = hb["ebl"]
            ebm, ebm_b = hb["ebm"]
            eblm, eblm_b = hb["eblm"]
            lbc = lb_t[:, idx, head:head + 1]
            c1c = c1_t[:, idx, head:head + 1]
            sl = slice(s * ST, (s + 1) * ST)

            def wi(kc, which):
                base = kc * 512 + which * 128
                return wt[:, base:base + 128]

            for which, bank in ((1, Bf), (0, Bq)):
                for kc in range(KC):
                    k.op(pe, lambda e, kc=kc, which=which, bank=bank: e.matmul(
                        pb_t[bank][:], lhsT=wi(kc, which), rhs=hn_t[:, kc, sl],
                        start=(kc == 0), stop=(kc == KC - 1)),
                        reads=[w_b[slot], hn_b[s]], writes=[pb_b[bank]], inc=(kc == KC - 1))
                if which == 1:
                    yield
            yield
            for blk in range(4):
                for kc in range(KC):
                    k.op(pe, lambda e, kc=kc, blk=blk: e.matmul(
                        pb_t[Bv][:, blk * 128:(blk + 1) * 128],
                        lhsT=hn_t[:, kc, s * ST + blk * 128:s * ST + (blk + 1) * 128], rhs=wi(kc, 2),
                        start=(kc == 0), stop=(kc == KC - 1)),
                        reads=[w_b[slot], hn_b[s]], writes=[pb_b[Bv]], inc=(kc == KC - 1))
                if blk == 1:
                    yield
            yield
            for kc in range(KC):
                k.op(pe, lambda e, kc=kc: e.matmul(
                    pb_t[Bg][:], lhsT=wi(kc, 3), rhs=hn_t[:, kc, sl],
                    start=(kc == 0), stop=(kc == KC - 1)),
                    reads=[w_b[slot], hn_b[s]], writes=[pb_b[Bg]], inc=(kc == KC - 1))
        def stageC(u, ui):
            head, s, slot = u
            hb = HB[ui % 2]
            QD, QD_b = hb["QD"]
            KDT, KDT_b = hb["KDT"]
            VT, VT_b = hb["VT"]
            AT, AT_b = hb["AT"]
            Tgs, Tgs_b = hb["Tgs"]
            ebl, ebl_b = hb["ebl"]
            ebm, ebm_b = hb["ebm"]
            eblm, eblm_b = hb["eblm"]
            lbc = lb_t[:, idx, head:head + 1]
            c1c = c1_t[:, idx, head:head + 1]
            k.op(act, lambda e: e.copy(out=VT, in_=pb_t[Bv][:]), reads=[pb_b[Bv]], writes=[VT_b])
            k.op(act, lambda e: e.activation(out=Te, in_=pb_t[Bf][:], func=AF.Exp, scale=-1.0),
                 reads=[pb_b[Bf]], writes=[Te_b])
            k.op(act, lambda e: e.activation(out=Tt, in_=pb_t[Bf][:], func=AF.Exp, scale=1.0),
                 reads=[pb_b[Bf]], writes=[Tt_b])
            k.op(act, lambda e: e.activation(out=Tqe, in_=pb_t[Bq][:], func=AF.Exp, scale=-1.0),
                 reads=[pb_b[Bq]], writes=[Tqe_b])
            k.op(dve, lambda e: e.tensor_copy(out=Tqq, in_=pb_t[Bq][:]), reads=[pb_b[Bq]], writes=[Tqq_b])
            k.op(act, lambda e: e.activation(out=Tge, in_=pb_t[Bg][:], func=AF.Exp, scale=-1.0),
                 reads=[pb_b[Bg]], writes=[Tge_b])
            k.op(dve, lambda e: e.tensor_copy(out=Tgg, in_=pb_t[Bg][:]), reads=[pb_b[Bg]], writes=[Tgg_b])
            yield
            k.op(act, lambda e: e.activation(out=TL1, in_=Te, func=AF.Ln, bias=1.0, scale=1.0),
                 reads=[Te_b], writes=[TL1_b])
            k.op(act, lambda e: e.activation(out=TL2, in_=Te, func=AF.Ln, bias=1.0, scale=lbc),
                 reads=[Te_b, const_b], writes=[TL2_b])
            k.op(act, lambda e: e.activation(out=Tt, in_=Tt, func=AF.Ln, bias=1.0, scale=1.0),
                 reads=[Tt_b], writes=[Tt_b])
            yield
            k.op(act, lambda e: e.activation(out=TqL, in_=Tqe, func=AF.Ln, bias=1.0, scale=1.0),
                 reads=[Tqe_b], writes=[TqL_b])
            for blk in range(4):
                cs = slice(blk * 128, (blk + 1) * 128)
                k.op(dve, lambda e, cs=cs: e.tensor_tensor_scan(out=Tb[:, cs], data0=TL2[:, cs], data1=TL1[:, cs],
                                                                initial=0.0, op0=ALU.add, op1=ALU.subtract),
                     reads=[TL1_b, TL2_b], writes=[Tb_b])
            yield
            k.op(act, lambda e: e.activation(out=TgL, in_=Tge, func=AF.Ln, bias=1.0, scale=1.0),
                 reads=[Tge_b], writes=[TgL_b])
            k.op(act, lambda e: e.activation(out=TgL, in_=TgL, func=AF.Exp, scale=-1.0),
                 reads=[TgL_b], writes=[TgL_b])
            k.op(dve, lambda e: e.tensor_tensor(out=Tt, in0=Tt, in1=Tb, op=ALU.add),
                 reads=[Tt_b, Tb_b], writes=[Tt_b])
            yield
            Tb4 = Tb.rearrange("p (b t) -> p b t", b=4)
            bm_v, bl_v = Tb4[:, :, 63], Tb4[:, :, 127]
            k.op(dve, lambda e: e.tensor_scalar(out=cbm, in0=bm_v, scalar1=c1c, scalar2=None, op0=ALU.add),
                 reads=[Tb_b, const_b], writes=[cbm_b])
            k.op(dve, lambda e: e.tensor_scalar(out=nbm, in0=bm_v, scalar1=-1.0, scalar2=None, op0=ALU.mult),
                 reads=[Tb_b], writes=[nbm_b])
            k.op(dve, lambda e: e.tensor_tensor(out=eblm, in0=bl_v, in1=bm_v, op=ALU.subtract),
                 reads=[Tb_b], writes=[eblm_b])
            k.op(dve, lambda e: e.tensor_tensor(out=Tqd, in0=Tb, in1=TqL, op=ALU.subtract),
                 reads=[Tb_b, TqL_b], writes=[Tqd_b])
            for blk in range(4):
                cs = slice(blk * 128, (blk + 1) * 128)
                k.op(act, lambda e, cs=cs, blk=blk: e.activation(out=KD[:, cs], in_=Tt[:, cs], func=AF.Exp,
                                                                 bias=cbm[:, blk:blk + 1], scale=-1.0),
                     reads=[Tt_b, cbm_b], writes=[KD_b])
            yield
            for blk in range(4):
                cs = slice(blk * 128, (blk + 1) * 128)
                k.op(pe, lambda e, cs=cs: e.transpose(BtB[:, cs], KD[:, cs], idb_t[:]),
                     reads=[KD_b, const_b], writes=[pb_b[Bt]], inc=(blk == 3))
            for blk in range(4):
                cs = slice(blk * 128, (blk + 1) * 128)
                k.op(act, lambda e, cs=cs, blk=blk: e.activation(out=Tqd[:, cs], in_=Tqd[:, cs], func=AF.Exp,
                                                                 bias=nbm[:, blk:blk + 1], scale=1.0),
                     reads=[Tqd_b, nbm_b], writes=[Tqd_b])
            k.op(dve, lambda e: e.tensor_tensor(out=Tgs, in0=Tgg, in1=TgL, op=ALU.mult),
                 reads=[Tgg_b, TgL_b], writes=[Tgs_b])
            k.op(dve, lambda e: e.tensor_copy(out=KDT, in_=BtB[:, 0:ST]), reads=[pb_b[Bt]], writes=[KDT_b])
            yield
            k.op(act, lambda e: e.activation(out=ebl, in_=bl_v, func=AF.Exp), reads=[Tb_b], writes=[ebl_b])
            k.op(act, lambda e: e.activation(out=ebm, in_=bm_v, func=AF.Exp), reads=[Tb_b], writes=[ebm_b])
            k.op(act, lambda e: e.activation(out=eblm, in_=eblm, func=AF.Exp), reads=[eblm_b], writes=[eblm_b])
            k.op(dve, lambda e: e.tensor_tensor(out=QD, in0=Tqq, in1=Tqd, op=ALU.mult),
                 reads=[Tqq_b, Tqd_b], writes=[QD_b])
            yield
            for blk in range(4):
                cs = slice(blk * 128, (blk + 1) * 128)
                k.op(pe, lambda e, cs=cs: e.matmul(pb_t[Ba][:, cs], lhsT=KD[:, cs], rhs=QD[:, cs],
                                                   start=True, stop=True),
                     reads=[KD_b, QD_b], writes=[pb_b[Ba]], inc=(blk == 3))
            k.op(dve, lambda e: e.tensor_tensor(out=AT, in0=pb_t[Ba][:], in1=cm4_t[:], op=ALU.mult),
                 reads=[pb_b[Ba], const_b], writes=[AT_b])

        def stage2(u, ui):
            head, s, slot = u
            wt = w_t[slot]
            hb = HB[ui % 2]
            QD, QD_b = hb["QD"]
            KDT, KDT_b = hb["KDT"]
            VT, VT_b = hb["VT"]
            AT, AT_b = hb["AT"]
            Tgs, Tgs_b = hb["Tgs"]
            ebl, ebl_b = hb["ebl"]
            ebm, ebm_b = hb["ebm"]
            eblm, eblm_b = hb["eblm"]
            Sst = S_t[:, idx * 8 + head, :]
            Sb_ = S_b[idx * 8 + head]
            sl = slice(s * ST, (s + 1) * ST)
            for blk in range(4):
                cs = slice(blk * 128, (blk + 1) * 128)
                k.op(act, lambda e, blk=blk: e.activation(out=Sbf, in_=Sst, func=AF.Identity,
                                                          scale=ebm[:, blk:blk + 1]),
                     reads=[Sb_, ebm_b], writes=[Sbf_b])
                k.op(pe, lambda e, cs=cs: e.matmul(pb_t[Bo][:, cs], lhsT=VT[:, cs], rhs=AT[:, cs],
                                                   start=True, stop=False),
                     reads=[VT_b, AT_b], writes=[pb_b[Bo]], inc=False)
                k.op(pe, lambda e, cs=cs: e.matmul(pb_t[Bo][:, cs], lhsT=Sbf, rhs=QD[:, cs],
                                                   start=False, stop=True),
                     reads=[Sbf_b, QD_b], writes=[pb_b[Bo]], inc=True)
                k.op(pe, lambda e, cs=cs: e.matmul(pb_t[Bp][:, 0:128], lhsT=KDT[:, cs], rhs=VT[:, cs],
                                                   start=True, stop=True),
                     reads=[KDT_b, VT_b], writes=[pb_b[Bp]], inc=True)
                k.op(act, lambda e, blk=blk: e.activation(out=Sp, in_=Sst, func=AF.Identity,
                                                          scale=ebl[:, blk:blk + 1]),
                     reads=[Sb_, ebl_b], writes=[Sp_b])
                k.op(dve, lambda e, blk=blk: e.scalar_tensor_tensor(
                    out=Sst, in0=pb_t[Bp][:, 0:128], scalar=eblm[:, blk:blk + 1], in1=Sp,
                    op0=ALU.mult, op1=ALU.add), reads=[pb_b[Bp], eblm_b, Sp_b], writes=[Sb_])
                yield
            k.op(act, lambda e: e.activation(out=OSQ, in_=pb_t[Bo][:], func=AF.Square),
                 reads=[pb_b[Bo]], writes=[OSQ_b])
            k.op(pe, lambda e: e.matmul(pb_t[Bp][:], lhsT=ones_t[:], rhs=OSQ, start=True, stop=True),
                 reads=[OSQ_b, const_b], writes=[pb_b[Bp]], inc=True)
            k.op(act, lambda e: e.activation(out=Tr, in_=pb_t[Bp][:], func=AF.Ln, bias=EPS, scale=1.0 / 128),
                 reads=[pb_b[Bp]], writes=[Tr_b])
            k.op(act, lambda e: e.activation(out=Tr, in_=Tr, func=AF.Exp, scale=-0.5), reads=[Tr_b], writes=[Tr_b])
            k.op(dve, lambda e: e.scalar_tensor_tensor(out=To, in0=pb_t[Bo][:], scalar=pcol(HN0 + idx), in1=Tr,
                                                       op0=ALU.mult, op1=ALU.mult),
                 reads=[pb_b[Bo], Tr_b, pc_b], writes=[To_b])
            k.op(dve, lambda e: e.tensor_tensor(out=OG, in0=To, in1=Tgs, op=ALU.mult),
                 reads=[To_b, Tgs_b], writes=[OG_b])
            yield
            for dc in range(KC):
                ob = (Bp, Bo)[dc % 2]
                k.op(pe, lambda e, dc=dc, ob=ob: e.matmul(pb_t[ob][:], lhsT=wt[:, KC * 512 + dc * 128:KC * 512 + (dc + 1) * 128],
                                                          rhs=OG, start=True, stop=True),
                     reads=[w_b[slot], OG_b], writes=[pb_b[ob]], inc=True)
                k.op(dve, lambda e, dc=dc, ob=ob: e.tensor_tensor(out=h_t[:, dc, sl], in0=pb_t[ob][:],
                                                                  in1=h_t[:, dc, sl], op=ALU.add),
                     reads=[pb_b[ob], h_b[dc][s]], writes=[h_b[dc][s]])
                if dc % 2 == 1:
                    yield

        nU = len(units)

        def run_rr(gens):
            gens = [g for g in gens if g is not None]
            while gens:
                nxt = []
                for g in gens:
                    try:
                        next(g)
                        nxt.append(g)
                    except StopIteration:
                        pass
                gens = nxt

        run_rr([stageP(units[0], 0)])
        run_rr([stageC(units[0], 0), stageP(units[1], 1)])
        for ui in range(nU):
            run_rr([stage2(units[ui], ui),
                    stageC(units[ui + 1], ui + 1) if ui + 1 < nU else None,
                    stageP(units[ui + 2], ui + 2) if ui + 2 < nU else None])

    out_streams = []
    for hf in range(NHALF):
        load_half(hf)
        for kind, l in plan:
            if kind == "attn":
                attn_phase(l, hf)
            elif kind == "hgrn":
                hgrn_phase(l, hf)
            else:
                ffn_phase(l, hf)
        out_streams += store_half(hf)
    for st in out_streams:
        sp.h.wait_ge(st.sem, st.count)
    return k


def pack_inputs(inp):
    f = np.float32
    pc = np.zeros((128, NPC), f)
    norms = [inp["norm_mix"][i] for i in range(4)] + [inp["norm_ffn"][i] for i in range(4)] + [inp["norm_final"]]
    for n, gvec in enumerate(norms):
        pc[:, GAIN0 + n * 8:GAIN0 + (n + 1) * 8] = np.asarray(gvec, f).reshape(8, 128).T
    cw = np.asarray(inp["ffn_conv_w"], f)
    cb = np.asarray(inp["ffn_conv_b"], f)
    for l in range(DEPTH):
        blk = np.concatenate([cw[l], cb[l][None]], axis=0)
        blk = blk.reshape(4, 44, 128).transpose(2, 1, 0)
        pc[:, CONV0 + l * 176:CONV0 + (l + 1) * 176] = blk.reshape(128, 176)
    lg = np.asarray(inp["hgrn_lb_logits"], f)
    pc[:, LG0:LG0 + 32] = lg.reshape(4, 8, 128).transpose(2, 1, 0).reshape(128, 32)
    pc[:, HN0:HN0 + 2] = np.asarray(inp["hgrn_norm"], f).T
    sk = np.asarray(inp["attn_sinks"], f)
    for idx in range(2):
        for g in range(4):
            for hh in range(2):
                col = SK0 + idx * 8 + g * 2 + hh
                pc[0:64, col] = sk[idx, 4 * g + 0 + 2 * hh]
                pc[64:128, col] = sk[idx, 4 * g + 1 + 2 * hh]

    awi = np.asarray(inp["attn_w_in"], f)
    q = awi[:, :, 0:1024]
    kk = awi[:, :, 1024:1280].reshape(2, D, 4, 64)
    kdup = np.concatenate([kk, kk], axis=3).reshape(2, D, 512)
    v = awi[:, :, 1280:1536]
    win = np.concatenate([q, kdup, v], axis=2)
    awin = np.ascontiguousarray(win.reshape(2, KC, 128, 1792).transpose(0, 2, 1, 3)).reshape(2, 128, KC * 1792)
    awo = np.asarray(inp["attn_w_out"], f)
    awout = np.ascontiguousarray(awo.reshape(2, KC, 128, D).transpose(0, 2, 1, 3)).reshape(2, 128, KC * D)

    hwi = np.asarray(inp["hgrn_w_in"], f)
    hwi = hwi.reshape(2, KC, 128, 4, 8, 128)
    hwin = np.ascontiguousarray(hwi.transpose(0, 4, 2, 1, 3, 5)).reshape(2, 8, 128, KC * 512)
    hwo = np.asarray(inp["hgrn_w_out"], f)
    hwout = np.ascontiguousarray(hwo.reshape(2, 8, 128, D))

    fu = np.asarray(inp["ffn_w_up"], f)
    fd = np.asarray(inp["ffn_w_down"], f)
    fup = np.zeros((DEPTH, NG, 128, KC, 2, GMAX * 128), f)
    fdn = np.zeros((DEPTH, NG, 128, GMAX, D), f)
    fu_r = fu.reshape(DEPTH, KC, 128, 2, D_FF)
    for g in range(NG):
        n = G_SIZES[g]
        c0 = G_OFF[g] * 128
        fup[:, g, :, :, :, 0:n * 128] = fu_r[:, :, :, :, c0:c0 + n * 128].transpose(0, 2, 1, 3, 4)
        fdn[:, g, :, 0:n, :] = fd[:, c0:c0 + n * 128, :].reshape(DEPTH, n, 128, D).transpose(0, 2, 1, 3)
    fup = fup.reshape(DEPTH, NG, 128, KC * 2 * GMAX * 128)
    fdn = fdn.reshape(DEPTH, NG, 128, GMAX * D)
    return dict(pc=pc, awin=awin, awout=awout, hwin=hwin, hwout=hwout, fup=fup, fdn=fdn)


FULL_PLAN = [("attn", 0), ("ffn", 0), ("hgrn", 1), ("ffn", 1), ("attn", 2), ("ffn", 2), ("hgrn", 3), ("ffn", 3)]
_CACHE = {}


def run_plan(inputs, plan, final_norm=True, x_override=None, trace=False, n_cores=8):
    key = (tuple(plan), final_norm)
    if key not in _CACHE:
        _CACHE[key] = build(plan, final_norm)
    k = _CACHE[key]
    shared = pack_inputs(inputs)
    x = np.asarray(inputs["x"] if x_override is None else x_override, np.float32)
    in_maps = []
    for b in range(n_cores):
        m = dict(shared)
        m["x"] = np.ascontiguousarray(x[b])
        in_maps.append(m)
    res = run_bass_kernel_spmd(k.nc, in_maps, core_ids=list(range(n_cores)), trace=trace)
    out = np.stack([np.asarray(r["out"], np.float32) for r in res.results], axis=0)
    return out, res


def kernel(**inputs):
    out, _ = run_plan(inputs, FULL_PLAN, True)
    return out
```
